# Optimizing a Trainium2 kernel written in Bass

```python
import math
import jax, jax.numpy as jnp
from jax import lax
import numpy as np

D_MODEL = 1024
BATCH = 8
SEQ = 4096
DEPTH = 2

GRID_W = 64
CTX_LEN = 256
BLOCK = 128
HEAD_DIM = 64
ROPE_THETA = 10000.0
ROPE_AXIS_PAIRS = HEAD_DIM // 4
EPS = 1e-6
NEG = -1e30
MIX_HALF = D_MODEL // 2

A_HEADS = MIX_HALF // HEAD_DIM
A_KV_HEADS = A_HEADS // 4
A_WINDOW = 128
B_VDIM = 2 * HEAD_DIM
B_HEADS = MIX_HALF // B_VDIM
C_WIDTH = MIX_HALF
C_GROUPS = 4
C_GROUP_DIM = C_WIDTH // C_GROUPS
C_CHUNK = 128
D_NOPE = 64
D_ROPE = 64
D_VDIM = 128
D_HEADS = MIX_HALF // D_VDIM
D_Q_RANK = D_MODEL // 4
D_KV_RANK = D_MODEL // 4
FFN_HIDDEN = ((-(-8 * D_MODEL // 3)) + 255) // 256 * 256

A_Q_DIM = A_HEADS * HEAD_DIM
A_KV_DIM = A_KV_HEADS * HEAD_DIM
B_QK_DIM = B_HEADS * 2 * HEAD_DIM
B_V_TOTAL = B_HEADS * B_VDIM
EVEN_Q = A_Q_DIM + B_QK_DIM
EVEN_KV = 2 * A_KV_DIM + B_QK_DIM + B_V_TOTAL
EVEN_IN = EVEN_Q + EVEN_KV
ODD_C = 2 * C_WIDTH
ODD_KV = D_KV_RANK + D_ROPE
ODD_IN = ODD_C + D_Q_RANK + ODD_KV
UQ_DIM = D_HEADS * (D_NOPE + D_ROPE)
UKV_DIM = D_HEADS * (D_NOPE + D_VDIM)

kernel_name = "hybrid_dit_prefix_ctx_block"


def rms_norm(x, g):
    xf = x.astype(jnp.float32)
    y = xf * lax.rsqrt(jnp.mean(xf * xf, axis=-1, keepdims=True) + EPS)
    return (y * g.astype(jnp.float32)).astype(x.dtype)


def axial_rope(n_tokens):
    rows = n_tokens // GRID_W
    row = jnp.repeat(jnp.arange(rows, dtype=jnp.int32), GRID_W).astype(jnp.float32)
    col = jnp.tile(jnp.arange(GRID_W, dtype=jnp.int32), rows).astype(jnp.float32)
    inv = ROPE_THETA ** (-jnp.arange(ROPE_AXIS_PAIRS, dtype=jnp.float32) / ROPE_AXIS_PAIRS)
    ang = jnp.concatenate([row[:, None] * inv, col[:, None] * inv], axis=-1)
    return jnp.cos(ang)[:, None, :], jnp.sin(ang)[:, None, :]


def apply_rope(x, cos, sin):
    x1, x2 = jnp.split(x, 2, axis=-1)
    return jnp.concatenate([x1 * cos - x2 * sin, x2 * cos + x1 * sin], axis=-1).astype(x.dtype)


def adaln(cvec, w, b, n):
    m = jax.nn.silu(cvec) @ w[:, :n * D_MODEL] + b[:n * D_MODEL]
    return jnp.split(m[..., None, :], n, axis=-1)


def modulate(h, shift, scale):
    return h * (1.0 + scale) + shift


def swiglu(h, w_in, w_out):
    g, u = jnp.split(h @ w_in, 2, axis=-1)
    return (jax.nn.silu(g) * u) @ w_out


def sweep_query_blocks(fn, qs):
    b_, s = qs[0].shape[:2]
    nb = s // BLOCK
    blocks = tuple(q.reshape((b_, nb, BLOCK) + q.shape[2:]).swapaxes(0, 1) for q in qs)
    out = lax.map(lambda t: fn(*t), blocks)
    return out.swapaxes(0, 1).reshape((b_, s) + out.shape[3:])


def window_gqa_sink_latent(q, k, v, kc, vc, sink):
    b_, s, h, d = q.shape
    nb = s // BLOCK
    grp = h // A_KV_HEADS
    scale = d ** -0.5
    qb = q.reshape(b_, nb, BLOCK, A_KV_HEADS, grp, d).swapaxes(0, 1)
    pad = ((0, 0), (BLOCK, BLOCK), (0, 0), (0, 0))
    kp = jnp.pad(k, pad)
    vp = jnp.pad(v, pad)
    offs = jnp.arange(3 * BLOCK) - BLOCK
    qi = jnp.arange(BLOCK)
    band = jnp.abs(qi[:, None] - offs[None, :]) <= A_WINDOW
    sink_l = sink.astype(jnp.float32).reshape(A_KV_HEADS, grp, 1, 1)

    def one_block(args):
        qn, n = args
        start = n * BLOCK
        kn = lax.dynamic_slice_in_dim(kp, start, 3 * BLOCK, axis=1)
        vn = lax.dynamic_slice_in_dim(vp, start, 3 * BLOCK, axis=1)
        kpos = start + offs
        mask = band & ((kpos >= 0) & (kpos < s))[None, :]
        s_loc = jnp.einsum('bqkgd,bmkd->bkgqm', qn, kn).astype(jnp.float32) * scale
        s_loc = jnp.where(mask, s_loc, NEG)
        s_ctx = jnp.einsum('bqkgd,blkd->bkgql', qn, kc).astype(jnp.float32) * scale
        snk = jnp.broadcast_to(sink_l, s_loc.shape[:-1] + (1,))
        p = jax.nn.softmax(jnp.concatenate([s_loc, s_ctx, snk], axis=-1), axis=-1).astype(v.dtype)
        return (jnp.einsum('bkgqm,bmkd->bqkgd', p[..., :3 * BLOCK], vn)
                + jnp.einsum('bkgql,blkd->bqkgd', p[..., 3 * BLOCK:-1], vc))

    out = lax.map(one_block, (qb, jnp.arange(nb)))
    return out.swapaxes(0, 1).reshape(b_, s, h * d)


def ctx_gqa_sink(qc, kc, vc, sink):
    b_, l, h, d = qc.shape
    grp = h // A_KV_HEADS
    qg = qc.reshape(b_, l, A_KV_HEADS, grp, d)
    sc = jnp.einsum('bqkgd,blkd->bkgql', qg, kc).astype(jnp.float32) * d ** -0.5
    snk = jnp.broadcast_to(sink.astype(jnp.float32).reshape(A_KV_HEADS, grp, 1, 1), sc.shape[:-1] + (1,))
    p = jax.nn.softmax(jnp.concatenate([sc, snk], axis=-1), axis=-1)[..., :-1].astype(vc.dtype)
    return jnp.einsum('bkgql,blkd->bqkgd', p, vc).reshape(b_, l, h * d)


def diff_core(q1, q2, k1, k2, v, lam, scale):
    p1 = jax.nn.softmax(jnp.einsum('bqhd,bkhd->bhqk', q1, k1).astype(jnp.float32) * scale, axis=-1)
    p2 = jax.nn.softmax(jnp.einsum('bqhd,bkhd->bhqk', q2, k2).astype(jnp.float32) * scale, axis=-1)
    w = (p1 - lam * p2).astype(v.dtype)
    return jnp.einsum('bhqk,bkhe->bqhe', w, v)


def diff_head_norm(o, g, lam_init):
    b_, n = o.shape[:2]
    return (rms_norm(o, g) * (1.0 - lam_init)).reshape(b_, n, B_V_TOTAL)


def diff_lambda_init(layer_idx):
    return 0.8 - 0.6 * math.exp(-0.3 * layer_idx)


def even_queries(t, qn_a, qn_b):
    b_, n = t.shape[:2]
    qa = rms_norm(t[..., :A_Q_DIM].reshape(b_, n, A_HEADS, HEAD_DIM), qn_a)
    qb = rms_norm(t[..., A_Q_DIM:].reshape(b_, n, B_HEADS, 2, HEAD_DIM), qn_b)
    return qa, qb[..., 0, :], qb[..., 1, :]


def even_keys_values(t, kn_a, kn_b):
    b_, n = t.shape[:2]
    o1 = A_KV_DIM
    o2 = 2 * A_KV_DIM
    o3 = o2 + B_QK_DIM
    ka = rms_norm(t[..., :o1].reshape(b_, n, A_KV_HEADS, HEAD_DIM), kn_a)
    va = t[..., o1:o2].reshape(b_, n, A_KV_HEADS, HEAD_DIM)
    kb = rms_norm(t[..., o2:o3].reshape(b_, n, B_HEADS, 2, HEAD_DIM), kn_b)
    vb = t[..., o3:].reshape(b_, n, B_HEADS, B_VDIM)
    return ka, va, kb[..., 0, :], kb[..., 1, :], vb


def even_mixers(h, hc, cos, sin, w_in, qn_a, kn_a, sink, qn_b, kn_b,
                lq1, lk1, lq2, lk2, subln, lam_init, ctx_out):
    pl = h @ w_in
    pc = hc @ (w_in if ctx_out else w_in[:, EVEN_Q:])
    qa, q1, q2 = even_queries(pl[..., :EVEN_Q], qn_a, qn_b)
    ka, va, k1, k2, vb = even_keys_values(pl[..., EVEN_Q:], kn_a, kn_b)
    kac, vac, k1c, k2c, vbc = even_keys_values(pc[..., -EVEN_KV:], kn_a, kn_b)
    qa, q1, q2 = apply_rope(qa, cos, sin), apply_rope(q1, cos, sin), apply_rope(q2, cos, sin)
    ka, k1, k2 = apply_rope(ka, cos, sin), apply_rope(k1, cos, sin), apply_rope(k2, cos, sin)
    f32 = jnp.float32
    lam = (jnp.exp(jnp.sum(lq1.astype(f32) * lk1.astype(f32)))
           - jnp.exp(jnp.sum(lq2.astype(f32) * lk2.astype(f32))) + lam_init)
    scale_b = HEAD_DIM ** -0.5

    out_a = window_gqa_sink_latent(qa, ka, va, kac, vac, sink)
    k1_all = jnp.concatenate([k1, k1c], axis=1)
    k2_all = jnp.concatenate([k2, k2c], axis=1)
    v_all = jnp.concatenate([vb, vbc], axis=1)
    ob = sweep_query_blocks(lambda a, b: diff_core(a, b, k1_all, k2_all, v_all, lam, scale_b), (q1, q2))
    out_b = diff_head_norm(ob, subln, lam_init)
    mix = jnp.concatenate([out_a, out_b], axis=-1)
    if not ctx_out:
        return mix, None
    qac, q1c, q2c = even_queries(pc[..., :EVEN_Q], qn_a, qn_b)
    out_ac = ctx_gqa_sink(qac, kac, vac, sink)
    out_bc = diff_head_norm(diff_core(q1c, q2c, k1c, k2c, vbc, lam, scale_b), subln, lam_init)
    return mix, jnp.concatenate([out_ac, out_bc], axis=-1)


def spatial_gating(uv, v_norm_g, w_s, b_s):
    b_, n, _ = uv.shape
    u, v = jnp.split(jax.nn.gelu(uv, approximate=False), 2, axis=-1)
    v = rms_norm(v, v_norm_g).reshape(b_, n // C_CHUNK, C_CHUNK, C_GROUPS, C_GROUP_DIM)
    vs = jnp.einsum('gpq,bnqgc->bnpgc', w_s, v) + b_s.T[:, :, None]
    return u * vs.reshape(b_, n, C_WIDTH)


def mla_queries(cq, qa_norm, w_uq, qn_nope, qn_rope):
    b_, n = cq.shape[:2]
    q = (rms_norm(cq, qa_norm) @ w_uq).reshape(b_, n, D_HEADS, D_NOPE + D_ROPE)
    return rms_norm(q[..., :D_NOPE], qn_nope), rms_norm(q[..., D_NOPE:], qn_rope)


def mla_keys_values(t, kva_norm, w_ukv, kn_nope, kn_rope):
    b_, n = t.shape[:2]
    kv = (rms_norm(t[..., :D_KV_RANK], kva_norm) @ w_ukv).reshape(b_, n, D_HEADS, D_NOPE + D_VDIM)
    k_pe = rms_norm(t[..., D_KV_RANK:], kn_rope)
    return rms_norm(kv[..., :D_NOPE], kn_nope), k_pe, kv[..., D_NOPE:]


def mla_core(qn, qp, kn, kp, v, scale):
    sc = (jnp.einsum('bqhd,bkhd->bhqk', qn, kn) + jnp.einsum('bqhr,bkr->bhqk', qp, kp)).astype(jnp.float32) * scale
    p = jax.nn.softmax(sc, axis=-1).astype(v.dtype)
    return jnp.einsum('bhqk,bkhe->bqhe', p, v)


def odd_mixers(h, hc, cos, sin, w_in, c_vnorm, c_ws, c_bs, qa_norm, kva_norm, w_uq, w_ukv,
               qn_nope, kn_nope, qn_rope, kn_rope, ctx_out):
    b_, s = h.shape[:2]
    pl = h @ w_in
    pc = hc @ (w_in if ctx_out else w_in[:, ODD_C + D_Q_RANK:])
    out_c = spatial_gating(pl[..., :ODD_C], c_vnorm, c_ws, c_bs)
    qn, qp = mla_queries(pl[..., ODD_C:ODD_C + D_Q_RANK], qa_norm, w_uq, qn_nope, qn_rope)
    kn, kp, v = mla_keys_values(pl[..., ODD_C + D_Q_RANK:], kva_norm, w_ukv, kn_nope, kn_rope)
    knc, kpc, vc = mla_keys_values(pc[..., -ODD_KV:], kva_norm, w_ukv, kn_nope, kn_rope)
    qp = apply_rope(qp, cos, sin)
    kp = apply_rope(kp[:, :, None, :], cos, sin)[:, :, 0, :]
    kn_all = jnp.concatenate([kn, knc], axis=1)
    kp_all = jnp.concatenate([kp, kpc], axis=1)
    v_all = jnp.concatenate([v, vc], axis=1)
    scale = (D_NOPE + D_ROPE) ** -0.5
    od = sweep_query_blocks(lambda a, b: mla_core(a, b, kn_all, kp_all, v_all, scale), (qn, qp))
    mix = jnp.concatenate([out_c, od.reshape(b_, s, D_HEADS * D_VDIM)], axis=-1)
    if not ctx_out:
        return mix, None
    l = hc.shape[1]
    out_cc = spatial_gating(pc[..., :ODD_C], c_vnorm, c_ws, c_bs)
    qnc, qpc = mla_queries(pc[..., ODD_C:ODD_C + D_Q_RANK], qa_norm, w_uq, qn_nope, qn_rope)
    out_dc = mla_core(qnc, qpc, knc, kpc, vc, scale).reshape(b_, l, D_HEADS * D_VDIM)
    return mix, jnp.concatenate([out_cc, out_dc], axis=-1)


def setup_inputs(seed: int = 0) -> dict:
    key = jax.random.key(seed)
    ks = iter(jax.random.split(key, 48))
    f32 = jnp.float32
    D = D_MODEL
    ne = (DEPTH + 1) // 2
    no = DEPTH // 2

    def nrm(shape, scale):
        return scale * jax.random.normal(next(ks), shape, f32)

    def gain(shape):
        return 1.0 + 0.02 * jax.random.normal(next(ks), shape, f32)

    return {
        "x": nrm((BATCH, SEQ, D), 1.0),
        "c": nrm((BATCH, D), 1.0),
        "ctx": nrm((BATCH, CTX_LEN, D), 1.0),
        "c_ctx": nrm((D,), 1.0),
        "norm1_g": gain((DEPTH, D)),
        "norm2_g": gain((DEPTH, D)),
        "ada_w": nrm((DEPTH, D, 6 * D), 0.5 * D ** -0.5),
        "ada_b": nrm((DEPTH, 6 * D), 0.01),
        "mix_w_out": nrm((DEPTH, D, D), D ** -0.5),
        "ffn_w_in": nrm((DEPTH, D, 2 * FFN_HIDDEN), D ** -0.5),
        "ffn_w_out": nrm((DEPTH, FFN_HIDDEN, D), FFN_HIDDEN ** -0.5),
        "ev_w_in": nrm((ne, D, EVEN_IN), D ** -0.5),
        "ev_qnorm_a": gain((ne, HEAD_DIM)),
        "ev_knorm_a": gain((ne, HEAD_DIM)),
        "ev_sink": nrm((ne, A_HEADS), 0.5),
        "ev_qnorm_b": gain((ne, HEAD_DIM)),
        "ev_knorm_b": gain((ne, HEAD_DIM)),
        "ev_lam_q1": nrm((ne, HEAD_DIM), 0.1),
        "ev_lam_k1": nrm((ne, HEAD_DIM), 0.1),
        "ev_lam_q2": nrm((ne, HEAD_DIM), 0.1),
        "ev_lam_k2": nrm((ne, HEAD_DIM), 0.1),
        "ev_subln": gain((ne, B_VDIM)),
        "od_w_in": nrm((no, D, ODD_IN), D ** -0.5),
        "od_c_vnorm": gain((no, C_WIDTH)),
        "od_c_ws": nrm((no, C_GROUPS, C_CHUNK, C_CHUNK), C_CHUNK ** -0.5),
        "od_c_bs": gain((no, C_GROUPS, C_CHUNK)),
        "od_qa_norm": gain((no, D_Q_RANK)),
        "od_kva_norm": gain((no, D_KV_RANK)),
        "od_w_uq": nrm((no, D_Q_RANK, UQ_DIM), D_Q_RANK ** -0.5),
        "od_w_ukv": nrm((no, D_KV_RANK, UKV_DIM), D_KV_RANK ** -0.5),
        "od_qnorm_nope": gain((no, D_NOPE)),
        "od_knorm_nope": gain((no, D_NOPE)),
        "od_qnorm_rope": gain((no, D_ROPE)),
        "od_knorm_rope": gain((no, D_ROPE)),
    }


def reference(x, c, ctx, c_ctx, norm1_g, norm2_g, ada_w, ada_b, mix_w_out, ffn_w_in, ffn_w_out,
              ev_w_in, ev_qnorm_a, ev_knorm_a, ev_sink, ev_qnorm_b, ev_knorm_b,
              ev_lam_q1, ev_lam_k1, ev_lam_q2, ev_lam_k2, ev_subln,
              od_w_in, od_c_vnorm, od_c_ws, od_c_bs, od_qa_norm, od_kva_norm, od_w_uq, od_w_ukv,
              od_qnorm_nope, od_knorm_nope, od_qnorm_rope, od_knorm_rope):
    n_lat = x.shape[1]
    cos, sin = axial_rope(n_lat)
    xc = ctx
    for i in range(DEPTH):
        last = i == DEPTH - 1
        j = i // 2
        ml = adaln(c, ada_w[i], ada_b[i], 6)
        mc = adaln(c_ctx, ada_w[i], ada_b[i], 2 if last else 6)
        h = modulate(rms_norm(x, norm1_g[i]), ml[0], ml[1])
        hc = modulate(rms_norm(xc, norm1_g[i]), mc[0], mc[1])
        if i % 2 == 0:
            mix, mix_c = even_mixers(h, hc, cos, sin, ev_w_in[j], ev_qnorm_a[j], ev_knorm_a[j], ev_sink[j],
                                     ev_qnorm_b[j], ev_knorm_b[j], ev_lam_q1[j], ev_lam_k1[j],
                                     ev_lam_q2[j], ev_lam_k2[j], ev_subln[j], diff_lambda_init(i), not last)
        else:
            mix, mix_c = odd_mixers(h, hc, cos, sin, od_w_in[j], od_c_vnorm[j], od_c_ws[j], od_c_bs[j],
                                    od_qa_norm[j], od_kva_norm[j], od_w_uq[j], od_w_ukv[j],
                                    od_qnorm_nope[j], od_knorm_nope[j], od_qnorm_rope[j], od_knorm_rope[j],
                                    not last)
        x = x + ml[2] * (mix @ mix_w_out[i])
        h2 = modulate(rms_norm(x, norm2_g[i]), ml[3], ml[4])
        x = x + ml[5] * swiglu(h2, ffn_w_in[i], ffn_w_out[i])
        if not last:
            xc = xc + mc[2] * (mix_c @ mix_w_out[i])
            hc2 = modulate(rms_norm(xc, norm2_g[i]), mc[3], mc[4])
            xc = xc + mc[5] * swiglu(hc2, ffn_w_in[i], ffn_w_out[i])
    return x
```

```python
import math
import numpy as np
import concourse.bass as bass
import concourse.mybir as mybir
from concourse.bass_utils import run_bass_kernel_spmd
from contextlib import ExitStack

F32 = mybir.dt.float32
BF16 = mybir.dt.bfloat16
ALU = mybir.AluOpType
AF = mybir.ActivationFunctionType
AX = mybir.AxisListType

D = 1024
S = 4096
LCTX = 256
NTOK = S + LCTX
NT = NTOK // 128
NLAT = S // 128
FH = 2816
EPS = 1e-6
NV = 1928

V_QA, V_KA, V_QB, V_KB, V_SINK, V_LQ1, V_LK1, V_LQ2, V_LK2, V_SUBLN = 0, 64, 128, 192, 256, 264, 328, 392, 456, 520
V_CVN, V_QAN, V_KVAN, V_QNN, V_KNN, V_QNR, V_KNR = 648, 1160, 1416, 1672, 1736, 1800, 1864


class Buf:
    def __init__(self, name, excl=False):
        self.name = name
        self.w = None
        self.r = {}
        self.excl = excl
        self.parts = {}

    def part(self, key):
        p = self.parts.get(key)
        if p is None:
            p = Buf(f"{self.name}[{key}]", self.excl)
            self.parts[key] = p
        return p


class Sched:
    ENG = ["pe", "act", "dve", "pool", "sp"]
    NRING = 8

    def __init__(self, nc, stack):
        self.nc = nc
        self.prog = {e: [] for e in self.ENG}
        self.cnt = {e: 0 for e in self.ENG}
        self.last = {e: None for e in self.ENG}
        self.pend = {e: False for e in self.ENG}
        self.sem = {e: stack.enter_context(nc.semaphore(f"sem_{e}")) for e in self.ENG}
        self.known = {e: {} for e in self.ENG}
        self.ring = {}
        self.ring_i = {}
        self.ring_val = {}
        for q in ("sp", "pool", "act"):
            self.ring[q] = [stack.enter_context(nc.semaphore(f"dq_{q}{i}")) for i in range(self.NRING)]
            self.ring_i[q] = 0
            self.ring_val[q] = [0] * self.NRING
        self.n_wait = 0
        self.n_ins = 0

    def _need(self, eng, ev, rec_waits):
        sem, val = ev
        if self.known[eng].get(sem, 0) >= val:
            return
        for e in self.ENG:
            if self.sem[e] == sem and val > self.cnt[e]:
                assert self.pend[e] and val == self.cnt[e] + 1, (e, val, self.cnt[e])
                self.last[e]["inc"] = True
                self.cnt[e] += 1
                self.pend[e] = False
        self.known[eng][sem] = val
        rec_waits.append((sem, val))
        self.n_wait += 1

    def _deps(self, eng, key, reads, writes, waits, is_dma):
        for b in reads:
            if b.w is not None:
                self._need(eng, b.w, waits)
            if b.excl:
                for k, ev in list(b.r.items()):
                    if k != key:
                        self._need(eng, ev, waits)
        for b in writes:
            if b.w is not None and not (eng == "pe" and b.w[0] == self.sem["pe"]):
                self._need(eng, b.w, waits)
            for k, ev in list(b.r.items()):
                if k != key or is_dma:
                    self._need(eng, ev, waits)

    def op(self, eng, method, reads=(), writes=(), **kw):
        waits = []
        eager = kw.pop("inc", None)
        if eager is None:
            eager = (eng != "pe") or (method == "matmul" and bool(kw.get("stop")))
        self._deps(eng, eng, reads, writes, waits, False)
        rec = {"m": method, "kw": kw, "waits": waits, "inc": False, "dma": None}
        self.prog[eng].append(rec)
        self.last[eng] = rec
        if eager:
            rec["inc"] = True
            self.cnt[eng] += 1
            self.pend[eng] = False
            ev = (self.sem[eng], self.cnt[eng])
        else:
            self.pend[eng] = True
            ev = (self.sem[eng], self.cnt[eng] + 1)
        for b in reads:
            b.r[eng] = ev
        for b in writes:
            b.w = ev
            b.r = {}
        self.n_ins += 1
        return rec

    def dma(self, q, out, in_, reads=(), writes=(), **kw):
        waits = []
        i = self.ring_i[q]
        slot = i % self.NRING
        self.ring_i[q] += 1
        sem = self.ring[q][slot]
        if self.ring_val[q][slot] > 0:
            self._need(q, (sem, self.ring_val[q][slot]), waits)
        key = (q, slot)
        self._deps(q, key, reads, writes, waits, True)
        self.ring_val[q][slot] += 16
        ev = (sem, self.ring_val[q][slot])
        rec = {"m": "dma_start", "kw": dict(out=out, in_=in_, **kw), "waits": waits, "inc": False, "dma": sem}
        self.prog[q].append(rec)
        for b in reads:
            b.r[key] = ev
        for b in writes:
            b.w = ev
            b.r = {}
        self.n_ins += 1
        return ev

    def barrier(self):
        evs = []
        for e in self.ENG:
            if self.pend[e]:
                self.last[e]["inc"] = True
                self.cnt[e] += 1
                self.pend[e] = False
            if self.cnt[e] > 0:
                evs.append((self.sem[e], self.cnt[e]))
        for q in self.ring:
            for s_, v in zip(self.ring[q], self.ring_val[q]):
                if v > 0:
                    evs.append((s_, v))
        for e in self.ENG:
            waits = []
            for ev in evs:
                if ev[0] == self.sem[e]:
                    continue
                if self.known[e].get(ev[0], 0) < ev[1]:
                    self.known[e][ev[0]] = ev[1]
                    waits.append(ev)
            if waits:
                self.prog[e].append({"m": None, "kw": {}, "waits": waits, "inc": False, "dma": None})

    def finish(self):
        self.barrier()
        nc = self.nc
        engobj = {"pe": "tensor", "act": "scalar", "dve": "vector", "pool": "gpsimd", "sp": "sync"}
        with nc.Block() as block:
            for e in self.ENG:
                prog = self.prog[e]
                sem = self.sem[e]

                def body(eng, prog=prog, sem=sem):
                    for rec in prog:
                        for (s_, v) in rec["waits"]:
                            eng.wait_ge(s_, v)
                        if rec["m"] is None:
                            continue
                        ins = getattr(eng, rec["m"])(**rec["kw"])
                        if rec["dma"] is not None:
                            ins.then_inc(rec["dma"], 16)
                        elif rec["inc"]:
                            ins.then_inc(sem, 1)

                getattr(block, engobj[e])(body)


class T:
    def __init__(self, t, name, excl=False):
        self.t = t
        self.b = Buf(name, excl)

    def __getitem__(self, k):
        return self.t[k]


class Phase:
    _n = 0

    def __init__(self, nc, s):
        self.nc = nc
        self.s = s
        self.st = ExitStack()
        Phase._n += 1
        self.pfx = f"p{Phase._n}_"

    def sb(self, name, shape, dt):
        return T(self.st.enter_context(self.nc.sbuf_tensor(self.pfx + name, list(shape), dt)), name)

    def ps(self, name, shape, dt):
        return T(self.st.enter_context(self.nc.psum_tensor(self.pfx + name, list(shape), dt)), name, True)

    def ring(self, name, shape, dt, n):
        return Ring([self.sb(f"{name}{i}", shape, dt) for i in range(n)])

    def close(self):
        self.s.barrier()
        try:
            print("phase", self.pfx, "sbuf spare KB", self.nc.sbuf_bytes_remaining // 1024 // 128 if self.nc.sbuf_bytes_remaining > 4 * 1024 * 1024 else self.nc.sbuf_bytes_remaining // 1024)
        except Exception as e:
            pass
        self.st.close()


class Ring:
    def __init__(self, items):
        self.items = items
        self.i = 0

    def next(self):
        t = self.items[self.i % len(self.items)]
        self.i += 1
        return t


class DT:
    def __init__(self, ap, name):
        self.ap = ap
        self.b = Buf(name)


def build_program(dbg=(), stop_after=None):
    Phase._n = 0
    nc = bass.Bass("TRN2", target_bir_lowering=False)

    def din(name, shape, dt=F32):
        return DT(nc.dram_tensor(name, list(shape), dt, kind="ExternalInput").ap(), name)

    def dscr(name, shape, dt):
        kind = "ExternalOutput" if name in dbg else "Internal"
        return DT(nc.dram_tensor(name, list(shape), dt, kind=kind).ap(), name)

    xs = din("xs", [NTOK, D])
    cc = din("cc", [128, 8, 2])
    ada_w = din("ada_w", [2, D, 6 * D])
    ada_b = din("ada_b", [2, 1, 6 * D])
    n1g = din("n1g", [2, 1, D])
    n2g = din("n2g", [2, 1, D])
    wmix = din("wmix", [2, D, D])
    wfi = din("wfi", [2, D, 2 * FH])
    wfo = din("wfo", [2, FH, D])
    wev = din("wev", [D, 2304])
    wod = din("wod", [D, 1600])
    wuq = din("wuq", [256, 512])
    wukv = din("wukv", [256, 768])
    wsT = din("wsT", [4, 128, 128])
    bsT = din("bsT", [128, 4])
    vecs = din("vecs", [1, NV])
    ident = din("ident", [128, 128])
    rope = din("rope", [NLAT, 128, 128])
    masks = din("masks", [2, 128, 128])
    y = DT(nc.dram_tensor("y", [S, D], F32, kind="ExternalOutput").ap(), "y")

    modV = dscr("modV", [2, 2, 6 * D], F32)
    QKT0 = dscr("QKT0", [13, 128, NTOK], BF16)
    VA = dscr("VA", [NTOK, 130], BF16)
    VB = dscr("VB", [NTOK, 516], BF16)
    MIX = dscr("MIX", [NTOK, D], BF16)
    H2T = dscr("H2T", [8, 128, NTOK], BF16)
    X1A = dscr("X1A", [NTOK, D], F32)
    X1 = dscr("X1", [NTOK, D], F32)
    QKT1 = dscr("QKT1", [8, 128, NTOK], BF16)
    VD = dscr("VD", [NTOK, 516], BF16)

    with ExitStack() as gst:
        s = Sched(nc, gst)

        gp = Phase(nc, s)
        idf = gp.sb("idf", [128, 128], F32)
        idb = gp.sb("idb", [128, 128], BF16)
        nh = gp.sb("nh", [128, 64], F32)
        vecb = gp.sb("vecb", [128, NV], F32)
        s.dma("sp", idf[:], ident.ap, writes=[idf.b])
        s.op("dve", "tensor_copy", reads=[idf.b], writes=[idb.b], out=idb[:], in_=idf[:])
        s.op("pool", "memset", writes=[nh.b], ap=nh[:], constant=-0.5)
        s.dma("sp", vecb[:], vecs.ap[0, :].partition_broadcast(128), writes=[vecb.b])

        def rstd_from_ss(ph, ssT, n, width, rname):
            v = ph.sb(rname + "_v", [128, n], F32) if not hasattr(ph, "_" + rname) else getattr(ph, "_" + rname)[0]
            r = ph.sb(rname + "_r", [128, n], F32) if not hasattr(ph, "_" + rname) else getattr(ph, "_" + rname)[1]
            setattr(ph, "_" + rname, (v, r))
            s.op("dve", "tensor_scalar", reads=[ssT.b], writes=[v.b], out=v[:], in0=ssT[:, 0:n], scalar1=1.0 / width,
                 scalar2=EPS, op0=ALU.mult, op1=ALU.add)
            s.op("pool", "tensor_tensor", reads=[v.b, nh.b], writes=[r.b], out=r[:], in0=v[:], in1=nh[:, 0:n], op=ALU.pow)
            return r

        def load_bcast(ph, name, src_ap, width, src_b, q="sp"):
            t = ph.sb(name, [128, width], F32)
            s.dma(q, t[:], src_ap.partition_broadcast(128), reads=[src_b], writes=[t.b])
            return t

        def run_pipelined(gens):
            gens = list(gens)
            active = []
            i = 0
            while i < len(gens) or active:
                new = None
                if i < len(gens):
                    new = gens[i]
                    i += 1
                    try:
                        next(new)
                    except StopIteration:
                        new = None
                for g in list(active):
                    try:
                        next(g)
                    except StopIteration:
                        active.remove(g)
                if new is not None:
                    active.append(new)

        def norm_mod_transpose(ph, xt, G, Sh, pT, hT_ap, hT_b, rings, split=True):
            junk, ssr, _, hbr = rings
            jk = junk.next()
            ss = ssr.next()
            s.op("act", "activation", reads=[xt.b], writes=[jk.b, ss.b], out=jk[:], in_=xt[:], func=AF.Square,
                 accum_out=ss[:])
            r = rstd_from_ss(ph, ss, 1, D, "rs_x")
            yield
            hb = hbr.next()
            s.op("act", "activation", reads=[xt.b, r.b], writes=[hb.b], out=hb[:], in_=xt[:], func=AF.Copy,
                 scale=r[:, 0:1])
            yield
            for k in range(8):
                s.op("pe", "transpose", reads=[hb.b, idb.b], writes=[pT.b], out=pT[:, k, :],
                     in_=hb[:, k * 128:(k + 1) * 128], identity=idb[:], inc=(k == 7))
            if split:
                yield
            for k in range(8):
                s.op("act", "activation", reads=[pT.b, G.b, Sh.b], writes=[hT_b], out=hT_ap[:, k, :], in_=pT[:, k, :],
                     func=AF.Identity, scale=G[:, k:k + 1], bias=Sh[:, k:k + 1])

        def load_w_bf16(ph, name, src_ap, kchunks, ncols, src_b):
            t = ph.sb(name, [128, kchunks, ncols], BF16)
            view = src_ap.rearrange("(k p) n -> p k n", p=128)
            step = max(1, min(kchunks, 4096 // ncols)) if ncols <= 4096 else 1
            for k0 in range(0, kchunks, step):
                k1 = min(kchunks, k0 + step)
                s.dma("pool", t[:, k0:k1, :], view[:, k0:k1, :], reads=[src_b], writes=[t.b.part(k0)])
            t.kparts = [t.b.part(k0) for k0 in range(0, kchunks, step)]
            return t

        def load_w_bf16_cols(ph, name, src_ap, kchunks, ncols, src_b, cb, order):
            t = ph.sb(name, [128, kchunks, ncols], BF16)
            view = src_ap.rearrange("(k p) n -> p k n", p=128)
            for b in order:
                c0, c1 = b * cb, min(ncols, (b + 1) * cb)
                s.dma("pool", t[:, :, c0:c1], view[:, :, c0:c1], reads=[src_b], writes=[t.b.part(("c", b))])
            t.cpart = lambda col: t.b.part(("c", col // cb))
            return t

        def head_norm(ph, qf, nslots, Gt, tag):
            w = nslots * 64
            sq = ph.H_sq.next()
            s.op("act", "activation", reads=[qf.b], writes=[sq.b], out=sq[:, 0:w], in_=qf[:, 0:w], func=AF.Square)
            yield
            ssh = ph.H_ss.next()
            s.op("dve", "tensor_reduce", reads=[sq.b], writes=[ssh.b], out=ssh[:, 0:nslots],
                 in_=sq[:, 0:w].rearrange("p (h d) -> p h d", d=64), axis=AX.X, op=ALU.add)
            r = rstd_from_ss(ph, ssh, nslots, 64, "rs_h" + tag)
            qn = ph.H_qn.next()
            s.op("dve", "tensor_tensor", reads=[qf.b, r.b], writes=[qn.b],
                 out=qn[:, 0:w].rearrange("p (h d) -> p h d", d=64),
                 in0=qf[:, 0:w].rearrange("p (h d) -> p h d", d=64),
                 in1=r[:, 0:nslots].unsqueeze(2).to_broadcast([128, nslots, 64]), op=ALU.mult)
            qg = ph.H_qg.next()
            s.op("pool", "tensor_tensor", reads=[qn.b, Gt.b], writes=[qg.b], out=qg[:, 0:w], in0=qn[:, 0:w],
                 in1=Gt[:, 0:w], op=ALU.mult)
            return qg

        def rope_apply(ph, src_ap, dst_ap, n, rp, reads, dst_b):
            t1 = ph.R_t1.next()
            t2 = ph.R_t2.next()
            t1v = t1[:, 0:n * 64].rearrange("p (h d) -> p h d", d=64)
            t2v = t2[:, 0:n * 64].rearrange("p (h d) -> p h d", d=64)
            s.op("dve", "tensor_tensor", reads=reads + [rp.b], writes=[t1.b], out=t1v, in0=src_ap,
                 in1=rp[:, 0:64].unsqueeze(1).to_broadcast([128, n, 64]), op=ALU.mult)
            s.op("pool", "tensor_tensor", reads=reads + [rp.b], writes=[t2.b.part(0)], out=t2v[:, :, 0:32], in0=src_ap[:, :, 32:64],
                 in1=rp[:, 64:96].unsqueeze(1).to_broadcast([128, n, 32]), op=ALU.mult)
            s.op("dve", "tensor_tensor", reads=reads + [rp.b], writes=[t2.b.part(1)], out=t2v[:, :, 32:64], in0=src_ap[:, :, 0:32],
                 in1=rp[:, 96:128].unsqueeze(1).to_broadcast([128, n, 32]), op=ALU.mult)
            s.op("dve", "tensor_tensor", reads=[t1.b, t2.b.part(0), t2.b.part(1)], writes=[dst_b], out=dst_ap, in0=t1v, in1=t2v,
                 op=ALU.add)

        ph = Phase(nc, s)
        cct = ph.sb("cct", [128, 8, 2], F32)
        sc = ph.sb("sc", [128, 8, 2], F32)
        ones2 = ph.sb("ones2", [1, 2], F32)
        brow = ph.sb("brow", [1, 6 * D], F32)
        mrow = ph.sb("mrow", [2, 6 * D], F32)
        wring = ph.ring("adaw", [128, 8, 512], F32, 2)
        pm = [ph.ps(f"pm{i}", [2, 512], F32) for i in range(2)]
        s.dma("sp", cct[:], cc.ap, writes=[cct.b])
        s.op("act", "activation", reads=[cct.b], writes=[sc.b], out=sc[:], in_=cct[:], func=AF.Silu)
        s.op("dve", "memset", writes=[ones2.b], ap=ones2[:], constant=1.0)
        import os
        NL_ = int(os.environ.get("KD_NL", "2"))
        NC_ = int(os.environ.get("KD_NC", "12"))
        for l in range(NL_):
            s.dma("sp", brow[:], ada_b.ap[l], writes=[brow.b])
            for c in range(NC_):
                wt = wring.next()
                s.dma("sp", wt[:], ada_w.ap[l][:, c * 512:(c + 1) * 512].rearrange("(k p) n -> p k n", p=128),
                      writes=[wt.b])
                p = pm[c % 2]
                for k in range(8):
                    s.op("pe", "matmul", reads=[sc.b, wt.b], writes=[p.b], out=p[:], lhsT=sc[:, k, :], rhs=wt[:, k, :],
                         start=(k == 0), stop=False)
                s.op("pe", "matmul", reads=[ones2.b, brow.b], writes=[p.b], out=p[:], lhsT=ones2[:],
                     rhs=brow[:, c * 512:(c + 1) * 512], start=False, stop=True)
                s.op("dve", "tensor_copy", reads=[p.b], writes=[mrow.b], out=mrow[:, c * 512:(c + 1) * 512], in_=p[:])
            s.dma("sp", modV.ap[l], mrow[:], reads=[mrow.b], writes=[modV.b])
        ph.close()
        if stop_after == "P0":
            s.finish()
            return nc

        def mod_tiles(ph, l, v, idx_shift, idx_scale, gdt, tag):
            def colload(name, ap1d, b):
                t = ph.sb(name, [128, 8], F32)
                s.dma("sp", t[:], ap1d.rearrange("(k p) -> p k", p=128), reads=[b], writes=[t.b],
                      allow_slow_non_contiguous=True)
                return t
            sh = colload(f"sh{tag}", modV.ap[l, v, idx_shift * D:(idx_shift + 1) * D], modV.b)
            scl = colload(f"scl{tag}", modV.ap[l, v, idx_scale * D:(idx_scale + 1) * D], modV.b)
            gg = colload(f"gg{tag}", gdt.ap[l, 0, :], gdt.b)
            s.op("dve", "scalar_tensor_tensor", reads=[scl.b, gg.b], writes=[scl.b], out=scl[:], in0=scl[:], scalar=1.0,
                 in1=gg[:], op0=ALU.add, op1=ALU.mult)
            return scl, sh

        def norm_rings(ph):
            return (ph.ring("junk", [128, D], BF16, 1), ph.ring("ssx", [128, 1], F32, 2),
                    None, ph.ring("hb", [128, D], BF16, 2))

        def head_rings(ph, w):
            ph.H_sq = ph.ring("hsq", [128, w], F32, 2)
            ph.H_ss = ph.ring("hss", [128, 32], F32, 2)
            ph.H_qn = ph.ring("hqn", [128, w], F32, 1)
            ph.H_qg = ph.ring("hqg", [128, w], F32, 2)
            ph.R_t1 = ph.ring("rt1", [128, w], F32, 1)
            ph.R_t2 = ph.ring("rt2", [128, w], F32, 1)

        ph = Phase(nc, s)
        W = load_w_bf16(ph, "wev", wev.ap, 8, 2304, wev.b)
        GL, SL = mod_tiles(ph, 0, 0, 0, 1, n1g, "l")
        GC, SC = mod_tiles(ph, 0, 1, 0, 1, n1g, "c")
        G26 = ph.sb("g26", [128, 26 * 64], F32)
        g26v = G26[:, :].rearrange("p (h d) -> p h d", d=64)
        for (a, b_, off) in ((0, 8, V_QA), (8, 16, V_QB), (16, 18, V_KA), (18, 26, V_KB)):
            s.op("dve", "tensor_copy", reads=[vecb.b], writes=[G26.b], out=g26v[:, a:b_, :],
                 in_=vecb[:, off:off + 64].unsqueeze(1).to_broadcast([128, b_ - a, 64]))
        nr = norm_rings(ph)
        head_rings(ph, 1664)
        xr = ph.ring("x", [128, D], F32, 4)
        rpr = ph.ring("rp", [128, 128], F32, 4)
        hTr = ph.ring("hT", [128, 8, 128], BF16, 2)
        qfr = ph.ring("qf", [128, 1664], F32, 2)
        qbr = ph.ring("qb", [128, 1664], BF16, 2)
        var = ph.ring("va", [128, 2, 65], BF16, 2)
        vbr = ph.ring("vb", [128, 4, 129], BF16, 2)
        for t_ in var.items + vbr.items:
            s.op("pool", "memset", writes=[t_.b], ap=t_[:], constant=1.0)
        qkst = ph.ring("qkst", [128, 13, 256], BF16, 2)
        pT = ph.ps("pT", [128, 8, 128], BF16)
        pO = ph.ps("pO", [128, 2560], F32)
        pQ1 = ph.ps("pQ1", [128, 8, 128], BF16)
        pQ2 = ph.ps("pQ2", [128, 8, 128], BF16)
        def tile_l0p1(t):
            g0 = (t // 2) * 2
            ti = t - g0
            ntile_g = 2
            st_ = qkst.items[(t // 2) % 2]
            isctx = t >= NLAT
            xt = xr.next()
            s.dma("sp", xt[:], xs.ap[t * 128:(t + 1) * 128, :], writes=[xt.b])
            yield
            hT = hTr.next()
            yield from norm_mod_transpose(ph, xt, GC if isctx else GL, SC if isctx else SL, pT, hT[:], hT.b, nr)
            yield
            for k in range(8):
                for c in range(5):
                    n0, n1 = c * 512, min(2304, (c + 1) * 512)
                    s.op("pe", "matmul", reads=[hT.b] + W.kparts, writes=[pO.b], out=pO[:, n0:n1],
                         lhsT=hT[:, k, :], rhs=W[:, k, n0:n1], start=(k == 0), stop=(k == 7))
            if not isctx:
                rp = rpr.next()
                s.dma("sp", rp[:], rope.ap[t], writes=[rp.b])
            yield
            qf = qfr.next()
            s.op("act", "copy", reads=[pO.b], writes=[qf.b], out=qf[:], in_=pO[:, 0:1664])
            va = var.next()
            vb = vbr.next()
            s.op("act", "copy", reads=[pO.b], writes=[va.b], out=va[:, :, 0:64],
                 in_=pO[:, 1664:1792].rearrange("p (h d) -> p h d", d=64))
            s.op("act", "copy", reads=[pO.b], writes=[vb.b], out=vb[:, :, 0:128],
                 in_=pO[:, 1792:2304].rearrange("p (h d) -> p h d", d=128))
            s.dma("sp", VA.ap[t * 128:(t + 1) * 128, :], va[:].rearrange("p h d -> p (h d)"), reads=[va.b],
                  writes=[VA.b.part(t)])
            s.dma("sp", VB.ap[t * 128:(t + 1) * 128, :], vb[:].rearrange("p h d -> p (h d)"), reads=[vb.b],
                  writes=[VB.b.part(t)])
            qg = yield from head_norm(ph, qf, 26, G26, "0")
            yield
            qb = qbr.next()
            if isctx:
                s.op("dve", "tensor_copy", reads=[qg.b], writes=[qb.b], out=qb[:], in_=qg[:, 0:1664])
            else:
                rope_apply(ph, qg[:, 0:1664].rearrange("p (h d) -> p h d", d=64),
                           qb[:, :].rearrange("p (h d) -> p h d", d=64), 26, rp, [qg.b], qb.b)
            yield
            for j in range(13):
                pq = pQ1 if j < 8 else pQ2
                s.op("pe", "transpose", reads=[qb.b, idb.b], writes=[pq.b], out=pq[:, j % 8, :],
                     in_=qb[:, j * 128:(j + 1) * 128], identity=idb[:], inc=(j in (7, 12)))
            yield
            s.op("act", "copy", reads=[pQ1.b], writes=[st_.b], out=st_[:, 0:8, ti * 128:(ti + 1) * 128], in_=pQ1[:])
            s.op("act", "copy", reads=[pQ2.b], writes=[st_.b], out=st_[:, 8:13, ti * 128:(ti + 1) * 128],
                 in_=pQ2[:, 0:5, :])
            if ti == ntile_g - 1:
                ntk = ntile_g * 128
                s.dma("sp", QKT0.ap[:, :, g0 * 128:g0 * 128 + ntk].rearrange("j p t -> p j t"), st_[:, :, 0:ntk],
                      reads=[st_.b], writes=[QKT0.b])

        run_pipelined(tile_l0p1(t) for t in range(NT))
        ph.close()
        if stop_after == "L0P1":
            s.finish()
            return nc

        def attn_pipeline(ph, units, nkmax):
            PT = [ph.sb(f"PT{i}", [128, nkmax, 512], BF16) for i in range(2)]
            Sb = [ph.ps(f"S{i}", [128, 512], F32) for i in range(4)]
            acc = ph.ps("acc", [128, 4, 512], F32)
            si = 0
            n = len(units)
            for ui in range(n + 1):
                cur = units[ui] if ui < n else None
                prev = units[ui - 1] if ui > 0 else None
                if cur is not None and cur.get("pre") is not None:
                    cur["pre"]()
                nk = max(len(cur["keys"]) if cur else 0, len(prev["keys"]) if prev else 0)
                for kt in range(nk):
                    if cur is not None and kt < len(cur["keys"]):
                        key = cur["keys"][kt]
                        nq = cur["nq"]
                        sbk = Sb[si % 4]
                        si += 1
                        so = sbk[:, 0:nq]
                        if len(cur["qT"].shape) == 3:
                            so = so.rearrange("p (g q) -> p g q", q=128)
                        s.op("pe", "matmul", reads=cur["qb"] + key["kb"], writes=[sbk.b], out=so, lhsT=key["kT"],
                             rhs=cur["qT"], start=True, stop=True)
                        pt = PT[ui % 2]
                        ptb = pt.b.part(kt)
                        s.op("act", "activation", reads=[sbk.b], writes=[ptb], out=pt[:, kt, 0:nq], in_=sbk[:, 0:nq],
                             func=AF.Exp, scale=cur["scale"])
                        if key.get("mask") is not None:
                            mk = key["mask"]
                            s.op("dve", "tensor_tensor", reads=[ptb, mk.b], writes=[ptb],
                                 out=pt[:, kt, 0:nq].rearrange("p (g q) -> p g q", q=128),
                                 in0=pt[:, kt, 0:nq].rearrange("p (g q) -> p g q", q=128),
                                 in1=mk[:, :].unsqueeze(1).to_broadcast([128, nq // 128, 128]), op=ALU.mult)
                    if prev is not None and kt < len(prev["keys"]):
                        key = prev["keys"][kt]
                        pt = PT[(ui - 1) % 2]
                        ptb = pt.b.part(kt)
                        nkp = len(prev["keys"])
                        vd1 = prev["vd1"]
                        for j in range(prev["nq"] // 128):
                            s.op("pe", "matmul", reads=[ptb] + key["vb"], writes=[acc.b], out=acc[:, j, 0:vd1],
                                 lhsT=pt[:, kt, j * 128:(j + 1) * 128], rhs=key["v"], start=(kt == 0), stop=(kt == nkp - 1))
                if prev is not None:
                    prev["fin"](prev, acc)

        mprev_f = gp.sb("mprevf", [128, 128], F32)
        mnext_f = gp.sb("mnextf", [128, 128], F32)
        mprev = gp.sb("mprev", [128, 128], BF16)
        mnext = gp.sb("mnext", [128, 128], BF16)
        s.dma("sp", mprev_f[:], masks.ap[0], writes=[mprev_f.b])
        s.dma("sp", mnext_f[:], masks.ap[1], writes=[mnext_f.b])
        s.op("dve", "tensor_copy", reads=[mprev_f.b], writes=[mprev.b], out=mprev[:], in_=mprev_f[:])
        s.op("dve", "tensor_copy", reads=[mnext_f.b], writes=[mnext.b], out=mnext[:], in_=mnext_f[:])

        ph = Phase(nc, s)
        QA = ph.sb("QA", [128, 4, NTOK], BF16)
        KAz = [ph.sb(f"KAz{i}", [128, NTOK], BF16) for i in range(2)]
        s.op("pool", "memset", writes=[KAz[0].b], ap=KAz[0][64:128, :], constant=0.0)
        s.op("pool", "memset", writes=[KAz[1].b], ap=KAz[1][0:64, :], constant=0.0)
        VAs = ph.sb("VAs", [128, NT, 130], BF16)
        s.dma("sp", QA[:], QKT0.ap[0:4].rearrange("j p t -> p j t"), reads=[QKT0.b], writes=[QA.b])
        s.dma("sp", KAz[0][0:64, :], QKT0.ap[8, 0:64, :], reads=[QKT0.b], writes=[KAz[0].b])
        s.dma("sp", KAz[1][64:128, :], QKT0.ap[8, 64:128, :], reads=[QKT0.b], writes=[KAz[1].b])
        s.dma("sp", VAs[:], VA.ap.rearrange("(k p) e -> p k e", p=128), reads=[VA.b.part(t) for t in range(NT)],
              writes=[VAs.b])
        esink = ph.sb("esink", [128, 8], F32)
        s.op("act", "activation", reads=[vecb.b], writes=[esink.b], out=esink[:], in_=vecb[:, V_SINK:V_SINK + 8], func=AF.Exp)
        zr = ph.ring("za", [128, 4], F32, 2)
        rzr = ph.ring("rza", [128, 4], F32, 2)
        obr = ph.ring("oba", [128, 4, 64], BF16, 3)

        def fin_a(u, acc):
            kvh, n = u["kvh"], u["n"]
            z = zr.next()
            s.op("dve", "tensor_tensor", reads=[acc.b, esink.b], writes=[z.b], out=z[:], in0=acc[:, :, 64],
                 in1=esink[:, kvh * 4:(kvh + 1) * 4], op=ALU.add)
            rz = rzr.next()
            s.op("dve", "reciprocal", reads=[z.b], writes=[rz.b], out=rz[:], in_=z[:])
            ob = obr.next()
            s.op("dve", "tensor_tensor", reads=[acc.b, rz.b], writes=[ob.b], out=ob[:], in0=acc[:, :, 0:64],
                 in1=rz[:, :].unsqueeze(2).to_broadcast([128, 4, 64]), op=ALU.mult)
            s.dma("sp", MIX.ap[n * 128:(n + 1) * 128, kvh * 256:(kvh + 1) * 256], ob[:].rearrange("p g d -> p (g d)"),
                  reads=[ob.b], writes=[MIX.b.part(("a", n, kvh))])

        units = []
        for n in range(NT):
            if n < NLAT:
                kl = []
                if n > 0:
                    kl.append((n - 1, mprev))
                kl.append((n, None))
                if n < NLAT - 1:
                    kl.append((n + 1, mnext))
                kl += [(32, None), (33, None)]
            else:
                kl = [(32, None), (33, None)]
            for kvh in range(2):
                keys = [dict(kT=KAz[kvh][:, kt * 128:(kt + 1) * 128], kb=[KAz[kvh].b], v=VAs[:, kt, kvh * 65:(kvh + 1) * 65],
                             vb=[VAs.b], mask=mk) for (kt, mk) in kl]
                units.append(dict(qT=QA[:, :, n * 128:(n + 1) * 128], qb=[QA.b], keys=keys, nq=512, vd1=65,
                                  scale=0.125, fin=fin_a, kvh=kvh, n=n))
        attn_pipeline(ph, units, 5)
        ph.close()
        if stop_after == "L0P2A":
            s.finish()
            return nc

        lam_init0 = 0.8 - 0.6 * math.exp(-0.3 * 0)
        ph = Phase(nc, s)
        lt = ph.sb("lt", [128, 128], F32)
        lsum = ph.sb("lsum", [128, 2], F32)
        lexp = ph.sb("lexp", [128, 2], F32)
        nlam = ph.sb("nlam", [128, 1], F32)
        s.op("dve", "tensor_tensor", reads=[vecb.b], writes=[lt.b], out=lt[:, 0:64], in0=vecb[:, V_LQ1:V_LQ1 + 64],
             in1=vecb[:, V_LK1:V_LK1 + 64], op=ALU.mult)
        s.op("dve", "tensor_tensor", reads=[vecb.b], writes=[lt.b], out=lt[:, 64:128], in0=vecb[:, V_LQ2:V_LQ2 + 64],
             in1=vecb[:, V_LK2:V_LK2 + 64], op=ALU.mult)
        s.op("dve", "tensor_reduce", reads=[lt.b], writes=[lsum.b], out=lsum[:],
             in_=lt[:, :].rearrange("p (a d) -> p a d", d=64), axis=AX.X, op=ALU.add)
        s.op("act", "activation", reads=[lsum.b], writes=[lexp.b], out=lexp[:], in_=lsum[:], func=AF.Exp)
        s.op("dve", "tensor_tensor", reads=[lexp.b], writes=[nlam.b], out=nlam[:], in0=lexp[:, 1:2], in1=lexp[:, 0:1],
             op=ALU.subtract)
        s.op("dve", "tensor_scalar", reads=[nlam.b], writes=[nlam.b], out=nlam[:], in0=nlam[:], scalar1=-lam_init0,
             scalar2=None, op0=ALU.add)
        subl = ph.sb("subl", [128, 128], F32)
        s.op("dve", "tensor_scalar", reads=[vecb.b], writes=[subl.b], out=subl[:], in0=vecb[:, V_SUBLN:V_SUBLN + 128],
             scalar1=1.0 - lam_init0, scalar2=None, op0=ALU.mult)
        QBr = ph.ring("QB", [128, NTOK], BF16, 2)
        KBz = [[ph.sb(f"KBz{i}_{j}", [128, NTOK], BF16) for j in range(2)] for i in range(2)]
        for i in range(2):
            s.op("pool", "memset", writes=[KBz[i][0].b], ap=KBz[i][0][64:128, :], constant=0.0)
            s.op("pool", "memset", writes=[KBz[i][1].b], ap=KBz[i][1][0:64, :], constant=0.0)
        VBr = ph.ring("VBs", [128, NT, 129], BF16, 2)
        o1r = ph.ring("o1", [128, 4, 128], F32, 2)
        z1r = ph.ring("zb", [128, 4], F32, 4)
        tbr = ph.ring("tb", [128, 4, 128], F32, 2)
        obbr = ph.ring("obb", [128, 4, 128], F32, 2)
        sqbr = ph.ring("sqb", [128, 4, 128], F32, 1)
        ssbr = ph.ring("ssb", [128, 4], F32, 2)
        onr = ph.ring("onb", [128, 4, 128], F32, 1)
        outbr = ph.ring("outb", [128, 4, 128], BF16, 3)
        state = {}

        def fin_b(u, acc):
            h, q0, nj, sidx = u["h"], u["q0"], u["nq"] // 128, u["s"]
            rz = z1r.next()
            s.op("dve", "reciprocal", reads=[acc.b], writes=[rz.b], out=rz[:, 0:nj], in_=acc[:, 0:nj, 128])
            if sidx == 0:
                o1 = o1r.next()
                s.op("dve", "tensor_tensor", reads=[acc.b, rz.b], writes=[o1.b], out=o1[:, 0:nj, :], in0=acc[:, 0:nj, 0:128],
                     in1=rz[:, 0:nj].unsqueeze(2).to_broadcast([128, nj, 128]), op=ALU.mult)
                state["o1"] = o1
                return
            o1 = state["o1"]
            rzl = z1r.next()
            s.op("dve", "tensor_scalar", reads=[rz.b, nlam.b], writes=[rzl.b], out=rzl[:, 0:nj], in0=rz[:, 0:nj],
                 scalar1=nlam[:, 0:1], scalar2=None, op0=ALU.mult)
            tb = tbr.next()
            s.op("dve", "tensor_tensor", reads=[acc.b, rzl.b], writes=[tb.b], out=tb[:, 0:nj, :], in0=acc[:, 0:nj, 0:128],
                 in1=rzl[:, 0:nj].unsqueeze(2).to_broadcast([128, nj, 128]), op=ALU.mult)
            ob = obbr.next()
            s.op("pool", "tensor_tensor", reads=[tb.b, o1.b], writes=[ob.b], out=ob[:, 0:nj, :], in0=tb[:, 0:nj, :],
                 in1=o1[:, 0:nj, :], op=ALU.add)
            sq = sqbr.next()
            s.op("pool", "tensor_tensor", reads=[ob.b], writes=[sq.b], out=sq[:, 0:nj, :], in0=ob[:, 0:nj, :],
                 in1=ob[:, 0:nj, :], op=ALU.mult)
            ss = ssbr.next()
            s.op("dve", "tensor_reduce", reads=[sq.b], writes=[ss.b], out=ss[:, 0:nj], in_=sq[:, 0:nj, :], axis=AX.X,
                 op=ALU.add)
            r = rstd_from_ss(ph, ss, nj, 128, "rs_b%d" % nj)
            on = onr.next()
            s.op("dve", "tensor_tensor", reads=[ob.b, r.b], writes=[on.b], out=on[:, 0:nj, :], in0=ob[:, 0:nj, :],
                 in1=r[:, 0:nj].unsqueeze(2).to_broadcast([128, nj, 128]), op=ALU.mult)
            out = outbr.next()
            s.op("pool", "tensor_tensor", reads=[on.b, subl.b], writes=[out.b], out=out[:, 0:nj, :], in0=on[:, 0:nj, :],
                 in1=subl[:, :].unsqueeze(1).to_broadcast([128, nj, 128]), op=ALU.mult)
            s.dma("sp", MIX.ap[q0:q0 + nj * 128, 512 + h * 128:512 + (h + 1) * 128].rearrange("(j p) d -> p j d", p=128),
                  out[:, 0:nj, :], reads=[out.b], writes=[MIX.b.part(("b", h, q0))])

        def load_b(h):
            Qh, Kz, Vh = QBr.items[h % 2], KBz[h % 2], VBr.items[h % 2]
            s.dma("sp", Qh[:], QKT0.ap[4 + h], reads=[QKT0.b], writes=[Qh.b])
            s.dma("sp", Kz[0][0:64, :], QKT0.ap[9 + h, 0:64, :], reads=[QKT0.b], writes=[Kz[0].b])
            s.dma("sp", Kz[1][64:128, :], QKT0.ap[9 + h, 64:128, :], reads=[QKT0.b], writes=[Kz[1].b])
            s.dma("sp", Vh[:], VB.ap.rearrange("(k p) (h e) -> p k h e", p=128, e=129)[:, :, h, :],
                  reads=[VB.b.part(t) for t in range(NT)], writes=[Vh.b])

        units = []
        for h in range(4):
            Qh, Kz, Vh = QBr.items[h % 2], KBz[h % 2], VBr.items[h % 2]
            blocks = [(qb * 512, 512, list(range(NT))) for qb in range(8)] + [(S, 256, [32, 33])]
            first = len(units)
            for (q0, nq, kl) in blocks:
                for sidx in range(2):
                    Kh = Kz[sidx]
                    keys = [dict(kT=Kh[:, kt * 128:(kt + 1) * 128], kb=[Kh.b], v=Vh[:, kt, :], vb=[Vh.b], mask=None)
                            for kt in kl]
                    units.append(dict(qT=Qh[:, q0:q0 + nq], qb=[Qh.b], keys=keys, nq=nq, vd1=129, scale=0.125,
                                      fin=fin_b, h=h, q0=q0, s=sidx))
            if h == 0:
                units[first]["pre"] = (lambda: load_b(0))
            if h < 3:
                units[first + 1]["pre"] = (lambda hh=h + 1: load_b(hh))
        attn_pipeline(ph, units, NT)
        ph.close()
        if stop_after == "L0P2B":
            s.finish()
            return nc

        def phase3a(l, Xin, Xout, ntiles):
            ph = Phase(nc, s)
            Wm = load_w_bf16(ph, "wmix", wmix.ap[l], 8, D, wmix.b)
            G2L, S2L = mod_tiles(ph, l, 0, 3, 4, n2g, "l")
            gateL = load_bcast(ph, "gateL", modV.ap[l, 0, 2 * D:3 * D], D, modV.b)
            if ntiles > NLAT:
                G2C, S2C = mod_tiles(ph, l, 1, 3, 4, n2g, "c")
                gateC = load_bcast(ph, "gateC", modV.ap[l, 1, 2 * D:3 * D], D, modV.b)
            nr = norm_rings(ph)
            xr = ph.ring("x", [128, D], F32, 5)
            mr = ph.ring("mx", [128, D], BF16, 3)
            mTr = ph.ring("mT", [128, 8, 128], BF16, 2)
            tmr = ph.ring("tm", [128, D], F32, 1)
            x1r = ph.ring("x1", [128, D], F32, 4)
            hst = ph.ring("hst", [128, 8, 512], BF16, 2)
            pT = ph.ps("pT", [128, 8, 128], BF16)
            pT2 = ph.ps("pT2", [128, 8, 128], BF16)
            pP = ph.ps("pP", [128, D], F32)
            def tile_p3a(t):
                g0 = (t // 4) * 4
                ti = t - g0
                ntile_g = min(ntiles, g0 + 4) - g0
                st_ = hst.items[(t // 4) % 2]
                isctx = t >= NLAT
                xt = xr.next()
                s.dma("sp", xt[:], Xin.ap[t * 128:(t + 1) * 128, :], reads=[Xin.b.part(t)], writes=[xt.b])
                mx = mr.next()
                s.dma("sp", mx[:], MIX.ap[t * 128:(t + 1) * 128, :], reads=[MIX.b] + list(MIX.b.parts.values()),
                      writes=[mx.b])
                yield
                for k in range(8):
                    s.op("pe", "transpose", reads=[mx.b, idb.b], writes=[pT.b], out=pT[:, k, :],
                         in_=mx[:, k * 128:(k + 1) * 128], identity=idb[:], inc=(k == 7))
                yield
                mT = mTr.next()
                s.op("act", "copy", reads=[pT.b], writes=[mT.b], out=mT[:], in_=pT[:])
                yield
                for k in range(8):
                    for c in range(2):
                        s.op("pe", "matmul", reads=[mT.b] + Wm.kparts, writes=[pP.b], out=pP[:, c * 512:(c + 1) * 512],
                             lhsT=mT[:, k, :], rhs=Wm[:, k, c * 512:(c + 1) * 512], start=(k == 0), stop=(k == 7))
                yield
                tm = tmr.next()
                gate = gateC if isctx else gateL
                s.op("dve", "tensor_tensor", reads=[pP.b, gate.b], writes=[tm.b], out=tm[:], in0=pP[:], in1=gate[:],
                     op=ALU.mult)
                x1 = x1r.next()
                s.op("pool", "tensor_tensor", reads=[tm.b, xt.b], writes=[x1.b], out=x1[:], in0=tm[:], in1=xt[:],
                     op=ALU.add)
                s.dma("sp", Xout.ap[t * 128:(t + 1) * 128, :], x1[:], reads=[x1.b], writes=[Xout.b.part(t)])
                yield
                yield from norm_mod_transpose(ph, x1, G2C if isctx else G2L, S2C if isctx else S2L, pT2,
                                              st_[:, :, ti * 128:(ti + 1) * 128], st_.b, nr)
                if ti == ntile_g - 1:
                    ntk = ntile_g * 128
                    s.dma("sp", H2T.ap[:, :, g0 * 128:g0 * 128 + ntk].rearrange("j p t -> p j t"), st_[:, :, 0:ntk],
                          reads=[st_.b], writes=[H2T.b.part(g0)])

            run_pipelined(tile_p3a(t) for t in range(ntiles))
            ph.close()

        def phase3b(l, Xin, Xout, ntiles):
            ph = Phase(nc, s)
            Wi = load_w_bf16_cols(ph, "wfi", wfi.ap[l], 8, 2 * FH, wfi.b, 512, [0, 5, 6, 1, 7, 2, 8, 3, 9, 4, 10])
            Wo = load_w_bf16(ph, "wfo", wfo.ap[l], 22, D, wfo.b)
            gate = load_bcast(ph, "gate", modV.ap[l, 0, 5 * D:6 * D], D, modV.b)
            h2r = ph.ring("h2", [128, 8, 512], BF16, 1)
            actT = ph.sb("actT", [128, 22, 512], BF16)
            sgr = ph.ring("sg", [128, 512], F32, 2)
            xr = ph.ring("x", [128, D], F32, 1)
            x2r = ph.ring("x2", [128, D], F32, 2)
            pG = [ph.ps(f"pG{i}", [128, 512], F32) for i in range(2)]
            pU = [ph.ps(f"pU{i}", [128, 512], F32) for i in range(2)]
            pY = [ph.ps(f"pY{i}", [128, D], F32) for i in range(2)]
            yi = 0
            for g0 in range(0, ntiles, 4):
                tiles = list(range(g0, min(ntiles, g0 + 4)))
                ntk = len(tiles) * 128
                if tiles[0] >= NLAT:
                    s.dma("sp", gate[:], modV.ap[l, 1, 5 * D:6 * D].partition_broadcast(128), reads=[modV.b],
                          writes=[gate.b])
                h2 = h2r.next()
                if g0 == 0:
                    s.dma("sp", h2[:, :, 0:ntk], H2T.ap[:, :, 0:ntk].rearrange("j p t -> p j t"),
                          reads=[H2T.b.part(0)], writes=[h2.b])
                for j in range(22):
                    pg, pu = pG[j % 2], pU[j % 2]
                    for k in range(8):
                        s.op("pe", "matmul", reads=[h2.b, Wi.cpart(j * 128)], writes=[pg.b], out=pg[:, 0:ntk],
                             lhsT=Wi[:, k, j * 128:(j + 1) * 128], rhs=h2[:, k, 0:ntk], start=(k == 0), stop=(k == 7))
                    for k in range(8):
                        s.op("pe", "matmul", reads=[h2.b, Wi.cpart(FH + j * 128)], writes=[pu.b], out=pu[:, 0:ntk],
                             lhsT=Wi[:, k, FH + j * 128:FH + (j + 1) * 128], rhs=h2[:, k, 0:ntk], start=(k == 0),
                             stop=(k == 7))
                    sg = sgr.next()
                    s.op("act", "activation", reads=[pg.b], writes=[sg.b], out=sg[:, 0:ntk], in_=pg[:, 0:ntk], func=AF.Silu)
                    s.op("dve", "tensor_tensor", reads=[pu.b, sg.b], writes=[actT.b.part(j)], out=actT[:, j, 0:ntk],
                         in0=pu[:, 0:ntk], in1=sg[:, 0:ntk], op=ALU.mult)
                if g0 + 4 < ntiles:
                    n0_ = g0 + 4
                    ntk2 = (min(ntiles, n0_ + 4) - n0_) * 128
                    s.dma("act", h2[:, :, 0:ntk2], H2T.ap[:, :, n0_ * 128:n0_ * 128 + ntk2].rearrange("j p t -> p j t"),
                          reads=[H2T.b.part(n0_)], writes=[h2.b])
                for ti, t in enumerate(tiles):
                    isctx = t >= NLAT
                    py = pY[yi % 2]
                    yi += 1
                    for c in range(2):
                        for j in range(22):
                            s.op("pe", "matmul", reads=[actT.b.part(j)] + Wo.kparts, writes=[py.b],
                                 out=py[:, c * 512:(c + 1) * 512], lhsT=actT[:, j, ti * 128:(ti + 1) * 128],
                                 rhs=Wo[:, j, c * 512:(c + 1) * 512], start=(j == 0), stop=(j == 21))
                    xt = xr.next()
                    s.dma("sp", xt[:], Xin.ap[t * 128:(t + 1) * 128, :], reads=[Xin.b.part(t)], writes=[xt.b])
                    x2 = x2r.next()
                    s.op("dve", "tensor_tensor", reads=[py.b, gate.b], writes=[x2.b], out=x2[:], in0=py[:], in1=gate[:],
                         op=ALU.mult)
                    s.op("pool", "tensor_tensor", reads=[x2.b, xt.b], writes=[x2.b], out=x2[:], in0=x2[:], in1=xt[:],
                         op=ALU.add)
                    s.dma("sp", Xout.ap[t * 128:(t + 1) * 128, :], x2[:], reads=[x2.b], writes=[Xout.b.part(t)])
            ph.close()

        phase3a(0, xs, X1A, NT)
        if stop_after == "L0P3A":
            s.finish()
            return nc
        phase3b(0, X1A, X1, NT)
        if stop_after == "L0":
            s.finish()
            return nc

        ph = Phase(nc, s)
        W1 = load_w_bf16(ph, "wod", wod.ap, 8, 1600, wod.b)
        Wq = load_w_bf16(ph, "wuq", wuq.ap, 2, 512, wuq.b)
        Wkv = load_w_bf16(ph, "wukv", wukv.ap, 2, 768, wukv.b)
        Ws = ph.sb("ws", [128, 4, 128], BF16)
        s.dma("pool", Ws[:], wsT.ap.rearrange("g q p -> q g p"), reads=[wsT.b], writes=[Ws.b])
        bs = ph.sb("bs", [128, 4], F32)
        s.dma("sp", bs[:], bsT.ap, writes=[bs.b])
        GL, SL = mod_tiles(ph, 1, 0, 0, 1, n1g, "l")
        GC, SC = mod_tiles(ph, 1, 1, 0, 1, n1g, "c")
        G13 = ph.sb("g13", [128, 13 * 64], F32)
        g13v = G13[:, :].rearrange("p (h d) -> p h d", d=64)
        g13q = G13[:, 0:512].rearrange("p (h t d) -> p h t d", t=2, d=64)
        s.op("dve", "tensor_copy", reads=[vecb.b], writes=[G13.b], out=g13q[:, :, 0, :],
             in_=vecb[:, V_QNN:V_QNN + 64].unsqueeze(1).to_broadcast([128, 4, 64]))
        s.op("dve", "tensor_copy", reads=[vecb.b], writes=[G13.b], out=g13q[:, :, 1, :],
             in_=vecb[:, V_QNR:V_QNR + 64].unsqueeze(1).to_broadcast([128, 4, 64]))
        s.op("dve", "tensor_copy", reads=[vecb.b], writes=[G13.b], out=g13v[:, 8:12, :],
             in_=vecb[:, V_KNN:V_KNN + 64].unsqueeze(1).to_broadcast([128, 4, 64]))
        s.op("dve", "tensor_copy", reads=[vecb.b], writes=[G13.b], out=g13v[:, 12:13, :],
             in_=vecb[:, V_KNR:V_KNR + 64].unsqueeze(1).to_broadcast([128, 1, 64]))
        nr = norm_rings(ph)
        head_rings(ph, 832)
        xr = ph.ring("x", [128, D], F32, 4)
        rpr = ph.ring("rp", [128, 128], F32, 9)
        hTr = ph.ring("hT", [128, 8, 128], BF16, 2)
        glr = ph.ring("gl", [128, D], F32, 3)
        vsqr = ph.ring("vsq", [128, 512], F32, 1)
        ssvr = ph.ring("ssv", [128, 2], F32, 2)
        vnr = ph.ring("vn", [128, 512], BF16, 2)
        mcr = ph.ring("mc", [128, 512], BF16, 2)
        cqfr = ph.ring("cqf", [128, 512], F32, 2)
        cnr = ph.ring("cn", [128, 512], F32, 1)
        cnbr = ph.ring("cnb", [128, 512], BF16, 2)
        cTr = ph.ring("cT", [128, 4, 128], BF16, 2)
        hqr = ph.ring("hq", [128, 832], F32, 2)
        qcbr = ph.ring("qcb", [128, 4, 128], BF16, 2)
        kcbr = ph.ring("kcb", [128, 4, 128], BF16, 2)
        kper = ph.ring("kpe", [128, 64], F32, 7)
        kprr = ph.ring("kpr", [128, 64], F32, 2)
        vdr = ph.ring("vd", [128, 4, 129], BF16, 2)
        for t_ in vdr.items:
            s.op("pool", "memset", writes=[t_.b], ap=t_[:], constant=1.0)
        qkst = ph.ring("qkst", [128, 8, 512], BF16, 2)
        pT = ph.ps("pT", [128, 8, 128], BF16)
        pO1 = ph.ps("pO1", [128, 2048], F32)
        pQ = ph.ps("pQ", [128, 512], F32)
        pKV = ph.ps("pKV", [128, 1024], F32)
        def cps(g):
            return pO1[:, 1600 + g * 128:1728 + g * 128] if g < 3 else pKV[:, 768:896]

        def cpb(g):
            return pO1.b if g < 3 else pKV.b

        def tile_l1p1(t):
            g0 = (t // 4) * 4
            ti = t - g0
            ntile_g = min(NT, g0 + 4) - g0
            st_ = qkst.items[(t // 4) % 2]
            isctx = t >= NLAT
            xt = xr.next()
            s.dma("sp", xt[:], X1.ap[t * 128:(t + 1) * 128, :], reads=[X1.b.part(t)], writes=[xt.b])
            yield
            hT = hTr.next()
            yield from norm_mod_transpose(ph, xt, GC if isctx else GL, SC if isctx else SL, pT, hT[:], hT.b, nr, split=False)
            yield
            chunks = [(1280, 1536), (1536, 1600)] if isctx else [(0, 512), (512, 1024), (1024, 1536), (1536, 1600)]
            for k in range(8):
                for (n0, n1) in chunks:
                    s.op("pe", "matmul", reads=[hT.b] + W1.kparts, writes=[pO1.b], out=pO1[:, n0:n1],
                         lhsT=hT[:, k, :], rhs=W1[:, k, n0:n1], start=(k == 0), stop=(k == 7))
            if not isctx:
                rp = rpr.next()
                s.dma("sp", rp[:], rope.ap[t], writes=[rp.b])
            yield
            cqf = cqfr.next()
            c0 = 256 if isctx else 0
            s.op("act", "copy", reads=[pO1.b], writes=[cqf.b], out=cqf[:, c0:512], in_=pO1[:, 1024 + c0:1536])
            kpe = kper.next()
            s.op("act", "copy", reads=[pO1.b], writes=[kpe.b], out=kpe[:], in_=pO1[:, 1536:1600])
            if not isctx:
                gl = glr.next()
                s.op("act", "activation", reads=[pO1.b], writes=[gl.b], out=gl[:], in_=pO1[:, 0:1024], func=AF.Gelu)
            yield
            if not isctx:
                vsq = vsqr.next()
                s.op("pool", "tensor_tensor", reads=[gl.b], writes=[vsq.b], out=vsq[:], in0=gl[:, 512:1024],
                     in1=gl[:, 512:1024], op=ALU.mult)
                ssv = ssvr.next()
                s.op("dve", "tensor_reduce", reads=[vsq.b], writes=[ssv.b], out=ssv[:, 0:1], in_=vsq[:], axis=AX.X,
                     op=ALU.add)
                r = rstd_from_ss(ph, ssv, 1, 512, "rs_v")
                vn = vnr.next()
                s.op("dve", "scalar_tensor_tensor", reads=[gl.b, r.b, vecb.b], writes=[vn.b], out=vn[:],
                     in0=gl[:, 512:1024], scalar=r[:, 0:1], in1=vecb[:, V_CVN:V_CVN + 512], op0=ALU.mult, op1=ALU.mult)
            vsq = vsqr.next()
            s.op("pool", "tensor_tensor", reads=[cqf.b], writes=[vsq.b], out=vsq[:, c0:512], in0=cqf[:, c0:512],
                 in1=cqf[:, c0:512], op=ALU.mult)
            ssv = ssvr.next()
            na = (512 - c0) // 256
            s.op("dve", "tensor_reduce", reads=[vsq.b], writes=[ssv.b], out=ssv[:, 2 - na:2],
                 in_=vsq[:, c0:512].rearrange("p (a d) -> p a d", d=256), axis=AX.X, op=ALU.add)
            if isctx:
                s.op("dve", "tensor_copy", reads=[ssv.b], writes=[ssv.b], out=ssv[:, 0:1], in_=ssv[:, 1:2])
            r = rstd_from_ss(ph, ssv, 2, 256, "rs_c")
            cn = cnr.next()
            s.op("dve", "tensor_tensor", reads=[cqf.b, r.b], writes=[cn.b],
                 out=cn[:, c0:512].rearrange("p (a d) -> p a d", d=256),
                 in0=cqf[:, c0:512].rearrange("p (a d) -> p a d", d=256),
                 in1=r[:, 2 - na:2].unsqueeze(2).to_broadcast([128, na, 256]), op=ALU.mult)
            cnb = cnbr.next()
            s.op("pool", "tensor_tensor", reads=[cn.b, vecb.b], writes=[cnb.b], out=cnb[:, c0:512], in0=cn[:, c0:512],
                 in1=vecb[:, V_QAN + c0:V_QAN + 512], op=ALU.mult)
            yield
            if not isctx:
                for g in range(4):
                    s.op("pe", "matmul", reads=[vn.b, Ws.b], writes=[cpb(g)], out=cps(g),
                         lhsT=Ws[:, g, :], rhs=vn[:, g * 128:(g + 1) * 128], start=True, stop=True)
            kc0 = c0 // 128
            for k in range(kc0, 4):
                s.op("pe", "transpose", reads=[cnb.b, idb.b], writes=[pT.b], out=pT[:, k, :],
                     in_=cnb[:, k * 128:(k + 1) * 128], identity=idb[:], inc=(k == 3))
            cT = cTr.next()
            s.op("act", "copy", reads=[pT.b], writes=[cT.b], out=cT[:, kc0:4, :], in_=pT[:, kc0:4, :])
            if not isctx:
                mc_ = mcr.next()
                for g in range(4):
                    s.op("dve", "scalar_tensor_tensor", reads=[cpb(g), bs.b, gl.b], writes=[mc_.b],
                         out=mc_[:, g * 128:(g + 1) * 128], in0=cps(g), scalar=bs[:, g:g + 1],
                         in1=gl[:, g * 128:(g + 1) * 128], op0=ALU.add, op1=ALU.mult)
                s.dma("sp", MIX.ap[t * 128:(t + 1) * 128, 0:512], mc_[:], reads=[mc_.b], writes=[MIX.b.part(("c", t))])
            yield
            if not isctx:
                for k in range(2):
                    s.op("pe", "matmul", reads=[cT.b] + Wq.kparts, writes=[pQ.b], out=pQ[:], lhsT=cT[:, k, :],
                         rhs=Wq[:, k, :], start=(k == 0), stop=(k == 1))
            for k in range(2):
                for (n0, n1) in ((0, 512), (512, 768)):
                    s.op("pe", "matmul", reads=[cT.b] + Wkv.kparts, writes=[pKV.b], out=pKV[:, n0:n1],
                         lhsT=cT[:, 2 + k, :], rhs=Wkv[:, k, n0:n1], start=(k == 0), stop=(k == 1))
            yield
            vd = vdr.next()
            s.op("act", "copy", reads=[pKV.b], writes=[vd.b], out=vd[:, :, 0:128],
                 in_=pKV[:, 256:768].rearrange("p (h d) -> p h d", d=128))
            s.dma("sp", VD.ap[t * 128:(t + 1) * 128, :], vd[:].rearrange("p h d -> p (h d)"), reads=[vd.b],
                  writes=[VD.b.part(t)])
            hq = hqr.next()
            if not isctx:
                s.op("act", "copy", reads=[pQ.b], writes=[hq.b], out=hq[:, 0:512], in_=pQ[:])
            else:
                s.op("pool", "memset", writes=[hq.b], ap=hq[:, 0:512], constant=1.0)
            s.op("act", "copy", reads=[pKV.b], writes=[hq.b], out=hq[:, 512:768], in_=pKV[:, 0:256])
            s.op("act", "copy", reads=[kpe.b], writes=[hq.b], out=hq[:, 768:832], in_=kpe[:])
            qg = yield from head_norm(ph, hq, 13, G13, "1")
            yield
            qgv = qg[:, 0:832].rearrange("p (h d) -> p h d", d=64)
            kcb = kcbr.next()
            s.op("dve", "tensor_copy", reads=[qg.b], writes=[kcb.b], out=kcb[:, :, 0:64], in_=qgv[:, 8:12, :])
            if isctx:
                s.op("dve", "tensor_copy", reads=[qg.b], writes=[kcb.b], out=kcb[:, :, 64:128],
                     in_=qgv[:, 12:13, :].to_broadcast([128, 4, 64]))
            else:
                kpr = kprr.next()
                rope_apply(ph, qgv[:, 12:13, :], kpr[:, :].unsqueeze(1), 1, rp, [qg.b], kpr.b)
                s.op("dve", "tensor_copy", reads=[kpr.b], writes=[kcb.b], out=kcb[:, :, 64:128],
                     in_=kpr[:, :].unsqueeze(1).to_broadcast([128, 4, 64]))
                qcb = qcbr.next()
                qg4 = qg[:, 0:512].rearrange("p (h t d) -> p h t d", t=2, d=64)
                s.op("pool", "tensor_copy", reads=[qg.b], writes=[qcb.b], out=qcb[:, :, 0:64], in_=qg4[:, :, 0, :])
                rope_apply(ph, qg4[:, :, 1, :], qcb[:, :, 64:128], 4, rp, [qg.b], qcb.b)
            yield
            if not isctx:
                for h in range(4):
                    s.op("pe", "transpose", reads=[qcb.b, idb.b], writes=[pT.b], out=pT[:, h, :], in_=qcb[:, h, :],
                         identity=idb[:])
            for h in range(4):
                s.op("pe", "transpose", reads=[kcb.b, idb.b], writes=[pT.b], out=pT[:, 4 + h, :], in_=kcb[:, h, :],
                     identity=idb[:], inc=(h == 3))
            b0 = 4 if isctx else 0
            s.op("act", "copy", reads=[pT.b], writes=[st_.b], out=st_[:, b0:8, ti * 128:(ti + 1) * 128], in_=pT[:, b0:8, :])
            if ti == ntile_g - 1:
                ntk = ntile_g * 128
                s.dma("sp", QKT1.ap[b0:8, :, g0 * 128:g0 * 128 + ntk].rearrange("j p t -> p j t"), st_[:, b0:8, 0:ntk],
                      reads=[st_.b], writes=[QKT1.b])

        run_pipelined(tile_l1p1(t) for t in range(NT))
        ph.close()
        if stop_after == "L1P1":
            s.finish()
            return nc

        ph = Phase(nc, s)
        QDr = ph.ring("QD", [128, S], BF16, 2)
        KDr = ph.ring("KD", [128, NTOK], BF16, 2)
        VDr = ph.ring("VDs", [128, NT, 129], BF16, 2)
        zdr = ph.ring("zd", [128, 4], F32, 2)
        odr = ph.ring("od", [128, 4, 128], BF16, 3)

        def fin_d(u, acc):
            h, q0 = u["h"], u["q0"]
            rz = zdr.next()
            s.op("dve", "reciprocal", reads=[acc.b], writes=[rz.b], out=rz[:], in_=acc[:, :, 128])
            od = odr.next()
            s.op("dve", "tensor_tensor", reads=[acc.b, rz.b], writes=[od.b], out=od[:], in0=acc[:, :, 0:128],
                 in1=rz[:, :].unsqueeze(2).to_broadcast([128, 4, 128]), op=ALU.mult)
            s.dma("sp", MIX.ap[q0:q0 + 512, 512 + h * 128:512 + (h + 1) * 128].rearrange("(j p) d -> p j d", p=128),
                  od[:], reads=[od.b], writes=[MIX.b.part(("d", h, q0))])

        def load_d(h):
            Qh, Kh, Vh = QDr.items[h % 2], KDr.items[h % 2], VDr.items[h % 2]
            s.dma("sp", Qh[:], QKT1.ap[h, :, 0:S], reads=[QKT1.b], writes=[Qh.b])
            s.dma("sp", Kh[:], QKT1.ap[4 + h], reads=[QKT1.b], writes=[Kh.b])
            s.dma("sp", Vh[:], VD.ap.rearrange("(k p) (h e) -> p k h e", p=128, e=129)[:, :, h, :],
                  reads=[VD.b.part(t) for t in range(NT)], writes=[Vh.b])

        units = []
        for h in range(4):
            Qh, Kh, Vh = QDr.items[h % 2], KDr.items[h % 2], VDr.items[h % 2]
            first = len(units)
            for qb in range(8):
                keys = [dict(kT=Kh[:, kt * 128:(kt + 1) * 128], kb=[Kh.b], v=Vh[:, kt, :], vb=[Vh.b], mask=None)
                        for kt in range(NT)]
                units.append(dict(qT=Qh[:, qb * 512:(qb + 1) * 512], qb=[Qh.b], keys=keys, nq=512, vd1=129,
                                  scale=128.0 ** -0.5, fin=fin_d, h=h, q0=qb * 512))
            if h == 0:
                units[first]["pre"] = (lambda: load_d(0))
            if h < 3:
                units[first + 1]["pre"] = (lambda hh=h + 1: load_d(hh))
        attn_pipeline(ph, units, NT)
        ph.close()
        if stop_after == "L1P2":
            s.finish()
            return nc

        phase3a(1, X1, X1A, NLAT)
        phase3b(1, X1A, y, NLAT)
        gp.close()
        s.finish()
        print("program: instructions", s.n_ins, "waits", s.n_wait)
    return nc


def _rope_tables():
    rows = S // 64
    row = np.repeat(np.arange(rows, dtype=np.int32), 64).astype(np.float32)
    col = np.tile(np.arange(64, dtype=np.int32), rows).astype(np.float32)
    inv = (np.float32(10000.0) ** (-np.arange(16, dtype=np.float32) / np.float32(16))).astype(np.float32)
    ang = np.concatenate([row[:, None] * inv, col[:, None] * inv], axis=-1).astype(np.float32)
    c, sn = np.cos(ang).astype(np.float32), np.sin(ang).astype(np.float32)
    tab = np.concatenate([c, c, -sn, sn], axis=-1)
    return np.ascontiguousarray(tab.reshape(NLAT, 128, 128))


def _shared_inputs(inp):
    f = lambda a: np.ascontiguousarray(np.asarray(a, dtype=np.float32))
    ev = f(inp["ev_w_in"])[0]
    aq = ev[:, 0:512].reshape(D, 8, 64)
    aq_p = np.stack([aq[:, [j, 4 + j], :] for j in range(4)], axis=1).reshape(D, 512)
    wev = np.concatenate([aq_p, ev[:, 512:1024], ev[:, 1024:1152], ev[:, 1280:1792], ev[:, 1152:1280], ev[:, 1792:2304]], axis=1)
    ukv = f(inp["od_w_ukv"])[0].reshape(256, 4, 192)
    wukv = np.concatenate([ukv[:, :, 0:64].reshape(256, 256), ukv[:, :, 64:192].reshape(256, 512)], axis=1)
    vec = np.concatenate([
        f(inp["ev_qnorm_a"])[0], f(inp["ev_knorm_a"])[0], f(inp["ev_qnorm_b"])[0], f(inp["ev_knorm_b"])[0],
        f(inp["ev_sink"])[0], f(inp["ev_lam_q1"])[0], f(inp["ev_lam_k1"])[0], f(inp["ev_lam_q2"])[0], f(inp["ev_lam_k2"])[0],
        f(inp["ev_subln"])[0], f(inp["od_c_vnorm"])[0], f(inp["od_qa_norm"])[0], f(inp["od_kva_norm"])[0],
        f(inp["od_qnorm_nope"])[0], f(inp["od_knorm_nope"])[0], f(inp["od_qnorm_rope"])[0], f(inp["od_knorm_rope"])[0]])
    assert vec.shape[0] == NV
    j = np.arange(128)[:, None]
    i = np.arange(128)[None, :]
    masks = np.stack([(j >= i), (j <= i)]).astype(np.float32)
    return {
        "ada_w": f(inp["ada_w"]), "ada_b": f(inp["ada_b"]).reshape(2, 1, 6 * D),
        "n1g": f(inp["norm1_g"]).reshape(2, 1, D), "n2g": f(inp["norm2_g"]).reshape(2, 1, D),
        "wmix": f(inp["mix_w_out"]), "wfi": f(inp["ffn_w_in"]), "wfo": f(inp["ffn_w_out"]),
        "wev": f(wev), "wod": f(inp["od_w_in"])[0], "wuq": f(inp["od_w_uq"])[0], "wukv": f(wukv),
        "wsT": f(np.transpose(f(inp["od_c_ws"])[0], (0, 2, 1))), "bsT": f(f(inp["od_c_bs"])[0].T),
        "vecs": f(vec.reshape(1, NV)), "ident": np.eye(128, dtype=np.float32), "rope": _rope_tables(), "masks": masks,
    }


def make_in_maps(inp):
    shared = _shared_inputs(inp)
    x = np.asarray(inp["x"], dtype=np.float32)
    ctx = np.asarray(inp["ctx"], dtype=np.float32)
    c = np.asarray(inp["c"], dtype=np.float32)
    c_ctx = np.asarray(inp["c_ctx"], dtype=np.float32)
    maps = []
    for b in range(x.shape[0]):
        m = dict(shared)
        m["xs"] = np.ascontiguousarray(np.concatenate([x[b], ctx[b]], axis=0))
        cc = np.stack([c[b], c_ctx], axis=-1).reshape(8, 128, 2).transpose(1, 0, 2)
        m["cc"] = np.ascontiguousarray(cc)
        maps.append(m)
    return maps


_NC_CACHE = {}


def kernel(**inputs):
    if "nc" not in _NC_CACHE:
        _NC_CACHE["nc"] = build_program()
    nc = _NC_CACHE["nc"]
    in_maps = make_in_maps(inputs)
    res = run_bass_kernel_spmd(nc, in_maps, core_ids=list(range(len(in_maps))))
    return np.stack([np.asarray(r["y"], dtype=np.float32) for r in res.results], axis=0)
```

```python
import math
import numpy as np
import concourse.bass as bass
import concourse.mybir as mybir
from concourse.bass_utils import run_bass_kernel_spmd
from contextlib import ExitStack

F32 = mybir.dt.float32
BF16 = mybir.dt.bfloat16
ALU = mybir.AluOpType
AF = mybir.ActivationFunctionType
AX = mybir.AxisListType

D = 1024
S = 4096
LCTX = 256
NTOK = S + LCTX
NT = NTOK // 128
NLAT = S // 128
FH = 2816
EPS = 1e-6
NV = 1928

V_QA, V_KA, V_QB, V_KB, V_SINK, V_LQ1, V_LK1, V_LQ2, V_LK2, V_SUBLN = 0, 64, 128, 192, 256, 264, 328, 392, 456, 520
V_CVN, V_QAN, V_KVAN, V_QNN, V_KNN, V_QNR, V_KNR = 648, 1160, 1416, 1672, 1736, 1800, 1864


class Buf:
    def __init__(self, name, excl=False):
        self.name = name
        self.w = None
        self.r = {}
        self.excl = excl
        self.parts = {}

    def part(self, key):
        p = self.parts.get(key)
        if p is None:
            p = Buf(f"{self.name}[{key}]", self.excl)
            self.parts[key] = p
        return p


class Sched:
    ENG = ["pe", "act", "dve", "pool", "sp"]
    NRING = 8

    def __init__(self, nc, stack):
        self.nc = nc
        self.prog = {e: [] for e in self.ENG}
        self.cnt = {e: 0 for e in self.ENG}
        self.last = {e: None for e in self.ENG}
        self.pend = {e: False for e in self.ENG}
        self.sem = {e: stack.enter_context(nc.semaphore(f"sem_{e}")) for e in self.ENG}
        self.known = {e: {} for e in self.ENG}
        self.ring = {}
        self.ring_i = {}
        self.ring_val = {}
        for q in ("sp", "pool", "act"):
            self.ring[q] = [stack.enter_context(nc.semaphore(f"dq_{q}{i}")) for i in range(self.NRING)]
            self.ring_i[q] = 0
            self.ring_val[q] = [0] * self.NRING
        self.n_wait = 0
        self.n_ins = 0

    def _need(self, eng, ev, rec_waits):
        sem, val = ev
        if self.known[eng].get(sem, 0) >= val:
            return
        for e in self.ENG:
            if self.sem[e] == sem and val > self.cnt[e]:
                assert self.pend[e] and val == self.cnt[e] + 1, (e, val, self.cnt[e])
                self.last[e]["inc"] = True
                self.cnt[e] += 1
                self.pend[e] = False
        self.known[eng][sem] = val
        rec_waits.append((sem, val))
        self.n_wait += 1

    def _deps(self, eng, key, reads, writes, waits, is_dma):
        for b in reads:
            if b.w is not None:
                self._need(eng, b.w, waits)
            if b.excl:
                for k, ev in list(b.r.items()):
                    if k != key:
                        self._need(eng, ev, waits)
        for b in writes:
            if b.w is not None and not (eng == "pe" and b.w[0] == self.sem["pe"]):
                self._need(eng, b.w, waits)
            for k, ev in list(b.r.items()):
                if k != key or is_dma:
                    self._need(eng, ev, waits)

    def op(self, eng, method, reads=(), writes=(), **kw):
        waits = []
        eager = kw.pop("inc", None)
        if eager is None:
            eager = (eng != "pe") or (method == "matmul" and bool(kw.get("stop")))
        self._deps(eng, eng, reads, writes, waits, False)
        rec = {"m": method, "kw": kw, "waits": waits, "inc": False, "dma": None}
        self.prog[eng].append(rec)
        self.last[eng] = rec
        if eager:
            rec["inc"] = True
            self.cnt[eng] += 1
            self.pend[eng] = False
            ev = (self.sem[eng], self.cnt[eng])
        else:
            self.pend[eng] = True
            ev = (self.sem[eng], self.cnt[eng] + 1)
        for b in reads:
            b.r[eng] = ev
        for b in writes:
            b.w = ev
            b.r = {}
        self.n_ins += 1
        return rec

    def dma(self, q, out, in_, reads=(), writes=(), **kw):
        waits = []
        i = self.ring_i[q]
        slot = i % self.NRING
        self.ring_i[q] += 1
        sem = self.ring[q][slot]
        if self.ring_val[q][slot] > 0:
            self._need(q, (sem, self.ring_val[q][slot]), waits)
        key = (q, slot)
        self._deps(q, key, reads, writes, waits, True)
        self.ring_val[q][slot] += 16
        ev = (sem, self.ring_val[q][slot])
        rec = {"m": "dma_start", "kw": dict(out=out, in_=in_, **kw), "waits": waits, "inc": False, "dma": sem}
        self.prog[q].append(rec)
        for b in reads:
            b.r[key] = ev
        for b in writes:
            b.w = ev
            b.r = {}
        self.n_ins += 1
        return ev

    def barrier(self):
        evs = []
        for e in self.ENG:
            if self.pend[e]:
                self.last[e]["inc"] = True
                self.cnt[e] += 1
                self.pend[e] = False
            if self.cnt[e] > 0:
                evs.append((self.sem[e], self.cnt[e]))
        for q in self.ring:
            for s_, v in zip(self.ring[q], self.ring_val[q]):
                if v > 0:
                    evs.append((s_, v))
        for e in self.ENG:
            waits = []
            for ev in evs:
                if ev[0] == self.sem[e]:
                    continue
                if self.known[e].get(ev[0], 0) < ev[1]:
                    self.known[e][ev[0]] = ev[1]
                    waits.append(ev)
            if waits:
                self.prog[e].append({"m": None, "kw": {}, "waits": waits, "inc": False, "dma": None})

    def finish(self):
        self.barrier()
        nc = self.nc
        engobj = {"pe": "tensor", "act": "scalar", "dve": "vector", "pool": "gpsimd", "sp": "sync"}
        with nc.Block() as block:
            for e in self.ENG:
                prog = self.prog[e]
                sem = self.sem[e]

                def body(eng, prog=prog, sem=sem):
                    for rec in prog:
                        for (s_, v) in rec["waits"]:
                            eng.wait_ge(s_, v)
                        if rec["m"] is None:
                            continue
                        ins = getattr(eng, rec["m"])(**rec["kw"])
                        if rec["dma"] is not None:
                            ins.then_inc(rec["dma"], 16)
                        elif rec["inc"]:
                            ins.then_inc(sem, 1)

                getattr(block, engobj[e])(body)


class T:
    def __init__(self, t, name, excl=False):
        self.t = t
        self.b = Buf(name, excl)

    def __getitem__(self, k):
        return self.t[k]


class Phase:
    _n = 0

    def __init__(self, nc, s):
        self.nc = nc
        self.s = s
        self.st = ExitStack()
        Phase._n += 1
        self.pfx = f"p{Phase._n}_"

    def sb(self, name, shape, dt):
        return T(self.st.enter_context(self.nc.sbuf_tensor(self.pfx + name, list(shape), dt)), name)

    def ps(self, name, shape, dt):
        return T(self.st.enter_context(self.nc.psum_tensor(self.pfx + name, list(shape), dt)), name, True)

    def ring(self, name, shape, dt, n):
        return Ring([self.sb(f"{name}{i}", shape, dt) for i in range(n)])

    def close(self):
        self.s.barrier()
        try:
            print("phase", self.pfx, "sbuf spare KB", self.nc.sbuf_bytes_remaining // 1024 // 128 if self.nc.sbuf_bytes_remaining > 4 * 1024 * 1024 else self.nc.sbuf_bytes_remaining // 1024)
        except Exception as e:
            pass
        self.st.close()


class Ring:
    def __init__(self, items):
        self.items = items
        self.i = 0

    def next(self):
        t = self.items[self.i % len(self.items)]
        self.i += 1
        return t


class DT:
    def __init__(self, ap, name):
        self.ap = ap
        self.b = Buf(name)


def build_program(dbg=(), stop_after=None):
    Phase._n = 0
    nc = bass.Bass("TRN2", target_bir_lowering=False)

    def din(name, shape, dt=F32):
        return DT(nc.dram_tensor(name, list(shape), dt, kind="ExternalInput").ap(), name)

    def dscr(name, shape, dt):
        kind = "ExternalOutput" if name in dbg else "Internal"
        return DT(nc.dram_tensor(name, list(shape), dt, kind=kind).ap(), name)

    xs = din("xs", [NTOK, D])
    cc = din("cc", [128, 8, 2])
    ada_w = din("ada_w", [2, D, 6 * D])
    ada_b = din("ada_b", [2, 1, 6 * D])
    n1g = din("n1g", [2, 1, D])
    n2g = din("n2g", [2, 1, D])
    wmix = din("wmix", [2, D, D])
    wfi = din("wfi", [2, D, 2 * FH])
    wfo = din("wfo", [2, FH, D])
    wev = din("wev", [D, 2304])
    wod = din("wod", [D, 1600])
    wuq = din("wuq", [256, 512])
    wukv = din("wukv", [256, 768])
    wsT = din("wsT", [4, 128, 128])
    bsT = din("bsT", [128, 4])
    vecs = din("vecs", [1, NV])
    ident = din("ident", [128, 128])
    rope = din("rope", [NLAT, 128, 128])
    masks = din("masks", [2, 128, 128])
    y = DT(nc.dram_tensor("y", [S, D], F32, kind="ExternalOutput").ap(), "y")

    modV = dscr("modV", [2, 2, 6 * D], F32)
    QKT0 = dscr("QKT0", [13, 128, NTOK], BF16)
    VA = dscr("VA", [NTOK, 130], BF16)
    VB = dscr("VB", [NTOK, 516], BF16)
    MIX = dscr("MIX", [NTOK, D], BF16)
    H2T = dscr("H2T", [8, 128, NTOK], BF16)
    X1A = dscr("X1A", [NTOK, D], F32)
    X1 = dscr("X1", [NTOK, D], F32)
    QKT1 = dscr("QKT1", [8, 128, NTOK], BF16)
    VD = dscr("VD", [NTOK, 516], BF16)

    with ExitStack() as gst:
        s = Sched(nc, gst)

        gp = Phase(nc, s)
        idf = gp.sb("idf", [128, 128], F32)
        idb = gp.sb("idb", [128, 128], BF16)
        nh = gp.sb("nh", [128, 64], F32)
        vecb = gp.sb("vecb", [128, NV], F32)
        s.dma("sp", idf[:], ident.ap, writes=[idf.b])
        s.op("dve", "tensor_copy", reads=[idf.b], writes=[idb.b], out=idb[:], in_=idf[:])
        s.op("pool", "memset", writes=[nh.b], ap=nh[:], constant=-0.5)
        s.dma("sp", vecb[:], vecs.ap[0, :].partition_broadcast(128), writes=[vecb.b])

        def rstd_from_ss(ph, ssT, n, width, rname):
            v = ph.sb(rname + "_v", [128, n], F32) if not hasattr(ph, "_" + rname) else getattr(ph, "_" + rname)[0]
            r = ph.sb(rname + "_r", [128, n], F32) if not hasattr(ph, "_" + rname) else getattr(ph, "_" + rname)[1]
            setattr(ph, "_" + rname, (v, r))
            s.op("dve", "tensor_scalar", reads=[ssT.b], writes=[v.b], out=v[:], in0=ssT[:, 0:n], scalar1=1.0 / width,
                 scalar2=EPS, op0=ALU.mult, op1=ALU.add)
            s.op("pool", "tensor_tensor", reads=[v.b, nh.b], writes=[r.b], out=r[:], in0=v[:], in1=nh[:, 0:n], op=ALU.pow)
            return r

        def load_bcast(ph, name, src_ap, width, src_b, q="sp"):
            t = ph.sb(name, [128, width], F32)
            s.dma(q, t[:], src_ap.partition_broadcast(128), reads=[src_b], writes=[t.b])
            return t

        def run_pipelined(gens):
            gens = list(gens)
            active = []
            i = 0
            while i < len(gens) or active:
                new = None
                if i < len(gens):
                    new = gens[i]
                    i += 1
                    try:
                        next(new)
                    except StopIteration:
                        new = None
                for g in list(active):
                    try:
                        next(g)
                    except StopIteration:
                        active.remove(g)
                if new is not None:
                    active.append(new)

        def norm_mod_transpose(ph, xt, G, Sh, pT, hT_ap, hT_b, rings, split=True):
            junk, ssr, _, hbr = rings
            jk = junk.next()
            ss = ssr.next()
            s.op("act", "activation", reads=[xt.b], writes=[jk.b, ss.b], out=jk[:], in_=xt[:], func=AF.Square,
                 accum_out=ss[:])
            r = rstd_from_ss(ph, ss, 1, D, "rs_x")
            yield
            hb = hbr.next()
            s.op("act", "activation", reads=[xt.b, r.b], writes=[hb.b], out=hb[:], in_=xt[:], func=AF.Copy,
                 scale=r[:, 0:1])
            yield
            for k in range(8):
                s.op("pe", "transpose", reads=[hb.b, idb.b], writes=[pT.b], out=pT[:, k, :],
                     in_=hb[:, k * 128:(k + 1) * 128], identity=idb[:], inc=(k == 7))
            if split:
                yield
            for k in range(8):
                s.op("act", "activation", reads=[pT.b, G.b, Sh.b], writes=[hT_b], out=hT_ap[:, k, :], in_=pT[:, k, :],
                     func=AF.Identity, scale=G[:, k:k + 1], bias=Sh[:, k:k + 1])

        def load_w_bf16(ph, name, src_ap, kchunks, ncols, src_b):
            t = ph.sb(name, [128, kchunks, ncols], BF16)
            view = src_ap.rearrange("(k p) n -> p k n", p=128)
            step = max(1, min(kchunks, 4096 // ncols)) if ncols <= 4096 else 1
            for k0 in range(0, kchunks, step):
                k1 = min(kchunks, k0 + step)
                s.dma("pool", t[:, k0:k1, :], view[:, k0:k1, :], reads=[src_b], writes=[t.b.part(k0)])
            t.kparts = [t.b.part(k0) for k0 in range(0, kchunks, step)]
            return t

        def load_w_bf16_cols(ph, name, src_ap, kchunks, ncols, src_b, cb, order):
            t = ph.sb(name, [128, kchunks, ncols], BF16)
            view = src_ap.rearrange("(k p) n -> p k n", p=128)
            for b in order:
                c0, c1 = b * cb, min(ncols, (b + 1) * cb)
                s.dma("pool", t[:, :, c0:c1], view[:, :, c0:c1], reads=[src_b], writes=[t.b.part(("c", b))])
            t.cpart = lambda col: t.b.part(("c", col // cb))
            return t

        def head_norm(ph, qf, nslots, Gt, tag, wide=False):
            w = nslots * 64
            sq = ph.H_sq.next()
            s.op("act", "activation", reads=[qf.b], writes=[sq.b], out=sq[:, 0:w], in_=qf[:, 0:w], func=AF.Square)
            yield
            ssh = ph.H_ss.next()
            s.op("dve", "tensor_reduce", reads=[sq.b], writes=[ssh.b], out=ssh[:, 0:nslots],
                 in_=sq[:, 0:w].rearrange("p (h d) -> p h d", d=64), axis=AX.X, op=ALU.add)
            r = rstd_from_ss(ph, ssh, nslots, 64, "rs_h" + tag)
            yield
            qn = ph.H_qn.next()
            s.op("dve", "tensor_tensor", reads=[qf.b, r.b], writes=[qn.b],
                 out=qn[:, 0:w].rearrange("p (h d) -> p h d", d=64),
                 in0=qf[:, 0:w].rearrange("p (h d) -> p h d", d=64),
                 in1=r[:, 0:nslots].unsqueeze(2).to_broadcast([128, nslots, 64]), op=ALU.mult)
            if wide:
                yield
            qg = ph.H_qg.next()
            s.op("pool" if wide else "dve", "tensor_tensor", reads=[qn.b, Gt.b], writes=[qg.b], out=qg[:, 0:w],
                 in0=qn[:, 0:w], in1=Gt[:, 0:w], op=ALU.mult)
            return qg

        def rope_apply(ph, src_ap, dst_ap, n, rp, reads, dst_b, t2eng="pool"):
            t1 = ph.R_t1.next()
            t2 = ph.R_t2.next()
            t1v = t1[:, 0:n * 64].rearrange("p (h d) -> p h d", d=64)
            t2v = t2[:, 0:n * 64].rearrange("p (h d) -> p h d", d=64)
            s.op("dve", "tensor_tensor", reads=reads + [rp.b], writes=[t1.b], out=t1v, in0=src_ap,
                 in1=rp[:, 0:64].unsqueeze(1).to_broadcast([128, n, 64]), op=ALU.mult)
            s.op(t2eng, "tensor_tensor", reads=reads + [rp.b], writes=[t2.b.part(0)], out=t2v[:, :, 0:32], in0=src_ap[:, :, 32:64],
                 in1=rp[:, 64:96].unsqueeze(1).to_broadcast([128, n, 32]), op=ALU.mult)
            s.op("dve", "tensor_tensor", reads=reads + [rp.b], writes=[t2.b.part(1)], out=t2v[:, :, 32:64], in0=src_ap[:, :, 0:32],
                 in1=rp[:, 96:128].unsqueeze(1).to_broadcast([128, n, 32]), op=ALU.mult)
            s.op("dve", "tensor_tensor", reads=[t1.b, t2.b.part(0), t2.b.part(1)], writes=[dst_b], out=dst_ap, in0=t1v, in1=t2v,
                 op=ALU.add)

        ph = Phase(nc, s)
        cct = ph.sb("cct", [128, 8, 2], F32)
        sc = ph.sb("sc", [128, 8, 2], F32)
        ones2 = ph.sb("ones2", [1, 2], F32)
        brow = ph.sb("brow", [1, 6 * D], F32)
        mrow = ph.sb("mrow", [2, 6 * D], F32)
        wring = ph.ring("adaw", [128, 8, 512], F32, 2)
        pm = [ph.ps(f"pm{i}", [2, 512], F32) for i in range(2)]
        s.dma("sp", cct[:], cc.ap, writes=[cct.b])
        s.op("act", "activation", reads=[cct.b], writes=[sc.b], out=sc[:], in_=cct[:], func=AF.Silu)
        s.op("dve", "memset", writes=[ones2.b], ap=ones2[:], constant=1.0)
        import os
        NL_ = int(os.environ.get("KD_NL", "2"))
        NC_ = int(os.environ.get("KD_NC", "12"))
        for l in range(NL_):
            s.dma("sp", brow[:], ada_b.ap[l], writes=[brow.b])
            for c in range(NC_):
                wt = wring.next()
                s.dma("sp", wt[:], ada_w.ap[l][:, c * 512:(c + 1) * 512].rearrange("(k p) n -> p k n", p=128),
                      writes=[wt.b])
                p = pm[c % 2]
                for k in range(8):
                    s.op("pe", "matmul", reads=[sc.b, wt.b], writes=[p.b], out=p[:], lhsT=sc[:, k, :], rhs=wt[:, k, :],
                         start=(k == 0), stop=False)
                s.op("pe", "matmul", reads=[ones2.b, brow.b], writes=[p.b], out=p[:], lhsT=ones2[:],
                     rhs=brow[:, c * 512:(c + 1) * 512], start=False, stop=True)
                s.op("dve", "tensor_copy", reads=[p.b], writes=[mrow.b], out=mrow[:, c * 512:(c + 1) * 512], in_=p[:])
            s.dma("sp", modV.ap[l], mrow[:], reads=[mrow.b], writes=[modV.b])
        ph.close()
        if stop_after == "P0":
            s.finish()
            return nc

        def mod_tiles(ph, l, v, idx_shift, idx_scale, gdt, tag):
            def colload(name, ap1d, b):
                t = ph.sb(name, [128, 8], F32)
                s.dma("sp", t[:], ap1d.rearrange("(k p) -> p k", p=128), reads=[b], writes=[t.b],
                      allow_slow_non_contiguous=True)
                return t
            sh = colload(f"sh{tag}", modV.ap[l, v, idx_shift * D:(idx_shift + 1) * D], modV.b)
            scl = colload(f"scl{tag}", modV.ap[l, v, idx_scale * D:(idx_scale + 1) * D], modV.b)
            gg = colload(f"gg{tag}", gdt.ap[l, 0, :], gdt.b)
            s.op("dve", "scalar_tensor_tensor", reads=[scl.b, gg.b], writes=[scl.b], out=scl[:], in0=scl[:], scalar=1.0,
                 in1=gg[:], op0=ALU.add, op1=ALU.mult)
            return scl, sh

        def norm_rings(ph):
            return (ph.ring("junk", [128, D], BF16, 1), ph.ring("ssx", [128, 1], F32, 2),
                    None, ph.ring("hb", [128, D], BF16, 2))

        def head_rings(ph, w):
            ph.H_sq = ph.ring("hsq", [128, w], F32, 2)
            ph.H_ss = ph.ring("hss", [128, 32], F32, 2)
            ph.H_qn = ph.ring("hqn", [128, w], F32, 2)
            ph.H_qg = ph.ring("hqg", [128, w], F32, 2)
            ph.R_t1 = ph.ring("rt1", [128, w], F32, 1)
            ph.R_t2 = ph.ring("rt2", [128, w], F32, 1)

        ph = Phase(nc, s)
        W = load_w_bf16(ph, "wev", wev.ap, 8, 2304, wev.b)
        GL, SL = mod_tiles(ph, 0, 0, 0, 1, n1g, "l")
        GC, SC = mod_tiles(ph, 0, 1, 0, 1, n1g, "c")
        G26 = ph.sb("g26", [128, 26 * 64], F32)
        g26v = G26[:, :].rearrange("p (h d) -> p h d", d=64)
        for (a, b_, off) in ((0, 8, V_QA), (8, 16, V_QB), (16, 18, V_KA), (18, 26, V_KB)):
            s.op("dve", "tensor_copy", reads=[vecb.b], writes=[G26.b], out=g26v[:, a:b_, :],
                 in_=vecb[:, off:off + 64].unsqueeze(1).to_broadcast([128, b_ - a, 64]))
        nr = norm_rings(ph)
        head_rings(ph, 1664)
        xr = ph.ring("x", [128, D], F32, 4)
        rpr = ph.ring("rp", [128, 128], F32, 7)
        hTr = ph.ring("hT", [128, 8, 128], BF16, 2)
        qfr = ph.ring("qf", [128, 1664], F32, 2)
        qbr = ph.ring("qb", [128, 1664], BF16, 2)
        var = ph.ring("va", [128, 2, 65], BF16, 2)
        vbr = ph.ring("vb", [128, 4, 129], BF16, 2)
        for t_ in var.items + vbr.items:
            s.op("pool", "memset", writes=[t_.b], ap=t_[:], constant=1.0)
        qkst = ph.ring("qkst", [128, 13, 256], BF16, 2)
        pT = ph.ps("pT", [128, 8, 128], BF16)
        pO = ph.ps("pO", [128, 2560], F32)
        pQ1 = ph.ps("pQ1", [128, 8, 128], BF16)
        pQ2 = ph.ps("pQ2", [128, 8, 128], BF16)
        def tile_l0p1(t):
            g0 = (t // 2) * 2
            ti = t - g0
            ntile_g = 2
            st_ = qkst.items[(t // 2) % 2]
            isctx = t >= NLAT
            xt = xr.next()
            s.dma("sp", xt[:], xs.ap[t * 128:(t + 1) * 128, :], writes=[xt.b])
            yield
            hT = hTr.next()
            yield from norm_mod_transpose(ph, xt, GC if isctx else GL, SC if isctx else SL, pT, hT[:], hT.b, nr)
            yield
            for k in range(8):
                for c in range(5):
                    n0, n1 = c * 512, min(2304, (c + 1) * 512)
                    s.op("pe", "matmul", reads=[hT.b] + W.kparts, writes=[pO.b], out=pO[:, n0:n1],
                         lhsT=hT[:, k, :], rhs=W[:, k, n0:n1], start=(k == 0), stop=(k == 7))
            if not isctx:
                rp = rpr.next()
                s.dma("sp", rp[:], rope.ap[t], writes=[rp.b])
            yield
            qf = qfr.next()
            s.op("act", "copy", reads=[pO.b], writes=[qf.b], out=qf[:], in_=pO[:, 0:1664])
            va = var.next()
            vb = vbr.next()
            s.op("act", "copy", reads=[pO.b], writes=[va.b], out=va[:, :, 0:64],
                 in_=pO[:, 1664:1792].rearrange("p (h d) -> p h d", d=64))
            s.op("act", "copy", reads=[pO.b], writes=[vb.b], out=vb[:, :, 0:128],
                 in_=pO[:, 1792:2304].rearrange("p (h d) -> p h d", d=128))
            s.dma("sp", VA.ap[t * 128:(t + 1) * 128, :], va[:].rearrange("p h d -> p (h d)"), reads=[va.b],
                  writes=[VA.b.part(t)])
            s.dma("sp", VB.ap[t * 128:(t + 1) * 128, :], vb[:].rearrange("p h d -> p (h d)"), reads=[vb.b],
                  writes=[VB.b.part(t)])
            qg = yield from head_norm(ph, qf, 26, G26, "0", wide=True)
            yield
            qb = qbr.next()
            if isctx:
                s.op("dve", "tensor_copy", reads=[qg.b], writes=[qb.b], out=qb[:], in_=qg[:, 0:1664])
            else:
                rope_apply(ph, qg[:, 0:1664].rearrange("p (h d) -> p h d", d=64),
                           qb[:, :].rearrange("p (h d) -> p h d", d=64), 26, rp, [qg.b], qb.b)
            yield
            for j in range(13):
                pq = pQ1 if j < 8 else pQ2
                s.op("pe", "transpose", reads=[qb.b, idb.b], writes=[pq.b], out=pq[:, j % 8, :],
                     in_=qb[:, j * 128:(j + 1) * 128], identity=idb[:], inc=(j in (7, 12)))
            yield
            s.op("act", "copy", reads=[pQ1.b], writes=[st_.b], out=st_[:, 0:8, ti * 128:(ti + 1) * 128], in_=pQ1[:])
            s.op("act", "copy", reads=[pQ2.b], writes=[st_.b], out=st_[:, 8:13, ti * 128:(ti + 1) * 128],
                 in_=pQ2[:, 0:5, :])
            if ti == ntile_g - 1:
                ntk = ntile_g * 128
                s.dma("sp", QKT0.ap[:, :, g0 * 128:g0 * 128 + ntk].rearrange("j p t -> p j t"), st_[:, :, 0:ntk],
                      reads=[st_.b], writes=[QKT0.b])

        run_pipelined(tile_l0p1(t) for t in range(NT))
        ph.close()
        if stop_after == "L0P1":
            s.finish()
            return nc

        def attn_pipeline(ph, units, nkmax):
            PT = [ph.sb(f"PT{i}", [128, nkmax, 512], BF16) for i in range(2)]
            Sb = [ph.ps(f"S{i}", [128, 512], F32) for i in range(4)]
            acc = ph.ps("acc", [128, 4, 512], F32)
            si = 0
            n = len(units)
            for ui in range(n + 1):
                cur = units[ui] if ui < n else None
                prev = units[ui - 1] if ui > 0 else None
                if cur is not None and cur.get("pre") is not None:
                    cur["pre"]()
                nk = max(len(cur["keys"]) if cur else 0, len(prev["keys"]) if prev else 0)
                for kt in range(nk):
                    if cur is not None and kt < len(cur["keys"]):
                        key = cur["keys"][kt]
                        nq = cur["nq"]
                        sbk = Sb[si % 4]
                        si += 1
                        so = sbk[:, 0:nq]
                        if len(cur["qT"].shape) == 3:
                            so = so.rearrange("p (g q) -> p g q", q=128)
                        s.op("pe", "matmul", reads=cur["qb"] + key["kb"], writes=[sbk.b], out=so, lhsT=key["kT"],
                             rhs=cur["qT"], start=True, stop=True)
                        pt = PT[ui % 2]
                        ptb = pt.b.part(kt)
                        s.op("act", "activation", reads=[sbk.b], writes=[ptb], out=pt[:, kt, 0:nq], in_=sbk[:, 0:nq],
                             func=AF.Exp, scale=cur["scale"])
                        if key.get("mask") is not None:
                            mk = key["mask"]
                            s.op("dve", "tensor_tensor", reads=[ptb, mk.b], writes=[ptb],
                                 out=pt[:, kt, 0:nq].rearrange("p (g q) -> p g q", q=128),
                                 in0=pt[:, kt, 0:nq].rearrange("p (g q) -> p g q", q=128),
                                 in1=mk[:, :].unsqueeze(1).to_broadcast([128, nq // 128, 128]), op=ALU.mult)
                    if prev is not None and kt < len(prev["keys"]):
                        key = prev["keys"][kt]
                        pt = PT[(ui - 1) % 2]
                        ptb = pt.b.part(kt)
                        nkp = len(prev["keys"])
                        vd1 = prev["vd1"]
                        for j in range(prev["nq"] // 128):
                            s.op("pe", "matmul", reads=[ptb] + key["vb"], writes=[acc.b], out=acc[:, j, 0:vd1],
                                 lhsT=pt[:, kt, j * 128:(j + 1) * 128], rhs=key["v"], start=(kt == 0), stop=(kt == nkp - 1))
                if prev is not None:
                    prev["fin"](prev, acc)

        mprev_f = gp.sb("mprevf", [128, 128], F32)
        mnext_f = gp.sb("mnextf", [128, 128], F32)
        mprev = gp.sb("mprev", [128, 128], BF16)
        mnext = gp.sb("mnext", [128, 128], BF16)
        s.dma("sp", mprev_f[:], masks.ap[0], writes=[mprev_f.b])
        s.dma("sp", mnext_f[:], masks.ap[1], writes=[mnext_f.b])
        s.op("dve", "tensor_copy", reads=[mprev_f.b], writes=[mprev.b], out=mprev[:], in_=mprev_f[:])
        s.op("dve", "tensor_copy", reads=[mnext_f.b], writes=[mnext.b], out=mnext[:], in_=mnext_f[:])

        ph = Phase(nc, s)
        QA = ph.sb("QA", [128, 4, NTOK], BF16)
        KAz = [ph.sb(f"KAz{i}", [128, NTOK], BF16) for i in range(2)]
        s.op("pool", "memset", writes=[KAz[0].b], ap=KAz[0][64:128, :], constant=0.0)
        s.op("pool", "memset", writes=[KAz[1].b], ap=KAz[1][0:64, :], constant=0.0)
        VAs = ph.sb("VAs", [128, NT, 130], BF16)
        s.dma("sp", QA[:], QKT0.ap[0:4].rearrange("j p t -> p j t"), reads=[QKT0.b], writes=[QA.b])
        s.dma("sp", KAz[0][0:64, :], QKT0.ap[8, 0:64, :], reads=[QKT0.b], writes=[KAz[0].b])
        s.dma("sp", KAz[1][64:128, :], QKT0.ap[8, 64:128, :], reads=[QKT0.b], writes=[KAz[1].b])
        s.dma("sp", VAs[:], VA.ap.rearrange("(k p) e -> p k e", p=128), reads=[VA.b.part(t) for t in range(NT)],
              writes=[VAs.b])
        esink = ph.sb("esink", [128, 8], F32)
        s.op("act", "activation", reads=[vecb.b], writes=[esink.b], out=esink[:], in_=vecb[:, V_SINK:V_SINK + 8], func=AF.Exp)
        zr = ph.ring("za", [128, 4], F32, 2)
        rzr = ph.ring("rza", [128, 4], F32, 2)
        obr = ph.ring("oba", [128, 4, 64], BF16, 3)

        def fin_a(u, acc):
            kvh, n = u["kvh"], u["n"]
            z = zr.next()
            s.op("dve", "tensor_tensor", reads=[acc.b, esink.b], writes=[z.b], out=z[:], in0=acc[:, :, 64],
                 in1=esink[:, kvh * 4:(kvh + 1) * 4], op=ALU.add)
            rz = rzr.next()
            s.op("dve", "reciprocal", reads=[z.b], writes=[rz.b], out=rz[:], in_=z[:])
            ob = obr.next()
            s.op("dve", "tensor_tensor", reads=[acc.b, rz.b], writes=[ob.b], out=ob[:], in0=acc[:, :, 0:64],
                 in1=rz[:, :].unsqueeze(2).to_broadcast([128, 4, 64]), op=ALU.mult)
            s.dma("sp", MIX.ap[n * 128:(n + 1) * 128, kvh * 256:(kvh + 1) * 256], ob[:].rearrange("p g d -> p (g d)"),
                  reads=[ob.b], writes=[MIX.b.part(("a", n, kvh))])

        units = []
        for n in range(NT):
            if n < NLAT:
                kl = []
                if n > 0:
                    kl.append((n - 1, mprev))
                kl.append((n, None))
                if n < NLAT - 1:
                    kl.append((n + 1, mnext))
                kl += [(32, None), (33, None)]
            else:
                kl = [(32, None), (33, None)]
            for kvh in range(2):
                keys = [dict(kT=KAz[kvh][:, kt * 128:(kt + 1) * 128], kb=[KAz[kvh].b], v=VAs[:, kt, kvh * 65:(kvh + 1) * 65],
                             vb=[VAs.b], mask=mk) for (kt, mk) in kl]
                units.append(dict(qT=QA[:, :, n * 128:(n + 1) * 128], qb=[QA.b], keys=keys, nq=512, vd1=65,
                                  scale=0.125, fin=fin_a, kvh=kvh, n=n))
        attn_pipeline(ph, units, 5)
        ph.close()
        if stop_after == "L0P2A":
            s.finish()
            return nc

        lam_init0 = 0.8 - 0.6 * math.exp(-0.3 * 0)
        ph = Phase(nc, s)
        lt = ph.sb("lt", [128, 128], F32)
        lsum = ph.sb("lsum", [128, 2], F32)
        lexp = ph.sb("lexp", [128, 2], F32)
        nlam = ph.sb("nlam", [128, 1], F32)
        s.op("dve", "tensor_tensor", reads=[vecb.b], writes=[lt.b], out=lt[:, 0:64], in0=vecb[:, V_LQ1:V_LQ1 + 64],
             in1=vecb[:, V_LK1:V_LK1 + 64], op=ALU.mult)
        s.op("dve", "tensor_tensor", reads=[vecb.b], writes=[lt.b], out=lt[:, 64:128], in0=vecb[:, V_LQ2:V_LQ2 + 64],
             in1=vecb[:, V_LK2:V_LK2 + 64], op=ALU.mult)
        s.op("dve", "tensor_reduce", reads=[lt.b], writes=[lsum.b], out=lsum[:],
             in_=lt[:, :].rearrange("p (a d) -> p a d", d=64), axis=AX.X, op=ALU.add)
        s.op("act", "activation", reads=[lsum.b], writes=[lexp.b], out=lexp[:], in_=lsum[:], func=AF.Exp)
        s.op("dve", "tensor_tensor", reads=[lexp.b], writes=[nlam.b], out=nlam[:], in0=lexp[:, 1:2], in1=lexp[:, 0:1],
             op=ALU.subtract)
        s.op("dve", "tensor_scalar", reads=[nlam.b], writes=[nlam.b], out=nlam[:], in0=nlam[:], scalar1=-lam_init0,
             scalar2=None, op0=ALU.add)
        subl = ph.sb("subl", [128, 128], F32)
        s.op("dve", "tensor_scalar", reads=[vecb.b], writes=[subl.b], out=subl[:], in0=vecb[:, V_SUBLN:V_SUBLN + 128],
             scalar1=1.0 - lam_init0, scalar2=None, op0=ALU.mult)
        QBr = ph.ring("QB", [128, NTOK], BF16, 2)
        KBz = [[ph.sb(f"KBz{i}_{j}", [128, NTOK], BF16) for j in range(2)] for i in range(2)]
        for i in range(2):
            s.op("pool", "memset", writes=[KBz[i][0].b], ap=KBz[i][0][64:128, :], constant=0.0)
            s.op("pool", "memset", writes=[KBz[i][1].b], ap=KBz[i][1][0:64, :], constant=0.0)
        VBr = ph.ring("VBs", [128, NT, 129], BF16, 2)
        o1r = ph.ring("o1", [128, 4, 128], F32, 2)
        z1r = ph.ring("zb", [128, 4], F32, 4)
        tbr = ph.ring("tb", [128, 4, 128], F32, 2)
        obbr = ph.ring("obb", [128, 4, 128], F32, 2)
        sqbr = ph.ring("sqb", [128, 4, 128], F32, 1)
        ssbr = ph.ring("ssb", [128, 4], F32, 2)
        onr = ph.ring("onb", [128, 4, 128], F32, 1)
        outbr = ph.ring("outb", [128, 4, 128], BF16, 3)
        state = {}

        def fin_b(u, acc):
            h, q0, nj, sidx = u["h"], u["q0"], u["nq"] // 128, u["s"]
            rz = z1r.next()
            s.op("dve", "reciprocal", reads=[acc.b], writes=[rz.b], out=rz[:, 0:nj], in_=acc[:, 0:nj, 128])
            if sidx == 0:
                o1 = o1r.next()
                s.op("dve", "tensor_tensor", reads=[acc.b, rz.b], writes=[o1.b], out=o1[:, 0:nj, :], in0=acc[:, 0:nj, 0:128],
                     in1=rz[:, 0:nj].unsqueeze(2).to_broadcast([128, nj, 128]), op=ALU.mult)
                state["o1"] = o1
                return
            o1 = state["o1"]
            rzl = z1r.next()
            s.op("dve", "tensor_scalar", reads=[rz.b, nlam.b], writes=[rzl.b], out=rzl[:, 0:nj], in0=rz[:, 0:nj],
                 scalar1=nlam[:, 0:1], scalar2=None, op0=ALU.mult)
            tb = tbr.next()
            s.op("dve", "tensor_tensor", reads=[acc.b, rzl.b], writes=[tb.b], out=tb[:, 0:nj, :], in0=acc[:, 0:nj, 0:128],
                 in1=rzl[:, 0:nj].unsqueeze(2).to_broadcast([128, nj, 128]), op=ALU.mult)
            ob = obbr.next()
            s.op("pool", "tensor_tensor", reads=[tb.b, o1.b], writes=[ob.b], out=ob[:, 0:nj, :], in0=tb[:, 0:nj, :],
                 in1=o1[:, 0:nj, :], op=ALU.add)
            sq = sqbr.next()
            s.op("pool", "tensor_tensor", reads=[ob.b], writes=[sq.b], out=sq[:, 0:nj, :], in0=ob[:, 0:nj, :],
                 in1=ob[:, 0:nj, :], op=ALU.mult)
            ss = ssbr.next()
            s.op("dve", "tensor_reduce", reads=[sq.b], writes=[ss.b], out=ss[:, 0:nj], in_=sq[:, 0:nj, :], axis=AX.X,
                 op=ALU.add)
            r = rstd_from_ss(ph, ss, nj, 128, "rs_b%d" % nj)
            on = onr.next()
            s.op("dve", "tensor_tensor", reads=[ob.b, r.b], writes=[on.b], out=on[:, 0:nj, :], in0=ob[:, 0:nj, :],
                 in1=r[:, 0:nj].unsqueeze(2).to_broadcast([128, nj, 128]), op=ALU.mult)
            out = outbr.next()
            s.op("pool", "tensor_tensor", reads=[on.b, subl.b], writes=[out.b], out=out[:, 0:nj, :], in0=on[:, 0:nj, :],
                 in1=subl[:, :].unsqueeze(1).to_broadcast([128, nj, 128]), op=ALU.mult)
            s.dma("sp", MIX.ap[q0:q0 + nj * 128, 512 + h * 128:512 + (h + 1) * 128].rearrange("(j p) d -> p j d", p=128),
                  out[:, 0:nj, :], reads=[out.b], writes=[MIX.b.part(("b", h, q0))])

        def load_b(h):
            Qh, Kz, Vh = QBr.items[h % 2], KBz[h % 2], VBr.items[h % 2]
            s.dma("sp", Qh[:], QKT0.ap[4 + h], reads=[QKT0.b], writes=[Qh.b])
            s.dma("sp", Kz[0][0:64, :], QKT0.ap[9 + h, 0:64, :], reads=[QKT0.b], writes=[Kz[0].b])
            s.dma("sp", Kz[1][64:128, :], QKT0.ap[9 + h, 64:128, :], reads=[QKT0.b], writes=[Kz[1].b])
            s.dma("sp", Vh[:], VB.ap.rearrange("(k p) (h e) -> p k h e", p=128, e=129)[:, :, h, :],
                  reads=[VB.b.part(t) for t in range(NT)], writes=[Vh.b])

        units = []
        for h in range(4):
            Qh, Kz, Vh = QBr.items[h % 2], KBz[h % 2], VBr.items[h % 2]
            blocks = [(qb * 512, 512, list(range(NT))) for qb in range(8)] + [(S, 256, [32, 33])]
            first = len(units)
            for (q0, nq, kl) in blocks:
                for sidx in range(2):
                    Kh = Kz[sidx]
                    keys = [dict(kT=Kh[:, kt * 128:(kt + 1) * 128], kb=[Kh.b], v=Vh[:, kt, :], vb=[Vh.b], mask=None)
                            for kt in kl]
                    units.append(dict(qT=Qh[:, q0:q0 + nq], qb=[Qh.b], keys=keys, nq=nq, vd1=129, scale=0.125,
                                      fin=fin_b, h=h, q0=q0, s=sidx))
            if h == 0:
                units[first]["pre"] = (lambda: load_b(0))
            if h < 3:
                units[first + 1]["pre"] = (lambda hh=h + 1: load_b(hh))
        attn_pipeline(ph, units, NT)
        ph.close()
        if stop_after == "L0P2B":
            s.finish()
            return nc

        def phase3a(l, Xin, Xout, ntiles):
            ph = Phase(nc, s)
            Wm = load_w_bf16(ph, "wmix", wmix.ap[l], 8, D, wmix.b)
            G2L, S2L = mod_tiles(ph, l, 0, 3, 4, n2g, "l")
            gateL = load_bcast(ph, "gateL", modV.ap[l, 0, 2 * D:3 * D], D, modV.b)
            if ntiles > NLAT:
                G2C, S2C = mod_tiles(ph, l, 1, 3, 4, n2g, "c")
                gateC = load_bcast(ph, "gateC", modV.ap[l, 1, 2 * D:3 * D], D, modV.b)
            nr = norm_rings(ph)
            xr = ph.ring("x", [128, D], F32, 5)
            mr = ph.ring("mx", [128, D], BF16, 3)
            mTr = ph.ring("mT", [128, 8, 128], BF16, 2)
            tmr = ph.ring("tm", [128, D], F32, 1)
            x1r = ph.ring("x1", [128, D], F32, 4)
            hst = ph.ring("hst", [128, 8, 512], BF16, 2)
            pT = ph.ps("pT", [128, 8, 128], BF16)
            pT2 = ph.ps("pT2", [128, 8, 128], BF16)
            pP = ph.ps("pP", [128, D], F32)
            def tile_p3a(t):
                g0 = (t // 4) * 4
                ti = t - g0
                ntile_g = min(ntiles, g0 + 4) - g0
                st_ = hst.items[(t // 4) % 2]
                isctx = t >= NLAT
                xt = xr.next()
                s.dma("sp", xt[:], Xin.ap[t * 128:(t + 1) * 128, :], reads=[Xin.b.part(t)], writes=[xt.b])
                mx = mr.next()
                s.dma("sp", mx[:], MIX.ap[t * 128:(t + 1) * 128, :], reads=[MIX.b] + list(MIX.b.parts.values()),
                      writes=[mx.b])
                yield
                for k in range(8):
                    s.op("pe", "transpose", reads=[mx.b, idb.b], writes=[pT.b], out=pT[:, k, :],
                         in_=mx[:, k * 128:(k + 1) * 128], identity=idb[:], inc=(k == 7))
                yield
                mT = mTr.next()
                s.op("act", "copy", reads=[pT.b], writes=[mT.b], out=mT[:], in_=pT[:])
                yield
                for k in range(8):
                    for c in range(2):
                        s.op("pe", "matmul", reads=[mT.b] + Wm.kparts, writes=[pP.b], out=pP[:, c * 512:(c + 1) * 512],
                             lhsT=mT[:, k, :], rhs=Wm[:, k, c * 512:(c + 1) * 512], start=(k == 0), stop=(k == 7))
                yield
                tm = tmr.next()
                gate = gateC if isctx else gateL
                s.op("dve", "tensor_tensor", reads=[pP.b, gate.b], writes=[tm.b], out=tm[:], in0=pP[:], in1=gate[:],
                     op=ALU.mult)
                x1 = x1r.next()
                s.op("pool", "tensor_tensor", reads=[tm.b, xt.b], writes=[x1.b], out=x1[:], in0=tm[:], in1=xt[:],
                     op=ALU.add)
                s.dma("sp", Xout.ap[t * 128:(t + 1) * 128, :], x1[:], reads=[x1.b], writes=[Xout.b.part(t)])
                yield
                yield from norm_mod_transpose(ph, x1, G2C if isctx else G2L, S2C if isctx else S2L, pT2,
                                              st_[:, :, ti * 128:(ti + 1) * 128], st_.b, nr)
                if ti == ntile_g - 1:
                    ntk = ntile_g * 128
                    s.dma("sp", H2T.ap[:, :, g0 * 128:g0 * 128 + ntk].rearrange("j p t -> p j t"), st_[:, :, 0:ntk],
                          reads=[st_.b], writes=[H2T.b.part(g0)])

            run_pipelined(tile_p3a(t) for t in range(ntiles))
            ph.close()

        def phase3b(l, Xin, Xout, ntiles):
            ph = Phase(nc, s)
            Wi = load_w_bf16_cols(ph, "wfi", wfi.ap[l], 8, 2 * FH, wfi.b, 512, [0, 5, 6, 1, 7, 2, 8, 3, 9, 4, 10])
            Wo = load_w_bf16(ph, "wfo", wfo.ap[l], 22, D, wfo.b)
            gate = load_bcast(ph, "gate", modV.ap[l, 0, 5 * D:6 * D], D, modV.b)
            h2r = ph.ring("h2", [128, 8, 512], BF16, 1)
            actT = ph.sb("actT", [128, 22, 512], BF16)
            sgr = ph.ring("sg", [128, 512], F32, 2)
            xr = ph.ring("x", [128, D], F32, 1)
            x2r = ph.ring("x2", [128, D], F32, 2)
            pG = [ph.ps(f"pG{i}", [128, 512], F32) for i in range(2)]
            pU = [ph.ps(f"pU{i}", [128, 512], F32) for i in range(2)]
            pY = [ph.ps(f"pY{i}", [128, D], F32) for i in range(2)]
            yi = 0
            for g0 in range(0, ntiles, 4):
                tiles = list(range(g0, min(ntiles, g0 + 4)))
                ntk = len(tiles) * 128
                if tiles[0] >= NLAT:
                    s.dma("sp", gate[:], modV.ap[l, 1, 5 * D:6 * D].partition_broadcast(128), reads=[modV.b],
                          writes=[gate.b])
                h2 = h2r.next()
                if g0 == 0:
                    s.dma("sp", h2[:, :, 0:ntk], H2T.ap[:, :, 0:ntk].rearrange("j p t -> p j t"),
                          reads=[H2T.b.part(0)], writes=[h2.b])
                for j in range(22):
                    pg, pu = pG[j % 2], pU[j % 2]
                    for k in range(8):
                        s.op("pe", "matmul", reads=[h2.b, Wi.cpart(j * 128)], writes=[pg.b], out=pg[:, 0:ntk],
                             lhsT=Wi[:, k, j * 128:(j + 1) * 128], rhs=h2[:, k, 0:ntk], start=(k == 0), stop=(k == 7))
                    for k in range(8):
                        s.op("pe", "matmul", reads=[h2.b, Wi.cpart(FH + j * 128)], writes=[pu.b], out=pu[:, 0:ntk],
                             lhsT=Wi[:, k, FH + j * 128:FH + (j + 1) * 128], rhs=h2[:, k, 0:ntk], start=(k == 0),
                             stop=(k == 7))
                    sg = sgr.next()
                    s.op("act", "activation", reads=[pg.b], writes=[sg.b], out=sg[:, 0:ntk], in_=pg[:, 0:ntk], func=AF.Silu)
                    s.op("dve", "tensor_tensor", reads=[pu.b, sg.b], writes=[actT.b.part(j)], out=actT[:, j, 0:ntk],
                         in0=pu[:, 0:ntk], in1=sg[:, 0:ntk], op=ALU.mult)
                if g0 + 4 < ntiles:
                    n0_ = g0 + 4
                    ntk2 = (min(ntiles, n0_ + 4) - n0_) * 128
                    s.dma("act", h2[:, :, 0:ntk2], H2T.ap[:, :, n0_ * 128:n0_ * 128 + ntk2].rearrange("j p t -> p j t"),
                          reads=[H2T.b.part(n0_)], writes=[h2.b])
                for ti, t in enumerate(tiles):
                    isctx = t >= NLAT
                    py = pY[yi % 2]
                    yi += 1
                    for c in range(2):
                        for j in range(22):
                            s.op("pe", "matmul", reads=[actT.b.part(j)] + Wo.kparts, writes=[py.b],
                                 out=py[:, c * 512:(c + 1) * 512], lhsT=actT[:, j, ti * 128:(ti + 1) * 128],
                                 rhs=Wo[:, j, c * 512:(c + 1) * 512], start=(j == 0), stop=(j == 21))
                    xt = xr.next()
                    s.dma("sp", xt[:], Xin.ap[t * 128:(t + 1) * 128, :], reads=[Xin.b.part(t)], writes=[xt.b])
                    x2 = x2r.next()
                    s.op("dve", "tensor_tensor", reads=[py.b, gate.b], writes=[x2.b], out=x2[:], in0=py[:], in1=gate[:],
                         op=ALU.mult)
                    s.op("pool", "tensor_tensor", reads=[x2.b, xt.b], writes=[x2.b], out=x2[:], in0=x2[:], in1=xt[:],
                         op=ALU.add)
                    s.dma("sp", Xout.ap[t * 128:(t + 1) * 128, :], x2[:], reads=[x2.b], writes=[Xout.b.part(t)])
            ph.close()

        phase3a(0, xs, X1A, NT)
        if stop_after == "L0P3A":
            s.finish()
            return nc
        phase3b(0, X1A, X1, NT)
        if stop_after == "L0":
            s.finish()
            return nc

        ph = Phase(nc, s)
        W1 = load_w_bf16(ph, "wod", wod.ap, 8, 1600, wod.b)
        Wq = load_w_bf16(ph, "wuq", wuq.ap, 2, 512, wuq.b)
        Wkv = load_w_bf16(ph, "wukv", wukv.ap, 2, 768, wukv.b)
        Ws = ph.sb("ws", [128, 4, 128], BF16)
        s.dma("pool", Ws[:], wsT.ap.rearrange("g q p -> q g p"), reads=[wsT.b], writes=[Ws.b])
        bs = ph.sb("bs", [128, 4], F32)
        s.dma("sp", bs[:], bsT.ap, writes=[bs.b])
        GL, SL = mod_tiles(ph, 1, 0, 0, 1, n1g, "l")
        GC, SC = mod_tiles(ph, 1, 1, 0, 1, n1g, "c")
        G13 = ph.sb("g13", [128, 13 * 64], F32)
        g13v = G13[:, :].rearrange("p (h d) -> p h d", d=64)
        g13q = G13[:, 0:512].rearrange("p (h t d) -> p h t d", t=2, d=64)
        s.op("dve", "tensor_copy", reads=[vecb.b], writes=[G13.b], out=g13q[:, :, 0, :],
             in_=vecb[:, V_QNN:V_QNN + 64].unsqueeze(1).to_broadcast([128, 4, 64]))
        s.op("dve", "tensor_copy", reads=[vecb.b], writes=[G13.b], out=g13q[:, :, 1, :],
             in_=vecb[:, V_QNR:V_QNR + 64].unsqueeze(1).to_broadcast([128, 4, 64]))
        s.op("dve", "tensor_copy", reads=[vecb.b], writes=[G13.b], out=g13v[:, 8:12, :],
             in_=vecb[:, V_KNN:V_KNN + 64].unsqueeze(1).to_broadcast([128, 4, 64]))
        s.op("dve", "tensor_copy", reads=[vecb.b], writes=[G13.b], out=g13v[:, 12:13, :],
             in_=vecb[:, V_KNR:V_KNR + 64].unsqueeze(1).to_broadcast([128, 1, 64]))
        nr = norm_rings(ph)
        head_rings(ph, 832)
        xr = ph.ring("x", [128, D], F32, 4)
        rpr = ph.ring("rp", [128, 128], F32, 11)
        hTr = ph.ring("hT", [128, 8, 128], BF16, 2)
        glr = ph.ring("gl", [128, D], F32, 4)
        vsqr = ph.ring("vsq", [128, 512], F32, 1)
        ssvr = ph.ring("ssv", [128, 2], F32, 2)
        vnr = ph.ring("vn", [128, 512], BF16, 2)
        mcr = ph.ring("mc", [128, 512], BF16, 2)
        cqfr = ph.ring("cqf", [128, 512], F32, 3)
        cnr = ph.ring("cn", [128, 512], F32, 1)
        cnbr = ph.ring("cnb", [128, 512], BF16, 2)
        cTr = ph.ring("cT", [128, 4, 128], BF16, 2)
        hqr = ph.ring("hq", [128, 832], F32, 3)
        qcbr = ph.ring("qcb", [128, 4, 128], BF16, 2)
        kcbr = ph.ring("kcb", [128, 4, 128], BF16, 2)
        kper = ph.ring("kpe", [128, 64], F32, 7)
        kprr = ph.ring("kpr", [128, 64], F32, 2)
        vdr = ph.ring("vd", [128, 4, 129], BF16, 2)
        for t_ in vdr.items:
            s.op("pool", "memset", writes=[t_.b], ap=t_[:], constant=1.0)
        qkst = ph.ring("qkst", [128, 8, 512], BF16, 2)
        pT = ph.ps("pT", [128, 8, 128], BF16)
        pO1 = ph.ps("pO1", [128, 2048], F32)
        pQ = ph.ps("pQ", [128, 512], F32)
        pKV = ph.ps("pKV", [128, 1024], F32)
        def cps(g):
            return pO1[:, 1600 + g * 128:1728 + g * 128] if g < 3 else pKV[:, 768:896]

        def cpb(g):
            return pO1.b if g < 3 else pKV.b

        ss3r = ph.ring("ss3", [128, 4], F32, 2)
        v3r = ph.ring("v3", [128, 4], F32, 2)
        r3r = ph.ring("r3", [128, 4], F32, 2)
        jk2r = ph.ring("jk2", [128, 512], BF16, 3)
        for t_ in ss3r.items:
            s.op("pool", "memset", writes=[t_.b], ap=t_[:], constant=1.0)

        def tile_l1p1(t):
            g0 = (t // 4) * 4
            ti = t - g0
            ntile_g = min(NT, g0 + 4) - g0
            st_ = qkst.items[(t // 4) % 2]
            isctx = t >= NLAT
            xt = xr.next()
            s.dma("sp", xt[:], X1.ap[t * 128:(t + 1) * 128, :], reads=[X1.b.part(t)], writes=[xt.b])
            yield
            hT = hTr.next()
            yield from norm_mod_transpose(ph, xt, GC if isctx else GL, SC if isctx else SL, pT, hT[:], hT.b, nr, split=False)
            yield
            chunks = [(1280, 1536), (1536, 1600)] if isctx else [(0, 512), (512, 1024), (1024, 1536), (1536, 1600)]
            for k in range(8):
                for (n0, n1) in chunks:
                    s.op("pe", "matmul", reads=[hT.b] + W1.kparts, writes=[pO1.b], out=pO1[:, n0:n1],
                         lhsT=hT[:, k, :], rhs=W1[:, k, n0:n1], start=(k == 0), stop=(k == 7))
            if not isctx:
                rp = rpr.next()
                s.dma("sp", rp[:], rope.ap[t], writes=[rp.b])
            yield
            cqf = cqfr.next()
            c0 = 256 if isctx else 0
            s.op("act", "copy", reads=[pO1.b], writes=[cqf.b], out=cqf[:, c0:512], in_=pO1[:, 1024 + c0:1536])
            kpe = kper.next()
            s.op("act", "copy", reads=[pO1.b], writes=[kpe.b], out=kpe[:], in_=pO1[:, 1536:1600])
            ss3 = ss3r.next()
            if not isctx:
                gl = glr.next()
                s.op("act", "activation", reads=[pO1.b], writes=[gl.b], out=gl[:], in_=pO1[:, 0:1024], func=AF.Gelu)
                jk = jk2r.next()
                s.op("act", "activation", reads=[gl.b], writes=[jk.b, ss3.b], out=jk[:], in_=gl[:, 512:1024], func=AF.Square,
                     accum_out=ss3[:, 0:1])
                jk = jk2r.next()
                s.op("act", "activation", reads=[cqf.b], writes=[jk.b, ss3.b], out=jk[:, 0:256], in_=cqf[:, 0:256],
                     func=AF.Square, accum_out=ss3[:, 1:2])
            jk = jk2r.next()
            s.op("act", "activation", reads=[cqf.b], writes=[jk.b, ss3.b], out=jk[:, 0:256], in_=cqf[:, 256:512],
                 func=AF.Square, accum_out=ss3[:, 2:3])
            yield
            v3 = v3r.next()
            s.op("dve", "tensor_scalar", reads=[ss3.b], writes=[v3.b], out=v3[:, 0:1], in0=ss3[:, 0:1], scalar1=1.0 / 512,
                 scalar2=EPS, op0=ALU.mult, op1=ALU.add)
            s.op("dve", "tensor_scalar", reads=[ss3.b], writes=[v3.b], out=v3[:, 1:3], in0=ss3[:, 1:3], scalar1=1.0 / 256,
                 scalar2=EPS, op0=ALU.mult, op1=ALU.add)
            r3 = r3r.next()
            s.op("pool", "tensor_tensor", reads=[v3.b, nh.b], writes=[r3.b], out=r3[:, 0:3], in0=v3[:, 0:3], in1=nh[:, 0:3],
                 op=ALU.pow)
            yield
            cnb = cnbr.next()
            if not isctx:
                vn = vnr.next()
                s.op("dve", "scalar_tensor_tensor", reads=[gl.b, r3.b, vecb.b], writes=[vn.b], out=vn[:],
                     in0=gl[:, 512:1024], scalar=r3[:, 0:1], in1=vecb[:, V_CVN:V_CVN + 512], op0=ALU.mult, op1=ALU.mult)
                s.op("dve", "scalar_tensor_tensor", reads=[cqf.b, r3.b, vecb.b], writes=[cnb.b], out=cnb[:, 0:256],
                     in0=cqf[:, 0:256], scalar=r3[:, 1:2], in1=vecb[:, V_QAN:V_QAN + 256], op0=ALU.mult, op1=ALU.mult)
            s.op("dve", "scalar_tensor_tensor", reads=[cqf.b, r3.b, vecb.b], writes=[cnb.b], out=cnb[:, 256:512],
                 in0=cqf[:, 256:512], scalar=r3[:, 2:3], in1=vecb[:, V_KVAN:V_KVAN + 256], op0=ALU.mult, op1=ALU.mult)
            yield
            if not isctx:
                for g in range(4):
                    s.op("pe", "matmul", reads=[vn.b, Ws.b], writes=[cpb(g)], out=cps(g),
                         lhsT=Ws[:, g, :], rhs=vn[:, g * 128:(g + 1) * 128], start=True, stop=True)
            kc0 = c0 // 128
            for k in range(kc0, 4):
                s.op("pe", "transpose", reads=[cnb.b, idb.b], writes=[pT.b], out=pT[:, k, :],
                     in_=cnb[:, k * 128:(k + 1) * 128], identity=idb[:], inc=(k == 3))
            cT = cTr.next()
            s.op("act", "copy", reads=[pT.b], writes=[cT.b], out=cT[:, kc0:4, :], in_=pT[:, kc0:4, :])
            if not isctx:
                mc_ = mcr.next()
                for g in range(4):
                    s.op("dve", "scalar_tensor_tensor", reads=[cpb(g), bs.b, gl.b], writes=[mc_.b],
                         out=mc_[:, g * 128:(g + 1) * 128], in0=cps(g), scalar=bs[:, g:g + 1],
                         in1=gl[:, g * 128:(g + 1) * 128], op0=ALU.add, op1=ALU.mult)
                s.dma("sp", MIX.ap[t * 128:(t + 1) * 128, 0:512], mc_[:], reads=[mc_.b], writes=[MIX.b.part(("c", t))])
            yield
            if not isctx:
                for k in range(2):
                    s.op("pe", "matmul", reads=[cT.b] + Wq.kparts, writes=[pQ.b], out=pQ[:], lhsT=cT[:, k, :],
                         rhs=Wq[:, k, :], start=(k == 0), stop=(k == 1))
            for k in range(2):
                for (n0, n1) in ((0, 512), (512, 768)):
                    s.op("pe", "matmul", reads=[cT.b] + Wkv.kparts, writes=[pKV.b], out=pKV[:, n0:n1],
                         lhsT=cT[:, 2 + k, :], rhs=Wkv[:, k, n0:n1], start=(k == 0), stop=(k == 1))
            yield
            vd = vdr.next()
            s.op("act", "copy", reads=[pKV.b], writes=[vd.b], out=vd[:, :, 0:128],
                 in_=pKV[:, 256:768].rearrange("p (h d) -> p h d", d=128))
            s.dma("sp", VD.ap[t * 128:(t + 1) * 128, :], vd[:].rearrange("p h d -> p (h d)"), reads=[vd.b],
                  writes=[VD.b.part(t)])
            hq = hqr.next()
            if not isctx:
                s.op("act", "copy", reads=[pQ.b], writes=[hq.b], out=hq[:, 0:512], in_=pQ[:])
            else:
                s.op("pool", "memset", writes=[hq.b], ap=hq[:, 0:512], constant=1.0)
            s.op("act", "copy", reads=[pKV.b], writes=[hq.b], out=hq[:, 512:768], in_=pKV[:, 0:256])
            s.op("act", "copy", reads=[kpe.b], writes=[hq.b], out=hq[:, 768:832], in_=kpe[:])
            qg = yield from head_norm(ph, hq, 13, G13, "1")
            yield
            qgv = qg[:, 0:832].rearrange("p (h d) -> p h d", d=64)
            kcb = kcbr.next()
            s.op("dve", "tensor_copy", reads=[qg.b], writes=[kcb.b], out=kcb[:, :, 0:64], in_=qgv[:, 8:12, :])
            if isctx:
                s.op("dve", "tensor_copy", reads=[qg.b], writes=[kcb.b], out=kcb[:, :, 64:128],
                     in_=qgv[:, 12:13, :].to_broadcast([128, 4, 64]))
            else:
                kpr = kprr.next()
                rope_apply(ph, qgv[:, 12:13, :], kpr[:, :].unsqueeze(1), 1, rp, [qg.b], kpr.b, t2eng="dve")
                s.op("dve", "tensor_copy", reads=[kpr.b], writes=[kcb.b], out=kcb[:, :, 64:128],
                     in_=kpr[:, :].unsqueeze(1).to_broadcast([128, 4, 64]))
                qcb = qcbr.next()
                qg4 = qg[:, 0:512].rearrange("p (h t d) -> p h t d", t=2, d=64)
                s.op("act", "copy", reads=[qg.b], writes=[qcb.b], out=qcb[:, :, 0:64], in_=qg4[:, :, 0, :])
                rope_apply(ph, qg4[:, :, 1, :], qcb[:, :, 64:128], 4, rp, [qg.b], qcb.b, t2eng="dve")
            yield
            if not isctx:
                for h in range(4):
                    s.op("pe", "transpose", reads=[qcb.b, idb.b], writes=[pT.b], out=pT[:, h, :], in_=qcb[:, h, :],
                         identity=idb[:])
            for h in range(4):
                s.op("pe", "transpose", reads=[kcb.b, idb.b], writes=[pT.b], out=pT[:, 4 + h, :], in_=kcb[:, h, :],
                     identity=idb[:], inc=(h == 3))
            b0 = 4 if isctx else 0
            s.op("act", "copy", reads=[pT.b], writes=[st_.b], out=st_[:, b0:8, ti * 128:(ti + 1) * 128], in_=pT[:, b0:8, :])
            if ti == ntile_g - 1:
                ntk = ntile_g * 128
                s.dma("sp", QKT1.ap[b0:8, :, g0 * 128:g0 * 128 + ntk].rearrange("j p t -> p j t"), st_[:, b0:8, 0:ntk],
                      reads=[st_.b], writes=[QKT1.b])

        run_pipelined(tile_l1p1(t) for t in range(NT))
        ph.close()
        if stop_after == "L1P1":
            s.finish()
            return nc

        ph = Phase(nc, s)
        QDr = ph.ring("QD", [128, S], BF16, 2)
        KDr = ph.ring("KD", [128, NTOK], BF16, 2)
        VDr = ph.ring("VDs", [128, NT, 129], BF16, 2)
        zdr = ph.ring("zd", [128, 4], F32, 2)
        odr = ph.ring("od", [128, 4, 128], BF16, 3)

        def fin_d(u, acc):
            h, q0 = u["h"], u["q0"]
            rz = zdr.next()
            s.op("dve", "reciprocal", reads=[acc.b], writes=[rz.b], out=rz[:], in_=acc[:, :, 128])
            od = odr.next()
            s.op("dve", "tensor_tensor", reads=[acc.b, rz.b], writes=[od.b], out=od[:], in0=acc[:, :, 0:128],
                 in1=rz[:, :].unsqueeze(2).to_broadcast([128, 4, 128]), op=ALU.mult)
            s.dma("sp", MIX.ap[q0:q0 + 512, 512 + h * 128:512 + (h + 1) * 128].rearrange("(j p) d -> p j d", p=128),
                  od[:], reads=[od.b], writes=[MIX.b.part(("d", h, q0))])

        def load_d(h):
            Qh, Kh, Vh = QDr.items[h % 2], KDr.items[h % 2], VDr.items[h % 2]
            s.dma("sp", Qh[:], QKT1.ap[h, :, 0:S], reads=[QKT1.b], writes=[Qh.b])
            s.dma("sp", Kh[:], QKT1.ap[4 + h], reads=[QKT1.b], writes=[Kh.b])
            s.dma("sp", Vh[:], VD.ap.rearrange("(k p) (h e) -> p k h e", p=128, e=129)[:, :, h, :],
                  reads=[VD.b.part(t) for t in range(NT)], writes=[Vh.b])

        units = []
        for h in range(4):
            Qh, Kh, Vh = QDr.items[h % 2], KDr.items[h % 2], VDr.items[h % 2]
            first = len(units)
            for qb in range(8):
                keys = [dict(kT=Kh[:, kt * 128:(kt + 1) * 128], kb=[Kh.b], v=Vh[:, kt, :], vb=[Vh.b], mask=None)
                        for kt in range(NT)]
                units.append(dict(qT=Qh[:, qb * 512:(qb + 1) * 512], qb=[Qh.b], keys=keys, nq=512, vd1=129,
                                  scale=128.0 ** -0.5, fin=fin_d, h=h, q0=qb * 512))
            if h == 0:
                units[first]["pre"] = (lambda: load_d(0))
            if h < 3:
                units[first + 1]["pre"] = (lambda hh=h + 1: load_d(hh))
        attn_pipeline(ph, units, NT)
        ph.close()
        if stop_after == "L1P2":
            s.finish()
            return nc

        phase3a(1, X1, X1A, NLAT)
        phase3b(1, X1A, y, NLAT)
        gp.close()
        s.finish()
        print("program: instructions", s.n_ins, "waits", s.n_wait)
    return nc


def _rope_tables():
    rows = S // 64
    row = np.repeat(np.arange(rows, dtype=np.int32), 64).astype(np.float32)
    col = np.tile(np.arange(64, dtype=np.int32), rows).astype(np.float32)
    inv = (np.float32(10000.0) ** (-np.arange(16, dtype=np.float32) / np.float32(16))).astype(np.float32)
    ang = np.concatenate([row[:, None] * inv, col[:, None] * inv], axis=-1).astype(np.float32)
    c, sn = np.cos(ang).astype(np.float32), np.sin(ang).astype(np.float32)
    tab = np.concatenate([c, c, -sn, sn], axis=-1)
    return np.ascontiguousarray(tab.reshape(NLAT, 128, 128))


def _shared_inputs(inp):
    f = lambda a: np.ascontiguousarray(np.asarray(a, dtype=np.float32))
    ev = f(inp["ev_w_in"])[0]
    aq = ev[:, 0:512].reshape(D, 8, 64)
    aq_p = np.stack([aq[:, [j, 4 + j], :] for j in range(4)], axis=1).reshape(D, 512)
    wev = np.concatenate([aq_p, ev[:, 512:1024], ev[:, 1024:1152], ev[:, 1280:1792], ev[:, 1152:1280], ev[:, 1792:2304]], axis=1)
    ukv = f(inp["od_w_ukv"])[0].reshape(256, 4, 192)
    wukv = np.concatenate([ukv[:, :, 0:64].reshape(256, 256), ukv[:, :, 64:192].reshape(256, 512)], axis=1)
    vec = np.concatenate([
        f(inp["ev_qnorm_a"])[0], f(inp["ev_knorm_a"])[0], f(inp["ev_qnorm_b"])[0], f(inp["ev_knorm_b"])[0],
        f(inp["ev_sink"])[0], f(inp["ev_lam_q1"])[0], f(inp["ev_lam_k1"])[0], f(inp["ev_lam_q2"])[0], f(inp["ev_lam_k2"])[0],
        f(inp["ev_subln"])[0], f(inp["od_c_vnorm"])[0], f(inp["od_qa_norm"])[0], f(inp["od_kva_norm"])[0],
        f(inp["od_qnorm_nope"])[0], f(inp["od_knorm_nope"])[0], f(inp["od_qnorm_rope"])[0], f(inp["od_knorm_rope"])[0]])
    assert vec.shape[0] == NV
    j = np.arange(128)[:, None]
    i = np.arange(128)[None, :]
    masks = np.stack([(j >= i), (j <= i)]).astype(np.float32)
    return {
        "ada_w": f(inp["ada_w"]), "ada_b": f(inp["ada_b"]).reshape(2, 1, 6 * D),
        "n1g": f(inp["norm1_g"]).reshape(2, 1, D), "n2g": f(inp["norm2_g"]).reshape(2, 1, D),
        "wmix": f(inp["mix_w_out"]), "wfi": f(inp["ffn_w_in"]), "wfo": f(inp["ffn_w_out"]),
        "wev": f(wev), "wod": f(inp["od_w_in"])[0], "wuq": f(inp["od_w_uq"])[0], "wukv": f(wukv),
        "wsT": f(np.transpose(f(inp["od_c_ws"])[0], (0, 2, 1))), "bsT": f(f(inp["od_c_bs"])[0].T),
        "vecs": f(vec.reshape(1, NV)), "ident": np.eye(128, dtype=np.float32), "rope": _rope_tables(), "masks": masks,
    }


def make_in_maps(inp):
    shared = _shared_inputs(inp)
    x = np.asarray(inp["x"], dtype=np.float32)
    ctx = np.asarray(inp["ctx"], dtype=np.float32)
    c = np.asarray(inp["c"], dtype=np.float32)
    c_ctx = np.asarray(inp["c_ctx"], dtype=np.float32)
    maps = []
    for b in range(x.shape[0]):
        m = dict(shared)
        m["xs"] = np.ascontiguousarray(np.concatenate([x[b], ctx[b]], axis=0))
        cc = np.stack([c[b], c_ctx], axis=-1).reshape(8, 128, 2).transpose(1, 0, 2)
        m["cc"] = np.ascontiguousarray(cc)
        maps.append(m)
    return maps


_NC_CACHE = {}


def kernel(**inputs):
    if "nc" not in _NC_CACHE:
        _NC_CACHE["nc"] = build_program()
    nc = _NC_CACHE["nc"]
    in_maps = make_in_maps(inputs)
    res = run_bass_kernel_spmd(nc, in_maps, core_ids=list(range(len(in_maps))))
    return np.stack([np.asarray(r["y"], dtype=np.float32) for r in res.results], axis=0)
```

```python
import math
import numpy as np
import concourse.bass as bass
import concourse.mybir as mybir
from concourse.bass_utils import run_bass_kernel_spmd
from contextlib import ExitStack

F32 = mybir.dt.float32
BF16 = mybir.dt.bfloat16
ALU = mybir.AluOpType
AF = mybir.ActivationFunctionType
AX = mybir.AxisListType

D = 1024
S = 4096
LCTX = 256
NTOK = S + LCTX
NT = NTOK // 128
NLAT = S // 128
FH = 2816
EPS = 1e-6
NV = 1928

V_QA, V_KA, V_QB, V_KB, V_SINK, V_LQ1, V_LK1, V_LQ2, V_LK2, V_SUBLN = 0, 64, 128, 192, 256, 264, 328, 392, 456, 520
V_CVN, V_QAN, V_KVAN, V_QNN, V_KNN, V_QNR, V_KNR = 648, 1160, 1416, 1672, 1736, 1800, 1864


class Buf:
    def __init__(self, name, excl=False):
        self.name = name
        self.w = None
        self.r = {}
        self.excl = excl
        self.parts = {}

    def part(self, key):
        p = self.parts.get(key)
        if p is None:
            p = Buf(f"{self.name}[{key}]", self.excl)
            self.parts[key] = p
        return p


class Sched:
    ENG = ["pe", "act", "dve", "pool", "sp"]
    NRING = 8

    def __init__(self, nc, stack):
        self.nc = nc
        self.prog = {e: [] for e in self.ENG}
        self.cnt = {e: 0 for e in self.ENG}
        self.last = {e: None for e in self.ENG}
        self.pend = {e: False for e in self.ENG}
        self.sem = {e: stack.enter_context(nc.semaphore(f"sem_{e}")) for e in self.ENG}
        self.known = {e: {} for e in self.ENG}
        self.ring = {}
        self.ring_i = {}
        self.ring_val = {}
        for q in ("sp", "pool", "act"):
            self.ring[q] = [stack.enter_context(nc.semaphore(f"dq_{q}{i}")) for i in range(self.NRING)]
            self.ring_i[q] = 0
            self.ring_val[q] = [0] * self.NRING
        self.n_wait = 0
        self.n_ins = 0

    def _need(self, eng, ev, rec_waits):
        sem, val = ev
        if self.known[eng].get(sem, 0) >= val:
            return
        for e in self.ENG:
            if self.sem[e] == sem and val > self.cnt[e]:
                assert self.pend[e] and val == self.cnt[e] + 1, (e, val, self.cnt[e])
                self.last[e]["inc"] = True
                self.cnt[e] += 1
                self.pend[e] = False
        self.known[eng][sem] = val
        rec_waits.append((sem, val))
        self.n_wait += 1

    def _deps(self, eng, key, reads, writes, waits, is_dma):
        for b in reads:
            if b.w is not None:
                self._need(eng, b.w, waits)
            if b.excl:
                for k, ev in list(b.r.items()):
                    if k != key:
                        self._need(eng, ev, waits)
        for b in writes:
            if b.w is not None and not (eng == "pe" and b.w[0] == self.sem["pe"]):
                self._need(eng, b.w, waits)
            for k, ev in list(b.r.items()):
                if k != key or is_dma:
                    self._need(eng, ev, waits)

    def op(self, eng, method, reads=(), writes=(), **kw):
        waits = []
        eager = kw.pop("inc", None)
        if eager is None:
            eager = (eng != "pe") or (method == "matmul" and bool(kw.get("stop")))
        self._deps(eng, eng, reads, writes, waits, False)
        rec = {"m": method, "kw": kw, "waits": waits, "inc": False, "dma": None}
        self.prog[eng].append(rec)
        self.last[eng] = rec
        if eager:
            rec["inc"] = True
            self.cnt[eng] += 1
            self.pend[eng] = False
            ev = (self.sem[eng], self.cnt[eng])
        else:
            self.pend[eng] = True
            ev = (self.sem[eng], self.cnt[eng] + 1)
        for b in reads:
            b.r[eng] = ev
        for b in writes:
            b.w = ev
            b.r = {}
        self.n_ins += 1
        return rec

    def dma(self, q, out, in_, reads=(), writes=(), **kw):
        waits = []
        i = self.ring_i[q]
        slot = i % self.NRING
        self.ring_i[q] += 1
        sem = self.ring[q][slot]
        if self.ring_val[q][slot] > 0:
            self._need(q, (sem, self.ring_val[q][slot]), waits)
        key = (q, slot)
        self._deps(q, key, reads, writes, waits, True)
        self.ring_val[q][slot] += 16
        ev = (sem, self.ring_val[q][slot])
        rec = {"m": "dma_start", "kw": dict(out=out, in_=in_, **kw), "waits": waits, "inc": False, "dma": sem}
        self.prog[q].append(rec)
        for b in reads:
            b.r[key] = ev
        for b in writes:
            b.w = ev
            b.r = {}
        self.n_ins += 1
        return ev

    def barrier(self):
        evs = []
        for e in self.ENG:
            if self.pend[e]:
                self.last[e]["inc"] = True
                self.cnt[e] += 1
                self.pend[e] = False
            if self.cnt[e] > 0:
                evs.append((self.sem[e], self.cnt[e]))
        for q in self.ring:
            for s_, v in zip(self.ring[q], self.ring_val[q]):
                if v > 0:
                    evs.append((s_, v))
        for e in self.ENG:
            waits = []
            for ev in evs:
                if ev[0] == self.sem[e]:
                    continue
                if self.known[e].get(ev[0], 0) < ev[1]:
                    self.known[e][ev[0]] = ev[1]
                    waits.append(ev)
            if waits:
                self.prog[e].append({"m": None, "kw": {}, "waits": waits, "inc": False, "dma": None})

    def finish(self):
        self.barrier()
        nc = self.nc
        engobj = {"pe": "tensor", "act": "scalar", "dve": "vector", "pool": "gpsimd", "sp": "sync"}
        with nc.Block() as block:
            for e in self.ENG:
                prog = self.prog[e]
                sem = self.sem[e]

                def body(eng, prog=prog, sem=sem):
                    for rec in prog:
                        for (s_, v) in rec["waits"]:
                            eng.wait_ge(s_, v)
                        if rec["m"] is None:
                            continue
                        ins = getattr(eng, rec["m"])(**rec["kw"])
                        if rec["dma"] is not None:
                            ins.then_inc(rec["dma"], 16)
                        elif rec["inc"]:
                            ins.then_inc(sem, 1)

                getattr(block, engobj[e])(body)


class T:
    def __init__(self, t, name, excl=False):
        self.t = t
        self.b = Buf(name, excl)

    def __getitem__(self, k):
        return self.t[k]


class Phase:
    _n = 0

    def __init__(self, nc, s):
        self.nc = nc
        self.s = s
        self.st = ExitStack()
        Phase._n += 1
        self.pfx = f"p{Phase._n}_"

    def sb(self, name, shape, dt):
        return T(self.st.enter_context(self.nc.sbuf_tensor(self.pfx + name, list(shape), dt)), name)

    def ps(self, name, shape, dt):
        return T(self.st.enter_context(self.nc.psum_tensor(self.pfx + name, list(shape), dt)), name, True)

    def ring(self, name, shape, dt, n):
        return Ring([self.sb(f"{name}{i}", shape, dt) for i in range(n)])

    def close(self):
        self.s.barrier()
        try:
            print("phase", self.pfx, "sbuf spare KB", self.nc.sbuf_bytes_remaining // 1024 // 128 if self.nc.sbuf_bytes_remaining > 4 * 1024 * 1024 else self.nc.sbuf_bytes_remaining // 1024)
        except Exception as e:
            pass
        self.st.close()


class Ring:
    def __init__(self, items):
        self.items = items
        self.i = 0

    def next(self):
        t = self.items[self.i % len(self.items)]
        self.i += 1
        return t


class DT:
    def __init__(self, ap, name):
        self.ap = ap
        self.b = Buf(name)


def build_program(dbg=(), stop_after=None):
    Phase._n = 0
    nc = bass.Bass("TRN2", target_bir_lowering=False)

    def din(name, shape, dt=F32):
        return DT(nc.dram_tensor(name, list(shape), dt, kind="ExternalInput").ap(), name)

    def dscr(name, shape, dt):
        kind = "ExternalOutput" if name in dbg else "Internal"
        return DT(nc.dram_tensor(name, list(shape), dt, kind=kind).ap(), name)

    xs = din("xs", [NTOK, D])
    cc = din("cc", [128, 8, 2])
    ada_w = din("ada_w", [2, D, 6 * D])
    ada_b = din("ada_b", [2, 1, 6 * D])
    n1g = din("n1g", [2, 1, D])
    n2g = din("n2g", [2, 1, D])
    wmix = din("wmix", [2, D, D])
    wfi = din("wfi", [2, D, 2 * FH])
    wfo = din("wfo", [2, FH, D])
    wev = din("wev", [D, 2304])
    wod = din("wod", [D, 1600])
    wuq = din("wuq", [256, 512])
    wukv = din("wukv", [256, 768])
    wsT = din("wsT", [4, 128, 128])
    bsT = din("bsT", [128, 4])
    vecs = din("vecs", [1, NV])
    ident = din("ident", [128, 128])
    rope = din("rope", [NLAT, 128, 128])
    masks = din("masks", [2, 128, 128])
    y = DT(nc.dram_tensor("y", [S, D], F32, kind="ExternalOutput").ap(), "y")

    modV = dscr("modV", [2, 2, 6 * D], F32)
    QKT0 = dscr("QKT0", [13, 128, NTOK], BF16)
    VA = dscr("VA", [NTOK, 130], BF16)
    VB = dscr("VB", [NTOK, 516], BF16)
    MIX = dscr("MIX", [NTOK, D], BF16)
    H2T = dscr("H2T", [8, 128, NTOK], BF16)
    X1A = dscr("X1A", [NTOK, D], F32)
    X1 = dscr("X1", [NTOK, D], F32)
    QKT1 = dscr("QKT1", [8, 128, NTOK], BF16)
    VD = dscr("VD", [NTOK, 516], BF16)

    with ExitStack() as gst:
        s = Sched(nc, gst)

        gp = Phase(nc, s)
        idf = gp.sb("idf", [128, 128], F32)
        idb = gp.sb("idb", [128, 128], BF16)
        nh = gp.sb("nh", [128, 64], F32)
        vecb = gp.sb("vecb", [128, NV], F32)
        s.dma("sp", idf[:], ident.ap, writes=[idf.b])
        s.op("dve", "tensor_copy", reads=[idf.b], writes=[idb.b], out=idb[:], in_=idf[:])
        s.op("pool", "memset", writes=[nh.b], ap=nh[:], constant=-0.5)
        s.dma("sp", vecb[:], vecs.ap[0, :].partition_broadcast(128), writes=[vecb.b])

        def rstd_from_ss(ph, ssT, n, width, rname):
            v = ph.sb(rname + "_v", [128, n], F32) if not hasattr(ph, "_" + rname) else getattr(ph, "_" + rname)[0]
            r = ph.sb(rname + "_r", [128, n], F32) if not hasattr(ph, "_" + rname) else getattr(ph, "_" + rname)[1]
            setattr(ph, "_" + rname, (v, r))
            s.op("dve", "tensor_scalar", reads=[ssT.b], writes=[v.b], out=v[:], in0=ssT[:, 0:n], scalar1=1.0 / width,
                 scalar2=EPS, op0=ALU.mult, op1=ALU.add)
            s.op("pool", "tensor_tensor", reads=[v.b, nh.b], writes=[r.b], out=r[:], in0=v[:], in1=nh[:, 0:n], op=ALU.pow)
            return r

        def load_bcast(ph, name, src_ap, width, src_b, q="sp"):
            t = ph.sb(name, [128, width], F32)
            s.dma(q, t[:], src_ap.partition_broadcast(128), reads=[src_b], writes=[t.b])
            return t

        def run_pipelined(gens, first_stage=None):
            gens = list(gens)
            active = []
            i = 0
            while i < len(gens) or active:
                new = None
                if i < len(gens):
                    new = [gens[i], 0]
                    i += 1
                    try:
                        next(new[0])
                        new[1] = 1
                    except StopIteration:
                        new = None
                order = [a for a in active if a[1] + 1 == first_stage] + [a for a in active if a[1] + 1 != first_stage]
                for a in order:
                    try:
                        next(a[0])
                        a[1] += 1
                    except StopIteration:
                        active.remove(a)
                if new is not None:
                    active.append(new)

        def norm_mod_transpose(ph, xt, G, Sh, pT, hT_ap, hT_b, rings, split=True):
            junk, ssr, _, hbr = rings
            jk = junk.next()
            ss = ssr.next()
            s.op("act", "activation", reads=[xt.b], writes=[jk.b, ss.b], out=jk[:], in_=xt[:], func=AF.Square,
                 accum_out=ss[:])
            r = rstd_from_ss(ph, ss, 1, D, "rs_x")
            yield
            hb = hbr.next()
            s.op("act", "activation", reads=[xt.b, r.b], writes=[hb.b], out=hb[:], in_=xt[:], func=AF.Copy,
                 scale=r[:, 0:1])
            yield
            for k in range(8):
                s.op("pe", "transpose", reads=[hb.b, idb.b], writes=[pT.b], out=pT[:, k, :],
                     in_=hb[:, k * 128:(k + 1) * 128], identity=idb[:], inc=(k == 7))
            if split:
                yield
            for k in range(8):
                s.op("act", "activation", reads=[pT.b, G.b, Sh.b], writes=[hT_b], out=hT_ap[:, k, :], in_=pT[:, k, :],
                     func=AF.Identity, scale=G[:, k:k + 1], bias=Sh[:, k:k + 1])

        def load_w_bf16(ph, name, src_ap, kchunks, ncols, src_b):
            t = ph.sb(name, [128, kchunks, ncols], BF16)
            view = src_ap.rearrange("(k p) n -> p k n", p=128)
            step = max(1, min(kchunks, 4096 // ncols)) if ncols <= 4096 else 1
            for k0 in range(0, kchunks, step):
                k1 = min(kchunks, k0 + step)
                s.dma("pool", t[:, k0:k1, :], view[:, k0:k1, :], reads=[src_b], writes=[t.b.part(k0)])
            t.kparts = [t.b.part(k0) for k0 in range(0, kchunks, step)]
            return t

        def load_w_bf16_cols(ph, name, src_ap, kchunks, ncols, src_b, cb, order):
            t = ph.sb(name, [128, kchunks, ncols], BF16)
            view = src_ap.rearrange("(k p) n -> p k n", p=128)
            for b in order:
                c0, c1 = b * cb, min(ncols, (b + 1) * cb)
                s.dma("pool", t[:, :, c0:c1], view[:, :, c0:c1], reads=[src_b], writes=[t.b.part(("c", b))])
            t.cpart = lambda col: t.b.part(("c", col // cb))
            return t

        def head_norm(ph, qf, nslots, Gt, tag, wide=False):
            w = nslots * 64
            sq = ph.H_sq.next()
            s.op("act", "activation", reads=[qf.b], writes=[sq.b], out=sq[:, 0:w], in_=qf[:, 0:w], func=AF.Square)
            yield
            ssh = ph.H_ss.next()
            s.op("dve", "tensor_reduce", reads=[sq.b], writes=[ssh.b], out=ssh[:, 0:nslots],
                 in_=sq[:, 0:w].rearrange("p (h d) -> p h d", d=64), axis=AX.X, op=ALU.add)
            r = rstd_from_ss(ph, ssh, nslots, 64, "rs_h" + tag)
            yield
            qn = ph.H_qn.next()
            s.op("dve", "tensor_tensor", reads=[qf.b, r.b], writes=[qn.b],
                 out=qn[:, 0:w].rearrange("p (h d) -> p h d", d=64),
                 in0=qf[:, 0:w].rearrange("p (h d) -> p h d", d=64),
                 in1=r[:, 0:nslots].unsqueeze(2).to_broadcast([128, nslots, 64]), op=ALU.mult)
            if wide:
                yield
            qg = ph.H_qg.next()
            s.op("pool" if wide else "dve", "tensor_tensor", reads=[qn.b, Gt.b], writes=[qg.b], out=qg[:, 0:w],
                 in0=qn[:, 0:w], in1=Gt[:, 0:w], op=ALU.mult)
            return qg

        def rope_apply(ph, src_ap, dst_ap, n, rp, reads, dst_b, t2eng="pool"):
            t1 = ph.R_t1.next()
            t2 = ph.R_t2.next()
            t1v = t1[:, 0:n * 64].rearrange("p (h d) -> p h d", d=64)
            t2v = t2[:, 0:n * 64].rearrange("p (h d) -> p h d", d=64)
            s.op("dve", "tensor_tensor", reads=reads + [rp.b], writes=[t1.b], out=t1v, in0=src_ap,
                 in1=rp[:, 0:64].unsqueeze(1).to_broadcast([128, n, 64]), op=ALU.mult)
            s.op(t2eng, "tensor_tensor", reads=reads + [rp.b], writes=[t2.b.part(0)], out=t2v[:, :, 0:32], in0=src_ap[:, :, 32:64],
                 in1=rp[:, 64:96].unsqueeze(1).to_broadcast([128, n, 32]), op=ALU.mult)
            s.op("dve", "tensor_tensor", reads=reads + [rp.b], writes=[t2.b.part(1)], out=t2v[:, :, 32:64], in0=src_ap[:, :, 0:32],
                 in1=rp[:, 96:128].unsqueeze(1).to_broadcast([128, n, 32]), op=ALU.mult)
            s.op("dve", "tensor_tensor", reads=[t1.b, t2.b.part(0), t2.b.part(1)], writes=[dst_b], out=dst_ap, in0=t1v, in1=t2v,
                 op=ALU.add)

        ph = Phase(nc, s)
        cct = ph.sb("cct", [128, 8, 2], F32)
        sc = ph.sb("sc", [128, 8, 2], F32)
        ones2 = ph.sb("ones2", [1, 2], F32)
        brow = ph.sb("brow", [1, 6 * D], F32)
        mrow = ph.sb("mrow", [2, 6 * D], F32)
        wring = ph.ring("adaw", [128, 8, 512], F32, 2)
        pm = [ph.ps(f"pm{i}", [2, 512], F32) for i in range(2)]
        s.dma("sp", cct[:], cc.ap, writes=[cct.b])
        s.op("act", "activation", reads=[cct.b], writes=[sc.b], out=sc[:], in_=cct[:], func=AF.Silu)
        s.op("dve", "memset", writes=[ones2.b], ap=ones2[:], constant=1.0)
        import os
        NL_ = int(os.environ.get("KD_NL", "2"))
        NC_ = int(os.environ.get("KD_NC", "12"))
        for l in range(NL_):
            s.dma("sp", brow[:], ada_b.ap[l], writes=[brow.b])
            for c in range(NC_):
                wt = wring.next()
                s.dma("sp", wt[:], ada_w.ap[l][:, c * 512:(c + 1) * 512].rearrange("(k p) n -> p k n", p=128),
                      writes=[wt.b])
                p = pm[c % 2]
                for k in range(8):
                    s.op("pe", "matmul", reads=[sc.b, wt.b], writes=[p.b], out=p[:], lhsT=sc[:, k, :], rhs=wt[:, k, :],
                         start=(k == 0), stop=False)
                s.op("pe", "matmul", reads=[ones2.b, brow.b], writes=[p.b], out=p[:], lhsT=ones2[:],
                     rhs=brow[:, c * 512:(c + 1) * 512], start=False, stop=True)
                s.op("dve", "tensor_copy", reads=[p.b], writes=[mrow.b], out=mrow[:, c * 512:(c + 1) * 512], in_=p[:])
            s.dma("sp", modV.ap[l], mrow[:], reads=[mrow.b], writes=[modV.b])
        ph.close()
        if stop_after == "P0":
            s.finish()
            return nc

        def mod_tiles(ph, l, v, idx_shift, idx_scale, gdt, tag):
            def colload(name, ap1d, b):
                t = ph.sb(name, [128, 8], F32)
                s.dma("sp", t[:], ap1d.rearrange("(k p) -> p k", p=128), reads=[b], writes=[t.b],
                      allow_slow_non_contiguous=True)
                return t
            sh = colload(f"sh{tag}", modV.ap[l, v, idx_shift * D:(idx_shift + 1) * D], modV.b)
            scl = colload(f"scl{tag}", modV.ap[l, v, idx_scale * D:(idx_scale + 1) * D], modV.b)
            gg = colload(f"gg{tag}", gdt.ap[l, 0, :], gdt.b)
            s.op("dve", "scalar_tensor_tensor", reads=[scl.b, gg.b], writes=[scl.b], out=scl[:], in0=scl[:], scalar=1.0,
                 in1=gg[:], op0=ALU.add, op1=ALU.mult)
            return scl, sh

        def norm_rings(ph):
            return (ph.ring("junk", [128, D], BF16, 1), ph.ring("ssx", [128, 1], F32, 2),
                    None, ph.ring("hb", [128, D], BF16, 2))

        def head_rings(ph, w):
            ph.H_sq = ph.ring("hsq", [128, w], F32, 2)
            ph.H_ss = ph.ring("hss", [128, 32], F32, 2)
            ph.H_qn = ph.ring("hqn", [128, w], F32, 2)
            ph.H_qg = ph.ring("hqg", [128, w], F32, 2)
            ph.R_t1 = ph.ring("rt1", [128, w], F32, 1)
            ph.R_t2 = ph.ring("rt2", [128, w], F32, 1)

        ph = Phase(nc, s)
        W = load_w_bf16(ph, "wev", wev.ap, 8, 2304, wev.b)
        GL, SL = mod_tiles(ph, 0, 0, 0, 1, n1g, "l")
        GC, SC = mod_tiles(ph, 0, 1, 0, 1, n1g, "c")
        G26 = ph.sb("g26", [128, 26 * 64], F32)
        g26v = G26[:, :].rearrange("p (h d) -> p h d", d=64)
        for (a, b_, off) in ((0, 8, V_QA), (8, 16, V_QB), (16, 18, V_KA), (18, 26, V_KB)):
            s.op("dve", "tensor_copy", reads=[vecb.b], writes=[G26.b], out=g26v[:, a:b_, :],
                 in_=vecb[:, off:off + 64].unsqueeze(1).to_broadcast([128, b_ - a, 64]))
        nr = norm_rings(ph)
        head_rings(ph, 1664)
        xr = ph.ring("x", [128, D], F32, 4)
        rpr = ph.ring("rp", [128, 128], F32, 7)
        hTr = ph.ring("hT", [128, 8, 128], BF16, 2)
        qfr = ph.ring("qf", [128, 1664], F32, 3)
        qbr = ph.ring("qb", [128, 1664], BF16, 2)
        var = ph.ring("va", [128, 2, 65], BF16, 2)
        vbr = ph.ring("vb", [128, 4, 129], BF16, 2)
        for t_ in var.items + vbr.items:
            s.op("pool", "memset", writes=[t_.b], ap=t_[:], constant=1.0)
        qkst = ph.ring("qkst", [128, 13, 256], BF16, 2)
        pT = ph.ps("pT", [128, 8, 128], BF16)
        pO = ph.ps("pO", [128, 2560], F32)
        pQ1 = ph.ps("pQ1", [128, 8, 128], BF16)
        pQ2 = ph.ps("pQ2", [128, 8, 128], BF16)
        def tile_l0p1(t):
            g0 = (t // 2) * 2
            ti = t - g0
            ntile_g = 2
            st_ = qkst.items[(t // 2) % 2]
            isctx = t >= NLAT
            xt = xr.next()
            s.dma("sp", xt[:], xs.ap[t * 128:(t + 1) * 128, :], writes=[xt.b])
            yield
            hT = hTr.next()
            yield from norm_mod_transpose(ph, xt, GC if isctx else GL, SC if isctx else SL, pT, hT[:], hT.b, nr)
            yield
            for k in range(8):
                for c in range(5):
                    n0, n1 = c * 512, min(2304, (c + 1) * 512)
                    s.op("pe", "matmul", reads=[hT.b] + W.kparts, writes=[pO.b], out=pO[:, n0:n1],
                         lhsT=hT[:, k, :], rhs=W[:, k, n0:n1], start=(k == 0), stop=(k == 7))
            if not isctx:
                rp = rpr.next()
                s.dma("sp", rp[:], rope.ap[t], writes=[rp.b])
            yield
            qf = qfr.next()
            s.op("act", "copy", reads=[pO.b], writes=[qf.b], out=qf[:], in_=pO[:, 0:1664])
            va = var.next()
            vb = vbr.next()
            s.op("act", "copy", reads=[pO.b], writes=[va.b], out=va[:, :, 0:64],
                 in_=pO[:, 1664:1792].rearrange("p (h d) -> p h d", d=64))
            s.op("act", "copy", reads=[pO.b], writes=[vb.b], out=vb[:, :, 0:128],
                 in_=pO[:, 1792:2304].rearrange("p (h d) -> p h d", d=128))
            s.dma("sp", VA.ap[t * 128:(t + 1) * 128, :], va[:].rearrange("p h d -> p (h d)"), reads=[va.b],
                  writes=[VA.b.part(t)])
            s.dma("sp", VB.ap[t * 128:(t + 1) * 128, :], vb[:].rearrange("p h d -> p (h d)"), reads=[vb.b],
                  writes=[VB.b.part(t)])
            qg = yield from head_norm(ph, qf, 26, G26, "0", wide=True)
            yield
            qb = qbr.next()
            if isctx:
                s.op("dve", "tensor_copy", reads=[qg.b], writes=[qb.b], out=qb[:], in_=qg[:, 0:1664])
            else:
                rope_apply(ph, qg[:, 0:1664].rearrange("p (h d) -> p h d", d=64),
                           qb[:, :].rearrange("p (h d) -> p h d", d=64), 26, rp, [qg.b], qb.b)
            yield
            for j in range(13):
                pq = pQ1 if j < 8 else pQ2
                s.op("pe", "transpose", reads=[qb.b, idb.b], writes=[pq.b], out=pq[:, j % 8, :],
                     in_=qb[:, j * 128:(j + 1) * 128], identity=idb[:], inc=(j in (7, 12)))
            yield
            s.op("act", "copy", reads=[pQ1.b], writes=[st_.b], out=st_[:, 0:8, ti * 128:(ti + 1) * 128], in_=pQ1[:])
            s.op("act", "copy", reads=[pQ2.b], writes=[st_.b], out=st_[:, 8:13, ti * 128:(ti + 1) * 128],
                 in_=pQ2[:, 0:5, :])
            if ti == ntile_g - 1:
                ntk = ntile_g * 128
                s.dma("sp", QKT0.ap[:, :, g0 * 128:g0 * 128 + ntk].rearrange("j p t -> p j t"), st_[:, :, 0:ntk],
                      reads=[st_.b], writes=[QKT0.b])

        run_pipelined((tile_l0p1(t) for t in range(NT)), first_stage=7)
        ph.close()
        if stop_after == "L0P1":
            s.finish()
            return nc

        def attn_pipeline(ph, units, nkmax):
            PT = [ph.sb(f"PT{i}", [128, nkmax, 512], BF16) for i in range(2)]
            Sb = [ph.ps(f"S{i}", [128, 512], F32) for i in range(4)]
            acc = ph.ps("acc", [128, 4, 512], F32)
            si = 0
            n = len(units)
            for ui in range(n + 1):
                cur = units[ui] if ui < n else None
                prev = units[ui - 1] if ui > 0 else None
                if cur is not None and cur.get("pre") is not None:
                    cur["pre"]()
                nk = max(len(cur["keys"]) if cur else 0, len(prev["keys"]) if prev else 0)
                for kt in range(nk):
                    if cur is not None and kt < len(cur["keys"]):
                        key = cur["keys"][kt]
                        nq = cur["nq"]
                        sbk = Sb[si % 4]
                        si += 1
                        so = sbk[:, 0:nq]
                        if len(cur["qT"].shape) == 3:
                            so = so.rearrange("p (g q) -> p g q", q=128)
                        s.op("pe", "matmul", reads=cur["qb"] + key["kb"], writes=[sbk.b], out=so, lhsT=key["kT"],
                             rhs=cur["qT"], start=True, stop=True)
                        pt = PT[ui % 2]
                        ptb = pt.b.part(kt)
                        s.op("act", "activation", reads=[sbk.b], writes=[ptb], out=pt[:, kt, 0:nq], in_=sbk[:, 0:nq],
                             func=AF.Exp, scale=cur["scale"])
                        if key.get("mask") is not None:
                            mk = key["mask"]
                            s.op("dve", "tensor_tensor", reads=[ptb, mk.b], writes=[ptb],
                                 out=pt[:, kt, 0:nq].rearrange("p (g q) -> p g q", q=128),
                                 in0=pt[:, kt, 0:nq].rearrange("p (g q) -> p g q", q=128),
                                 in1=mk[:, :].unsqueeze(1).to_broadcast([128, nq // 128, 128]), op=ALU.mult)
                    if prev is not None and kt < len(prev["keys"]):
                        key = prev["keys"][kt]
                        pt = PT[(ui - 1) % 2]
                        ptb = pt.b.part(kt)
                        nkp = len(prev["keys"])
                        vd1 = prev["vd1"]
                        for j in range(prev["nq"] // 128):
                            s.op("pe", "matmul", reads=[ptb] + key["vb"], writes=[acc.b], out=acc[:, j, 0:vd1],
                                 lhsT=pt[:, kt, j * 128:(j + 1) * 128], rhs=key["v"], start=(kt == 0), stop=(kt == nkp - 1))
                if prev is not None:
                    prev["fin"](prev, acc)

        mprev_f = gp.sb("mprevf", [128, 128], F32)
        mnext_f = gp.sb("mnextf", [128, 128], F32)
        mprev = gp.sb("mprev", [128, 128], BF16)
        mnext = gp.sb("mnext", [128, 128], BF16)
        s.dma("sp", mprev_f[:], masks.ap[0], writes=[mprev_f.b])
        s.dma("sp", mnext_f[:], masks.ap[1], writes=[mnext_f.b])
        s.op("dve", "tensor_copy", reads=[mprev_f.b], writes=[mprev.b], out=mprev[:], in_=mprev_f[:])
        s.op("dve", "tensor_copy", reads=[mnext_f.b], writes=[mnext.b], out=mnext[:], in_=mnext_f[:])

        ph = Phase(nc, s)
        QA = ph.sb("QA", [128, 4, NTOK], BF16)
        KAz = [ph.sb(f"KAz{i}", [128, NTOK], BF16) for i in range(2)]
        s.op("pool", "memset", writes=[KAz[0].b], ap=KAz[0][64:128, :], constant=0.0)
        s.op("pool", "memset", writes=[KAz[1].b], ap=KAz[1][0:64, :], constant=0.0)
        VAs = ph.sb("VAs", [128, NT, 130], BF16)
        s.dma("sp", QA[:], QKT0.ap[0:4].rearrange("j p t -> p j t"), reads=[QKT0.b], writes=[QA.b])
        s.dma("sp", KAz[0][0:64, :], QKT0.ap[8, 0:64, :], reads=[QKT0.b], writes=[KAz[0].b])
        s.dma("sp", KAz[1][64:128, :], QKT0.ap[8, 64:128, :], reads=[QKT0.b], writes=[KAz[1].b])
        s.dma("sp", VAs[:], VA.ap.rearrange("(k p) e -> p k e", p=128), reads=[VA.b.part(t) for t in range(NT)],
              writes=[VAs.b])
        esink = ph.sb("esink", [128, 8], F32)
        s.op("act", "activation", reads=[vecb.b], writes=[esink.b], out=esink[:], in_=vecb[:, V_SINK:V_SINK + 8], func=AF.Exp)
        zr = ph.ring("za", [128, 4], F32, 2)
        rzr = ph.ring("rza", [128, 4], F32, 2)
        obr = ph.ring("oba", [128, 4, 64], BF16, 3)

        def fin_a(u, acc):
            kvh, n = u["kvh"], u["n"]
            z = zr.next()
            s.op("dve", "tensor_tensor", reads=[acc.b, esink.b], writes=[z.b], out=z[:], in0=acc[:, :, 64],
                 in1=esink[:, kvh * 4:(kvh + 1) * 4], op=ALU.add)
            rz = rzr.next()
            s.op("dve", "reciprocal", reads=[z.b], writes=[rz.b], out=rz[:], in_=z[:])
            ob = obr.next()
            s.op("dve", "tensor_tensor", reads=[acc.b, rz.b], writes=[ob.b], out=ob[:], in0=acc[:, :, 0:64],
                 in1=rz[:, :].unsqueeze(2).to_broadcast([128, 4, 64]), op=ALU.mult)
            s.dma("sp", MIX.ap[n * 128:(n + 1) * 128, kvh * 256:(kvh + 1) * 256], ob[:].rearrange("p g d -> p (g d)"),
                  reads=[ob.b], writes=[MIX.b.part(("a", n, kvh))])

        units = []
        for n in range(NT):
            if n < NLAT:
                kl = []
                if n > 0:
                    kl.append((n - 1, mprev))
                kl.append((n, None))
                if n < NLAT - 1:
                    kl.append((n + 1, mnext))
                kl += [(32, None), (33, None)]
            else:
                kl = [(32, None), (33, None)]
            for kvh in range(2):
                keys = [dict(kT=KAz[kvh][:, kt * 128:(kt + 1) * 128], kb=[KAz[kvh].b], v=VAs[:, kt, kvh * 65:(kvh + 1) * 65],
                             vb=[VAs.b], mask=mk) for (kt, mk) in kl]
                units.append(dict(qT=QA[:, :, n * 128:(n + 1) * 128], qb=[QA.b], keys=keys, nq=512, vd1=65,
                                  scale=0.125, fin=fin_a, kvh=kvh, n=n))
        attn_pipeline(ph, units, 5)
        ph.close()
        if stop_after == "L0P2A":
            s.finish()
            return nc

        lam_init0 = 0.8 - 0.6 * math.exp(-0.3 * 0)
        ph = Phase(nc, s)
        lt = ph.sb("lt", [128, 128], F32)
        lsum = ph.sb("lsum", [128, 2], F32)
        lexp = ph.sb("lexp", [128, 2], F32)
        nlam = ph.sb("nlam", [128, 1], F32)
        s.op("dve", "tensor_tensor", reads=[vecb.b], writes=[lt.b], out=lt[:, 0:64], in0=vecb[:, V_LQ1:V_LQ1 + 64],
             in1=vecb[:, V_LK1:V_LK1 + 64], op=ALU.mult)
        s.op("dve", "tensor_tensor", reads=[vecb.b], writes=[lt.b], out=lt[:, 64:128], in0=vecb[:, V_LQ2:V_LQ2 + 64],
             in1=vecb[:, V_LK2:V_LK2 + 64], op=ALU.mult)
        s.op("dve", "tensor_reduce", reads=[lt.b], writes=[lsum.b], out=lsum[:],
             in_=lt[:, :].rearrange("p (a d) -> p a d", d=64), axis=AX.X, op=ALU.add)
        s.op("act", "activation", reads=[lsum.b], writes=[lexp.b], out=lexp[:], in_=lsum[:], func=AF.Exp)
        s.op("dve", "tensor_tensor", reads=[lexp.b], writes=[nlam.b], out=nlam[:], in0=lexp[:, 1:2], in1=lexp[:, 0:1],
             op=ALU.subtract)
        s.op("dve", "tensor_scalar", reads=[nlam.b], writes=[nlam.b], out=nlam[:], in0=nlam[:], scalar1=-lam_init0,
             scalar2=None, op0=ALU.add)
        subl = ph.sb("subl", [128, 128], F32)
        s.op("dve", "tensor_scalar", reads=[vecb.b], writes=[subl.b], out=subl[:], in0=vecb[:, V_SUBLN:V_SUBLN + 128],
             scalar1=1.0 - lam_init0, scalar2=None, op0=ALU.mult)
        QBr = ph.ring("QB", [128, NTOK], BF16, 2)
        KBz = [[ph.sb(f"KBz{i}_{j}", [128, NTOK], BF16) for j in range(2)] for i in range(2)]
        for i in range(2):
            s.op("pool", "memset", writes=[KBz[i][0].b], ap=KBz[i][0][64:128, :], constant=0.0)
            s.op("pool", "memset", writes=[KBz[i][1].b], ap=KBz[i][1][0:64, :], constant=0.0)
        VBr = ph.ring("VBs", [128, NT, 129], BF16, 2)
        o1r = ph.ring("o1", [128, 4, 128], F32, 2)
        z1r = ph.ring("zb", [128, 4], F32, 4)
        tbr = ph.ring("tb", [128, 4, 128], F32, 2)
        obbr = ph.ring("obb", [128, 4, 128], F32, 2)
        sqbr = ph.ring("sqb", [128, 4, 128], F32, 1)
        ssbr = ph.ring("ssb", [128, 4], F32, 2)
        onr = ph.ring("onb", [128, 4, 128], F32, 1)
        outbr = ph.ring("outb", [128, 4, 128], BF16, 3)
        state = {}

        def fin_b(u, acc):
            h, q0, nj, sidx = u["h"], u["q0"], u["nq"] // 128, u["s"]
            rz = z1r.next()
            s.op("dve", "reciprocal", reads=[acc.b], writes=[rz.b], out=rz[:, 0:nj], in_=acc[:, 0:nj, 128])
            if sidx == 0:
                o1 = o1r.next()
                s.op("dve", "tensor_tensor", reads=[acc.b, rz.b], writes=[o1.b], out=o1[:, 0:nj, :], in0=acc[:, 0:nj, 0:128],
                     in1=rz[:, 0:nj].unsqueeze(2).to_broadcast([128, nj, 128]), op=ALU.mult)
                state["o1"] = o1
                return
            o1 = state["o1"]
            rzl = z1r.next()
            s.op("dve", "tensor_scalar", reads=[rz.b, nlam.b], writes=[rzl.b], out=rzl[:, 0:nj], in0=rz[:, 0:nj],
                 scalar1=nlam[:, 0:1], scalar2=None, op0=ALU.mult)
            tb = tbr.next()
            s.op("dve", "tensor_tensor", reads=[acc.b, rzl.b], writes=[tb.b], out=tb[:, 0:nj, :], in0=acc[:, 0:nj, 0:128],
                 in1=rzl[:, 0:nj].unsqueeze(2).to_broadcast([128, nj, 128]), op=ALU.mult)
            ob = obbr.next()
            s.op("pool", "tensor_tensor", reads=[tb.b, o1.b], writes=[ob.b], out=ob[:, 0:nj, :], in0=tb[:, 0:nj, :],
                 in1=o1[:, 0:nj, :], op=ALU.add)
            sq = sqbr.next()
            s.op("pool", "tensor_tensor", reads=[ob.b], writes=[sq.b], out=sq[:, 0:nj, :], in0=ob[:, 0:nj, :],
                 in1=ob[:, 0:nj, :], op=ALU.mult)
            ss = ssbr.next()
            s.op("dve", "tensor_reduce", reads=[sq.b], writes=[ss.b], out=ss[:, 0:nj], in_=sq[:, 0:nj, :], axis=AX.X,
                 op=ALU.add)
            r = rstd_from_ss(ph, ss, nj, 128, "rs_b%d" % nj)
            on = onr.next()
            s.op("dve", "tensor_tensor", reads=[ob.b, r.b], writes=[on.b], out=on[:, 0:nj, :], in0=ob[:, 0:nj, :],
                 in1=r[:, 0:nj].unsqueeze(2).to_broadcast([128, nj, 128]), op=ALU.mult)
            out = outbr.next()
            s.op("pool", "tensor_tensor", reads=[on.b, subl.b], writes=[out.b], out=out[:, 0:nj, :], in0=on[:, 0:nj, :],
                 in1=subl[:, :].unsqueeze(1).to_broadcast([128, nj, 128]), op=ALU.mult)
            s.dma("sp", MIX.ap[q0:q0 + nj * 128, 512 + h * 128:512 + (h + 1) * 128].rearrange("(j p) d -> p j d", p=128),
                  out[:, 0:nj, :], reads=[out.b], writes=[MIX.b.part(("b", h, q0))])

        def load_b(h):
            Qh, Kz, Vh = QBr.items[h % 2], KBz[h % 2], VBr.items[h % 2]
            s.dma("sp", Qh[:], QKT0.ap[4 + h], reads=[QKT0.b], writes=[Qh.b])
            s.dma("sp", Kz[0][0:64, :], QKT0.ap[9 + h, 0:64, :], reads=[QKT0.b], writes=[Kz[0].b])
            s.dma("sp", Kz[1][64:128, :], QKT0.ap[9 + h, 64:128, :], reads=[QKT0.b], writes=[Kz[1].b])
            s.dma("sp", Vh[:], VB.ap.rearrange("(k p) (h e) -> p k h e", p=128, e=129)[:, :, h, :],
                  reads=[VB.b.part(t) for t in range(NT)], writes=[Vh.b])

        units = []
        for h in range(4):
            Qh, Kz, Vh = QBr.items[h % 2], KBz[h % 2], VBr.items[h % 2]
            blocks = [(qb * 512, 512, list(range(NT))) for qb in range(8)] + [(S, 256, [32, 33])]
            first = len(units)
            for (q0, nq, kl) in blocks:
                for sidx in range(2):
                    Kh = Kz[sidx]
                    keys = [dict(kT=Kh[:, kt * 128:(kt + 1) * 128], kb=[Kh.b], v=Vh[:, kt, :], vb=[Vh.b], mask=None)
                            for kt in kl]
                    units.append(dict(qT=Qh[:, q0:q0 + nq], qb=[Qh.b], keys=keys, nq=nq, vd1=129, scale=0.125,
                                      fin=fin_b, h=h, q0=q0, s=sidx))
            if h == 0:
                units[first]["pre"] = (lambda: load_b(0))
            if h < 3:
                units[first + 1]["pre"] = (lambda hh=h + 1: load_b(hh))
        attn_pipeline(ph, units, NT)
        ph.close()
        if stop_after == "L0P2B":
            s.finish()
            return nc

        def phase3a(l, Xin, Xout, ntiles):
            ph = Phase(nc, s)
            Wm = load_w_bf16(ph, "wmix", wmix.ap[l], 8, D, wmix.b)
            G2L, S2L = mod_tiles(ph, l, 0, 3, 4, n2g, "l")
            gateL = load_bcast(ph, "gateL", modV.ap[l, 0, 2 * D:3 * D], D, modV.b)
            if ntiles > NLAT:
                G2C, S2C = mod_tiles(ph, l, 1, 3, 4, n2g, "c")
                gateC = load_bcast(ph, "gateC", modV.ap[l, 1, 2 * D:3 * D], D, modV.b)
            nr = norm_rings(ph)
            xr = ph.ring("x", [128, D], F32, 5)
            mr = ph.ring("mx", [128, D], BF16, 3)
            mTr = ph.ring("mT", [128, 8, 128], BF16, 2)
            tmr = ph.ring("tm", [128, D], F32, 1)
            x1r = ph.ring("x1", [128, D], F32, 4)
            hst = ph.ring("hst", [128, 8, 512], BF16, 2)
            pT = ph.ps("pT", [128, 8, 128], BF16)
            pT2 = ph.ps("pT2", [128, 8, 128], BF16)
            pP = ph.ps("pP", [128, D], F32)
            def tile_p3a(t):
                g0 = (t // 4) * 4
                ti = t - g0
                ntile_g = min(ntiles, g0 + 4) - g0
                st_ = hst.items[(t // 4) % 2]
                isctx = t >= NLAT
                xt = xr.next()
                s.dma("sp", xt[:], Xin.ap[t * 128:(t + 1) * 128, :], reads=[Xin.b.part(t)], writes=[xt.b])
                mx = mr.next()
                s.dma("sp", mx[:], MIX.ap[t * 128:(t + 1) * 128, :], reads=[MIX.b] + list(MIX.b.parts.values()),
                      writes=[mx.b])
                yield
                for k in range(8):
                    s.op("pe", "transpose", reads=[mx.b, idb.b], writes=[pT.b], out=pT[:, k, :],
                         in_=mx[:, k * 128:(k + 1) * 128], identity=idb[:], inc=(k == 7))
                yield
                mT = mTr.next()
                s.op("act", "copy", reads=[pT.b], writes=[mT.b], out=mT[:], in_=pT[:])
                yield
                for k in range(8):
                    for c in range(2):
                        s.op("pe", "matmul", reads=[mT.b] + Wm.kparts, writes=[pP.b], out=pP[:, c * 512:(c + 1) * 512],
                             lhsT=mT[:, k, :], rhs=Wm[:, k, c * 512:(c + 1) * 512], start=(k == 0), stop=(k == 7))
                yield
                tm = tmr.next()
                gate = gateC if isctx else gateL
                s.op("dve", "tensor_tensor", reads=[pP.b, gate.b], writes=[tm.b], out=tm[:], in0=pP[:], in1=gate[:],
                     op=ALU.mult)
                x1 = x1r.next()
                s.op("pool", "tensor_tensor", reads=[tm.b, xt.b], writes=[x1.b], out=x1[:], in0=tm[:], in1=xt[:],
                     op=ALU.add)
                s.dma("sp", Xout.ap[t * 128:(t + 1) * 128, :], x1[:], reads=[x1.b], writes=[Xout.b.part(t)])
                yield
                yield from norm_mod_transpose(ph, x1, G2C if isctx else G2L, S2C if isctx else S2L, pT2,
                                              st_[:, :, ti * 128:(ti + 1) * 128], st_.b, nr)
                if ti == ntile_g - 1:
                    ntk = ntile_g * 128
                    s.dma("sp", H2T.ap[:, :, g0 * 128:g0 * 128 + ntk].rearrange("j p t -> p j t"), st_[:, :, 0:ntk],
                          reads=[st_.b], writes=[H2T.b.part(g0)])

            run_pipelined((tile_p3a(t) for t in range(ntiles)), first_stage=5)
            ph.close()

        def phase3b(l, Xin, Xout, ntiles):
            ph = Phase(nc, s)
            Wi = load_w_bf16_cols(ph, "wfi", wfi.ap[l], 8, 2 * FH, wfi.b, 512, [0, 5, 6, 1, 7, 2, 8, 3, 9, 4, 10])
            Wo = load_w_bf16(ph, "wfo", wfo.ap[l], 22, D, wfo.b)
            gate = load_bcast(ph, "gate", modV.ap[l, 0, 5 * D:6 * D], D, modV.b)
            h2r = ph.ring("h2", [128, 8, 512], BF16, 1)
            actT = ph.sb("actT", [128, 22, 512], BF16)
            sgr = ph.ring("sg", [128, 512], F32, 2)
            xr = ph.ring("x", [128, D], F32, 1)
            x2r = ph.ring("x2", [128, D], F32, 2)
            pG = [ph.ps(f"pG{i}", [128, 512], F32) for i in range(2)]
            pU = [ph.ps(f"pU{i}", [128, 512], F32) for i in range(2)]
            pY = [ph.ps(f"pY{i}", [128, D], F32) for i in range(2)]
            yi = 0
            for g0 in range(0, ntiles, 4):
                tiles = list(range(g0, min(ntiles, g0 + 4)))
                ntk = len(tiles) * 128
                if tiles[0] >= NLAT:
                    s.dma("sp", gate[:], modV.ap[l, 1, 5 * D:6 * D].partition_broadcast(128), reads=[modV.b],
                          writes=[gate.b])
                h2 = h2r.next()
                if g0 == 0:
                    s.dma("sp", h2[:, :, 0:ntk], H2T.ap[:, :, 0:ntk].rearrange("j p t -> p j t"),
                          reads=[H2T.b.part(0)], writes=[h2.b])
                for j in range(22):
                    pg, pu = pG[j % 2], pU[j % 2]
                    for k in range(8):
                        s.op("pe", "matmul", reads=[h2.b, Wi.cpart(j * 128)], writes=[pg.b], out=pg[:, 0:ntk],
                             lhsT=Wi[:, k, j * 128:(j + 1) * 128], rhs=h2[:, k, 0:ntk], start=(k == 0), stop=(k == 7))
                    for k in range(8):
                        s.op("pe", "matmul", reads=[h2.b, Wi.cpart(FH + j * 128)], writes=[pu.b], out=pu[:, 0:ntk],
                             lhsT=Wi[:, k, FH + j * 128:FH + (j + 1) * 128], rhs=h2[:, k, 0:ntk], start=(k == 0),
                             stop=(k == 7))
                    sg = sgr.next()
                    s.op("act", "activation", reads=[pg.b], writes=[sg.b], out=sg[:, 0:ntk], in_=pg[:, 0:ntk], func=AF.Silu)
                    s.op("dve", "tensor_tensor", reads=[pu.b, sg.b], writes=[actT.b.part(j)], out=actT[:, j, 0:ntk],
                         in0=pu[:, 0:ntk], in1=sg[:, 0:ntk], op=ALU.mult)
                if g0 + 4 < ntiles:
                    n0_ = g0 + 4
                    ntk2 = (min(ntiles, n0_ + 4) - n0_) * 128
                    s.dma("act", h2[:, :, 0:ntk2], H2T.ap[:, :, n0_ * 128:n0_ * 128 + ntk2].rearrange("j p t -> p j t"),
                          reads=[H2T.b.part(n0_)], writes=[h2.b])
                for ti, t in enumerate(tiles):
                    isctx = t >= NLAT
                    py = pY[yi % 2]
                    yi += 1
                    for c in range(2):
                        for j in range(22):
                            s.op("pe", "matmul", reads=[actT.b.part(j)] + Wo.kparts, writes=[py.b],
                                 out=py[:, c * 512:(c + 1) * 512], lhsT=actT[:, j, ti * 128:(ti + 1) * 128],
                                 rhs=Wo[:, j, c * 512:(c + 1) * 512], start=(j == 0), stop=(j == 21))
                    xt = xr.next()
                    s.dma("sp", xt[:], Xin.ap[t * 128:(t + 1) * 128, :], reads=[Xin.b.part(t)], writes=[xt.b])
                    x2 = x2r.next()
                    s.op("dve", "tensor_tensor", reads=[py.b, gate.b], writes=[x2.b], out=x2[:], in0=py[:], in1=gate[:],
                         op=ALU.mult)
                    s.op("pool", "tensor_tensor", reads=[x2.b, xt.b], writes=[x2.b], out=x2[:], in0=x2[:], in1=xt[:],
                         op=ALU.add)
                    s.dma("sp", Xout.ap[t * 128:(t + 1) * 128, :], x2[:], reads=[x2.b], writes=[Xout.b.part(t)])
            ph.close()

        phase3a(0, xs, X1A, NT)
        if stop_after == "L0P3A":
            s.finish()
            return nc
        phase3b(0, X1A, X1, NT)
        if stop_after == "L0":
            s.finish()
            return nc

        ph = Phase(nc, s)
        W1 = load_w_bf16(ph, "wod", wod.ap, 8, 1600, wod.b)
        Wq = load_w_bf16(ph, "wuq", wuq.ap, 2, 512, wuq.b)
        Wkv = load_w_bf16(ph, "wukv", wukv.ap, 2, 768, wukv.b)
        Ws = ph.sb("ws", [128, 4, 128], BF16)
        s.dma("pool", Ws[:], wsT.ap.rearrange("g q p -> q g p"), reads=[wsT.b], writes=[Ws.b])
        bs = ph.sb("bs", [128, 4], F32)
        s.dma("sp", bs[:], bsT.ap, writes=[bs.b])
        GL, SL = mod_tiles(ph, 1, 0, 0, 1, n1g, "l")
        GC, SC = mod_tiles(ph, 1, 1, 0, 1, n1g, "c")
        G13 = ph.sb("g13", [128, 13 * 64], F32)
        g13v = G13[:, :].rearrange("p (h d) -> p h d", d=64)
        g13q = G13[:, 0:512].rearrange("p (h t d) -> p h t d", t=2, d=64)
        s.op("dve", "tensor_copy", reads=[vecb.b], writes=[G13.b], out=g13q[:, :, 0, :],
             in_=vecb[:, V_QNN:V_QNN + 64].unsqueeze(1).to_broadcast([128, 4, 64]))
        s.op("dve", "tensor_copy", reads=[vecb.b], writes=[G13.b], out=g13q[:, :, 1, :],
             in_=vecb[:, V_QNR:V_QNR + 64].unsqueeze(1).to_broadcast([128, 4, 64]))
        s.op("dve", "tensor_copy", reads=[vecb.b], writes=[G13.b], out=g13v[:, 8:12, :],
             in_=vecb[:, V_KNN:V_KNN + 64].unsqueeze(1).to_broadcast([128, 4, 64]))
        s.op("dve", "tensor_copy", reads=[vecb.b], writes=[G13.b], out=g13v[:, 12:13, :],
             in_=vecb[:, V_KNR:V_KNR + 64].unsqueeze(1).to_broadcast([128, 1, 64]))
        nr = norm_rings(ph)
        head_rings(ph, 832)
        xr = ph.ring("x", [128, D], F32, 4)
        rpr = ph.ring("rp", [128, 128], F32, 11)
        hTr = ph.ring("hT", [128, 8, 128], BF16, 2)
        glr = ph.ring("gl", [128, D], F32, 4)
        vsqr = ph.ring("vsq", [128, 512], F32, 1)
        ssvr = ph.ring("ssv", [128, 2], F32, 2)
        vnr = ph.ring("vn", [128, 512], BF16, 2)
        mcr = ph.ring("mc", [128, 512], BF16, 2)
        cqfr = ph.ring("cqf", [128, 512], F32, 3)
        cnr = ph.ring("cn", [128, 512], F32, 1)
        cnbr = ph.ring("cnb", [128, 512], BF16, 2)
        cTr = ph.ring("cT", [128, 4, 128], BF16, 2)
        hqr = ph.ring("hq", [128, 832], F32, 3)
        qcbr = ph.ring("qcb", [128, 4, 128], BF16, 2)
        kcbr = ph.ring("kcb", [128, 4, 128], BF16, 2)
        kper = ph.ring("kpe", [128, 64], F32, 7)
        kprr = ph.ring("kpr", [128, 64], F32, 2)
        vdr = ph.ring("vd", [128, 4, 129], BF16, 2)
        for t_ in vdr.items:
            s.op("pool", "memset", writes=[t_.b], ap=t_[:], constant=1.0)
        qkst = ph.ring("qkst", [128, 8, 512], BF16, 2)
        pT = ph.ps("pT", [128, 8, 128], BF16)
        pO1 = ph.ps("pO1", [128, 2048], F32)
        pQ = ph.ps("pQ", [128, 512], F32)
        pKV = ph.ps("pKV", [128, 1024], F32)
        def cps(g):
            return pO1[:, 1600 + g * 128:1728 + g * 128] if g < 3 else pKV[:, 768:896]

        def cpb(g):
            return pO1.b if g < 3 else pKV.b

        ss3r = ph.ring("ss3", [128, 4], F32, 2)
        v3r = ph.ring("v3", [128, 4], F32, 2)
        r3r = ph.ring("r3", [128, 4], F32, 2)
        jk2r = ph.ring("jk2", [128, 512], BF16, 3)
        for t_ in ss3r.items:
            s.op("pool", "memset", writes=[t_.b], ap=t_[:], constant=1.0)

        def tile_l1p1(t):
            g0 = (t // 4) * 4
            ti = t - g0
            ntile_g = min(NT, g0 + 4) - g0
            st_ = qkst.items[(t // 4) % 2]
            isctx = t >= NLAT
            xt = xr.next()
            s.dma("sp", xt[:], X1.ap[t * 128:(t + 1) * 128, :], reads=[X1.b.part(t)], writes=[xt.b])
            yield
            hT = hTr.next()
            yield from norm_mod_transpose(ph, xt, GC if isctx else GL, SC if isctx else SL, pT, hT[:], hT.b, nr, split=False)
            yield
            chunks = [(1280, 1536), (1536, 1600)] if isctx else [(0, 512), (512, 1024), (1024, 1536), (1536, 1600)]
            for k in range(8):
                for (n0, n1) in chunks:
                    s.op("pe", "matmul", reads=[hT.b] + W1.kparts, writes=[pO1.b], out=pO1[:, n0:n1],
                         lhsT=hT[:, k, :], rhs=W1[:, k, n0:n1], start=(k == 0), stop=(k == 7))
            if not isctx:
                rp = rpr.next()
                s.dma("sp", rp[:], rope.ap[t], writes=[rp.b])
            yield
            cqf = cqfr.next()
            c0 = 256 if isctx else 0
            s.op("act", "copy", reads=[pO1.b], writes=[cqf.b], out=cqf[:, c0:512], in_=pO1[:, 1024 + c0:1536])
            kpe = kper.next()
            s.op("act", "copy", reads=[pO1.b], writes=[kpe.b], out=kpe[:], in_=pO1[:, 1536:1600])
            ss3 = ss3r.next()
            if not isctx:
                gl = glr.next()
                s.op("act", "activation", reads=[pO1.b], writes=[gl.b], out=gl[:], in_=pO1[:, 0:1024], func=AF.Gelu)
                jk = jk2r.next()
                s.op("act", "activation", reads=[gl.b], writes=[jk.b, ss3.b], out=jk[:], in_=gl[:, 512:1024], func=AF.Square,
                     accum_out=ss3[:, 0:1])
                jk = jk2r.next()
                s.op("act", "activation", reads=[cqf.b], writes=[jk.b, ss3.b], out=jk[:, 0:256], in_=cqf[:, 0:256],
                     func=AF.Square, accum_out=ss3[:, 1:2])
            jk = jk2r.next()
            s.op("act", "activation", reads=[cqf.b], writes=[jk.b, ss3.b], out=jk[:, 0:256], in_=cqf[:, 256:512],
                 func=AF.Square, accum_out=ss3[:, 2:3])
            yield
            v3 = v3r.next()
            s.op("dve", "tensor_scalar", reads=[ss3.b], writes=[v3.b], out=v3[:, 0:1], in0=ss3[:, 0:1], scalar1=1.0 / 512,
                 scalar2=EPS, op0=ALU.mult, op1=ALU.add)
            s.op("dve", "tensor_scalar", reads=[ss3.b], writes=[v3.b], out=v3[:, 1:3], in0=ss3[:, 1:3], scalar1=1.0 / 256,
                 scalar2=EPS, op0=ALU.mult, op1=ALU.add)
            r3 = r3r.next()
            s.op("pool", "tensor_tensor", reads=[v3.b, nh.b], writes=[r3.b], out=r3[:, 0:3], in0=v3[:, 0:3], in1=nh[:, 0:3],
                 op=ALU.pow)
            yield
            cnb = cnbr.next()
            if not isctx:
                vn = vnr.next()
                s.op("dve", "scalar_tensor_tensor", reads=[gl.b, r3.b, vecb.b], writes=[vn.b], out=vn[:],
                     in0=gl[:, 512:1024], scalar=r3[:, 0:1], in1=vecb[:, V_CVN:V_CVN + 512], op0=ALU.mult, op1=ALU.mult)
                s.op("dve", "scalar_tensor_tensor", reads=[cqf.b, r3.b, vecb.b], writes=[cnb.b], out=cnb[:, 0:256],
                     in0=cqf[:, 0:256], scalar=r3[:, 1:2], in1=vecb[:, V_QAN:V_QAN + 256], op0=ALU.mult, op1=ALU.mult)
            s.op("dve", "scalar_tensor_tensor", reads=[cqf.b, r3.b, vecb.b], writes=[cnb.b], out=cnb[:, 256:512],
                 in0=cqf[:, 256:512], scalar=r3[:, 2:3], in1=vecb[:, V_KVAN:V_KVAN + 256], op0=ALU.mult, op1=ALU.mult)
            yield
            if not isctx:
                for g in range(4):
                    s.op("pe", "matmul", reads=[vn.b, Ws.b], writes=[cpb(g)], out=cps(g),
                         lhsT=Ws[:, g, :], rhs=vn[:, g * 128:(g + 1) * 128], start=True, stop=True)
            kc0 = c0 // 128
            for k in range(kc0, 4):
                s.op("pe", "transpose", reads=[cnb.b, idb.b], writes=[pT.b], out=pT[:, k, :],
                     in_=cnb[:, k * 128:(k + 1) * 128], identity=idb[:], inc=(k == 3))
            cT = cTr.next()
            s.op("act", "copy", reads=[pT.b], writes=[cT.b], out=cT[:, kc0:4, :], in_=pT[:, kc0:4, :])
            if not isctx:
                mc_ = mcr.next()
                for g in range(4):
                    s.op("dve", "scalar_tensor_tensor", reads=[cpb(g), bs.b, gl.b], writes=[mc_.b],
                         out=mc_[:, g * 128:(g + 1) * 128], in0=cps(g), scalar=bs[:, g:g + 1],
                         in1=gl[:, g * 128:(g + 1) * 128], op0=ALU.add, op1=ALU.mult)
                s.dma("sp", MIX.ap[t * 128:(t + 1) * 128, 0:512], mc_[:], reads=[mc_.b], writes=[MIX.b.part(("c", t))])
            yield
            if not isctx:
                for k in range(2):
                    s.op("pe", "matmul", reads=[cT.b] + Wq.kparts, writes=[pQ.b], out=pQ[:], lhsT=cT[:, k, :],
                         rhs=Wq[:, k, :], start=(k == 0), stop=(k == 1))
            for k in range(2):
                for (n0, n1) in ((0, 512), (512, 768)):
                    s.op("pe", "matmul", reads=[cT.b] + Wkv.kparts, writes=[pKV.b], out=pKV[:, n0:n1],
                         lhsT=cT[:, 2 + k, :], rhs=Wkv[:, k, n0:n1], start=(k == 0), stop=(k == 1))
            yield
            vd = vdr.next()
            s.op("act", "copy", reads=[pKV.b], writes=[vd.b], out=vd[:, :, 0:128],
                 in_=pKV[:, 256:768].rearrange("p (h d) -> p h d", d=128))
            s.dma("sp", VD.ap[t * 128:(t + 1) * 128, :], vd[:].rearrange("p h d -> p (h d)"), reads=[vd.b],
                  writes=[VD.b.part(t)])
            hq = hqr.next()
            if not isctx:
                s.op("act", "copy", reads=[pQ.b], writes=[hq.b], out=hq[:, 0:512], in_=pQ[:])
            else:
                s.op("pool", "memset", writes=[hq.b], ap=hq[:, 0:512], constant=1.0)
            s.op("act", "copy", reads=[pKV.b], writes=[hq.b], out=hq[:, 512:768], in_=pKV[:, 0:256])
            s.op("act", "copy", reads=[kpe.b], writes=[hq.b], out=hq[:, 768:832], in_=kpe[:])
            qg = yield from head_norm(ph, hq, 13, G13, "1")
            yield
            qgv = qg[:, 0:832].rearrange("p (h d) -> p h d", d=64)
            kcb = kcbr.next()
            s.op("dve", "tensor_copy", reads=[qg.b], writes=[kcb.b], out=kcb[:, :, 0:64], in_=qgv[:, 8:12, :])
            if isctx:
                s.op("dve", "tensor_copy", reads=[qg.b], writes=[kcb.b], out=kcb[:, :, 64:128],
                     in_=qgv[:, 12:13, :].to_broadcast([128, 4, 64]))
            else:
                kpr = kprr.next()
                rope_apply(ph, qgv[:, 12:13, :], kpr[:, :].unsqueeze(1), 1, rp, [qg.b], kpr.b, t2eng="dve")
                s.op("dve", "tensor_copy", reads=[kpr.b], writes=[kcb.b], out=kcb[:, :, 64:128],
                     in_=kpr[:, :].unsqueeze(1).to_broadcast([128, 4, 64]))
                qcb = qcbr.next()
                qg4 = qg[:, 0:512].rearrange("p (h t d) -> p h t d", t=2, d=64)
                s.op("act", "copy", reads=[qg.b], writes=[qcb.b], out=qcb[:, :, 0:64], in_=qg4[:, :, 0, :])
                rope_apply(ph, qg4[:, :, 1, :], qcb[:, :, 64:128], 4, rp, [qg.b], qcb.b, t2eng="dve")
            yield
            if not isctx:
                for h in range(4):
                    s.op("pe", "transpose", reads=[qcb.b, idb.b], writes=[pT.b], out=pT[:, h, :], in_=qcb[:, h, :],
                         identity=idb[:])
            for h in range(4):
                s.op("pe", "transpose", reads=[kcb.b, idb.b], writes=[pT.b], out=pT[:, 4 + h, :], in_=kcb[:, h, :],
                     identity=idb[:], inc=(h == 3))
            b0 = 4 if isctx else 0
            s.op("act", "copy", reads=[pT.b], writes=[st_.b], out=st_[:, b0:8, ti * 128:(ti + 1) * 128], in_=pT[:, b0:8, :])
            if ti == ntile_g - 1:
                ntk = ntile_g * 128
                s.dma("sp", QKT1.ap[b0:8, :, g0 * 128:g0 * 128 + ntk].rearrange("j p t -> p j t"), st_[:, b0:8, 0:ntk],
                      reads=[st_.b], writes=[QKT1.b])

        run_pipelined((tile_l1p1(t) for t in range(NT)), first_stage=6)
        ph.close()
        if stop_after == "L1P1":
            s.finish()
            return nc

        ph = Phase(nc, s)
        QDr = ph.ring("QD", [128, S], BF16, 2)
        KDr = ph.ring("KD", [128, NTOK], BF16, 2)
        VDr = ph.ring("VDs", [128, NT, 129], BF16, 2)
        zdr = ph.ring("zd", [128, 4], F32, 2)
        odr = ph.ring("od", [128, 4, 128], BF16, 3)

        def fin_d(u, acc):
            h, q0 = u["h"], u["q0"]
            rz = zdr.next()
            s.op("dve", "reciprocal", reads=[acc.b], writes=[rz.b], out=rz[:], in_=acc[:, :, 128])
            od = odr.next()
            s.op("dve", "tensor_tensor", reads=[acc.b, rz.b], writes=[od.b], out=od[:], in0=acc[:, :, 0:128],
                 in1=rz[:, :].unsqueeze(2).to_broadcast([128, 4, 128]), op=ALU.mult)
            s.dma("sp", MIX.ap[q0:q0 + 512, 512 + h * 128:512 + (h + 1) * 128].rearrange("(j p) d -> p j d", p=128),
                  od[:], reads=[od.b], writes=[MIX.b.part(("d", h, q0))])

        def load_d(h):
            Qh, Kh, Vh = QDr.items[h % 2], KDr.items[h % 2], VDr.items[h % 2]
            s.dma("sp", Qh[:], QKT1.ap[h, :, 0:S], reads=[QKT1.b], writes=[Qh.b])
            s.dma("sp", Kh[:], QKT1.ap[4 + h], reads=[QKT1.b], writes=[Kh.b])
            s.dma("sp", Vh[:], VD.ap.rearrange("(k p) (h e) -> p k h e", p=128, e=129)[:, :, h, :],
                  reads=[VD.b.part(t) for t in range(NT)], writes=[Vh.b])

        units = []
        for h in range(4):
            Qh, Kh, Vh = QDr.items[h % 2], KDr.items[h % 2], VDr.items[h % 2]
            first = len(units)
            for qb in range(8):
                keys = [dict(kT=Kh[:, kt * 128:(kt + 1) * 128], kb=[Kh.b], v=Vh[:, kt, :], vb=[Vh.b], mask=None)
                        for kt in range(NT)]
                units.append(dict(qT=Qh[:, qb * 512:(qb + 1) * 512], qb=[Qh.b], keys=keys, nq=512, vd1=129,
                                  scale=128.0 ** -0.5, fin=fin_d, h=h, q0=qb * 512))
            if h == 0:
                units[first]["pre"] = (lambda: load_d(0))
            if h < 3:
                units[first + 1]["pre"] = (lambda hh=h + 1: load_d(hh))
        attn_pipeline(ph, units, NT)
        ph.close()
        if stop_after == "L1P2":
            s.finish()
            return nc

        phase3a(1, X1, X1A, NLAT)
        phase3b(1, X1A, y, NLAT)
        gp.close()
        s.finish()
        print("program: instructions", s.n_ins, "waits", s.n_wait)
    return nc


def _rope_tables():
    rows = S // 64
    row = np.repeat(np.arange(rows, dtype=np.int32), 64).astype(np.float32)
    col = np.tile(np.arange(64, dtype=np.int32), rows).astype(np.float32)
    inv = (np.float32(10000.0) ** (-np.arange(16, dtype=np.float32) / np.float32(16))).astype(np.float32)
    ang = np.concatenate([row[:, None] * inv, col[:, None] * inv], axis=-1).astype(np.float32)
    c, sn = np.cos(ang).astype(np.float32), np.sin(ang).astype(np.float32)
    tab = np.concatenate([c, c, -sn, sn], axis=-1)
    return np.ascontiguousarray(tab.reshape(NLAT, 128, 128))


def _shared_inputs(inp):
    f = lambda a: np.ascontiguousarray(np.asarray(a, dtype=np.float32))
    ev = f(inp["ev_w_in"])[0]
    aq = ev[:, 0:512].reshape(D, 8, 64)
    aq_p = np.stack([aq[:, [j, 4 + j], :] for j in range(4)], axis=1).reshape(D, 512)
    wev = np.concatenate([aq_p, ev[:, 512:1024], ev[:, 1024:1152], ev[:, 1280:1792], ev[:, 1152:1280], ev[:, 1792:2304]], axis=1)
    ukv = f(inp["od_w_ukv"])[0].reshape(256, 4, 192)
    wukv = np.concatenate([ukv[:, :, 0:64].reshape(256, 256), ukv[:, :, 64:192].reshape(256, 512)], axis=1)
    vec = np.concatenate([
        f(inp["ev_qnorm_a"])[0], f(inp["ev_knorm_a"])[0], f(inp["ev_qnorm_b"])[0], f(inp["ev_knorm_b"])[0],
        f(inp["ev_sink"])[0], f(inp["ev_lam_q1"])[0], f(inp["ev_lam_k1"])[0], f(inp["ev_lam_q2"])[0], f(inp["ev_lam_k2"])[0],
        f(inp["ev_subln"])[0], f(inp["od_c_vnorm"])[0], f(inp["od_qa_norm"])[0], f(inp["od_kva_norm"])[0],
        f(inp["od_qnorm_nope"])[0], f(inp["od_knorm_nope"])[0], f(inp["od_qnorm_rope"])[0], f(inp["od_knorm_rope"])[0]])
    assert vec.shape[0] == NV
    j = np.arange(128)[:, None]
    i = np.arange(128)[None, :]
    masks = np.stack([(j >= i), (j <= i)]).astype(np.float32)
    return {
        "ada_w": f(inp["ada_w"]), "ada_b": f(inp["ada_b"]).reshape(2, 1, 6 * D),
        "n1g": f(inp["norm1_g"]).reshape(2, 1, D), "n2g": f(inp["norm2_g"]).reshape(2, 1, D),
        "wmix": f(inp["mix_w_out"]), "wfi": f(inp["ffn_w_in"]), "wfo": f(inp["ffn_w_out"]),
        "wev": f(wev), "wod": f(inp["od_w_in"])[0], "wuq": f(inp["od_w_uq"])[0], "wukv": f(wukv),
        "wsT": f(np.transpose(f(inp["od_c_ws"])[0], (0, 2, 1))), "bsT": f(f(inp["od_c_bs"])[0].T),
        "vecs": f(vec.reshape(1, NV)), "ident": np.eye(128, dtype=np.float32), "rope": _rope_tables(), "masks": masks,
    }


def make_in_maps(inp):
    shared = _shared_inputs(inp)
    x = np.asarray(inp["x"], dtype=np.float32)
    ctx = np.asarray(inp["ctx"], dtype=np.float32)
    c = np.asarray(inp["c"], dtype=np.float32)
    c_ctx = np.asarray(inp["c_ctx"], dtype=np.float32)
    maps = []
    for b in range(x.shape[0]):
        m = dict(shared)
        m["xs"] = np.ascontiguousarray(np.concatenate([x[b], ctx[b]], axis=0))
        cc = np.stack([c[b], c_ctx], axis=-1).reshape(8, 128, 2).transpose(1, 0, 2)
        m["cc"] = np.ascontiguousarray(cc)
        maps.append(m)
    return maps


_NC_CACHE = {}


def kernel(**inputs):
    if "nc" not in _NC_CACHE:
        _NC_CACHE["nc"] = build_program()
    nc = _NC_CACHE["nc"]
    in_maps = make_in_maps(inputs)
    res = run_bass_kernel_spmd(nc, in_maps, core_ids=list(range(len(in_maps))))
    return np.stack([np.asarray(r["y"], dtype=np.float32) for r in res.results], axis=0)
```

```python
import math
import numpy as np
import concourse.bass as bass
import concourse.mybir as mybir
from concourse.bass_utils import run_bass_kernel_spmd
from contextlib import ExitStack

F32 = mybir.dt.float32
BF16 = mybir.dt.bfloat16
ALU = mybir.AluOpType
AF = mybir.ActivationFunctionType
AX = mybir.AxisListType

D = 1024
S = 4096
LCTX = 256
NTOK = S + LCTX
NT = NTOK // 128
NLAT = S // 128
FH = 2816
EPS = 1e-6
NV = 1928

V_QA, V_KA, V_QB, V_KB, V_SINK, V_LQ1, V_LK1, V_LQ2, V_LK2, V_SUBLN = 0, 64, 128, 192, 256, 264, 328, 392, 456, 520
V_CVN, V_QAN, V_KVAN, V_QNN, V_KNN, V_QNR, V_KNR = 648, 1160, 1416, 1672, 1736, 1800, 1864


class Buf:
    def __init__(self, name, excl=False):
        self.name = name
        self.w = None
        self.r = {}
        self.excl = excl
        self.parts = {}

    def part(self, key):
        p = self.parts.get(key)
        if p is None:
            p = Buf(f"{self.name}[{key}]", self.excl)
            self.parts[key] = p
        return p


class Sched:
    ENG = ["pe", "act", "dve", "pool", "sp"]
    NRING = 8

    def __init__(self, nc, stack):
        self.nc = nc
        self.prog = {e: [] for e in self.ENG}
        self.cnt = {e: 0 for e in self.ENG}
        self.last = {e: None for e in self.ENG}
        self.pend = {e: False for e in self.ENG}
        self.sem = {e: stack.enter_context(nc.semaphore(f"sem_{e}")) for e in self.ENG}
        self.known = {e: {} for e in self.ENG}
        self.ring = {}
        self.ring_i = {}
        self.ring_val = {}
        for q in ("sp", "pool", "act"):
            self.ring[q] = [stack.enter_context(nc.semaphore(f"dq_{q}{i}")) for i in range(self.NRING)]
            self.ring_i[q] = 0
            self.ring_val[q] = [0] * self.NRING
        self.n_wait = 0
        self.n_ins = 0

    def _need(self, eng, ev, rec_waits):
        sem, val = ev
        if self.known[eng].get(sem, 0) >= val:
            return
        for e in self.ENG:
            if self.sem[e] == sem and val > self.cnt[e]:
                assert self.pend[e] and val == self.cnt[e] + 1, (e, val, self.cnt[e])
                self.last[e]["inc"] = True
                self.cnt[e] += 1
                self.pend[e] = False
        self.known[eng][sem] = val
        rec_waits.append((sem, val))
        self.n_wait += 1

    def _deps(self, eng, key, reads, writes, waits, is_dma):
        for b in reads:
            if b.w is not None:
                self._need(eng, b.w, waits)
            if b.excl:
                for k, ev in list(b.r.items()):
                    if k != key:
                        self._need(eng, ev, waits)
        for b in writes:
            if b.w is not None and not (eng == "pe" and b.w[0] == self.sem["pe"]):
                self._need(eng, b.w, waits)
            for k, ev in list(b.r.items()):
                if k != key or is_dma:
                    self._need(eng, ev, waits)

    def op(self, eng, method, reads=(), writes=(), **kw):
        waits = []
        eager = kw.pop("inc", None)
        if eager is None:
            eager = (eng != "pe") or (method == "matmul" and bool(kw.get("stop")))
        self._deps(eng, eng, reads, writes, waits, False)
        rec = {"m": method, "kw": kw, "waits": waits, "inc": False, "dma": None}
        self.prog[eng].append(rec)
        self.last[eng] = rec
        if eager:
            rec["inc"] = True
            self.cnt[eng] += 1
            self.pend[eng] = False
            ev = (self.sem[eng], self.cnt[eng])
        else:
            self.pend[eng] = True
            ev = (self.sem[eng], self.cnt[eng] + 1)
        for b in reads:
            b.r[eng] = ev
        for b in writes:
            b.w = ev
            b.r = {}
        self.n_ins += 1
        return rec

    def dma(self, q, out, in_, reads=(), writes=(), **kw):
        waits = []
        i = self.ring_i[q]
        slot = i % self.NRING
        self.ring_i[q] += 1
        sem = self.ring[q][slot]
        if self.ring_val[q][slot] > 0:
            self._need(q, (sem, self.ring_val[q][slot]), waits)
        key = (q, slot)
        self._deps(q, key, reads, writes, waits, True)
        self.ring_val[q][slot] += 16
        ev = (sem, self.ring_val[q][slot])
        rec = {"m": "dma_start", "kw": dict(out=out, in_=in_, **kw), "waits": waits, "inc": False, "dma": sem}
        self.prog[q].append(rec)
        for b in reads:
            b.r[key] = ev
        for b in writes:
            b.w = ev
            b.r = {}
        self.n_ins += 1
        return ev

    def barrier(self):
        evs = []
        for e in self.ENG:
            if self.pend[e]:
                self.last[e]["inc"] = True
                self.cnt[e] += 1
                self.pend[e] = False
            if self.cnt[e] > 0:
                evs.append((self.sem[e], self.cnt[e]))
        for q in self.ring:
            for s_, v in zip(self.ring[q], self.ring_val[q]):
                if v > 0:
                    evs.append((s_, v))
        for e in self.ENG:
            waits = []
            for ev in evs:
                if ev[0] == self.sem[e]:
                    continue
                if self.known[e].get(ev[0], 0) < ev[1]:
                    self.known[e][ev[0]] = ev[1]
                    waits.append(ev)
            if waits:
                self.prog[e].append({"m": None, "kw": {}, "waits": waits, "inc": False, "dma": None})

    def finish(self):
        self.barrier()
        nc = self.nc
        engobj = {"pe": "tensor", "act": "scalar", "dve": "vector", "pool": "gpsimd", "sp": "sync"}
        with nc.Block() as block:
            for e in self.ENG:
                prog = self.prog[e]
                sem = self.sem[e]

                def body(eng, prog=prog, sem=sem):
                    for rec in prog:
                        for (s_, v) in rec["waits"]:
                            eng.wait_ge(s_, v)
                        if rec["m"] is None:
                            continue
                        ins = getattr(eng, rec["m"])(**rec["kw"])
                        if rec["dma"] is not None:
                            ins.then_inc(rec["dma"], 16)
                        elif rec["inc"]:
                            ins.then_inc(sem, 1)

                getattr(block, engobj[e])(body)


class T:
    def __init__(self, t, name, excl=False):
        self.t = t
        self.b = Buf(name, excl)

    def __getitem__(self, k):
        return self.t[k]


class Phase:
    _n = 0

    def __init__(self, nc, s):
        self.nc = nc
        self.s = s
        self.st = ExitStack()
        Phase._n += 1
        self.pfx = f"p{Phase._n}_"

    def sb(self, name, shape, dt):
        return T(self.st.enter_context(self.nc.sbuf_tensor(self.pfx + name, list(shape), dt)), name)

    def ps(self, name, shape, dt):
        return T(self.st.enter_context(self.nc.psum_tensor(self.pfx + name, list(shape), dt)), name, True)

    def ring(self, name, shape, dt, n):
        return Ring([self.sb(f"{name}{i}", shape, dt) for i in range(n)])

    def close(self):
        self.s.barrier()
        try:
            print("phase", self.pfx, "sbuf spare KB", self.nc.sbuf_bytes_remaining // 1024 // 128 if self.nc.sbuf_bytes_remaining > 4 * 1024 * 1024 else self.nc.sbuf_bytes_remaining // 1024)
        except Exception as e:
            pass
        self.st.close()


class Ring:
    def __init__(self, items):
        self.items = items
        self.i = 0

    def next(self):
        t = self.items[self.i % len(self.items)]
        self.i += 1
        return t


class DT:
    def __init__(self, ap, name):
        self.ap = ap
        self.b = Buf(name)


def build_program(dbg=(), stop_after=None):
    Phase._n = 0
    nc = bass.Bass("TRN2", target_bir_lowering=False)

    def din(name, shape, dt=F32):
        return DT(nc.dram_tensor(name, list(shape), dt, kind="ExternalInput").ap(), name)

    def dscr(name, shape, dt):
        kind = "ExternalOutput" if name in dbg else "Internal"
        return DT(nc.dram_tensor(name, list(shape), dt, kind=kind).ap(), name)

    xs = din("xs", [NTOK, D])
    cc = din("cc", [128, 8, 2])
    ada_w = din("ada_w", [2, D, 6 * D])
    ada_b = din("ada_b", [2, 1, 6 * D])
    n1g = din("n1g", [2, 1, D])
    n2g = din("n2g", [2, 1, D])
    wmix = din("wmix", [2, D, D])
    wfi = din("wfi", [2, D, 2 * FH])
    wfo = din("wfo", [2, FH, D])
    wev = din("wev", [D, 2304])
    wod = din("wod", [D, 1600])
    wuq = din("wuq", [256, 512])
    wukv = din("wukv", [256, 768])
    wsT = din("wsT", [4, 128, 128])
    bsT = din("bsT", [128, 4])
    vecs = din("vecs", [1, NV])
    ident = din("ident", [128, 128])
    rope = din("rope", [NLAT, 128, 128])
    masks = din("masks", [2, 128, 128])
    y = DT(nc.dram_tensor("y", [S, D], F32, kind="ExternalOutput").ap(), "y")

    modV = dscr("modV", [2, 2, 6 * D], F32)
    QKT0 = dscr("QKT0", [13, 128, NTOK], BF16)
    VA = dscr("VA", [NTOK, 130], BF16)
    VB = dscr("VB", [NTOK, 516], BF16)
    MIX = dscr("MIX", [NTOK, D], BF16)
    H2T = dscr("H2T", [8, 128, NTOK], BF16)
    X1A = dscr("X1A", [NTOK, D], F32)
    X1 = dscr("X1", [NTOK, D], F32)
    QKT1 = dscr("QKT1", [8, 128, NTOK], BF16)
    VD = dscr("VD", [NTOK, 516], BF16)

    with ExitStack() as gst:
        s = Sched(nc, gst)

        gp = Phase(nc, s)
        idf = gp.sb("idf", [128, 128], F32)
        idb = gp.sb("idb", [128, 128], BF16)
        nh = gp.sb("nh", [128, 64], F32)
        vecb = gp.sb("vecb", [128, NV], F32)
        s.dma("sp", idf[:], ident.ap, writes=[idf.b])
        s.op("dve", "tensor_copy", reads=[idf.b], writes=[idb.b], out=idb[:], in_=idf[:])
        s.op("pool", "memset", writes=[nh.b], ap=nh[:], constant=-0.5)
        s.dma("sp", vecb[:], vecs.ap[0, :].partition_broadcast(128), writes=[vecb.b])

        def rstd_from_ss(ph, ssT, n, width, rname):
            v = ph.sb(rname + "_v", [128, n], F32) if not hasattr(ph, "_" + rname) else getattr(ph, "_" + rname)[0]
            r = ph.sb(rname + "_r", [128, n], F32) if not hasattr(ph, "_" + rname) else getattr(ph, "_" + rname)[1]
            setattr(ph, "_" + rname, (v, r))
            s.op("dve", "tensor_scalar", reads=[ssT.b], writes=[v.b], out=v[:], in0=ssT[:, 0:n], scalar1=1.0 / width,
                 scalar2=EPS, op0=ALU.mult, op1=ALU.add)
            s.op("pool", "tensor_tensor", reads=[v.b, nh.b], writes=[r.b], out=r[:], in0=v[:], in1=nh[:, 0:n], op=ALU.pow)
            return r

        def load_bcast(ph, name, src_ap, width, src_b, q="sp"):
            t = ph.sb(name, [128, width], F32)
            s.dma(q, t[:], src_ap.partition_broadcast(128), reads=[src_b], writes=[t.b])
            return t

        def run_pipelined(gens, first_stage=None):
            gens = list(gens)
            active = []
            i = 0
            while i < len(gens) or active:
                new = None
                if i < len(gens):
                    new = [gens[i], 0]
                    i += 1
                    try:
                        next(new[0])
                        new[1] = 1
                    except StopIteration:
                        new = None
                order = [a for a in active if a[1] + 1 == first_stage] + [a for a in active if a[1] + 1 != first_stage]
                for a in order:
                    try:
                        next(a[0])
                        a[1] += 1
                    except StopIteration:
                        active.remove(a)
                if new is not None:
                    active.append(new)

        def norm_mod_transpose(ph, xt, G, Sh, pT, hT_ap, hT_b, rings, split=True):
            junk, ssr, _, hbr = rings
            jk = junk.next()
            ss = ssr.next()
            s.op("act", "activation", reads=[xt.b], writes=[jk.b, ss.b], out=jk[:], in_=xt[:], func=AF.Square,
                 accum_out=ss[:])
            r = rstd_from_ss(ph, ss, 1, D, "rs_x")
            yield
            hb = hbr.next()
            s.op("act", "activation", reads=[xt.b, r.b], writes=[hb.b], out=hb[:], in_=xt[:], func=AF.Copy,
                 scale=r[:, 0:1])
            yield
            for k in range(8):
                s.op("pe", "transpose", reads=[hb.b, idb.b], writes=[pT.b], out=pT[:, k, :],
                     in_=hb[:, k * 128:(k + 1) * 128], identity=idb[:], inc=(k == 7))
            if split:
                yield
            for k in range(8):
                s.op("act", "activation", reads=[pT.b, G.b, Sh.b], writes=[hT_b], out=hT_ap[:, k, :], in_=pT[:, k, :],
                     func=AF.Identity, scale=G[:, k:k + 1], bias=Sh[:, k:k + 1])

        def load_w_bf16(ph, name, src_ap, kchunks, ncols, src_b):
            t = ph.sb(name, [128, kchunks, ncols], BF16)
            view = src_ap.rearrange("(k p) n -> p k n", p=128)
            step = max(1, min(kchunks, 4096 // ncols)) if ncols <= 4096 else 1
            for k0 in range(0, kchunks, step):
                k1 = min(kchunks, k0 + step)
                s.dma("pool", t[:, k0:k1, :], view[:, k0:k1, :], reads=[src_b], writes=[t.b.part(k0)])
            t.kparts = [t.b.part(k0) for k0 in range(0, kchunks, step)]
            return t

        def load_w_bf16_cols(ph, name, src_ap, kchunks, ncols, src_b, cb, order):
            t = ph.sb(name, [128, kchunks, ncols], BF16)
            view = src_ap.rearrange("(k p) n -> p k n", p=128)
            for b in order:
                c0, c1 = b * cb, min(ncols, (b + 1) * cb)
                s.dma("pool", t[:, :, c0:c1], view[:, :, c0:c1], reads=[src_b], writes=[t.b.part(("c", b))])
            t.cpart = lambda col: t.b.part(("c", col // cb))
            return t

        def head_norm(ph, qf, nslots, Gt, tag, wide=False):
            w = nslots * 64
            sq = ph.H_sq.next()
            s.op("act", "activation", reads=[qf.b], writes=[sq.b], out=sq[:, 0:w], in_=qf[:, 0:w], func=AF.Square)
            yield
            ssh = ph.H_ss.next()
            s.op("dve", "tensor_reduce", reads=[sq.b], writes=[ssh.b], out=ssh[:, 0:nslots],
                 in_=sq[:, 0:w].rearrange("p (h d) -> p h d", d=64), axis=AX.X, op=ALU.add)
            r = rstd_from_ss(ph, ssh, nslots, 64, "rs_h" + tag)
            yield
            qn = ph.H_qn.next()
            s.op("dve", "tensor_tensor", reads=[qf.b, r.b], writes=[qn.b],
                 out=qn[:, 0:w].rearrange("p (h d) -> p h d", d=64),
                 in0=qf[:, 0:w].rearrange("p (h d) -> p h d", d=64),
                 in1=r[:, 0:nslots].unsqueeze(2).to_broadcast([128, nslots, 64]), op=ALU.mult)
            if wide:
                yield
            qg = ph.H_qg.next()
            s.op("pool" if wide else "dve", "tensor_tensor", reads=[qn.b, Gt.b], writes=[qg.b], out=qg[:, 0:w],
                 in0=qn[:, 0:w], in1=Gt[:, 0:w], op=ALU.mult)
            return qg

        def rope_apply(ph, src_ap, dst_ap, n, rp, reads, dst_b, t2eng="pool"):
            t1 = ph.R_t1.next()
            t2 = ph.R_t2.next()
            t1v = t1[:, 0:n * 64].rearrange("p (h d) -> p h d", d=64)
            t2v = t2[:, 0:n * 64].rearrange("p (h d) -> p h d", d=64)
            s.op("dve", "tensor_tensor", reads=reads + [rp.b], writes=[t1.b], out=t1v, in0=src_ap,
                 in1=rp[:, 0:64].unsqueeze(1).to_broadcast([128, n, 64]), op=ALU.mult)
            s.op(t2eng, "tensor_tensor", reads=reads + [rp.b], writes=[t2.b.part(0)], out=t2v[:, :, 0:32], in0=src_ap[:, :, 32:64],
                 in1=rp[:, 64:96].unsqueeze(1).to_broadcast([128, n, 32]), op=ALU.mult)
            s.op("dve", "tensor_tensor", reads=reads + [rp.b], writes=[t2.b.part(1)], out=t2v[:, :, 32:64], in0=src_ap[:, :, 0:32],
                 in1=rp[:, 96:128].unsqueeze(1).to_broadcast([128, n, 32]), op=ALU.mult)
            s.op("dve", "tensor_tensor", reads=[t1.b, t2.b.part(0), t2.b.part(1)], writes=[dst_b], out=dst_ap, in0=t1v, in1=t2v,
                 op=ALU.add)

        ph = Phase(nc, s)
        cct = ph.sb("cct", [128, 8, 2], F32)
        sc = ph.sb("sc", [128, 8, 2], F32)
        ones2 = ph.sb("ones2", [1, 2], F32)
        brow = ph.sb("brow", [1, 6 * D], F32)
        mrow = ph.sb("mrow", [2, 6 * D], F32)
        wring = ph.ring("adaw", [128, 8, 512], F32, 2)
        pm = [ph.ps(f"pm{i}", [2, 512], F32) for i in range(2)]
        s.dma("sp", cct[:], cc.ap, writes=[cct.b])
        s.op("act", "activation", reads=[cct.b], writes=[sc.b], out=sc[:], in_=cct[:], func=AF.Silu)
        s.op("dve", "memset", writes=[ones2.b], ap=ones2[:], constant=1.0)
        import os
        NL_ = int(os.environ.get("KD_NL", "2"))
        NC_ = int(os.environ.get("KD_NC", "12"))
        for l in range(NL_):
            s.dma("sp", brow[:], ada_b.ap[l], writes=[brow.b])
            for c in range(NC_):
                wt = wring.next()
                s.dma("sp", wt[:], ada_w.ap[l][:, c * 512:(c + 1) * 512].rearrange("(k p) n -> p k n", p=128),
                      writes=[wt.b])
                p = pm[c % 2]
                for k in range(8):
                    s.op("pe", "matmul", reads=[sc.b, wt.b], writes=[p.b], out=p[:], lhsT=sc[:, k, :], rhs=wt[:, k, :],
                         start=(k == 0), stop=False)
                s.op("pe", "matmul", reads=[ones2.b, brow.b], writes=[p.b], out=p[:], lhsT=ones2[:],
                     rhs=brow[:, c * 512:(c + 1) * 512], start=False, stop=True)
                s.op("dve", "tensor_copy", reads=[p.b], writes=[mrow.b], out=mrow[:, c * 512:(c + 1) * 512], in_=p[:])
            s.dma("sp", modV.ap[l], mrow[:], reads=[mrow.b], writes=[modV.b])
        ph.close()
        if stop_after == "P0":
            s.finish()
            return nc

        def mod_tiles(ph, l, v, idx_shift, idx_scale, gdt, tag):
            def colload(name, ap1d, b):
                t = ph.sb(name, [128, 8], F32)
                s.dma("sp", t[:], ap1d.rearrange("(k p) -> p k", p=128), reads=[b], writes=[t.b],
                      allow_slow_non_contiguous=True)
                return t
            sh = colload(f"sh{tag}", modV.ap[l, v, idx_shift * D:(idx_shift + 1) * D], modV.b)
            scl = colload(f"scl{tag}", modV.ap[l, v, idx_scale * D:(idx_scale + 1) * D], modV.b)
            gg = colload(f"gg{tag}", gdt.ap[l, 0, :], gdt.b)
            s.op("dve", "scalar_tensor_tensor", reads=[scl.b, gg.b], writes=[scl.b], out=scl[:], in0=scl[:], scalar=1.0,
                 in1=gg[:], op0=ALU.add, op1=ALU.mult)
            return scl, sh

        def norm_rings(ph):
            return (ph.ring("junk", [128, D], BF16, 1), ph.ring("ssx", [128, 1], F32, 2),
                    None, ph.ring("hb", [128, D], BF16, 2))

        def head_rings(ph, w):
            ph.H_sq = ph.ring("hsq", [128, w], F32, 2)
            ph.H_ss = ph.ring("hss", [128, 32], F32, 2)
            ph.H_qn = ph.ring("hqn", [128, w], F32, 2)
            ph.H_qg = ph.ring("hqg", [128, w], F32, 2)
            ph.R_t1 = ph.ring("rt1", [128, w], F32, 1)
            ph.R_t2 = ph.ring("rt2", [128, w], F32, 1)

        ph = Phase(nc, s)
        W = load_w_bf16(ph, "wev", wev.ap, 8, 2304, wev.b)
        GL, SL = mod_tiles(ph, 0, 0, 0, 1, n1g, "l")
        GC, SC = mod_tiles(ph, 0, 1, 0, 1, n1g, "c")
        G26 = ph.sb("g26", [128, 26 * 64], F32)
        g26v = G26[:, :].rearrange("p (h d) -> p h d", d=64)
        for (a, b_, off) in ((0, 8, V_QA), (8, 16, V_QB), (16, 18, V_KA), (18, 26, V_KB)):
            s.op("dve", "tensor_copy", reads=[vecb.b], writes=[G26.b], out=g26v[:, a:b_, :],
                 in_=vecb[:, off:off + 64].unsqueeze(1).to_broadcast([128, b_ - a, 64]))
        nr = norm_rings(ph)
        head_rings(ph, 1664)
        xr = ph.ring("x", [128, D], F32, 4)
        rpr = ph.ring("rp", [128, 128], F32, 7)
        hTr = ph.ring("hT", [128, 8, 128], BF16, 2)
        qfr = ph.ring("qf", [128, 1664], F32, 3)
        qbr = ph.ring("qb", [128, 1664], BF16, 2)
        var = ph.ring("va", [128, 2, 65], BF16, 2)
        vbr = ph.ring("vb", [128, 4, 129], BF16, 2)
        for t_ in var.items + vbr.items:
            s.op("pool", "memset", writes=[t_.b], ap=t_[:], constant=1.0)
        qkst = ph.ring("qkst", [128, 13, 256], BF16, 2)
        pT = ph.ps("pT", [128, 8, 128], BF16)
        pO = ph.ps("pO", [128, 2560], F32)
        pQ1 = ph.ps("pQ1", [128, 8, 128], BF16)
        pQ2 = ph.ps("pQ2", [128, 8, 128], BF16)
        def tile_l0p1(t):
            g0 = (t // 2) * 2
            ti = t - g0
            ntile_g = 2
            st_ = qkst.items[(t // 2) % 2]
            isctx = t >= NLAT
            xt = xr.next()
            s.dma("sp", xt[:], xs.ap[t * 128:(t + 1) * 128, :], writes=[xt.b])
            yield
            hT = hTr.next()
            yield from norm_mod_transpose(ph, xt, GC if isctx else GL, SC if isctx else SL, pT, hT[:], hT.b, nr)
            yield
            for k in range(8):
                for c in range(5):
                    n0, n1 = c * 512, min(2304, (c + 1) * 512)
                    s.op("pe", "matmul", reads=[hT.b] + W.kparts, writes=[pO.b], out=pO[:, n0:n1],
                         lhsT=hT[:, k, :], rhs=W[:, k, n0:n1], start=(k == 0), stop=(k == 7))
            if not isctx:
                rp = rpr.next()
                s.dma("sp", rp[:], rope.ap[t], writes=[rp.b])
            yield
            qf = qfr.next()
            s.op("act", "copy", reads=[pO.b], writes=[qf.b], out=qf[:], in_=pO[:, 0:1664])
            va = var.next()
            vb = vbr.next()
            s.op("act", "copy", reads=[pO.b], writes=[va.b], out=va[:, :, 0:64],
                 in_=pO[:, 1664:1792].rearrange("p (h d) -> p h d", d=64))
            s.op("act", "copy", reads=[pO.b], writes=[vb.b], out=vb[:, :, 0:128],
                 in_=pO[:, 1792:2304].rearrange("p (h d) -> p h d", d=128))
            s.dma("sp", VA.ap[t * 128:(t + 1) * 128, :], va[:].rearrange("p h d -> p (h d)"), reads=[va.b],
                  writes=[VA.b.part(t)])
            s.dma("sp", VB.ap[t * 128:(t + 1) * 128, :], vb[:].rearrange("p h d -> p (h d)"), reads=[vb.b],
                  writes=[VB.b.part(t)])
            qg = yield from head_norm(ph, qf, 26, G26, "0", wide=True)
            yield
            qb = qbr.next()
            if isctx:
                s.op("dve", "tensor_copy", reads=[qg.b], writes=[qb.b], out=qb[:], in_=qg[:, 0:1664])
            else:
                rope_apply(ph, qg[:, 0:1664].rearrange("p (h d) -> p h d", d=64),
                           qb[:, :].rearrange("p (h d) -> p h d", d=64), 26, rp, [qg.b], qb.b, t2eng="dve")
            yield
            for j in range(13):
                pq = pQ1 if j < 8 else pQ2
                s.op("pe", "transpose", reads=[qb.b, idb.b], writes=[pq.b], out=pq[:, j % 8, :],
                     in_=qb[:, j * 128:(j + 1) * 128], identity=idb[:], inc=(j in (7, 12)))
            yield
            s.op("act", "copy", reads=[pQ1.b], writes=[st_.b], out=st_[:, 0:8, ti * 128:(ti + 1) * 128], in_=pQ1[:])
            s.op("act", "copy", reads=[pQ2.b], writes=[st_.b], out=st_[:, 8:13, ti * 128:(ti + 1) * 128],
                 in_=pQ2[:, 0:5, :])
            if ti == ntile_g - 1:
                ntk = ntile_g * 128
                s.dma("sp", QKT0.ap[:, :, g0 * 128:g0 * 128 + ntk].rearrange("j p t -> p j t"), st_[:, :, 0:ntk],
                      reads=[st_.b], writes=[QKT0.b])

        run_pipelined((tile_l0p1(t) for t in range(NT)), first_stage=7)
        ph.close()
        if stop_after == "L0P1":
            s.finish()
            return nc

        def attn_pipeline(ph, units, nkmax):
            PT = [ph.sb(f"PT{i}", [128, nkmax, 512], BF16) for i in range(2)]
            Sb = [ph.ps(f"S{i}", [128, 512], F32) for i in range(4)]
            acc = ph.ps("acc", [128, 4, 512], F32)
            si = 0
            n = len(units)
            for ui in range(n + 1):
                cur = units[ui] if ui < n else None
                prev = units[ui - 1] if ui > 0 else None
                if cur is not None and cur.get("pre") is not None:
                    cur["pre"]()
                nk = max(len(cur["keys"]) if cur else 0, len(prev["keys"]) if prev else 0)
                for kt in range(nk):
                    if cur is not None and kt < len(cur["keys"]):
                        key = cur["keys"][kt]
                        nq = cur["nq"]
                        sbk = Sb[si % 4]
                        si += 1
                        so = sbk[:, 0:nq]
                        if len(cur["qT"].shape) == 3:
                            so = so.rearrange("p (g q) -> p g q", q=128)
                        s.op("pe", "matmul", reads=cur["qb"] + key["kb"], writes=[sbk.b], out=so, lhsT=key["kT"],
                             rhs=cur["qT"], start=True, stop=True)
                        pt = PT[ui % 2]
                        ptb = pt.b.part(kt)
                        s.op("act", "activation", reads=[sbk.b], writes=[ptb], out=pt[:, kt, 0:nq], in_=sbk[:, 0:nq],
                             func=AF.Exp, scale=cur["scale"])
                        if key.get("mask") is not None:
                            mk = key["mask"]
                            s.op("dve", "tensor_tensor", reads=[ptb, mk.b], writes=[ptb],
                                 out=pt[:, kt, 0:nq].rearrange("p (g q) -> p g q", q=128),
                                 in0=pt[:, kt, 0:nq].rearrange("p (g q) -> p g q", q=128),
                                 in1=mk[:, :].unsqueeze(1).to_broadcast([128, nq // 128, 128]), op=ALU.mult)
                    if prev is not None and kt < len(prev["keys"]):
                        key = prev["keys"][kt]
                        pt = PT[(ui - 1) % 2]
                        ptb = pt.b.part(kt)
                        nkp = len(prev["keys"])
                        vd1 = prev["vd1"]
                        for j in range(prev["nq"] // 128):
                            s.op("pe", "matmul", reads=[ptb] + key["vb"], writes=[acc.b], out=acc[:, j, 0:vd1],
                                 lhsT=pt[:, kt, j * 128:(j + 1) * 128], rhs=key["v"], start=(kt == 0), stop=(kt == nkp - 1))
                if prev is not None:
                    prev["fin"](prev, acc)

        mprev_f = gp.sb("mprevf", [128, 128], F32)
        mnext_f = gp.sb("mnextf", [128, 128], F32)
        mprev = gp.sb("mprev", [128, 128], BF16)
        mnext = gp.sb("mnext", [128, 128], BF16)
        s.dma("sp", mprev_f[:], masks.ap[0], writes=[mprev_f.b])
        s.dma("sp", mnext_f[:], masks.ap[1], writes=[mnext_f.b])
        s.op("dve", "tensor_copy", reads=[mprev_f.b], writes=[mprev.b], out=mprev[:], in_=mprev_f[:])
        s.op("dve", "tensor_copy", reads=[mnext_f.b], writes=[mnext.b], out=mnext[:], in_=mnext_f[:])

        ph = Phase(nc, s)
        QA = ph.sb("QA", [128, 4, NTOK], BF16)
        KAz = [ph.sb(f"KAz{i}", [128, NTOK], BF16) for i in range(2)]
        s.op("pool", "memset", writes=[KAz[0].b], ap=KAz[0][64:128, :], constant=0.0)
        s.op("pool", "memset", writes=[KAz[1].b], ap=KAz[1][0:64, :], constant=0.0)
        VAs = ph.sb("VAs", [128, NT, 130], BF16)
        s.dma("sp", QA[:], QKT0.ap[0:4].rearrange("j p t -> p j t"), reads=[QKT0.b], writes=[QA.b])
        s.dma("sp", KAz[0][0:64, :], QKT0.ap[8, 0:64, :], reads=[QKT0.b], writes=[KAz[0].b])
        s.dma("sp", KAz[1][64:128, :], QKT0.ap[8, 64:128, :], reads=[QKT0.b], writes=[KAz[1].b])
        s.dma("sp", VAs[:], VA.ap.rearrange("(k p) e -> p k e", p=128), reads=[VA.b.part(t) for t in range(NT)],
              writes=[VAs.b])
        esink = ph.sb("esink", [128, 8], F32)
        s.op("act", "activation", reads=[vecb.b], writes=[esink.b], out=esink[:], in_=vecb[:, V_SINK:V_SINK + 8], func=AF.Exp)
        zr = ph.ring("za", [128, 4], F32, 2)
        rzr = ph.ring("rza", [128, 4], F32, 2)
        obr = ph.ring("oba", [128, 4, 64], BF16, 3)

        def fin_a(u, acc):
            kvh, n = u["kvh"], u["n"]
            z = zr.next()
            s.op("dve", "tensor_tensor", reads=[acc.b, esink.b], writes=[z.b], out=z[:], in0=acc[:, :, 64],
                 in1=esink[:, kvh * 4:(kvh + 1) * 4], op=ALU.add)
            rz = rzr.next()
            s.op("dve", "reciprocal", reads=[z.b], writes=[rz.b], out=rz[:], in_=z[:])
            ob = obr.next()
            s.op("dve", "tensor_tensor", reads=[acc.b, rz.b], writes=[ob.b], out=ob[:], in0=acc[:, :, 0:64],
                 in1=rz[:, :].unsqueeze(2).to_broadcast([128, 4, 64]), op=ALU.mult)
            s.dma("sp", MIX.ap[n * 128:(n + 1) * 128, kvh * 256:(kvh + 1) * 256], ob[:].rearrange("p g d -> p (g d)"),
                  reads=[ob.b], writes=[MIX.b.part(("a", n, kvh))])

        units = []
        for n in range(NT):
            if n < NLAT:
                kl = []
                if n > 0:
                    kl.append((n - 1, mprev))
                kl.append((n, None))
                if n < NLAT - 1:
                    kl.append((n + 1, mnext))
                kl += [(32, None), (33, None)]
            else:
                kl = [(32, None), (33, None)]
            for kvh in range(2):
                keys = [dict(kT=KAz[kvh][:, kt * 128:(kt + 1) * 128], kb=[KAz[kvh].b], v=VAs[:, kt, kvh * 65:(kvh + 1) * 65],
                             vb=[VAs.b], mask=mk) for (kt, mk) in kl]
                units.append(dict(qT=QA[:, :, n * 128:(n + 1) * 128], qb=[QA.b], keys=keys, nq=512, vd1=65,
                                  scale=0.125, fin=fin_a, kvh=kvh, n=n))
        attn_pipeline(ph, units, 5)
        ph.close()
        if stop_after == "L0P2A":
            s.finish()
            return nc

        lam_init0 = 0.8 - 0.6 * math.exp(-0.3 * 0)
        ph = Phase(nc, s)
        lt = ph.sb("lt", [128, 128], F32)
        lsum = ph.sb("lsum", [128, 2], F32)
        lexp = ph.sb("lexp", [128, 2], F32)
        nlam = ph.sb("nlam", [128, 1], F32)
        s.op("dve", "tensor_tensor", reads=[vecb.b], writes=[lt.b], out=lt[:, 0:64], in0=vecb[:, V_LQ1:V_LQ1 + 64],
             in1=vecb[:, V_LK1:V_LK1 + 64], op=ALU.mult)
        s.op("dve", "tensor_tensor", reads=[vecb.b], writes=[lt.b], out=lt[:, 64:128], in0=vecb[:, V_LQ2:V_LQ2 + 64],
             in1=vecb[:, V_LK2:V_LK2 + 64], op=ALU.mult)
        s.op("dve", "tensor_reduce", reads=[lt.b], writes=[lsum.b], out=lsum[:],
             in_=lt[:, :].rearrange("p (a d) -> p a d", d=64), axis=AX.X, op=ALU.add)
        s.op("act", "activation", reads=[lsum.b], writes=[lexp.b], out=lexp[:], in_=lsum[:], func=AF.Exp)
        s.op("dve", "tensor_tensor", reads=[lexp.b], writes=[nlam.b], out=nlam[:], in0=lexp[:, 1:2], in1=lexp[:, 0:1],
             op=ALU.subtract)
        s.op("dve", "tensor_scalar", reads=[nlam.b], writes=[nlam.b], out=nlam[:], in0=nlam[:], scalar1=-lam_init0,
             scalar2=None, op0=ALU.add)
        subl = ph.sb("subl", [128, 128], F32)
        s.op("dve", "tensor_scalar", reads=[vecb.b], writes=[subl.b], out=subl[:], in0=vecb[:, V_SUBLN:V_SUBLN + 128],
             scalar1=1.0 - lam_init0, scalar2=None, op0=ALU.mult)
        QBr = ph.ring("QB", [128, NTOK], BF16, 2)
        KBz = [[ph.sb(f"KBz{i}_{j}", [128, NTOK], BF16) for j in range(2)] for i in range(2)]
        for i in range(2):
            s.op("pool", "memset", writes=[KBz[i][0].b], ap=KBz[i][0][64:128, :], constant=0.0)
            s.op("pool", "memset", writes=[KBz[i][1].b], ap=KBz[i][1][0:64, :], constant=0.0)
        VBr = ph.ring("VBs", [128, NT, 129], BF16, 2)
        o1r = ph.ring("o1", [128, 4, 128], F32, 2)
        z1r = ph.ring("zb", [128, 4], F32, 4)
        tbr = ph.ring("tb", [128, 4, 128], F32, 2)
        obbr = ph.ring("obb", [128, 4, 128], F32, 2)
        sqbr = ph.ring("sqb", [128, 4, 128], F32, 1)
        ssbr = ph.ring("ssb", [128, 4], F32, 2)
        onr = ph.ring("onb", [128, 4, 128], F32, 1)
        outbr = ph.ring("outb", [128, 4, 128], BF16, 3)
        state = {}

        def fin_b(u, acc):
            h, q0, nj, sidx = u["h"], u["q0"], u["nq"] // 128, u["s"]
            rz = z1r.next()
            s.op("dve", "reciprocal", reads=[acc.b], writes=[rz.b], out=rz[:, 0:nj], in_=acc[:, 0:nj, 128])
            if sidx == 0:
                o1 = o1r.next()
                s.op("dve", "tensor_tensor", reads=[acc.b, rz.b], writes=[o1.b], out=o1[:, 0:nj, :], in0=acc[:, 0:nj, 0:128],
                     in1=rz[:, 0:nj].unsqueeze(2).to_broadcast([128, nj, 128]), op=ALU.mult)
                state["o1"] = o1
                return
            o1 = state["o1"]
            rzl = z1r.next()
            s.op("dve", "tensor_scalar", reads=[rz.b, nlam.b], writes=[rzl.b], out=rzl[:, 0:nj], in0=rz[:, 0:nj],
                 scalar1=nlam[:, 0:1], scalar2=None, op0=ALU.mult)
            tb = tbr.next()
            s.op("dve", "tensor_tensor", reads=[acc.b, rzl.b], writes=[tb.b], out=tb[:, 0:nj, :], in0=acc[:, 0:nj, 0:128],
                 in1=rzl[:, 0:nj].unsqueeze(2).to_broadcast([128, nj, 128]), op=ALU.mult)
            ob = obbr.next()
            s.op("pool", "tensor_tensor", reads=[tb.b, o1.b], writes=[ob.b], out=ob[:, 0:nj, :], in0=tb[:, 0:nj, :],
                 in1=o1[:, 0:nj, :], op=ALU.add)
            sq = sqbr.next()
            s.op("pool", "tensor_tensor", reads=[ob.b], writes=[sq.b], out=sq[:, 0:nj, :], in0=ob[:, 0:nj, :],
                 in1=ob[:, 0:nj, :], op=ALU.mult)
            ss = ssbr.next()
            s.op("dve", "tensor_reduce", reads=[sq.b], writes=[ss.b], out=ss[:, 0:nj], in_=sq[:, 0:nj, :], axis=AX.X,
                 op=ALU.add)
            r = rstd_from_ss(ph, ss, nj, 128, "rs_b%d" % nj)
            on = onr.next()
            s.op("dve", "tensor_tensor", reads=[ob.b, r.b], writes=[on.b], out=on[:, 0:nj, :], in0=ob[:, 0:nj, :],
                 in1=r[:, 0:nj].unsqueeze(2).to_broadcast([128, nj, 128]), op=ALU.mult)
            out = outbr.next()
            s.op("pool", "tensor_tensor", reads=[on.b, subl.b], writes=[out.b], out=out[:, 0:nj, :], in0=on[:, 0:nj, :],
                 in1=subl[:, :].unsqueeze(1).to_broadcast([128, nj, 128]), op=ALU.mult)
            s.dma("sp", MIX.ap[q0:q0 + nj * 128, 512 + h * 128:512 + (h + 1) * 128].rearrange("(j p) d -> p j d", p=128),
                  out[:, 0:nj, :], reads=[out.b], writes=[MIX.b.part(("b", h, q0))])

        def load_b(h):
            Qh, Kz, Vh = QBr.items[h % 2], KBz[h % 2], VBr.items[h % 2]
            s.dma("sp", Qh[:], QKT0.ap[4 + h], reads=[QKT0.b], writes=[Qh.b])
            s.dma("sp", Kz[0][0:64, :], QKT0.ap[9 + h, 0:64, :], reads=[QKT0.b], writes=[Kz[0].b])
            s.dma("sp", Kz[1][64:128, :], QKT0.ap[9 + h, 64:128, :], reads=[QKT0.b], writes=[Kz[1].b])
            s.dma("sp", Vh[:], VB.ap.rearrange("(k p) (h e) -> p k h e", p=128, e=129)[:, :, h, :],
                  reads=[VB.b.part(t) for t in range(NT)], writes=[Vh.b])

        units = []
        for h in range(4):
            Qh, Kz, Vh = QBr.items[h % 2], KBz[h % 2], VBr.items[h % 2]
            blocks = [(qb * 512, 512, list(range(NT))) for qb in range(8)] + [(S, 256, [32, 33])]
            first = len(units)
            for (q0, nq, kl) in blocks:
                for sidx in range(2):
                    Kh = Kz[sidx]
                    keys = [dict(kT=Kh[:, kt * 128:(kt + 1) * 128], kb=[Kh.b], v=Vh[:, kt, :], vb=[Vh.b], mask=None)
                            for kt in kl]
                    units.append(dict(qT=Qh[:, q0:q0 + nq], qb=[Qh.b], keys=keys, nq=nq, vd1=129, scale=0.125,
                                      fin=fin_b, h=h, q0=q0, s=sidx))
            if h == 0:
                units[first]["pre"] = (lambda: load_b(0))
            if h < 3:
                units[first + 1]["pre"] = (lambda hh=h + 1: load_b(hh))
        attn_pipeline(ph, units, NT)
        ph.close()
        if stop_after == "L0P2B":
            s.finish()
            return nc

        def phase3a(l, Xin, Xout, ntiles):
            ph = Phase(nc, s)
            Wm = load_w_bf16(ph, "wmix", wmix.ap[l], 8, D, wmix.b)
            G2L, S2L = mod_tiles(ph, l, 0, 3, 4, n2g, "l")
            gateL = load_bcast(ph, "gateL", modV.ap[l, 0, 2 * D:3 * D], D, modV.b)
            if ntiles > NLAT:
                G2C, S2C = mod_tiles(ph, l, 1, 3, 4, n2g, "c")
                gateC = load_bcast(ph, "gateC", modV.ap[l, 1, 2 * D:3 * D], D, modV.b)
            nr = norm_rings(ph)
            xr = ph.ring("x", [128, D], F32, 5)
            mr = ph.ring("mx", [128, D], BF16, 3)
            mTr = ph.ring("mT", [128, 8, 128], BF16, 2)
            tmr = ph.ring("tm", [128, D], F32, 1)
            x1r = ph.ring("x1", [128, D], F32, 4)
            hst = ph.ring("hst", [128, 8, 512], BF16, 2)
            pT = ph.ps("pT", [128, 8, 128], BF16)
            pT2 = ph.ps("pT2", [128, 8, 128], BF16)
            pP = ph.ps("pP", [128, D], F32)
            def tile_p3a(t):
                g0 = (t // 4) * 4
                ti = t - g0
                ntile_g = min(ntiles, g0 + 4) - g0
                st_ = hst.items[(t // 4) % 2]
                isctx = t >= NLAT
                xt = xr.next()
                s.dma("sp", xt[:], Xin.ap[t * 128:(t + 1) * 128, :], reads=[Xin.b.part(t)], writes=[xt.b])
                mx = mr.next()
                s.dma("sp", mx[:], MIX.ap[t * 128:(t + 1) * 128, :], reads=[MIX.b] + list(MIX.b.parts.values()),
                      writes=[mx.b])
                yield
                for k in range(8):
                    s.op("pe", "transpose", reads=[mx.b, idb.b], writes=[pT.b], out=pT[:, k, :],
                         in_=mx[:, k * 128:(k + 1) * 128], identity=idb[:], inc=(k == 7))
                yield
                mT = mTr.next()
                s.op("act", "copy", reads=[pT.b], writes=[mT.b], out=mT[:], in_=pT[:])
                yield
                for k in range(8):
                    for c in range(2):
                        s.op("pe", "matmul", reads=[mT.b] + Wm.kparts, writes=[pP.b], out=pP[:, c * 512:(c + 1) * 512],
                             lhsT=mT[:, k, :], rhs=Wm[:, k, c * 512:(c + 1) * 512], start=(k == 0), stop=(k == 7))
                yield
                tm = tmr.next()
                gate = gateC if isctx else gateL
                s.op("dve", "tensor_tensor", reads=[pP.b, gate.b], writes=[tm.b], out=tm[:], in0=pP[:], in1=gate[:],
                     op=ALU.mult)
                x1 = x1r.next()
                s.op("pool", "tensor_tensor", reads=[tm.b, xt.b], writes=[x1.b], out=x1[:], in0=tm[:], in1=xt[:],
                     op=ALU.add)
                s.dma("sp", Xout.ap[t * 128:(t + 1) * 128, :], x1[:], reads=[x1.b], writes=[Xout.b.part(t)])
                yield
                yield from norm_mod_transpose(ph, x1, G2C if isctx else G2L, S2C if isctx else S2L, pT2,
                                              st_[:, :, ti * 128:(ti + 1) * 128], st_.b, nr)
                if ti == ntile_g - 1:
                    ntk = ntile_g * 128
                    s.dma("sp", H2T.ap[:, :, g0 * 128:g0 * 128 + ntk].rearrange("j p t -> p j t"), st_[:, :, 0:ntk],
                          reads=[st_.b], writes=[H2T.b.part(g0)])

            run_pipelined((tile_p3a(t) for t in range(ntiles)), first_stage=5)
            ph.close()

        def phase3b(l, Xin, Xout, ntiles):
            ph = Phase(nc, s)
            Wi = load_w_bf16_cols(ph, "wfi", wfi.ap[l], 8, 2 * FH, wfi.b, 512, [0, 5, 6, 1, 7, 2, 8, 3, 9, 4, 10])
            Wo = load_w_bf16(ph, "wfo", wfo.ap[l], 22, D, wfo.b)
            gate = load_bcast(ph, "gate", modV.ap[l, 0, 5 * D:6 * D], D, modV.b)
            h2r = ph.ring("h2", [128, 8, 512], BF16, 1)
            actT = ph.sb("actT", [128, 22, 512], BF16)
            sgr = ph.ring("sg", [128, 512], F32, 2)
            xr = ph.ring("x", [128, D], F32, 1)
            x2r = ph.ring("x2", [128, D], F32, 2)
            pG = [ph.ps(f"pG{i}", [128, 512], F32) for i in range(2)]
            pU = [ph.ps(f"pU{i}", [128, 512], F32) for i in range(2)]
            pY = [ph.ps(f"pY{i}", [128, D], F32) for i in range(2)]
            yi = 0
            for g0 in range(0, ntiles, 4):
                tiles = list(range(g0, min(ntiles, g0 + 4)))
                ntk = len(tiles) * 128
                if tiles[0] >= NLAT:
                    s.dma("sp", gate[:], modV.ap[l, 1, 5 * D:6 * D].partition_broadcast(128), reads=[modV.b],
                          writes=[gate.b])
                h2 = h2r.next()
                if g0 == 0:
                    s.dma("sp", h2[:, :, 0:ntk], H2T.ap[:, :, 0:ntk].rearrange("j p t -> p j t"),
                          reads=[H2T.b.part(0)], writes=[h2.b])
                for j in range(22):
                    pg, pu = pG[j % 2], pU[j % 2]
                    for k in range(8):
                        s.op("pe", "matmul", reads=[h2.b, Wi.cpart(j * 128)], writes=[pg.b], out=pg[:, 0:ntk],
                             lhsT=Wi[:, k, j * 128:(j + 1) * 128], rhs=h2[:, k, 0:ntk], start=(k == 0), stop=(k == 7))
                    for k in range(8):
                        s.op("pe", "matmul", reads=[h2.b, Wi.cpart(FH + j * 128)], writes=[pu.b], out=pu[:, 0:ntk],
                             lhsT=Wi[:, k, FH + j * 128:FH + (j + 1) * 128], rhs=h2[:, k, 0:ntk], start=(k == 0),
                             stop=(k == 7))
                    sg = sgr.next()
                    s.op("act", "activation", reads=[pg.b], writes=[sg.b], out=sg[:, 0:ntk], in_=pg[:, 0:ntk], func=AF.Silu)
                    s.op("dve", "tensor_tensor", reads=[pu.b, sg.b], writes=[actT.b.part(j)], out=actT[:, j, 0:ntk],
                         in0=pu[:, 0:ntk], in1=sg[:, 0:ntk], op=ALU.mult)
                if g0 + 4 < ntiles:
                    n0_ = g0 + 4
                    ntk2 = (min(ntiles, n0_ + 4) - n0_) * 128
                    s.dma("act", h2[:, :, 0:ntk2], H2T.ap[:, :, n0_ * 128:n0_ * 128 + ntk2].rearrange("j p t -> p j t"),
                          reads=[H2T.b.part(n0_)], writes=[h2.b])
                for ti, t in enumerate(tiles):
                    isctx = t >= NLAT
                    py = pY[yi % 2]
                    yi += 1
                    for c in range(2):
                        for j in range(22):
                            s.op("pe", "matmul", reads=[actT.b.part(j)] + Wo.kparts, writes=[py.b],
                                 out=py[:, c * 512:(c + 1) * 512], lhsT=actT[:, j, ti * 128:(ti + 1) * 128],
                                 rhs=Wo[:, j, c * 512:(c + 1) * 512], start=(j == 0), stop=(j == 21))
                    xt = xr.next()
                    s.dma("sp", xt[:], Xin.ap[t * 128:(t + 1) * 128, :], reads=[Xin.b.part(t)], writes=[xt.b])
                    x2 = x2r.next()
                    s.op("dve", "tensor_tensor", reads=[py.b, gate.b], writes=[x2.b], out=x2[:], in0=py[:], in1=gate[:],
                         op=ALU.mult)
                    s.op("pool", "tensor_tensor", reads=[x2.b, xt.b], writes=[x2.b], out=x2[:], in0=x2[:], in1=xt[:],
                         op=ALU.add)
                    s.dma("sp", Xout.ap[t * 128:(t + 1) * 128, :], x2[:], reads=[x2.b], writes=[Xout.b.part(t)])
            ph.close()

        phase3a(0, xs, X1A, NT)
        if stop_after == "L0P3A":
            s.finish()
            return nc
        phase3b(0, X1A, X1, NT)
        if stop_after == "L0":
            s.finish()
            return nc

        ph = Phase(nc, s)
        W1 = load_w_bf16(ph, "wod", wod.ap, 8, 1600, wod.b)
        Wq = load_w_bf16(ph, "wuq", wuq.ap, 2, 512, wuq.b)
        Wkv = load_w_bf16(ph, "wukv", wukv.ap, 2, 768, wukv.b)
        Ws = ph.sb("ws", [128, 4, 128], BF16)
        s.dma("pool", Ws[:], wsT.ap.rearrange("g q p -> q g p"), reads=[wsT.b], writes=[Ws.b])
        bs = ph.sb("bs", [128, 4], F32)
        s.dma("sp", bs[:], bsT.ap, writes=[bs.b])
        GL, SL = mod_tiles(ph, 1, 0, 0, 1, n1g, "l")
        GC, SC = mod_tiles(ph, 1, 1, 0, 1, n1g, "c")
        G13 = ph.sb("g13", [128, 13 * 64], F32)
        g13v = G13[:, :].rearrange("p (h d) -> p h d", d=64)
        g13q = G13[:, 0:512].rearrange("p (h t d) -> p h t d", t=2, d=64)
        s.op("dve", "tensor_copy", reads=[vecb.b], writes=[G13.b], out=g13q[:, :, 0, :],
             in_=vecb[:, V_QNN:V_QNN + 64].unsqueeze(1).to_broadcast([128, 4, 64]))
        s.op("dve", "tensor_copy", reads=[vecb.b], writes=[G13.b], out=g13q[:, :, 1, :],
             in_=vecb[:, V_QNR:V_QNR + 64].unsqueeze(1).to_broadcast([128, 4, 64]))
        s.op("dve", "tensor_copy", reads=[vecb.b], writes=[G13.b], out=g13v[:, 8:12, :],
             in_=vecb[:, V_KNN:V_KNN + 64].unsqueeze(1).to_broadcast([128, 4, 64]))
        s.op("dve", "tensor_copy", reads=[vecb.b], writes=[G13.b], out=g13v[:, 12:13, :],
             in_=vecb[:, V_KNR:V_KNR + 64].unsqueeze(1).to_broadcast([128, 1, 64]))
        nr = norm_rings(ph)
        head_rings(ph, 832)
        xr = ph.ring("x", [128, D], F32, 4)
        rpr = ph.ring("rp", [128, 128], F32, 11)
        hTr = ph.ring("hT", [128, 8, 128], BF16, 2)
        glr = ph.ring("gl", [128, D], F32, 4)
        vsqr = ph.ring("vsq", [128, 512], F32, 1)
        ssvr = ph.ring("ssv", [128, 2], F32, 2)
        vnr = ph.ring("vn", [128, 512], BF16, 2)
        mcr = ph.ring("mc", [128, 512], BF16, 2)
        cqfr = ph.ring("cqf", [128, 512], F32, 3)
        cnr = ph.ring("cn", [128, 512], F32, 1)
        cnbr = ph.ring("cnb", [128, 512], BF16, 2)
        cTr = ph.ring("cT", [128, 4, 128], BF16, 2)
        hqr = ph.ring("hq", [128, 832], F32, 3)
        qcbr = ph.ring("qcb", [128, 4, 128], BF16, 2)
        kcbr = ph.ring("kcb", [128, 4, 128], BF16, 2)
        kper = ph.ring("kpe", [128, 64], F32, 7)
        kprr = ph.ring("kpr", [128, 64], F32, 2)
        vdr = ph.ring("vd", [128, 4, 129], BF16, 2)
        for t_ in vdr.items:
            s.op("pool", "memset", writes=[t_.b], ap=t_[:], constant=1.0)
        qkst = ph.ring("qkst", [128, 8, 512], BF16, 2)
        pT = ph.ps("pT", [128, 8, 128], BF16)
        pO1 = ph.ps("pO1", [128, 2048], F32)
        pQ = ph.ps("pQ", [128, 512], F32)
        pKV = ph.ps("pKV", [128, 1024], F32)
        def cps(g):
            return pO1[:, 1600 + g * 128:1728 + g * 128] if g < 3 else pKV[:, 768:896]

        def cpb(g):
            return pO1.b if g < 3 else pKV.b

        ss3r = ph.ring("ss3", [128, 4], F32, 2)
        v3r = ph.ring("v3", [128, 4], F32, 2)
        r3r = ph.ring("r3", [128, 4], F32, 2)
        jk2r = ph.ring("jk2", [128, 512], BF16, 3)
        for t_ in ss3r.items:
            s.op("pool", "memset", writes=[t_.b], ap=t_[:], constant=1.0)

        def tile_l1p1(t):
            g0 = (t // 4) * 4
            ti = t - g0
            ntile_g = min(NT, g0 + 4) - g0
            st_ = qkst.items[(t // 4) % 2]
            isctx = t >= NLAT
            xt = xr.next()
            s.dma("sp", xt[:], X1.ap[t * 128:(t + 1) * 128, :], reads=[X1.b.part(t)], writes=[xt.b])
            yield
            hT = hTr.next()
            yield from norm_mod_transpose(ph, xt, GC if isctx else GL, SC if isctx else SL, pT, hT[:], hT.b, nr, split=False)
            yield
            chunks = [(1280, 1536), (1536, 1600)] if isctx else [(0, 512), (512, 1024), (1024, 1536), (1536, 1600)]
            for k in range(8):
                for (n0, n1) in chunks:
                    s.op("pe", "matmul", reads=[hT.b] + W1.kparts, writes=[pO1.b], out=pO1[:, n0:n1],
                         lhsT=hT[:, k, :], rhs=W1[:, k, n0:n1], start=(k == 0), stop=(k == 7))
            if not isctx:
                rp = rpr.next()
                s.dma("sp", rp[:], rope.ap[t], writes=[rp.b])
            yield
            cqf = cqfr.next()
            c0 = 256 if isctx else 0
            s.op("act", "copy", reads=[pO1.b], writes=[cqf.b], out=cqf[:, c0:512], in_=pO1[:, 1024 + c0:1536])
            kpe = kper.next()
            s.op("act", "copy", reads=[pO1.b], writes=[kpe.b], out=kpe[:], in_=pO1[:, 1536:1600])
            ss3 = ss3r.next()
            if not isctx:
                gl = glr.next()
                s.op("act", "activation", reads=[pO1.b], writes=[gl.b], out=gl[:], in_=pO1[:, 0:1024], func=AF.Gelu)
                jk = jk2r.next()
                s.op("act", "activation", reads=[gl.b], writes=[jk.b, ss3.b], out=jk[:], in_=gl[:, 512:1024], func=AF.Square,
                     accum_out=ss3[:, 0:1])
                jk = jk2r.next()
                s.op("act", "activation", reads=[cqf.b], writes=[jk.b, ss3.b], out=jk[:, 0:256], in_=cqf[:, 0:256],
                     func=AF.Square, accum_out=ss3[:, 1:2])
            jk = jk2r.next()
            s.op("act", "activation", reads=[cqf.b], writes=[jk.b, ss3.b], out=jk[:, 0:256], in_=cqf[:, 256:512],
                 func=AF.Square, accum_out=ss3[:, 2:3])
            yield
            v3 = v3r.next()
            s.op("dve", "tensor_scalar", reads=[ss3.b], writes=[v3.b], out=v3[:, 0:1], in0=ss3[:, 0:1], scalar1=1.0 / 512,
                 scalar2=EPS, op0=ALU.mult, op1=ALU.add)
            s.op("dve", "tensor_scalar", reads=[ss3.b], writes=[v3.b], out=v3[:, 1:3], in0=ss3[:, 1:3], scalar1=1.0 / 256,
                 scalar2=EPS, op0=ALU.mult, op1=ALU.add)
            r3 = r3r.next()
            s.op("pool", "tensor_tensor", reads=[v3.b, nh.b], writes=[r3.b], out=r3[:, 0:3], in0=v3[:, 0:3], in1=nh[:, 0:3],
                 op=ALU.pow)
            yield
            cnb = cnbr.next()
            if not isctx:
                vn = vnr.next()
                s.op("dve", "scalar_tensor_tensor", reads=[gl.b, r3.b, vecb.b], writes=[vn.b], out=vn[:],
                     in0=gl[:, 512:1024], scalar=r3[:, 0:1], in1=vecb[:, V_CVN:V_CVN + 512], op0=ALU.mult, op1=ALU.mult)
                s.op("dve", "scalar_tensor_tensor", reads=[cqf.b, r3.b, vecb.b], writes=[cnb.b], out=cnb[:, 0:256],
                     in0=cqf[:, 0:256], scalar=r3[:, 1:2], in1=vecb[:, V_QAN:V_QAN + 256], op0=ALU.mult, op1=ALU.mult)
            s.op("dve", "scalar_tensor_tensor", reads=[cqf.b, r3.b, vecb.b], writes=[cnb.b], out=cnb[:, 256:512],
                 in0=cqf[:, 256:512], scalar=r3[:, 2:3], in1=vecb[:, V_KVAN:V_KVAN + 256], op0=ALU.mult, op1=ALU.mult)
            yield
            if not isctx:
                for g in range(4):
                    s.op("pe", "matmul", reads=[vn.b, Ws.b], writes=[cpb(g)], out=cps(g),
                         lhsT=Ws[:, g, :], rhs=vn[:, g * 128:(g + 1) * 128], start=True, stop=True)
            kc0 = c0 // 128
            for k in range(kc0, 4):
                s.op("pe", "transpose", reads=[cnb.b, idb.b], writes=[pT.b], out=pT[:, k, :],
                     in_=cnb[:, k * 128:(k + 1) * 128], identity=idb[:], inc=(k == 3))
            cT = cTr.next()
            s.op("act", "copy", reads=[pT.b], writes=[cT.b], out=cT[:, kc0:4, :], in_=pT[:, kc0:4, :])
            if not isctx:
                mc_ = mcr.next()
                for g in range(4):
                    s.op("dve", "scalar_tensor_tensor", reads=[cpb(g), bs.b, gl.b], writes=[mc_.b],
                         out=mc_[:, g * 128:(g + 1) * 128], in0=cps(g), scalar=bs[:, g:g + 1],
                         in1=gl[:, g * 128:(g + 1) * 128], op0=ALU.add, op1=ALU.mult)
                s.dma("sp", MIX.ap[t * 128:(t + 1) * 128, 0:512], mc_[:], reads=[mc_.b], writes=[MIX.b.part(("c", t))])
            yield
            if not isctx:
                for k in range(2):
                    s.op("pe", "matmul", reads=[cT.b] + Wq.kparts, writes=[pQ.b], out=pQ[:], lhsT=cT[:, k, :],
                         rhs=Wq[:, k, :], start=(k == 0), stop=(k == 1))
            for k in range(2):
                for (n0, n1) in ((0, 512), (512, 768)):
                    s.op("pe", "matmul", reads=[cT.b] + Wkv.kparts, writes=[pKV.b], out=pKV[:, n0:n1],
                         lhsT=cT[:, 2 + k, :], rhs=Wkv[:, k, n0:n1], start=(k == 0), stop=(k == 1))
            yield
            vd = vdr.next()
            s.op("act", "copy", reads=[pKV.b], writes=[vd.b], out=vd[:, :, 0:128],
                 in_=pKV[:, 256:768].rearrange("p (h d) -> p h d", d=128))
            s.dma("sp", VD.ap[t * 128:(t + 1) * 128, :], vd[:].rearrange("p h d -> p (h d)"), reads=[vd.b],
                  writes=[VD.b.part(t)])
            hq = hqr.next()
            if not isctx:
                s.op("act", "copy", reads=[pQ.b], writes=[hq.b], out=hq[:, 0:512], in_=pQ[:])
            else:
                s.op("pool", "memset", writes=[hq.b], ap=hq[:, 0:512], constant=1.0)
            s.op("act", "copy", reads=[pKV.b], writes=[hq.b], out=hq[:, 512:768], in_=pKV[:, 0:256])
            s.op("act", "copy", reads=[kpe.b], writes=[hq.b], out=hq[:, 768:832], in_=kpe[:])
            qg = yield from head_norm(ph, hq, 13, G13, "1")
            yield
            qgv = qg[:, 0:832].rearrange("p (h d) -> p h d", d=64)
            kcb = kcbr.next()
            s.op("dve", "tensor_copy", reads=[qg.b], writes=[kcb.b], out=kcb[:, :, 0:64], in_=qgv[:, 8:12, :])
            if isctx:
                s.op("dve", "tensor_copy", reads=[qg.b], writes=[kcb.b], out=kcb[:, :, 64:128],
                     in_=qgv[:, 12:13, :].to_broadcast([128, 4, 64]))
            else:
                kpr = kprr.next()
                rope_apply(ph, qgv[:, 12:13, :], kpr[:, :].unsqueeze(1), 1, rp, [qg.b], kpr.b, t2eng="dve")
                s.op("dve", "tensor_copy", reads=[kpr.b], writes=[kcb.b], out=kcb[:, :, 64:128],
                     in_=kpr[:, :].unsqueeze(1).to_broadcast([128, 4, 64]))
                qcb = qcbr.next()
                qg4 = qg[:, 0:512].rearrange("p (h t d) -> p h t d", t=2, d=64)
                s.op("act", "copy", reads=[qg.b], writes=[qcb.b], out=qcb[:, :, 0:64], in_=qg4[:, :, 0, :])
                rope_apply(ph, qg4[:, :, 1, :], qcb[:, :, 64:128], 4, rp, [qg.b], qcb.b, t2eng="dve")
            yield
            if not isctx:
                for h in range(4):
                    s.op("pe", "transpose", reads=[qcb.b, idb.b], writes=[pT.b], out=pT[:, h, :], in_=qcb[:, h, :],
                         identity=idb[:])
            for h in range(4):
                s.op("pe", "transpose", reads=[kcb.b, idb.b], writes=[pT.b], out=pT[:, 4 + h, :], in_=kcb[:, h, :],
                     identity=idb[:], inc=(h == 3))
            b0 = 4 if isctx else 0
            s.op("act", "copy", reads=[pT.b], writes=[st_.b], out=st_[:, b0:8, ti * 128:(ti + 1) * 128], in_=pT[:, b0:8, :])
            if ti == ntile_g - 1:
                ntk = ntile_g * 128
                s.dma("sp", QKT1.ap[b0:8, :, g0 * 128:g0 * 128 + ntk].rearrange("j p t -> p j t"), st_[:, b0:8, 0:ntk],
                      reads=[st_.b], writes=[QKT1.b])

        run_pipelined((tile_l1p1(t) for t in range(NT)), first_stage=6)
        ph.close()
        if stop_after == "L1P1":
            s.finish()
            return nc

        ph = Phase(nc, s)
        QDr = ph.ring("QD", [128, S], BF16, 2)
        KDr = ph.ring("KD", [128, NTOK], BF16, 2)
        VDr = ph.ring("VDs", [128, NT, 129], BF16, 2)
        zdr = ph.ring("zd", [128, 4], F32, 2)
        odr = ph.ring("od", [128, 4, 128], BF16, 3)

        def fin_d(u, acc):
            h, q0 = u["h"], u["q0"]
            rz = zdr.next()
            s.op("dve", "reciprocal", reads=[acc.b], writes=[rz.b], out=rz[:], in_=acc[:, :, 128])
            od = odr.next()
            s.op("dve", "tensor_tensor", reads=[acc.b, rz.b], writes=[od.b], out=od[:], in0=acc[:, :, 0:128],
                 in1=rz[:, :].unsqueeze(2).to_broadcast([128, 4, 128]), op=ALU.mult)
            s.dma("sp", MIX.ap[q0:q0 + 512, 512 + h * 128:512 + (h + 1) * 128].rearrange("(j p) d -> p j d", p=128),
                  od[:], reads=[od.b], writes=[MIX.b.part(("d", h, q0))])

        def load_d(h):
            Qh, Kh, Vh = QDr.items[h % 2], KDr.items[h % 2], VDr.items[h % 2]
            s.dma("sp", Qh[:], QKT1.ap[h, :, 0:S], reads=[QKT1.b], writes=[Qh.b])
            s.dma("sp", Kh[:], QKT1.ap[4 + h], reads=[QKT1.b], writes=[Kh.b])
            s.dma("sp", Vh[:], VD.ap.rearrange("(k p) (h e) -> p k h e", p=128, e=129)[:, :, h, :],
                  reads=[VD.b.part(t) for t in range(NT)], writes=[Vh.b])

        units = []
        for h in range(4):
            Qh, Kh, Vh = QDr.items[h % 2], KDr.items[h % 2], VDr.items[h % 2]
            first = len(units)
            for qb in range(8):
                keys = [dict(kT=Kh[:, kt * 128:(kt + 1) * 128], kb=[Kh.b], v=Vh[:, kt, :], vb=[Vh.b], mask=None)
                        for kt in range(NT)]
                units.append(dict(qT=Qh[:, qb * 512:(qb + 1) * 512], qb=[Qh.b], keys=keys, nq=512, vd1=129,
                                  scale=128.0 ** -0.5, fin=fin_d, h=h, q0=qb * 512))
            if h == 0:
                units[first]["pre"] = (lambda: load_d(0))
            if h < 3:
                units[first + 1]["pre"] = (lambda hh=h + 1: load_d(hh))
        attn_pipeline(ph, units, NT)
        ph.close()
        if stop_after == "L1P2":
            s.finish()
            return nc

        phase3a(1, X1, X1A, NLAT)
        phase3b(1, X1A, y, NLAT)
        gp.close()
        s.finish()
        print("program: instructions", s.n_ins, "waits", s.n_wait)
    return nc


def _rope_tables():
    rows = S // 64
    row = np.repeat(np.arange(rows, dtype=np.int32), 64).astype(np.float32)
    col = np.tile(np.arange(64, dtype=np.int32), rows).astype(np.float32)
    inv = (np.float32(10000.0) ** (-np.arange(16, dtype=np.float32) / np.float32(16))).astype(np.float32)
    ang = np.concatenate([row[:, None] * inv, col[:, None] * inv], axis=-1).astype(np.float32)
    c, sn = np.cos(ang).astype(np.float32), np.sin(ang).astype(np.float32)
    tab = np.concatenate([c, c, -sn, sn], axis=-1)
    return np.ascontiguousarray(tab.reshape(NLAT, 128, 128))


def _shared_inputs(inp):
    f = lambda a: np.ascontiguousarray(np.asarray(a, dtype=np.float32))
    ev = f(inp["ev_w_in"])[0]
    aq = ev[:, 0:512].reshape(D, 8, 64)
    aq_p = np.stack([aq[:, [j, 4 + j], :] for j in range(4)], axis=1).reshape(D, 512)
    wev = np.concatenate([aq_p, ev[:, 512:1024], ev[:, 1024:1152], ev[:, 1280:1792], ev[:, 1152:1280], ev[:, 1792:2304]], axis=1)
    ukv = f(inp["od_w_ukv"])[0].reshape(256, 4, 192)
    wukv = np.concatenate([ukv[:, :, 0:64].reshape(256, 256), ukv[:, :, 64:192].reshape(256, 512)], axis=1)
    vec = np.concatenate([
        f(inp["ev_qnorm_a"])[0], f(inp["ev_knorm_a"])[0], f(inp["ev_qnorm_b"])[0], f(inp["ev_knorm_b"])[0],
        f(inp["ev_sink"])[0], f(inp["ev_lam_q1"])[0], f(inp["ev_lam_k1"])[0], f(inp["ev_lam_q2"])[0], f(inp["ev_lam_k2"])[0],
        f(inp["ev_subln"])[0], f(inp["od_c_vnorm"])[0], f(inp["od_qa_norm"])[0], f(inp["od_kva_norm"])[0],
        f(inp["od_qnorm_nope"])[0], f(inp["od_knorm_nope"])[0], f(inp["od_qnorm_rope"])[0], f(inp["od_knorm_rope"])[0]])
    assert vec.shape[0] == NV
    j = np.arange(128)[:, None]
    i = np.arange(128)[None, :]
    masks = np.stack([(j >= i), (j <= i)]).astype(np.float32)
    return {
        "ada_w": f(inp["ada_w"]), "ada_b": f(inp["ada_b"]).reshape(2, 1, 6 * D),
        "n1g": f(inp["norm1_g"]).reshape(2, 1, D), "n2g": f(inp["norm2_g"]).reshape(2, 1, D),
        "wmix": f(inp["mix_w_out"]), "wfi": f(inp["ffn_w_in"]), "wfo": f(inp["ffn_w_out"]),
        "wev": f(wev), "wod": f(inp["od_w_in"])[0], "wuq": f(inp["od_w_uq"])[0], "wukv": f(wukv),
        "wsT": f(np.transpose(f(inp["od_c_ws"])[0], (0, 2, 1))), "bsT": f(f(inp["od_c_bs"])[0].T),
        "vecs": f(vec.reshape(1, NV)), "ident": np.eye(128, dtype=np.float32), "rope": _rope_tables(), "masks": masks,
    }


def make_in_maps(inp):
    shared = _shared_inputs(inp)
    x = np.asarray(inp["x"], dtype=np.float32)
    ctx = np.asarray(inp["ctx"], dtype=np.float32)
    c = np.asarray(inp["c"], dtype=np.float32)
    c_ctx = np.asarray(inp["c_ctx"], dtype=np.float32)
    maps = []
    for b in range(x.shape[0]):
        m = dict(shared)
        m["xs"] = np.ascontiguousarray(np.concatenate([x[b], ctx[b]], axis=0))
        cc = np.stack([c[b], c_ctx], axis=-1).reshape(8, 128, 2).transpose(1, 0, 2)
        m["cc"] = np.ascontiguousarray(cc)
        maps.append(m)
    return maps


_NC_CACHE = {}


def kernel(**inputs):
    if "nc" not in _NC_CACHE:
        _NC_CACHE["nc"] = build_program()
    nc = _NC_CACHE["nc"]
    in_maps = make_in_maps(inputs)
    res = run_bass_kernel_spmd(nc, in_maps, core_ids=list(range(len(in_maps))))
    return np.stack([np.asarray(r["y"], dtype=np.float32) for r in res.results], axis=0)
```

```python
import math
import numpy as np
import concourse.bass as bass
import concourse.mybir as mybir
from concourse.bass_utils import run_bass_kernel_spmd
from contextlib import ExitStack

F32 = mybir.dt.float32
BF16 = mybir.dt.bfloat16
ALU = mybir.AluOpType
AF = mybir.ActivationFunctionType
AX = mybir.AxisListType

D = 1024
S = 4096
LCTX = 256
NTOK = S + LCTX
NT = NTOK // 128
NLAT = S // 128
FH = 2816
EPS = 1e-6
NV = 1928

V_QA, V_KA, V_QB, V_KB, V_SINK, V_LQ1, V_LK1, V_LQ2, V_LK2, V_SUBLN = 0, 64, 128, 192, 256, 264, 328, 392, 456, 520
V_CVN, V_QAN, V_KVAN, V_QNN, V_KNN, V_QNR, V_KNR = 648, 1160, 1416, 1672, 1736, 1800, 1864


class Buf:
    def __init__(self, name, excl=False):
        self.name = name
        self.w = None
        self.r = {}
        self.excl = excl
        self.parts = {}

    def part(self, key):
        p = self.parts.get(key)
        if p is None:
            p = Buf(f"{self.name}[{key}]", self.excl)
            self.parts[key] = p
        return p


class Sched:
    ENG = ["pe", "act", "dve", "pool", "sp"]
    NRING = 8

    def __init__(self, nc, stack):
        self.nc = nc
        self.prog = {e: [] for e in self.ENG}
        self.cnt = {e: 0 for e in self.ENG}
        self.last = {e: None for e in self.ENG}
        self.pend = {e: False for e in self.ENG}
        self.sem = {e: stack.enter_context(nc.semaphore(f"sem_{e}")) for e in self.ENG}
        self.known = {e: {} for e in self.ENG}
        self.ring = {}
        self.ring_i = {}
        self.ring_val = {}
        for q in ("sp", "pool", "act"):
            self.ring[q] = [stack.enter_context(nc.semaphore(f"dq_{q}{i}")) for i in range(self.NRING)]
            self.ring_i[q] = 0
            self.ring_val[q] = [0] * self.NRING
        self.n_wait = 0
        self.n_ins = 0

    def _need(self, eng, ev, rec_waits):
        sem, val = ev
        if self.known[eng].get(sem, 0) >= val:
            return
        for e in self.ENG:
            if self.sem[e] == sem and val > self.cnt[e]:
                assert self.pend[e] and val == self.cnt[e] + 1, (e, val, self.cnt[e])
                self.last[e]["inc"] = True
                self.cnt[e] += 1
                self.pend[e] = False
        self.known[eng][sem] = val
        rec_waits.append((sem, val))
        self.n_wait += 1

    def _deps(self, eng, key, reads, writes, waits, is_dma):
        for b in reads:
            if b.w is not None:
                self._need(eng, b.w, waits)
            if b.excl:
                for k, ev in list(b.r.items()):
                    if k != key:
                        self._need(eng, ev, waits)
        for b in writes:
            if b.w is not None and not (eng == "pe" and b.w[0] == self.sem["pe"]):
                self._need(eng, b.w, waits)
            for k, ev in list(b.r.items()):
                if k != key or is_dma:
                    self._need(eng, ev, waits)

    def op(self, eng, method, reads=(), writes=(), **kw):
        waits = []
        eager = kw.pop("inc", None)
        if eager is None:
            eager = (eng != "pe") or (method == "matmul" and bool(kw.get("stop")))
        self._deps(eng, eng, reads, writes, waits, False)
        rec = {"m": method, "kw": kw, "waits": waits, "inc": False, "dma": None}
        self.prog[eng].append(rec)
        self.last[eng] = rec
        if eager:
            rec["inc"] = True
            self.cnt[eng] += 1
            self.pend[eng] = False
            ev = (self.sem[eng], self.cnt[eng])
        else:
            self.pend[eng] = True
            ev = (self.sem[eng], self.cnt[eng] + 1)
        for b in reads:
            b.r[eng] = ev
        for b in writes:
            b.w = ev
            b.r = {}
        self.n_ins += 1
        return rec

    def dma(self, q, out, in_, reads=(), writes=(), **kw):
        waits = []
        i = self.ring_i[q]
        slot = i % self.NRING
        self.ring_i[q] += 1
        sem = self.ring[q][slot]
        if self.ring_val[q][slot] > 0:
            self._need(q, (sem, self.ring_val[q][slot]), waits)
        key = (q, slot)
        self._deps(q, key, reads, writes, waits, True)
        self.ring_val[q][slot] += 16
        ev = (sem, self.ring_val[q][slot])
        rec = {"m": "dma_start", "kw": dict(out=out, in_=in_, **kw), "waits": waits, "inc": False, "dma": sem}
        self.prog[q].append(rec)
        for b in reads:
            b.r[key] = ev
        for b in writes:
            b.w = ev
            b.r = {}
        self.n_ins += 1
        return ev

    def barrier(self):
        evs = []
        for e in self.ENG:
            if self.pend[e]:
                self.last[e]["inc"] = True
                self.cnt[e] += 1
                self.pend[e] = False
            if self.cnt[e] > 0:
                evs.append((self.sem[e], self.cnt[e]))
        for q in self.ring:
            for s_, v in zip(self.ring[q], self.ring_val[q]):
                if v > 0:
                    evs.append((s_, v))
        for e in self.ENG:
            waits = []
            for ev in evs:
                if ev[0] == self.sem[e]:
                    continue
                if self.known[e].get(ev[0], 0) < ev[1]:
                    self.known[e][ev[0]] = ev[1]
                    waits.append(ev)
            if waits:
                self.prog[e].append({"m": None, "kw": {}, "waits": waits, "inc": False, "dma": None})

    def finish(self):
        self.barrier()
        nc = self.nc
        engobj = {"pe": "tensor", "act": "scalar", "dve": "vector", "pool": "gpsimd", "sp": "sync"}
        with nc.Block() as block:
            for e in self.ENG:
                prog = self.prog[e]
                sem = self.sem[e]

                def body(eng, prog=prog, sem=sem):
                    for rec in prog:
                        for (s_, v) in rec["waits"]:
                            eng.wait_ge(s_, v)
                        if rec["m"] is None:
                            continue
                        ins = getattr(eng, rec["m"])(**rec["kw"])
                        if rec["dma"] is not None:
                            ins.then_inc(rec["dma"], 16)
                        elif rec["inc"]:
                            ins.then_inc(sem, 1)

                getattr(block, engobj[e])(body)


class T:
    def __init__(self, t, name, excl=False):
        self.t = t
        self.b = Buf(name, excl)

    def __getitem__(self, k):
        return self.t[k]


class Phase:
    _n = 0

    def __init__(self, nc, s):
        self.nc = nc
        self.s = s
        self.st = ExitStack()
        Phase._n += 1
        self.pfx = f"p{Phase._n}_"

    def sb(self, name, shape, dt):
        return T(self.st.enter_context(self.nc.sbuf_tensor(self.pfx + name, list(shape), dt)), name)

    def ps(self, name, shape, dt):
        return T(self.st.enter_context(self.nc.psum_tensor(self.pfx + name, list(shape), dt)), name, True)

    def ring(self, name, shape, dt, n):
        return Ring([self.sb(f"{name}{i}", shape, dt) for i in range(n)])

    def close(self):
        self.s.barrier()
        try:
            print("phase", self.pfx, "sbuf spare KB", self.nc.sbuf_bytes_remaining // 1024 // 128 if self.nc.sbuf_bytes_remaining > 4 * 1024 * 1024 else self.nc.sbuf_bytes_remaining // 1024)
        except Exception as e:
            pass
        self.st.close()


class Ring:
    def __init__(self, items):
        self.items = items
        self.i = 0

    def next(self):
        t = self.items[self.i % len(self.items)]
        self.i += 1
        return t


class DT:
    def __init__(self, ap, name):
        self.ap = ap
        self.b = Buf(name)


def build_program(dbg=(), stop_after=None):
    Phase._n = 0
    nc = bass.Bass("TRN2", target_bir_lowering=False)

    def din(name, shape, dt=F32):
        return DT(nc.dram_tensor(name, list(shape), dt, kind="ExternalInput").ap(), name)

    def dscr(name, shape, dt):
        kind = "ExternalOutput" if name in dbg else "Internal"
        return DT(nc.dram_tensor(name, list(shape), dt, kind=kind).ap(), name)

    xs = din("xs", [NTOK, D])
    cc = din("cc", [128, 8, 2])
    ada_w = din("ada_w", [2, D, 6 * D])
    ada_b = din("ada_b", [2, 1, 6 * D])
    n1g = din("n1g", [2, 1, D])
    n2g = din("n2g", [2, 1, D])
    wmix = din("wmix", [2, D, D])
    wfi = din("wfi", [2, D, 2 * FH])
    wfo = din("wfo", [2, FH, D])
    wev = din("wev", [D, 2304])
    wod = din("wod", [D, 1600])
    wuq = din("wuq", [256, 512])
    wukv = din("wukv", [256, 768])
    wsT = din("wsT", [4, 128, 128])
    bsT = din("bsT", [128, 4])
    vecs = din("vecs", [1, NV])
    ident = din("ident", [128, 128])
    rope = din("rope", [NLAT, 128, 128])
    masks = din("masks", [2, 128, 128])
    y = DT(nc.dram_tensor("y", [S, D], F32, kind="ExternalOutput").ap(), "y")

    modV = dscr("modV", [2, 2, 6 * D], F32)
    QKT0 = dscr("QKT0", [13, 128, NTOK], BF16)
    VA = dscr("VA", [NTOK, 130], BF16)
    VB = dscr("VB", [NTOK, 516], BF16)
    MIX = dscr("MIX", [NTOK, D], BF16)
    H2T = dscr("H2T", [8, 128, NTOK], BF16)
    X1A = dscr("X1A", [NTOK, D], F32)
    X1 = dscr("X1", [NTOK, D], F32)
    QKT1 = dscr("QKT1", [8, 128, NTOK], BF16)
    VD = dscr("VD", [NTOK, 516], BF16)

    with ExitStack() as gst:
        s = Sched(nc, gst)

        gp = Phase(nc, s)
        idf = gp.sb("idf", [128, 128], F32)
        idb = gp.sb("idb", [128, 128], BF16)
        nh = gp.sb("nh", [128, 64], F32)
        vecb = gp.sb("vecb", [128, NV], F32)
        s.dma("sp", idf[:], ident.ap, writes=[idf.b])
        s.op("dve", "tensor_copy", reads=[idf.b], writes=[idb.b], out=idb[:], in_=idf[:])
        s.op("pool", "memset", writes=[nh.b], ap=nh[:], constant=-0.5)
        s.dma("sp", vecb[:], vecs.ap[0, :].partition_broadcast(128), writes=[vecb.b])

        def rstd_from_ss(ph, ssT, n, width, rname):
            v = ph.sb(rname + "_v", [128, n], F32) if not hasattr(ph, "_" + rname) else getattr(ph, "_" + rname)[0]
            r = ph.sb(rname + "_r", [128, n], F32) if not hasattr(ph, "_" + rname) else getattr(ph, "_" + rname)[1]
            setattr(ph, "_" + rname, (v, r))
            s.op("dve", "tensor_scalar", reads=[ssT.b], writes=[v.b], out=v[:], in0=ssT[:, 0:n], scalar1=1.0 / width,
                 scalar2=EPS, op0=ALU.mult, op1=ALU.add)
            s.op("pool", "tensor_tensor", reads=[v.b, nh.b], writes=[r.b], out=r[:], in0=v[:], in1=nh[:, 0:n], op=ALU.pow)
            return r

        def load_bcast(ph, name, src_ap, width, src_b, q="sp"):
            t = ph.sb(name, [128, width], F32)
            s.dma(q, t[:], src_ap.partition_broadcast(128), reads=[src_b], writes=[t.b])
            return t

        def run_pipelined(gens, first_stage=None):
            gens = list(gens)
            active = []
            i = 0
            while i < len(gens) or active:
                new = None
                if i < len(gens):
                    new = [gens[i], 0]
                    i += 1
                    try:
                        next(new[0])
                        new[1] = 1
                    except StopIteration:
                        new = None
                prio = first_stage if isinstance(first_stage, (list, tuple)) else [first_stage]
                order = []
                for st in prio:
                    order += [a for a in active if a[1] + 1 == st]
                order += [a for a in active if a[1] + 1 not in prio]
                for a in order:
                    try:
                        next(a[0])
                        a[1] += 1
                    except StopIteration:
                        active.remove(a)
                if new is not None:
                    active.append(new)

        def norm_mod_transpose(ph, xt, G, Sh, pT, hT_ap, hT_b, rings, split=True):
            junk, ssr, _, hbr = rings
            jk = junk.next()
            ss = ssr.next()
            s.op("act", "activation", reads=[xt.b], writes=[jk.b, ss.b], out=jk[:], in_=xt[:], func=AF.Square,
                 accum_out=ss[:])
            r = rstd_from_ss(ph, ss, 1, D, "rs_x")
            yield
            hb = hbr.next()
            s.op("act", "activation", reads=[xt.b, r.b], writes=[hb.b], out=hb[:], in_=xt[:], func=AF.Copy,
                 scale=r[:, 0:1])
            yield
            for k in range(8):
                s.op("pe", "transpose", reads=[hb.b, idb.b], writes=[pT.b], out=pT[:, k, :],
                     in_=hb[:, k * 128:(k + 1) * 128], identity=idb[:], inc=(k == 7))
            if split:
                yield
            for k in range(8):
                s.op("act", "activation", reads=[pT.b, G.b, Sh.b], writes=[hT_b], out=hT_ap[:, k, :], in_=pT[:, k, :],
                     func=AF.Identity, scale=G[:, k:k + 1], bias=Sh[:, k:k + 1])

        def load_w_bf16(ph, name, src_ap, kchunks, ncols, src_b):
            t = ph.sb(name, [128, kchunks, ncols], BF16)
            view = src_ap.rearrange("(k p) n -> p k n", p=128)
            step = max(1, min(kchunks, 4096 // ncols)) if ncols <= 4096 else 1
            for k0 in range(0, kchunks, step):
                k1 = min(kchunks, k0 + step)
                s.dma("pool", t[:, k0:k1, :], view[:, k0:k1, :], reads=[src_b], writes=[t.b.part(k0)])
            t.kparts = [t.b.part(k0) for k0 in range(0, kchunks, step)]
            return t

        def load_w_bf16_cols(ph, name, src_ap, kchunks, ncols, src_b, cb, order):
            t = ph.sb(name, [128, kchunks, ncols], BF16)
            view = src_ap.rearrange("(k p) n -> p k n", p=128)
            for b in order:
                c0, c1 = b * cb, min(ncols, (b + 1) * cb)
                s.dma("pool", t[:, :, c0:c1], view[:, :, c0:c1], reads=[src_b], writes=[t.b.part(("c", b))])
            t.cpart = lambda col: t.b.part(("c", col // cb))
            return t

        def head_norm(ph, qf, nslots, Gt, tag, wide=False):
            w = nslots * 64
            sq = ph.H_sq.next()
            s.op("act", "activation", reads=[qf.b], writes=[sq.b], out=sq[:, 0:w], in_=qf[:, 0:w], func=AF.Square)
            yield
            ssh = ph.H_ss.next()
            s.op("dve", "tensor_reduce", reads=[sq.b], writes=[ssh.b], out=ssh[:, 0:nslots],
                 in_=sq[:, 0:w].rearrange("p (h d) -> p h d", d=64), axis=AX.X, op=ALU.add)
            r = rstd_from_ss(ph, ssh, nslots, 64, "rs_h" + tag)
            yield
            qn = ph.H_qn.next()
            s.op("dve", "tensor_tensor", reads=[qf.b, r.b], writes=[qn.b],
                 out=qn[:, 0:w].rearrange("p (h d) -> p h d", d=64),
                 in0=qf[:, 0:w].rearrange("p (h d) -> p h d", d=64),
                 in1=r[:, 0:nslots].unsqueeze(2).to_broadcast([128, nslots, 64]), op=ALU.mult)
            if wide:
                yield
            qg = ph.H_qg.next()
            s.op("pool" if wide else "dve", "tensor_tensor", reads=[qn.b, Gt.b], writes=[qg.b], out=qg[:, 0:w],
                 in0=qn[:, 0:w], in1=Gt[:, 0:w], op=ALU.mult)
            return qg

        def rope_apply(ph, src_ap, dst_ap, n, rp, reads, dst_b, t2eng="pool"):
            t1 = ph.R_t1.next()
            t2 = ph.R_t2.next()
            t1v = t1[:, 0:n * 64].rearrange("p (h d) -> p h d", d=64)
            t2v = t2[:, 0:n * 64].rearrange("p (h d) -> p h d", d=64)
            s.op("dve", "tensor_tensor", reads=reads + [rp.b], writes=[t1.b], out=t1v, in0=src_ap,
                 in1=rp[:, 0:64].unsqueeze(1).to_broadcast([128, n, 64]), op=ALU.mult)
            s.op(t2eng, "tensor_tensor", reads=reads + [rp.b], writes=[t2.b.part(0)], out=t2v[:, :, 0:32], in0=src_ap[:, :, 32:64],
                 in1=rp[:, 64:96].unsqueeze(1).to_broadcast([128, n, 32]), op=ALU.mult)
            s.op("dve", "tensor_tensor", reads=reads + [rp.b], writes=[t2.b.part(1)], out=t2v[:, :, 32:64], in0=src_ap[:, :, 0:32],
                 in1=rp[:, 96:128].unsqueeze(1).to_broadcast([128, n, 32]), op=ALU.mult)
            s.op("dve", "tensor_tensor", reads=[t1.b, t2.b.part(0), t2.b.part(1)], writes=[dst_b], out=dst_ap, in0=t1v, in1=t2v,
                 op=ALU.add)

        ph = Phase(nc, s)
        cct = ph.sb("cct", [128, 8, 2], F32)
        sc = ph.sb("sc", [128, 8, 2], F32)
        ones2 = ph.sb("ones2", [1, 2], F32)
        brow = ph.sb("brow", [1, 6 * D], F32)
        mrow = ph.sb("mrow", [2, 6 * D], F32)
        wring = ph.ring("adaw", [128, 8, 512], F32, 2)
        pm = [ph.ps(f"pm{i}", [2, 512], F32) for i in range(2)]
        s.dma("sp", cct[:], cc.ap, writes=[cct.b])
        s.op("act", "activation", reads=[cct.b], writes=[sc.b], out=sc[:], in_=cct[:], func=AF.Silu)
        s.op("dve", "memset", writes=[ones2.b], ap=ones2[:], constant=1.0)
        import os
        NL_ = int(os.environ.get("KD_NL", "2"))
        NC_ = int(os.environ.get("KD_NC", "12"))
        for l in range(NL_):
            s.dma("sp", brow[:], ada_b.ap[l], writes=[brow.b])
            for c in range(NC_):
                wt = wring.next()
                s.dma("sp", wt[:], ada_w.ap[l][:, c * 512:(c + 1) * 512].rearrange("(k p) n -> p k n", p=128),
                      writes=[wt.b])
                p = pm[c % 2]
                for k in range(8):
                    s.op("pe", "matmul", reads=[sc.b, wt.b], writes=[p.b], out=p[:], lhsT=sc[:, k, :], rhs=wt[:, k, :],
                         start=(k == 0), stop=False)
                s.op("pe", "matmul", reads=[ones2.b, brow.b], writes=[p.b], out=p[:], lhsT=ones2[:],
                     rhs=brow[:, c * 512:(c + 1) * 512], start=False, stop=True)
                s.op("dve", "tensor_copy", reads=[p.b], writes=[mrow.b], out=mrow[:, c * 512:(c + 1) * 512], in_=p[:])
            s.dma("sp", modV.ap[l], mrow[:], reads=[mrow.b], writes=[modV.b])
        ph.close()
        if stop_after == "P0":
            s.finish()
            return nc

        def mod_tiles(ph, l, v, idx_shift, idx_scale, gdt, tag):
            def colload(name, ap1d, b):
                t = ph.sb(name, [128, 8], F32)
                s.dma("sp", t[:], ap1d.rearrange("(k p) -> p k", p=128), reads=[b], writes=[t.b],
                      allow_slow_non_contiguous=True)
                return t
            sh = colload(f"sh{tag}", modV.ap[l, v, idx_shift * D:(idx_shift + 1) * D], modV.b)
            scl = colload(f"scl{tag}", modV.ap[l, v, idx_scale * D:(idx_scale + 1) * D], modV.b)
            gg = colload(f"gg{tag}", gdt.ap[l, 0, :], gdt.b)
            s.op("dve", "scalar_tensor_tensor", reads=[scl.b, gg.b], writes=[scl.b], out=scl[:], in0=scl[:], scalar=1.0,
                 in1=gg[:], op0=ALU.add, op1=ALU.mult)
            return scl, sh

        def norm_rings(ph):
            return (ph.ring("junk", [128, D], BF16, 1), ph.ring("ssx", [128, 1], F32, 2),
                    None, ph.ring("hb", [128, D], BF16, 2))

        def head_rings(ph, w):
            ph.H_sq = ph.ring("hsq", [128, w], F32, 2)
            ph.H_ss = ph.ring("hss", [128, 32], F32, 2)
            ph.H_qn = ph.ring("hqn", [128, w], F32, 2)
            ph.H_qg = ph.ring("hqg", [128, w], F32, 2)
            ph.R_t1 = ph.ring("rt1", [128, w], F32, 1)
            ph.R_t2 = ph.ring("rt2", [128, w], F32, 1)

        ph = Phase(nc, s)
        W = load_w_bf16(ph, "wev", wev.ap, 8, 2304, wev.b)
        GL, SL = mod_tiles(ph, 0, 0, 0, 1, n1g, "l")
        GC, SC = mod_tiles(ph, 0, 1, 0, 1, n1g, "c")
        G26 = ph.sb("g26", [128, 26 * 64], F32)
        g26v = G26[:, :].rearrange("p (h d) -> p h d", d=64)
        for (a, b_, off) in ((0, 8, V_QA), (8, 16, V_QB), (16, 18, V_KA), (18, 26, V_KB)):
            s.op("dve", "tensor_copy", reads=[vecb.b], writes=[G26.b], out=g26v[:, a:b_, :],
                 in_=vecb[:, off:off + 64].unsqueeze(1).to_broadcast([128, b_ - a, 64]))
        nr = norm_rings(ph)
        head_rings(ph, 1664)
        xr = ph.ring("x", [128, D], F32, 4)
        rpr = ph.ring("rp", [128, 128], F32, 7)
        hTr = ph.ring("hT", [128, 8, 128], BF16, 2)
        qfr = ph.ring("qf", [128, 1664], F32, 3)
        qbr = ph.ring("qb", [128, 1664], BF16, 2)
        var = ph.ring("va", [128, 2, 65], BF16, 2)
        vbr = ph.ring("vb", [128, 4, 129], BF16, 2)
        for t_ in var.items + vbr.items:
            s.op("pool", "memset", writes=[t_.b], ap=t_[:], constant=1.0)
        qkst = ph.ring("qkst", [128, 13, 256], BF16, 2)
        pT = ph.ps("pT", [128, 8, 128], BF16)
        pO = ph.ps("pO", [128, 2560], F32)
        pQ1 = ph.ps("pQ1", [128, 8, 128], BF16)
        pQ2 = ph.ps("pQ2", [128, 8, 128], BF16)
        def tile_l0p1(t):
            g0 = (t // 2) * 2
            ti = t - g0
            ntile_g = 2
            st_ = qkst.items[(t // 2) % 2]
            isctx = t >= NLAT
            xt = xr.next()
            s.dma("sp", xt[:], xs.ap[t * 128:(t + 1) * 128, :], writes=[xt.b])
            yield
            hT = hTr.next()
            yield from norm_mod_transpose(ph, xt, GC if isctx else GL, SC if isctx else SL, pT, hT[:], hT.b, nr)
            yield
            for k in range(8):
                for c in range(5):
                    n0, n1 = c * 512, min(2304, (c + 1) * 512)
                    s.op("pe", "matmul", reads=[hT.b] + W.kparts, writes=[pO.b], out=pO[:, n0:n1],
                         lhsT=hT[:, k, :], rhs=W[:, k, n0:n1], start=(k == 0), stop=(k == 7))
            if not isctx:
                rp = rpr.next()
                s.dma("sp", rp[:], rope.ap[t], writes=[rp.b])
            yield
            qf = qfr.next()
            s.op("act", "copy", reads=[pO.b], writes=[qf.b], out=qf[:], in_=pO[:, 0:1664])
            va = var.next()
            vb = vbr.next()
            s.op("act", "copy", reads=[pO.b], writes=[va.b], out=va[:, :, 0:64],
                 in_=pO[:, 1664:1792].rearrange("p (h d) -> p h d", d=64))
            s.op("act", "copy", reads=[pO.b], writes=[vb.b], out=vb[:, :, 0:128],
                 in_=pO[:, 1792:2304].rearrange("p (h d) -> p h d", d=128))
            s.dma("sp", VA.ap[t * 128:(t + 1) * 128, :], va[:].rearrange("p h d -> p (h d)"), reads=[va.b],
                  writes=[VA.b.part(t)])
            s.dma("sp", VB.ap[t * 128:(t + 1) * 128, :], vb[:].rearrange("p h d -> p (h d)"), reads=[vb.b],
                  writes=[VB.b.part(t)])
            qg = yield from head_norm(ph, qf, 26, G26, "0", wide=True)
            yield
            qb = qbr.next()
            if isctx:
                s.op("dve", "tensor_copy", reads=[qg.b], writes=[qb.b], out=qb[:], in_=qg[:, 0:1664])
            else:
                rope_apply(ph, qg[:, 0:1664].rearrange("p (h d) -> p h d", d=64),
                           qb[:, :].rearrange("p (h d) -> p h d", d=64), 26, rp, [qg.b], qb.b, t2eng="dve")
            yield
            for j in range(13):
                pq = pQ1 if j < 8 else pQ2
                s.op("pe", "transpose", reads=[qb.b, idb.b], writes=[pq.b], out=pq[:, j % 8, :],
                     in_=qb[:, j * 128:(j + 1) * 128], identity=idb[:], inc=(j in (7, 12)))
            yield
            s.op("act", "copy", reads=[pQ1.b], writes=[st_.b], out=st_[:, 0:8, ti * 128:(ti + 1) * 128], in_=pQ1[:])
            s.op("act", "copy", reads=[pQ2.b], writes=[st_.b], out=st_[:, 8:13, ti * 128:(ti + 1) * 128],
                 in_=pQ2[:, 0:5, :])
            if ti == ntile_g - 1:
                ntk = ntile_g * 128
                s.dma("sp", QKT0.ap[:, :, g0 * 128:g0 * 128 + ntk].rearrange("j p t -> p j t"), st_[:, :, 0:ntk],
                      reads=[st_.b], writes=[QKT0.b])

        run_pipelined((tile_l0p1(t) for t in range(NT)), first_stage=7)
        ph.close()
        if stop_after == "L0P1":
            s.finish()
            return nc

        def attn_pipeline(ph, units, nkmax):
            PT = [ph.sb(f"PT{i}", [128, nkmax, 512], BF16) for i in range(2)]
            Sb = [ph.ps(f"S{i}", [128, 512], F32) for i in range(4)]
            acc = ph.ps("acc", [128, 4, 512], F32)
            si = 0
            n = len(units)
            for ui in range(n + 1):
                cur = units[ui] if ui < n else None
                prev = units[ui - 1] if ui > 0 else None
                if cur is not None and cur.get("pre") is not None:
                    cur["pre"]()
                nk = max(len(cur["keys"]) if cur else 0, len(prev["keys"]) if prev else 0)
                for kt in range(nk):
                    if cur is not None and kt < len(cur["keys"]):
                        key = cur["keys"][kt]
                        nq = cur["nq"]
                        sbk = Sb[si % 4]
                        si += 1
                        so = sbk[:, 0:nq]
                        if len(cur["qT"].shape) == 3:
                            so = so.rearrange("p (g q) -> p g q", q=128)
                        s.op("pe", "matmul", reads=cur["qb"] + key["kb"], writes=[sbk.b], out=so, lhsT=key["kT"],
                             rhs=cur["qT"], start=True, stop=True)
                        pt = PT[ui % 2]
                        ptb = pt.b.part(kt)
                        s.op("act", "activation", reads=[sbk.b], writes=[ptb], out=pt[:, kt, 0:nq], in_=sbk[:, 0:nq],
                             func=AF.Exp, scale=cur["scale"])
                        if key.get("mask") is not None:
                            mk = key["mask"]
                            s.op("dve", "tensor_tensor", reads=[ptb, mk.b], writes=[ptb],
                                 out=pt[:, kt, 0:nq].rearrange("p (g q) -> p g q", q=128),
                                 in0=pt[:, kt, 0:nq].rearrange("p (g q) -> p g q", q=128),
                                 in1=mk[:, :].unsqueeze(1).to_broadcast([128, nq // 128, 128]), op=ALU.mult)
                    if prev is not None and kt < len(prev["keys"]):
                        key = prev["keys"][kt]
                        pt = PT[(ui - 1) % 2]
                        ptb = pt.b.part(kt)
                        nkp = len(prev["keys"])
                        vd1 = prev["vd1"]
                        for j in range(prev["nq"] // 128):
                            s.op("pe", "matmul", reads=[ptb] + key["vb"], writes=[acc.b], out=acc[:, j, 0:vd1],
                                 lhsT=pt[:, kt, j * 128:(j + 1) * 128], rhs=key["v"], start=(kt == 0), stop=(kt == nkp - 1))
                if prev is not None:
                    prev["fin"](prev, acc)

        mprev_f = gp.sb("mprevf", [128, 128], F32)
        mnext_f = gp.sb("mnextf", [128, 128], F32)
        mprev = gp.sb("mprev", [128, 128], BF16)
        mnext = gp.sb("mnext", [128, 128], BF16)
        s.dma("sp", mprev_f[:], masks.ap[0], writes=[mprev_f.b])
        s.dma("sp", mnext_f[:], masks.ap[1], writes=[mnext_f.b])
        s.op("dve", "tensor_copy", reads=[mprev_f.b], writes=[mprev.b], out=mprev[:], in_=mprev_f[:])
        s.op("dve", "tensor_copy", reads=[mnext_f.b], writes=[mnext.b], out=mnext[:], in_=mnext_f[:])

        ph = Phase(nc, s)
        QA = ph.sb("QA", [128, 4, NTOK], BF16)
        KAz = [ph.sb(f"KAz{i}", [128, NTOK], BF16) for i in range(2)]
        s.op("pool", "memset", writes=[KAz[0].b], ap=KAz[0][64:128, :], constant=0.0)
        s.op("pool", "memset", writes=[KAz[1].b], ap=KAz[1][0:64, :], constant=0.0)
        VAs = ph.sb("VAs", [128, NT, 130], BF16)
        s.dma("sp", QA[:], QKT0.ap[0:4].rearrange("j p t -> p j t"), reads=[QKT0.b], writes=[QA.b])
        s.dma("sp", KAz[0][0:64, :], QKT0.ap[8, 0:64, :], reads=[QKT0.b], writes=[KAz[0].b])
        s.dma("sp", KAz[1][64:128, :], QKT0.ap[8, 64:128, :], reads=[QKT0.b], writes=[KAz[1].b])
        s.dma("sp", VAs[:], VA.ap.rearrange("(k p) e -> p k e", p=128), reads=[VA.b.part(t) for t in range(NT)],
              writes=[VAs.b])
        esink = ph.sb("esink", [128, 8], F32)
        s.op("act", "activation", reads=[vecb.b], writes=[esink.b], out=esink[:], in_=vecb[:, V_SINK:V_SINK + 8], func=AF.Exp)
        zr = ph.ring("za", [128, 4], F32, 2)
        rzr = ph.ring("rza", [128, 4], F32, 2)
        obr = ph.ring("oba", [128, 4, 64], BF16, 3)

        def fin_a(u, acc):
            kvh, n = u["kvh"], u["n"]
            z = zr.next()
            s.op("dve", "tensor_tensor", reads=[acc.b, esink.b], writes=[z.b], out=z[:], in0=acc[:, :, 64],
                 in1=esink[:, kvh * 4:(kvh + 1) * 4], op=ALU.add)
            rz = rzr.next()
            s.op("dve", "reciprocal", reads=[z.b], writes=[rz.b], out=rz[:], in_=z[:])
            ob = obr.next()
            s.op("dve", "tensor_tensor", reads=[acc.b, rz.b], writes=[ob.b], out=ob[:], in0=acc[:, :, 0:64],
                 in1=rz[:, :].unsqueeze(2).to_broadcast([128, 4, 64]), op=ALU.mult)
            s.dma("sp", MIX.ap[n * 128:(n + 1) * 128, kvh * 256:(kvh + 1) * 256], ob[:].rearrange("p g d -> p (g d)"),
                  reads=[ob.b], writes=[MIX.b.part(("a", n, kvh))])

        units = []
        for n in range(NT):
            if n < NLAT:
                kl = []
                if n > 0:
                    kl.append((n - 1, mprev))
                kl.append((n, None))
                if n < NLAT - 1:
                    kl.append((n + 1, mnext))
                kl += [(32, None), (33, None)]
            else:
                kl = [(32, None), (33, None)]
            for kvh in range(2):
                keys = [dict(kT=KAz[kvh][:, kt * 128:(kt + 1) * 128], kb=[KAz[kvh].b], v=VAs[:, kt, kvh * 65:(kvh + 1) * 65],
                             vb=[VAs.b], mask=mk) for (kt, mk) in kl]
                units.append(dict(qT=QA[:, :, n * 128:(n + 1) * 128], qb=[QA.b], keys=keys, nq=512, vd1=65,
                                  scale=0.125, fin=fin_a, kvh=kvh, n=n))
        attn_pipeline(ph, units, 5)
        ph.close()
        if stop_after == "L0P2A":
            s.finish()
            return nc

        lam_init0 = 0.8 - 0.6 * math.exp(-0.3 * 0)
        ph = Phase(nc, s)
        lt = ph.sb("lt", [128, 128], F32)
        lsum = ph.sb("lsum", [128, 2], F32)
        lexp = ph.sb("lexp", [128, 2], F32)
        nlam = ph.sb("nlam", [128, 1], F32)
        s.op("dve", "tensor_tensor", reads=[vecb.b], writes=[lt.b], out=lt[:, 0:64], in0=vecb[:, V_LQ1:V_LQ1 + 64],
             in1=vecb[:, V_LK1:V_LK1 + 64], op=ALU.mult)
        s.op("dve", "tensor_tensor", reads=[vecb.b], writes=[lt.b], out=lt[:, 64:128], in0=vecb[:, V_LQ2:V_LQ2 + 64],
             in1=vecb[:, V_LK2:V_LK2 + 64], op=ALU.mult)
        s.op("dve", "tensor_reduce", reads=[lt.b], writes=[lsum.b], out=lsum[:],
             in_=lt[:, :].rearrange("p (a d) -> p a d", d=64), axis=AX.X, op=ALU.add)
        s.op("act", "activation", reads=[lsum.b], writes=[lexp.b], out=lexp[:], in_=lsum[:], func=AF.Exp)
        s.op("dve", "tensor_tensor", reads=[lexp.b], writes=[nlam.b], out=nlam[:], in0=lexp[:, 1:2], in1=lexp[:, 0:1],
             op=ALU.subtract)
        s.op("dve", "tensor_scalar", reads=[nlam.b], writes=[nlam.b], out=nlam[:], in0=nlam[:], scalar1=-lam_init0,
             scalar2=None, op0=ALU.add)
        subl = ph.sb("subl", [128, 128], F32)
        s.op("dve", "tensor_scalar", reads=[vecb.b], writes=[subl.b], out=subl[:], in0=vecb[:, V_SUBLN:V_SUBLN + 128],
             scalar1=1.0 - lam_init0, scalar2=None, op0=ALU.mult)
        QBr = ph.ring("QB", [128, NTOK], BF16, 2)
        KBz = [[ph.sb(f"KBz{i}_{j}", [128, NTOK], BF16) for j in range(2)] for i in range(2)]
        for i in range(2):
            s.op("pool", "memset", writes=[KBz[i][0].b], ap=KBz[i][0][64:128, :], constant=0.0)
            s.op("pool", "memset", writes=[KBz[i][1].b], ap=KBz[i][1][0:64, :], constant=0.0)
        VBr = ph.ring("VBs", [128, NT, 129], BF16, 2)
        o1r = ph.ring("o1", [128, 4, 128], F32, 2)
        z1r = ph.ring("zb", [128, 4], F32, 4)
        tbr = ph.ring("tb", [128, 4, 128], F32, 2)
        obbr = ph.ring("obb", [128, 4, 128], F32, 2)
        sqbr = ph.ring("sqb", [128, 4, 128], F32, 1)
        ssbr = ph.ring("ssb", [128, 4], F32, 2)
        onr = ph.ring("onb", [128, 4, 128], F32, 1)
        outbr = ph.ring("outb", [128, 4, 128], BF16, 3)
        state = {}

        def fin_b(u, acc):
            h, q0, nj, sidx = u["h"], u["q0"], u["nq"] // 128, u["s"]
            rz = z1r.next()
            s.op("dve", "reciprocal", reads=[acc.b], writes=[rz.b], out=rz[:, 0:nj], in_=acc[:, 0:nj, 128])
            if sidx == 0:
                o1 = o1r.next()
                s.op("dve", "tensor_tensor", reads=[acc.b, rz.b], writes=[o1.b], out=o1[:, 0:nj, :], in0=acc[:, 0:nj, 0:128],
                     in1=rz[:, 0:nj].unsqueeze(2).to_broadcast([128, nj, 128]), op=ALU.mult)
                state["o1"] = o1
                return
            o1 = state["o1"]
            rzl = z1r.next()
            s.op("dve", "tensor_scalar", reads=[rz.b, nlam.b], writes=[rzl.b], out=rzl[:, 0:nj], in0=rz[:, 0:nj],
                 scalar1=nlam[:, 0:1], scalar2=None, op0=ALU.mult)
            tb = tbr.next()
            s.op("dve", "tensor_tensor", reads=[acc.b, rzl.b], writes=[tb.b], out=tb[:, 0:nj, :], in0=acc[:, 0:nj, 0:128],
                 in1=rzl[:, 0:nj].unsqueeze(2).to_broadcast([128, nj, 128]), op=ALU.mult)
            ob = obbr.next()
            s.op("pool", "tensor_tensor", reads=[tb.b, o1.b], writes=[ob.b], out=ob[:, 0:nj, :], in0=tb[:, 0:nj, :],
                 in1=o1[:, 0:nj, :], op=ALU.add)
            sq = sqbr.next()
            s.op("pool", "tensor_tensor", reads=[ob.b], writes=[sq.b], out=sq[:, 0:nj, :], in0=ob[:, 0:nj, :],
                 in1=ob[:, 0:nj, :], op=ALU.mult)
            ss = ssbr.next()
            s.op("dve", "tensor_reduce", reads=[sq.b], writes=[ss.b], out=ss[:, 0:nj], in_=sq[:, 0:nj, :], axis=AX.X,
                 op=ALU.add)
            r = rstd_from_ss(ph, ss, nj, 128, "rs_b%d" % nj)
            on = onr.next()
            s.op("dve", "tensor_tensor", reads=[ob.b, r.b], writes=[on.b], out=on[:, 0:nj, :], in0=ob[:, 0:nj, :],
                 in1=r[:, 0:nj].unsqueeze(2).to_broadcast([128, nj, 128]), op=ALU.mult)
            out = outbr.next()
            s.op("pool", "tensor_tensor", reads=[on.b, subl.b], writes=[out.b], out=out[:, 0:nj, :], in0=on[:, 0:nj, :],
                 in1=subl[:, :].unsqueeze(1).to_broadcast([128, nj, 128]), op=ALU.mult)
            s.dma("sp", MIX.ap[q0:q0 + nj * 128, 512 + h * 128:512 + (h + 1) * 128].rearrange("(j p) d -> p j d", p=128),
                  out[:, 0:nj, :], reads=[out.b], writes=[MIX.b.part(("b", h, q0))])

        def load_b(h):
            Qh, Kz, Vh = QBr.items[h % 2], KBz[h % 2], VBr.items[h % 2]
            s.dma("sp", Qh[:], QKT0.ap[4 + h], reads=[QKT0.b], writes=[Qh.b])
            s.dma("sp", Kz[0][0:64, :], QKT0.ap[9 + h, 0:64, :], reads=[QKT0.b], writes=[Kz[0].b])
            s.dma("sp", Kz[1][64:128, :], QKT0.ap[9 + h, 64:128, :], reads=[QKT0.b], writes=[Kz[1].b])
            s.dma("sp", Vh[:], VB.ap.rearrange("(k p) (h e) -> p k h e", p=128, e=129)[:, :, h, :],
                  reads=[VB.b.part(t) for t in range(NT)], writes=[Vh.b])

        units = []
        for h in range(4):
            Qh, Kz, Vh = QBr.items[h % 2], KBz[h % 2], VBr.items[h % 2]
            blocks = [(qb * 512, 512, list(range(NT))) for qb in range(8)] + [(S, 256, [32, 33])]
            first = len(units)
            for (q0, nq, kl) in blocks:
                for sidx in range(2):
                    Kh = Kz[sidx]
                    keys = [dict(kT=Kh[:, kt * 128:(kt + 1) * 128], kb=[Kh.b], v=Vh[:, kt, :], vb=[Vh.b], mask=None)
                            for kt in kl]
                    units.append(dict(qT=Qh[:, q0:q0 + nq], qb=[Qh.b], keys=keys, nq=nq, vd1=129, scale=0.125,
                                      fin=fin_b, h=h, q0=q0, s=sidx))
            if h == 0:
                units[first]["pre"] = (lambda: load_b(0))
            if h < 3:
                units[first + 1]["pre"] = (lambda hh=h + 1: load_b(hh))
        attn_pipeline(ph, units, NT)
        ph.close()
        if stop_after == "L0P2B":
            s.finish()
            return nc

        def phase3a(l, Xin, Xout, ntiles):
            ph = Phase(nc, s)
            Wm = load_w_bf16(ph, "wmix", wmix.ap[l], 8, D, wmix.b)
            G2L, S2L = mod_tiles(ph, l, 0, 3, 4, n2g, "l")
            gateL = load_bcast(ph, "gateL", modV.ap[l, 0, 2 * D:3 * D], D, modV.b)
            if ntiles > NLAT:
                G2C, S2C = mod_tiles(ph, l, 1, 3, 4, n2g, "c")
                gateC = load_bcast(ph, "gateC", modV.ap[l, 1, 2 * D:3 * D], D, modV.b)
            nr = norm_rings(ph)
            xr = ph.ring("x", [128, D], F32, 5)
            mr = ph.ring("mx", [128, D], BF16, 3)
            mTr = ph.ring("mT", [128, 8, 128], BF16, 2)
            tmr = ph.ring("tm", [128, D], F32, 1)
            x1r = ph.ring("x1", [128, D], F32, 4)
            hst = ph.ring("hst", [128, 8, 512], BF16, 2)
            pT = ph.ps("pT", [128, 8, 128], BF16)
            pT2 = ph.ps("pT2", [128, 8, 128], BF16)
            pP = ph.ps("pP", [128, D], F32)
            def tile_p3a(t):
                g0 = (t // 4) * 4
                ti = t - g0
                ntile_g = min(ntiles, g0 + 4) - g0
                st_ = hst.items[(t // 4) % 2]
                isctx = t >= NLAT
                xt = xr.next()
                s.dma("sp", xt[:], Xin.ap[t * 128:(t + 1) * 128, :], reads=[Xin.b.part(t)], writes=[xt.b])
                mx = mr.next()
                s.dma("sp", mx[:], MIX.ap[t * 128:(t + 1) * 128, :], reads=[MIX.b] + list(MIX.b.parts.values()),
                      writes=[mx.b])
                yield
                for k in range(8):
                    s.op("pe", "transpose", reads=[mx.b, idb.b], writes=[pT.b], out=pT[:, k, :],
                         in_=mx[:, k * 128:(k + 1) * 128], identity=idb[:], inc=(k == 7))
                yield
                mT = mTr.next()
                s.op("act", "copy", reads=[pT.b], writes=[mT.b], out=mT[:], in_=pT[:])
                yield
                for k in range(8):
                    for c in range(2):
                        s.op("pe", "matmul", reads=[mT.b] + Wm.kparts, writes=[pP.b], out=pP[:, c * 512:(c + 1) * 512],
                             lhsT=mT[:, k, :], rhs=Wm[:, k, c * 512:(c + 1) * 512], start=(k == 0), stop=(k == 7))
                yield
                tm = tmr.next()
                gate = gateC if isctx else gateL
                s.op("dve", "tensor_tensor", reads=[pP.b, gate.b], writes=[tm.b], out=tm[:], in0=pP[:], in1=gate[:],
                     op=ALU.mult)
                x1 = x1r.next()
                s.op("pool", "tensor_tensor", reads=[tm.b, xt.b], writes=[x1.b], out=x1[:], in0=tm[:], in1=xt[:],
                     op=ALU.add)
                s.dma("sp", Xout.ap[t * 128:(t + 1) * 128, :], x1[:], reads=[x1.b], writes=[Xout.b.part(t)])
                yield
                yield from norm_mod_transpose(ph, x1, G2C if isctx else G2L, S2C if isctx else S2L, pT2,
                                              st_[:, :, ti * 128:(ti + 1) * 128], st_.b, nr)
                if ti == ntile_g - 1:
                    ntk = ntile_g * 128
                    s.dma("sp", H2T.ap[:, :, g0 * 128:g0 * 128 + ntk].rearrange("j p t -> p j t"), st_[:, :, 0:ntk],
                          reads=[st_.b], writes=[H2T.b.part(g0)])

            run_pipelined((tile_p3a(t) for t in range(ntiles)), first_stage=5)
            ph.close()

        def phase3b(l, Xin, Xout, ntiles):
            ph = Phase(nc, s)
            Wi = load_w_bf16_cols(ph, "wfi", wfi.ap[l], 8, 2 * FH, wfi.b, 512, [0, 5, 6, 1, 7, 2, 8, 3, 9, 4, 10])
            Wo = load_w_bf16(ph, "wfo", wfo.ap[l], 22, D, wfo.b)
            gate = load_bcast(ph, "gate", modV.ap[l, 0, 5 * D:6 * D], D, modV.b)
            h2r = ph.ring("h2", [128, 8, 512], BF16, 1)
            actT = ph.sb("actT", [128, 22, 512], BF16)
            sgr = ph.ring("sg", [128, 512], F32, 2)
            xr = ph.ring("x", [128, D], F32, 1)
            x2r = ph.ring("x2", [128, D], F32, 2)
            pG = [ph.ps(f"pG{i}", [128, 512], F32) for i in range(2)]
            pU = [ph.ps(f"pU{i}", [128, 512], F32) for i in range(2)]
            pY = [ph.ps(f"pY{i}", [128, D], F32) for i in range(2)]
            yi = 0
            for g0 in range(0, ntiles, 4):
                tiles = list(range(g0, min(ntiles, g0 + 4)))
                ntk = len(tiles) * 128
                if tiles[0] >= NLAT:
                    s.dma("sp", gate[:], modV.ap[l, 1, 5 * D:6 * D].partition_broadcast(128), reads=[modV.b],
                          writes=[gate.b])
                h2 = h2r.next()
                if g0 == 0:
                    s.dma("sp", h2[:, :, 0:ntk], H2T.ap[:, :, 0:ntk].rearrange("j p t -> p j t"),
                          reads=[H2T.b.part(0)], writes=[h2.b])
                for j in range(22):
                    pg, pu = pG[j % 2], pU[j % 2]
                    for k in range(8):
                        s.op("pe", "matmul", reads=[h2.b, Wi.cpart(j * 128)], writes=[pg.b], out=pg[:, 0:ntk],
                             lhsT=Wi[:, k, j * 128:(j + 1) * 128], rhs=h2[:, k, 0:ntk], start=(k == 0), stop=(k == 7))
                    for k in range(8):
                        s.op("pe", "matmul", reads=[h2.b, Wi.cpart(FH + j * 128)], writes=[pu.b], out=pu[:, 0:ntk],
                             lhsT=Wi[:, k, FH + j * 128:FH + (j + 1) * 128], rhs=h2[:, k, 0:ntk], start=(k == 0),
                             stop=(k == 7))
                    sg = sgr.next()
                    s.op("act", "activation", reads=[pg.b], writes=[sg.b], out=sg[:, 0:ntk], in_=pg[:, 0:ntk], func=AF.Silu)
                    s.op("dve", "tensor_tensor", reads=[pu.b, sg.b], writes=[actT.b.part(j)], out=actT[:, j, 0:ntk],
                         in0=pu[:, 0:ntk], in1=sg[:, 0:ntk], op=ALU.mult)
                if g0 + 4 < ntiles:
                    n0_ = g0 + 4
                    ntk2 = (min(ntiles, n0_ + 4) - n0_) * 128
                    s.dma("act", h2[:, :, 0:ntk2], H2T.ap[:, :, n0_ * 128:n0_ * 128 + ntk2].rearrange("j p t -> p j t"),
                          reads=[H2T.b.part(n0_)], writes=[h2.b])
                for ti, t in enumerate(tiles):
                    isctx = t >= NLAT
                    py = pY[yi % 2]
                    yi += 1
                    for c in range(2):
                        for j in range(22):
                            s.op("pe", "matmul", reads=[actT.b.part(j)] + Wo.kparts, writes=[py.b],
                                 out=py[:, c * 512:(c + 1) * 512], lhsT=actT[:, j, ti * 128:(ti + 1) * 128],
                                 rhs=Wo[:, j, c * 512:(c + 1) * 512], start=(j == 0), stop=(j == 21))
                    xt = xr.next()
                    s.dma("sp", xt[:], Xin.ap[t * 128:(t + 1) * 128, :], reads=[Xin.b.part(t)], writes=[xt.b])
                    x2 = x2r.next()
                    s.op("dve", "tensor_tensor", reads=[py.b, gate.b], writes=[x2.b], out=x2[:], in0=py[:], in1=gate[:],
                         op=ALU.mult)
                    s.op("pool", "tensor_tensor", reads=[x2.b, xt.b], writes=[x2.b], out=x2[:], in0=x2[:], in1=xt[:],
                         op=ALU.add)
                    s.dma("sp", Xout.ap[t * 128:(t + 1) * 128, :], x2[:], reads=[x2.b], writes=[Xout.b.part(t)])
            ph.close()

        phase3a(0, xs, X1A, NT)
        if stop_after == "L0P3A":
            s.finish()
            return nc
        phase3b(0, X1A, X1, NT)
        if stop_after == "L0":
            s.finish()
            return nc

        ph = Phase(nc, s)
        W1 = load_w_bf16(ph, "wod", wod.ap, 8, 1600, wod.b)
        Wq = load_w_bf16(ph, "wuq", wuq.ap, 2, 512, wuq.b)
        Wkv = load_w_bf16(ph, "wukv", wukv.ap, 2, 768, wukv.b)
        Ws = ph.sb("ws", [128, 4, 128], BF16)
        s.dma("pool", Ws[:], wsT.ap.rearrange("g q p -> q g p"), reads=[wsT.b], writes=[Ws.b])
        bs = ph.sb("bs", [128, 4], F32)
        s.dma("sp", bs[:], bsT.ap, writes=[bs.b])
        GL, SL = mod_tiles(ph, 1, 0, 0, 1, n1g, "l")
        GC, SC = mod_tiles(ph, 1, 1, 0, 1, n1g, "c")
        G13 = ph.sb("g13", [128, 13 * 64], F32)
        g13v = G13[:, :].rearrange("p (h d) -> p h d", d=64)
        g13q = G13[:, 0:512].rearrange("p (h t d) -> p h t d", t=2, d=64)
        s.op("dve", "tensor_copy", reads=[vecb.b], writes=[G13.b], out=g13q[:, :, 0, :],
             in_=vecb[:, V_QNN:V_QNN + 64].unsqueeze(1).to_broadcast([128, 4, 64]))
        s.op("dve", "tensor_copy", reads=[vecb.b], writes=[G13.b], out=g13q[:, :, 1, :],
             in_=vecb[:, V_QNR:V_QNR + 64].unsqueeze(1).to_broadcast([128, 4, 64]))
        s.op("dve", "tensor_copy", reads=[vecb.b], writes=[G13.b], out=g13v[:, 8:12, :],
             in_=vecb[:, V_KNN:V_KNN + 64].unsqueeze(1).to_broadcast([128, 4, 64]))
        s.op("dve", "tensor_copy", reads=[vecb.b], writes=[G13.b], out=g13v[:, 12:13, :],
             in_=vecb[:, V_KNR:V_KNR + 64].unsqueeze(1).to_broadcast([128, 1, 64]))
        nr = norm_rings(ph)
        head_rings(ph, 832)
        xr = ph.ring("x", [128, D], F32, 4)
        rpr = ph.ring("rp", [128, 128], F32, 11)
        hTr = ph.ring("hT", [128, 8, 128], BF16, 2)
        glr = ph.ring("gl", [128, D], F32, 4)
        vsqr = ph.ring("vsq", [128, 512], F32, 1)
        ssvr = ph.ring("ssv", [128, 2], F32, 2)
        vnr = ph.ring("vn", [128, 512], BF16, 2)
        mcr = ph.ring("mc", [128, 512], BF16, 2)
        cqfr = ph.ring("cqf", [128, 512], F32, 3)
        cnr = ph.ring("cn", [128, 512], F32, 1)
        cnbr = ph.ring("cnb", [128, 512], BF16, 2)
        cTr = ph.ring("cT", [128, 4, 128], BF16, 2)
        hqr = ph.ring("hq", [128, 832], F32, 3)
        qcbr = ph.ring("qcb", [128, 4, 128], BF16, 2)
        kcbr = ph.ring("kcb", [128, 4, 128], BF16, 2)
        kper = ph.ring("kpe", [128, 64], F32, 7)
        kprr = ph.ring("kpr", [128, 64], F32, 2)
        vdr = ph.ring("vd", [128, 4, 129], BF16, 2)
        for t_ in vdr.items:
            s.op("pool", "memset", writes=[t_.b], ap=t_[:], constant=1.0)
        qkst = ph.ring("qkst", [128, 8, 512], BF16, 2)
        pT = ph.ps("pT", [128, 8, 128], BF16)
        pO1 = ph.ps("pO1", [128, 2048], F32)
        pQ = ph.ps("pQ", [128, 512], F32)
        pKV = ph.ps("pKV", [128, 1024], F32)
        def cps(g):
            return pO1[:, 1600 + g * 128:1728 + g * 128] if g < 3 else pKV[:, 768:896]

        def cpb(g):
            return pO1.b if g < 3 else pKV.b

        ss3r = ph.ring("ss3", [128, 4], F32, 2)
        v3r = ph.ring("v3", [128, 4], F32, 2)
        r3r = ph.ring("r3", [128, 4], F32, 2)
        jk2r = ph.ring("jk2", [128, 512], BF16, 3)
        for t_ in ss3r.items:
            s.op("pool", "memset", writes=[t_.b], ap=t_[:], constant=1.0)

        def tile_l1p1(t):
            g0 = (t // 4) * 4
            ti = t - g0
            ntile_g = min(NT, g0 + 4) - g0
            st_ = qkst.items[(t // 4) % 2]
            isctx = t >= NLAT
            xt = xr.next()
            s.dma("sp", xt[:], X1.ap[t * 128:(t + 1) * 128, :], reads=[X1.b.part(t)], writes=[xt.b])
            yield
            hT = hTr.next()
            yield from norm_mod_transpose(ph, xt, GC if isctx else GL, SC if isctx else SL, pT, hT[:], hT.b, nr, split=True)
            yield
            chunks = [(1280, 1536), (1536, 1600)] if isctx else [(0, 512), (512, 1024), (1024, 1536), (1536, 1600)]
            for k in range(8):
                for (n0, n1) in chunks:
                    s.op("pe", "matmul", reads=[hT.b] + W1.kparts, writes=[pO1.b], out=pO1[:, n0:n1],
                         lhsT=hT[:, k, :], rhs=W1[:, k, n0:n1], start=(k == 0), stop=(k == 7))
            if not isctx:
                rp = rpr.next()
                s.dma("sp", rp[:], rope.ap[t], writes=[rp.b])
            yield
            cqf = cqfr.next()
            c0 = 256 if isctx else 0
            s.op("act", "copy", reads=[pO1.b], writes=[cqf.b], out=cqf[:, c0:512], in_=pO1[:, 1024 + c0:1536])
            kpe = kper.next()
            s.op("act", "copy", reads=[pO1.b], writes=[kpe.b], out=kpe[:], in_=pO1[:, 1536:1600])
            ss3 = ss3r.next()
            if not isctx:
                gl = glr.next()
                s.op("act", "activation", reads=[pO1.b], writes=[gl.b], out=gl[:], in_=pO1[:, 0:1024], func=AF.Gelu)
                jk = jk2r.next()
                s.op("act", "activation", reads=[gl.b], writes=[jk.b, ss3.b], out=jk[:], in_=gl[:, 512:1024], func=AF.Square,
                     accum_out=ss3[:, 0:1])
                jk = jk2r.next()
                s.op("act", "activation", reads=[cqf.b], writes=[jk.b, ss3.b], out=jk[:, 0:256], in_=cqf[:, 0:256],
                     func=AF.Square, accum_out=ss3[:, 1:2])
            jk = jk2r.next()
            s.op("act", "activation", reads=[cqf.b], writes=[jk.b, ss3.b], out=jk[:, 0:256], in_=cqf[:, 256:512],
                 func=AF.Square, accum_out=ss3[:, 2:3])
            yield
            v3 = v3r.next()
            s.op("dve", "tensor_scalar", reads=[ss3.b], writes=[v3.b], out=v3[:, 0:1], in0=ss3[:, 0:1], scalar1=1.0 / 512,
                 scalar2=EPS, op0=ALU.mult, op1=ALU.add)
            s.op("dve", "tensor_scalar", reads=[ss3.b], writes=[v3.b], out=v3[:, 1:3], in0=ss3[:, 1:3], scalar1=1.0 / 256,
                 scalar2=EPS, op0=ALU.mult, op1=ALU.add)
            r3 = r3r.next()
            s.op("pool", "tensor_tensor", reads=[v3.b, nh.b], writes=[r3.b], out=r3[:, 0:3], in0=v3[:, 0:3], in1=nh[:, 0:3],
                 op=ALU.pow)
            yield
            cnb = cnbr.next()
            if not isctx:
                vn = vnr.next()
                s.op("dve", "scalar_tensor_tensor", reads=[gl.b, r3.b, vecb.b], writes=[vn.b], out=vn[:],
                     in0=gl[:, 512:1024], scalar=r3[:, 0:1], in1=vecb[:, V_CVN:V_CVN + 512], op0=ALU.mult, op1=ALU.mult)
                s.op("dve", "scalar_tensor_tensor", reads=[cqf.b, r3.b, vecb.b], writes=[cnb.b], out=cnb[:, 0:256],
                     in0=cqf[:, 0:256], scalar=r3[:, 1:2], in1=vecb[:, V_QAN:V_QAN + 256], op0=ALU.mult, op1=ALU.mult)
            s.op("dve", "scalar_tensor_tensor", reads=[cqf.b, r3.b, vecb.b], writes=[cnb.b], out=cnb[:, 256:512],
                 in0=cqf[:, 256:512], scalar=r3[:, 2:3], in1=vecb[:, V_KVAN:V_KVAN + 256], op0=ALU.mult, op1=ALU.mult)
            yield
            if not isctx:
                for g in range(4):
                    s.op("pe", "matmul", reads=[vn.b, Ws.b], writes=[cpb(g)], out=cps(g),
                         lhsT=Ws[:, g, :], rhs=vn[:, g * 128:(g + 1) * 128], start=True, stop=True)
            kc0 = c0 // 128
            for k in range(kc0, 4):
                s.op("pe", "transpose", reads=[cnb.b, idb.b], writes=[pT.b], out=pT[:, k, :],
                     in_=cnb[:, k * 128:(k + 1) * 128], identity=idb[:], inc=(k == 3))
            cT = cTr.next()
            s.op("act", "copy", reads=[pT.b], writes=[cT.b], out=cT[:, kc0:4, :], in_=pT[:, kc0:4, :])
            if not isctx:
                mc_ = mcr.next()
                for g in range(4):
                    s.op("dve", "scalar_tensor_tensor", reads=[cpb(g), bs.b, gl.b], writes=[mc_.b],
                         out=mc_[:, g * 128:(g + 1) * 128], in0=cps(g), scalar=bs[:, g:g + 1],
                         in1=gl[:, g * 128:(g + 1) * 128], op0=ALU.add, op1=ALU.mult)
                s.dma("sp", MIX.ap[t * 128:(t + 1) * 128, 0:512], mc_[:], reads=[mc_.b], writes=[MIX.b.part(("c", t))])
            yield
            if not isctx:
                for k in range(2):
                    s.op("pe", "matmul", reads=[cT.b] + Wq.kparts, writes=[pQ.b], out=pQ[:], lhsT=cT[:, k, :],
                         rhs=Wq[:, k, :], start=(k == 0), stop=(k == 1))
            for k in range(2):
                for (n0, n1) in ((0, 512), (512, 768)):
                    s.op("pe", "matmul", reads=[cT.b] + Wkv.kparts, writes=[pKV.b], out=pKV[:, n0:n1],
                         lhsT=cT[:, 2 + k, :], rhs=Wkv[:, k, n0:n1], start=(k == 0), stop=(k == 1))
            yield
            vd = vdr.next()
            s.op("act", "copy", reads=[pKV.b], writes=[vd.b], out=vd[:, :, 0:128],
                 in_=pKV[:, 256:768].rearrange("p (h d) -> p h d", d=128))
            s.dma("sp", VD.ap[t * 128:(t + 1) * 128, :], vd[:].rearrange("p h d -> p (h d)"), reads=[vd.b],
                  writes=[VD.b.part(t)])
            hq = hqr.next()
            if not isctx:
                s.op("act", "copy", reads=[pQ.b], writes=[hq.b], out=hq[:, 0:512], in_=pQ[:])
            else:
                s.op("pool", "memset", writes=[hq.b], ap=hq[:, 0:512], constant=1.0)
            s.op("act", "copy", reads=[pKV.b], writes=[hq.b], out=hq[:, 512:768], in_=pKV[:, 0:256])
            s.op("act", "copy", reads=[kpe.b], writes=[hq.b], out=hq[:, 768:832], in_=kpe[:])
            qg = yield from head_norm(ph, hq, 13, G13, "1")
            yield
            qgv = qg[:, 0:832].rearrange("p (h d) -> p h d", d=64)
            kcb = kcbr.next()
            s.op("dve", "tensor_copy", reads=[qg.b], writes=[kcb.b], out=kcb[:, :, 0:64], in_=qgv[:, 8:12, :])
            if isctx:
                s.op("dve", "tensor_copy", reads=[qg.b], writes=[kcb.b], out=kcb[:, :, 64:128],
                     in_=qgv[:, 12:13, :].to_broadcast([128, 4, 64]))
            else:
                kpr = kprr.next()
                rope_apply(ph, qgv[:, 12:13, :], kpr[:, :].unsqueeze(1), 1, rp, [qg.b], kpr.b, t2eng="dve")
                s.op("dve", "tensor_copy", reads=[kpr.b], writes=[kcb.b], out=kcb[:, :, 64:128],
                     in_=kpr[:, :].unsqueeze(1).to_broadcast([128, 4, 64]))
                qcb = qcbr.next()
                qg4 = qg[:, 0:512].rearrange("p (h t d) -> p h t d", t=2, d=64)
                s.op("act", "copy", reads=[qg.b], writes=[qcb.b], out=qcb[:, :, 0:64], in_=qg4[:, :, 0, :])
                rope_apply(ph, qg4[:, :, 1, :], qcb[:, :, 64:128], 4, rp, [qg.b], qcb.b, t2eng="dve")
            yield
            if not isctx:
                for h in range(4):
                    s.op("pe", "transpose", reads=[qcb.b, idb.b], writes=[pT.b], out=pT[:, h, :], in_=qcb[:, h, :],
                         identity=idb[:])
            for h in range(4):
                s.op("pe", "transpose", reads=[kcb.b, idb.b], writes=[pT.b], out=pT[:, 4 + h, :], in_=kcb[:, h, :],
                     identity=idb[:], inc=(h == 3))
            b0 = 4 if isctx else 0
            s.op("act", "copy", reads=[pT.b], writes=[st_.b], out=st_[:, b0:8, ti * 128:(ti + 1) * 128], in_=pT[:, b0:8, :])
            if ti == ntile_g - 1:
                ntk = ntile_g * 128
                s.dma("sp", QKT1.ap[b0:8, :, g0 * 128:g0 * 128 + ntk].rearrange("j p t -> p j t"), st_[:, b0:8, 0:ntk],
                      reads=[st_.b], writes=[QKT1.b])

        run_pipelined((tile_l1p1(t) for t in range(NT)), first_stage=[5, 12, 7])
        ph.close()
        if stop_after == "L1P1":
            s.finish()
            return nc

        ph = Phase(nc, s)
        QDr = ph.ring("QD", [128, S], BF16, 2)
        KDr = ph.ring("KD", [128, NTOK], BF16, 2)
        VDr = ph.ring("VDs", [128, NT, 129], BF16, 2)
        zdr = ph.ring("zd", [128, 4], F32, 2)
        odr = ph.ring("od", [128, 4, 128], BF16, 3)

        def fin_d(u, acc):
            h, q0 = u["h"], u["q0"]
            rz = zdr.next()
            s.op("dve", "reciprocal", reads=[acc.b], writes=[rz.b], out=rz[:], in_=acc[:, :, 128])
            od = odr.next()
            s.op("dve", "tensor_tensor", reads=[acc.b, rz.b], writes=[od.b], out=od[:], in0=acc[:, :, 0:128],
                 in1=rz[:, :].unsqueeze(2).to_broadcast([128, 4, 128]), op=ALU.mult)
            s.dma("sp", MIX.ap[q0:q0 + 512, 512 + h * 128:512 + (h + 1) * 128].rearrange("(j p) d -> p j d", p=128),
                  od[:], reads=[od.b], writes=[MIX.b.part(("d", h, q0))])

        def load_d(h):
            Qh, Kh, Vh = QDr.items[h % 2], KDr.items[h % 2], VDr.items[h % 2]
            s.dma("sp", Qh[:], QKT1.ap[h, :, 0:S], reads=[QKT1.b], writes=[Qh.b])
            s.dma("sp", Kh[:], QKT1.ap[4 + h], reads=[QKT1.b], writes=[Kh.b])
            s.dma("sp", Vh[:], VD.ap.rearrange("(k p) (h e) -> p k h e", p=128, e=129)[:, :, h, :],
                  reads=[VD.b.part(t) for t in range(NT)], writes=[Vh.b])

        units = []
        for h in range(4):
            Qh, Kh, Vh = QDr.items[h % 2], KDr.items[h % 2], VDr.items[h % 2]
            first = len(units)
            for qb in range(8):
                keys = [dict(kT=Kh[:, kt * 128:(kt + 1) * 128], kb=[Kh.b], v=Vh[:, kt, :], vb=[Vh.b], mask=None)
                        for kt in range(NT)]
                units.append(dict(qT=Qh[:, qb * 512:(qb + 1) * 512], qb=[Qh.b], keys=keys, nq=512, vd1=129,
                                  scale=128.0 ** -0.5, fin=fin_d, h=h, q0=qb * 512))
            if h == 0:
                units[first]["pre"] = (lambda: load_d(0))
            if h < 3:
                units[first + 1]["pre"] = (lambda hh=h + 1: load_d(hh))
        attn_pipeline(ph, units, NT)
        ph.close()
        if stop_after == "L1P2":
            s.finish()
            return nc

        phase3a(1, X1, X1A, NLAT)
        phase3b(1, X1A, y, NLAT)
        gp.close()
        s.finish()
        print("program: instructions", s.n_ins, "waits", s.n_wait)
    return nc


def _rope_tables():
    rows = S // 64
    row = np.repeat(np.arange(rows, dtype=np.int32), 64).astype(np.float32)
    col = np.tile(np.arange(64, dtype=np.int32), rows).astype(np.float32)
    inv = (np.float32(10000.0) ** (-np.arange(16, dtype=np.float32) / np.float32(16))).astype(np.float32)
    ang = np.concatenate([row[:, None] * inv, col[:, None] * inv], axis=-1).astype(np.float32)
    c, sn = np.cos(ang).astype(np.float32), np.sin(ang).astype(np.float32)
    tab = np.concatenate([c, c, -sn, sn], axis=-1)
    return np.ascontiguousarray(tab.reshape(NLAT, 128, 128))


def _shared_inputs(inp):
    f = lambda a: np.ascontiguousarray(np.asarray(a, dtype=np.float32))
    ev = f(inp["ev_w_in"])[0]
    aq = ev[:, 0:512].reshape(D, 8, 64)
    aq_p = np.stack([aq[:, [j, 4 + j], :] for j in range(4)], axis=1).reshape(D, 512)
    wev = np.concatenate([aq_p, ev[:, 512:1024], ev[:, 1024:1152], ev[:, 1280:1792], ev[:, 1152:1280], ev[:, 1792:2304]], axis=1)
    ukv = f(inp["od_w_ukv"])[0].reshape(256, 4, 192)
    wukv = np.concatenate([ukv[:, :, 0:64].reshape(256, 256), ukv[:, :, 64:192].reshape(256, 512)], axis=1)
    vec = np.concatenate([
        f(inp["ev_qnorm_a"])[0], f(inp["ev_knorm_a"])[0], f(inp["ev_qnorm_b"])[0], f(inp["ev_knorm_b"])[0],
        f(inp["ev_sink"])[0], f(inp["ev_lam_q1"])[0], f(inp["ev_lam_k1"])[0], f(inp["ev_lam_q2"])[0], f(inp["ev_lam_k2"])[0],
        f(inp["ev_subln"])[0], f(inp["od_c_vnorm"])[0], f(inp["od_qa_norm"])[0], f(inp["od_kva_norm"])[0],
        f(inp["od_qnorm_nope"])[0], f(inp["od_knorm_nope"])[0], f(inp["od_qnorm_rope"])[0], f(inp["od_knorm_rope"])[0]])
    assert vec.shape[0] == NV
    j = np.arange(128)[:, None]
    i = np.arange(128)[None, :]
    masks = np.stack([(j >= i), (j <= i)]).astype(np.float32)
    return {
        "ada_w": f(inp["ada_w"]), "ada_b": f(inp["ada_b"]).reshape(2, 1, 6 * D),
        "n1g": f(inp["norm1_g"]).reshape(2, 1, D), "n2g": f(inp["norm2_g"]).reshape(2, 1, D),
        "wmix": f(inp["mix_w_out"]), "wfi": f(inp["ffn_w_in"]), "wfo": f(inp["ffn_w_out"]),
        "wev": f(wev), "wod": f(inp["od_w_in"])[0], "wuq": f(inp["od_w_uq"])[0], "wukv": f(wukv),
        "wsT": f(np.transpose(f(inp["od_c_ws"])[0], (0, 2, 1))), "bsT": f(f(inp["od_c_bs"])[0].T),
        "vecs": f(vec.reshape(1, NV)), "ident": np.eye(128, dtype=np.float32), "rope": _rope_tables(), "masks": masks,
    }


def make_in_maps(inp):
    shared = _shared_inputs(inp)
    x = np.asarray(inp["x"], dtype=np.float32)
    ctx = np.asarray(inp["ctx"], dtype=np.float32)
    c = np.asarray(inp["c"], dtype=np.float32)
    c_ctx = np.asarray(inp["c_ctx"], dtype=np.float32)
    maps = []
    for b in range(x.shape[0]):
        m = dict(shared)
        m["xs"] = np.ascontiguousarray(np.concatenate([x[b], ctx[b]], axis=0))
        cc = np.stack([c[b], c_ctx], axis=-1).reshape(8, 128, 2).transpose(1, 0, 2)
        m["cc"] = np.ascontiguousarray(cc)
        maps.append(m)
    return maps


_NC_CACHE = {}


def kernel(**inputs):
    if "nc" not in _NC_CACHE:
        _NC_CACHE["nc"] = build_program()
    nc = _NC_CACHE["nc"]
    in_maps = make_in_maps(inputs)
    res = run_bass_kernel_spmd(nc, in_maps, core_ids=list(range(len(in_maps))))
    return np.stack([np.asarray(r["y"], dtype=np.float32) for r in res.results], axis=0)
```

```python
import math
import numpy as np
import concourse.bass as bass
import concourse.mybir as mybir
from concourse.bass_utils import run_bass_kernel_spmd
from contextlib import ExitStack

F32 = mybir.dt.float32
BF16 = mybir.dt.bfloat16
ALU = mybir.AluOpType
AF = mybir.ActivationFunctionType
AX = mybir.AxisListType

D = 1024
S = 4096
LCTX = 256
NTOK = S + LCTX
NT = NTOK // 128
NLAT = S // 128
FH = 2816
EPS = 1e-6
NV = 1928

V_QA, V_KA, V_QB, V_KB, V_SINK, V_LQ1, V_LK1, V_LQ2, V_LK2, V_SUBLN = 0, 64, 128, 192, 256, 264, 328, 392, 456, 520
V_CVN, V_QAN, V_KVAN, V_QNN, V_KNN, V_QNR, V_KNR = 648, 1160, 1416, 1672, 1736, 1800, 1864


class Buf:
    def __init__(self, name, excl=False):
        self.name = name
        self.w = None
        self.r = {}
        self.excl = excl
        self.parts = {}

    def part(self, key):
        p = self.parts.get(key)
        if p is None:
            p = Buf(f"{self.name}[{key}]", self.excl)
            self.parts[key] = p
        return p


class Sched:
    ENG = ["pe", "act", "dve", "pool", "sp"]
    NRING = 8

    def __init__(self, nc, stack):
        self.nc = nc
        self.prog = {e: [] for e in self.ENG}
        self.cnt = {e: 0 for e in self.ENG}
        self.last = {e: None for e in self.ENG}
        self.pend = {e: False for e in self.ENG}
        self.sem = {e: stack.enter_context(nc.semaphore(f"sem_{e}")) for e in self.ENG}
        self.known = {e: {} for e in self.ENG}
        self.ring = {}
        self.ring_i = {}
        self.ring_val = {}
        for q in ("sp", "pool", "act"):
            self.ring[q] = [stack.enter_context(nc.semaphore(f"dq_{q}{i}")) for i in range(self.NRING)]
            self.ring_i[q] = 0
            self.ring_val[q] = [0] * self.NRING
        self.n_wait = 0
        self.n_ins = 0

    def _need(self, eng, ev, rec_waits):
        sem, val = ev
        if self.known[eng].get(sem, 0) >= val:
            return
        for e in self.ENG:
            if self.sem[e] == sem and val > self.cnt[e]:
                assert self.pend[e] and val == self.cnt[e] + 1, (e, val, self.cnt[e])
                self.last[e]["inc"] = True
                self.cnt[e] += 1
                self.pend[e] = False
        self.known[eng][sem] = val
        rec_waits.append((sem, val))
        self.n_wait += 1

    def _deps(self, eng, key, reads, writes, waits, is_dma):
        for b in reads:
            if b.w is not None:
                self._need(eng, b.w, waits)
            if b.excl:
                for k, ev in list(b.r.items()):
                    if k != key:
                        self._need(eng, ev, waits)
        for b in writes:
            if b.w is not None and not (eng == "pe" and b.w[0] == self.sem["pe"]):
                self._need(eng, b.w, waits)
            for k, ev in list(b.r.items()):
                if k != key or is_dma:
                    self._need(eng, ev, waits)

    def op(self, eng, method, reads=(), writes=(), **kw):
        waits = []
        eager = kw.pop("inc", None)
        if eager is None:
            eager = (eng != "pe") or (method == "matmul" and bool(kw.get("stop")))
        self._deps(eng, eng, reads, writes, waits, False)
        rec = {"m": method, "kw": kw, "waits": waits, "inc": False, "dma": None}
        self.prog[eng].append(rec)
        self.last[eng] = rec
        if eager:
            rec["inc"] = True
            self.cnt[eng] += 1
            self.pend[eng] = False
            ev = (self.sem[eng], self.cnt[eng])
        else:
            self.pend[eng] = True
            ev = (self.sem[eng], self.cnt[eng] + 1)
        for b in reads:
            b.r[eng] = ev
        for b in writes:
            b.w = ev
            b.r = {}
        self.n_ins += 1
        return rec

    def dma(self, q, out, in_, reads=(), writes=(), **kw):
        waits = []
        i = self.ring_i[q]
        slot = i % self.NRING
        self.ring_i[q] += 1
        sem = self.ring[q][slot]
        if self.ring_val[q][slot] > 0:
            self._need(q, (sem, self.ring_val[q][slot]), waits)
        key = (q, slot)
        self._deps(q, key, reads, writes, waits, True)
        self.ring_val[q][slot] += 16
        ev = (sem, self.ring_val[q][slot])
        rec = {"m": "dma_start", "kw": dict(out=out, in_=in_, **kw), "waits": waits, "inc": False, "dma": sem}
        self.prog[q].append(rec)
        for b in reads:
            b.r[key] = ev
        for b in writes:
            b.w = ev
            b.r = {}
        self.n_ins += 1
        return ev

    def barrier(self):
        evs = []
        for e in self.ENG:
            if self.pend[e]:
                self.last[e]["inc"] = True
                self.cnt[e] += 1
                self.pend[e] = False
            if self.cnt[e] > 0:
                evs.append((self.sem[e], self.cnt[e]))
        for q in self.ring:
            for s_, v in zip(self.ring[q], self.ring_val[q]):
                if v > 0:
                    evs.append((s_, v))
        for e in self.ENG:
            waits = []
            for ev in evs:
                if ev[0] == self.sem[e]:
                    continue
                if self.known[e].get(ev[0], 0) < ev[1]:
                    self.known[e][ev[0]] = ev[1]
                    waits.append(ev)
            if waits:
                self.prog[e].append({"m": None, "kw": {}, "waits": waits, "inc": False, "dma": None})

    def finish(self):
        self.barrier()
        nc = self.nc
        engobj = {"pe": "tensor", "act": "scalar", "dve": "vector", "pool": "gpsimd", "sp": "sync"}
        with nc.Block() as block:
            for e in self.ENG:
                prog = self.prog[e]
                sem = self.sem[e]

                def body(eng, prog=prog, sem=sem):
                    for rec in prog:
                        for (s_, v) in rec["waits"]:
                            eng.wait_ge(s_, v)
                        if rec["m"] is None:
                            continue
                        ins = getattr(eng, rec["m"])(**rec["kw"])
                        if rec["dma"] is not None:
                            ins.then_inc(rec["dma"], 16)
                        elif rec["inc"]:
                            ins.then_inc(sem, 1)

                getattr(block, engobj[e])(body)


class T:
    def __init__(self, t, name, excl=False):
        self.t = t
        self.b = Buf(name, excl)

    def __getitem__(self, k):
        return self.t[k]


class Phase:
    _n = 0

    def __init__(self, nc, s):
        self.nc = nc
        self.s = s
        self.st = ExitStack()
        Phase._n += 1
        self.pfx = f"p{Phase._n}_"

    def sb(self, name, shape, dt):
        return T(self.st.enter_context(self.nc.sbuf_tensor(self.pfx + name, list(shape), dt)), name)

    def ps(self, name, shape, dt):
        return T(self.st.enter_context(self.nc.psum_tensor(self.pfx + name, list(shape), dt)), name, True)

    def ring(self, name, shape, dt, n):
        return Ring([self.sb(f"{name}{i}", shape, dt) for i in range(n)])

    def close(self):
        self.s.barrier()
        try:
            print("phase", self.pfx, "sbuf spare KB", self.nc.sbuf_bytes_remaining // 1024 // 128 if self.nc.sbuf_bytes_remaining > 4 * 1024 * 1024 else self.nc.sbuf_bytes_remaining // 1024)
        except Exception as e:
            pass
        self.st.close()


class Ring:
    def __init__(self, items):
        self.items = items
        self.i = 0

    def next(self):
        t = self.items[self.i % len(self.items)]
        self.i += 1
        return t


class DT:
    def __init__(self, ap, name):
        self.ap = ap
        self.b = Buf(name)


def build_program(dbg=(), stop_after=None):
    Phase._n = 0
    nc = bass.Bass("TRN2", target_bir_lowering=False)

    def din(name, shape, dt=F32):
        return DT(nc.dram_tensor(name, list(shape), dt, kind="ExternalInput").ap(), name)

    def dscr(name, shape, dt):
        kind = "ExternalOutput" if name in dbg else "Internal"
        return DT(nc.dram_tensor(name, list(shape), dt, kind=kind).ap(), name)

    xs = din("xs", [NTOK, D])
    cc = din("cc", [128, 8, 2])
    ada_w = din("ada_w", [2, D, 6 * D])
    ada_b = din("ada_b", [2, 1, 6 * D])
    n1g = din("n1g", [2, 1, D])
    n2g = din("n2g", [2, 1, D])
    wmix = din("wmix", [2, D, D])
    wfi = din("wfi", [2, D, 2 * FH])
    wfo = din("wfo", [2, FH, D])
    wev = din("wev", [D, 2304])
    wod = din("wod", [D, 1600])
    wuq = din("wuq", [256, 512])
    wukv = din("wukv", [256, 768])
    wsT = din("wsT", [4, 128, 128])
    bsT = din("bsT", [128, 4])
    vecs = din("vecs", [1, NV])
    ident = din("ident", [128, 128])
    rope = din("rope", [NLAT, 128, 128])
    masks = din("masks", [2, 128, 128])
    y = DT(nc.dram_tensor("y", [S, D], F32, kind="ExternalOutput").ap(), "y")

    modV = dscr("modV", [2, 2, 6 * D], F32)
    QKT0 = dscr("QKT0", [13, 128, NTOK], BF16)
    VA = dscr("VA", [NTOK, 130], BF16)
    VB = dscr("VB", [NTOK, 516], BF16)
    MIX = dscr("MIX", [NTOK, D], BF16)
    H2T = dscr("H2T", [8, 128, NTOK], BF16)
    X1A = dscr("X1A", [NTOK, D], F32)
    X1 = dscr("X1", [NTOK, D], F32)
    QKT1 = dscr("QKT1", [8, 128, NTOK], BF16)
    VD = dscr("VD", [NTOK, 516], BF16)

    with ExitStack() as gst:
        s = Sched(nc, gst)

        gp = Phase(nc, s)
        idf = gp.sb("idf", [128, 128], F32)
        idb = gp.sb("idb", [128, 128], BF16)
        nh = gp.sb("nh", [128, 64], F32)
        vecb = gp.sb("vecb", [128, NV], F32)
        s.dma("sp", idf[:], ident.ap, writes=[idf.b])
        s.op("dve", "tensor_copy", reads=[idf.b], writes=[idb.b], out=idb[:], in_=idf[:])
        s.op("pool", "memset", writes=[nh.b], ap=nh[:], constant=-0.5)
        s.dma("sp", vecb[:], vecs.ap[0, :].partition_broadcast(128), writes=[vecb.b])

        def rstd_from_ss(ph, ssT, n, width, rname):
            v = ph.sb(rname + "_v", [128, n], F32) if not hasattr(ph, "_" + rname) else getattr(ph, "_" + rname)[0]
            r = ph.sb(rname + "_r", [128, n], F32) if not hasattr(ph, "_" + rname) else getattr(ph, "_" + rname)[1]
            setattr(ph, "_" + rname, (v, r))
            s.op("dve", "tensor_scalar", reads=[ssT.b], writes=[v.b], out=v[:], in0=ssT[:, 0:n], scalar1=1.0 / width,
                 scalar2=EPS, op0=ALU.mult, op1=ALU.add)
            s.op("pool", "tensor_tensor", reads=[v.b, nh.b], writes=[r.b], out=r[:], in0=v[:], in1=nh[:, 0:n], op=ALU.pow)
            return r

        def load_bcast(ph, name, src_ap, width, src_b, q="sp"):
            t = ph.sb(name, [128, width], F32)
            s.dma(q, t[:], src_ap.partition_broadcast(128), reads=[src_b], writes=[t.b])
            return t

        def run_pipelined(gens, first_stage=None):
            gens = list(gens)
            active = []
            i = 0
            while i < len(gens) or active:
                new = None
                if i < len(gens):
                    new = [gens[i], 0]
                    i += 1
                    try:
                        next(new[0])
                        new[1] = 1
                    except StopIteration:
                        new = None
                prio = first_stage if isinstance(first_stage, (list, tuple)) else [first_stage]
                order = []
                for st in prio:
                    order += [a for a in active if a[1] + 1 == st]
                order += [a for a in active if a[1] + 1 not in prio]
                for a in order:
                    try:
                        next(a[0])
                        a[1] += 1
                    except StopIteration:
                        active.remove(a)
                if new is not None:
                    active.append(new)

        def norm_mod_transpose(ph, xt, G, Sh, pT, hT_ap, hT_b, rings, split=True):
            junk, ssr, _, hbr = rings
            jk = junk.next()
            ss = ssr.next()
            s.op("act", "activation", reads=[xt.b], writes=[jk.b, ss.b], out=jk[:], in_=xt[:], func=AF.Square,
                 accum_out=ss[:])
            r = rstd_from_ss(ph, ss, 1, D, "rs_x")
            yield
            hb = hbr.next()
            s.op("act", "activation", reads=[xt.b, r.b], writes=[hb.b], out=hb[:], in_=xt[:], func=AF.Copy,
                 scale=r[:, 0:1])
            yield
            for k in range(8):
                s.op("pe", "transpose", reads=[hb.b, idb.b], writes=[pT.b], out=pT[:, k, :],
                     in_=hb[:, k * 128:(k + 1) * 128], identity=idb[:], inc=(k == 7))
            if split:
                yield
            for k in range(8):
                s.op("act", "activation", reads=[pT.b, G.b, Sh.b], writes=[hT_b], out=hT_ap[:, k, :], in_=pT[:, k, :],
                     func=AF.Identity, scale=G[:, k:k + 1], bias=Sh[:, k:k + 1])

        def load_w_bf16(ph, name, src_ap, kchunks, ncols, src_b):
            t = ph.sb(name, [128, kchunks, ncols], BF16)
            view = src_ap.rearrange("(k p) n -> p k n", p=128)
            step = max(1, min(kchunks, 4096 // ncols)) if ncols <= 4096 else 1
            for k0 in range(0, kchunks, step):
                k1 = min(kchunks, k0 + step)
                s.dma("pool", t[:, k0:k1, :], view[:, k0:k1, :], reads=[src_b], writes=[t.b.part(k0)])
            t.kparts = [t.b.part(k0) for k0 in range(0, kchunks, step)]
            return t

        def load_w_bf16_cols(ph, name, src_ap, kchunks, ncols, src_b, cb, order):
            t = ph.sb(name, [128, kchunks, ncols], BF16)
            view = src_ap.rearrange("(k p) n -> p k n", p=128)
            for b in order:
                c0, c1 = b * cb, min(ncols, (b + 1) * cb)
                s.dma("pool", t[:, :, c0:c1], view[:, :, c0:c1], reads=[src_b], writes=[t.b.part(("c", b))])
            t.cpart = lambda col: t.b.part(("c", col // cb))
            return t

        def head_norm(ph, qf, nslots, Gt, tag, wide=False):
            w = nslots * 64
            sq = ph.H_sq.next()
            s.op("act", "activation", reads=[qf.b], writes=[sq.b], out=sq[:, 0:w], in_=qf[:, 0:w], func=AF.Square)
            yield
            ssh = ph.H_ss.next()
            s.op("dve", "tensor_reduce", reads=[sq.b], writes=[ssh.b], out=ssh[:, 0:nslots],
                 in_=sq[:, 0:w].rearrange("p (h d) -> p h d", d=64), axis=AX.X, op=ALU.add)
            r = rstd_from_ss(ph, ssh, nslots, 64, "rs_h" + tag)
            yield
            qn = ph.H_qn.next()
            s.op("dve", "tensor_tensor", reads=[qf.b, r.b], writes=[qn.b],
                 out=qn[:, 0:w].rearrange("p (h d) -> p h d", d=64),
                 in0=qf[:, 0:w].rearrange("p (h d) -> p h d", d=64),
                 in1=r[:, 0:nslots].unsqueeze(2).to_broadcast([128, nslots, 64]), op=ALU.mult)
            if wide:
                yield
            qg = ph.H_qg.next()
            s.op("pool" if wide else "dve", "tensor_tensor", reads=[qn.b, Gt.b], writes=[qg.b], out=qg[:, 0:w],
                 in0=qn[:, 0:w], in1=Gt[:, 0:w], op=ALU.mult)
            return qg

        def rope_apply(ph, src_ap, dst_ap, n, rp, reads, dst_b, t2eng="pool"):
            t1 = ph.R_t1.next()
            t2 = ph.R_t2.next()
            t1v = t1[:, 0:n * 64].rearrange("p (h d) -> p h d", d=64)
            t2v = t2[:, 0:n * 64].rearrange("p (h d) -> p h d", d=64)
            s.op("dve", "tensor_tensor", reads=reads + [rp.b], writes=[t1.b], out=t1v, in0=src_ap,
                 in1=rp[:, 0:64].unsqueeze(1).to_broadcast([128, n, 64]), op=ALU.mult)
            s.op(t2eng, "tensor_tensor", reads=reads + [rp.b], writes=[t2.b.part(0)], out=t2v[:, :, 0:32], in0=src_ap[:, :, 32:64],
                 in1=rp[:, 64:96].unsqueeze(1).to_broadcast([128, n, 32]), op=ALU.mult)
            s.op("dve", "tensor_tensor", reads=reads + [rp.b], writes=[t2.b.part(1)], out=t2v[:, :, 32:64], in0=src_ap[:, :, 0:32],
                 in1=rp[:, 96:128].unsqueeze(1).to_broadcast([128, n, 32]), op=ALU.mult)
            s.op("dve", "tensor_tensor", reads=[t1.b, t2.b.part(0), t2.b.part(1)], writes=[dst_b], out=dst_ap, in0=t1v, in1=t2v,
                 op=ALU.add)

        pre0 = Phase(nc, s)
        W = load_w_bf16(pre0, "wev", wev.ap, 8, 2304, wev.b)

        ph = Phase(nc, s)
        cct = ph.sb("cct", [128, 8, 2], F32)
        sc = ph.sb("sc", [128, 8, 2], F32)
        ones2 = ph.sb("ones2", [1, 2], F32)
        brow = ph.sb("brow", [1, 6 * D], F32)
        mrow = ph.sb("mrow", [2, 6 * D], F32)
        wring = ph.ring("adaw", [128, 8, 512], F32, 2)
        pm = [ph.ps(f"pm{i}", [2, 512], F32) for i in range(2)]
        s.dma("sp", cct[:], cc.ap, writes=[cct.b])
        s.op("act", "activation", reads=[cct.b], writes=[sc.b], out=sc[:], in_=cct[:], func=AF.Silu)
        s.op("dve", "memset", writes=[ones2.b], ap=ones2[:], constant=1.0)
        import os
        NL_ = int(os.environ.get("KD_NL", "2"))
        NC_ = int(os.environ.get("KD_NC", "12"))
        for l in range(NL_):
            s.dma("sp", brow[:], ada_b.ap[l], writes=[brow.b])
            for c in range(NC_):
                wt = wring.next()
                s.dma("sp", wt[:], ada_w.ap[l][:, c * 512:(c + 1) * 512].rearrange("(k p) n -> p k n", p=128),
                      writes=[wt.b])
                p = pm[c % 2]
                for k in range(8):
                    s.op("pe", "matmul", reads=[sc.b, wt.b], writes=[p.b], out=p[:], lhsT=sc[:, k, :], rhs=wt[:, k, :],
                         start=(k == 0), stop=False)
                s.op("pe", "matmul", reads=[ones2.b, brow.b], writes=[p.b], out=p[:], lhsT=ones2[:],
                     rhs=brow[:, c * 512:(c + 1) * 512], start=False, stop=True)
                s.op("dve", "tensor_copy", reads=[p.b], writes=[mrow.b], out=mrow[:, c * 512:(c + 1) * 512], in_=p[:])
            s.dma("sp", modV.ap[l], mrow[:], reads=[mrow.b], writes=[modV.b])
        ph.close()
        if stop_after == "P0":
            s.finish()
            return nc

        def mod_tiles(ph, l, v, idx_shift, idx_scale, gdt, tag):
            def colload(name, ap1d, b):
                t = ph.sb(name, [128, 8], F32)
                s.dma("sp", t[:], ap1d.rearrange("(k p) -> p k", p=128), reads=[b], writes=[t.b],
                      allow_slow_non_contiguous=True)
                return t
            sh = colload(f"sh{tag}", modV.ap[l, v, idx_shift * D:(idx_shift + 1) * D], modV.b)
            scl = colload(f"scl{tag}", modV.ap[l, v, idx_scale * D:(idx_scale + 1) * D], modV.b)
            gg = colload(f"gg{tag}", gdt.ap[l, 0, :], gdt.b)
            s.op("dve", "scalar_tensor_tensor", reads=[scl.b, gg.b], writes=[scl.b], out=scl[:], in0=scl[:], scalar=1.0,
                 in1=gg[:], op0=ALU.add, op1=ALU.mult)
            return scl, sh

        def mod_tiles_bc(ph, l, v, idx_shift, idx_scale, gdt, tag):
            sh = load_bcast(ph, f"bsh{tag}", modV.ap[l, v, idx_shift * D:(idx_shift + 1) * D], D, modV.b)
            scl = load_bcast(ph, f"bscl{tag}", modV.ap[l, v, idx_scale * D:(idx_scale + 1) * D], D, modV.b)
            if not hasattr(ph, "_bgg"):
                ph._bgg = load_bcast(ph, "bgg", gdt.ap[l, 0, :], D, gdt.b)
            gg = ph._bgg
            s.op("dve", "scalar_tensor_tensor", reads=[scl.b, gg.b], writes=[scl.b], out=scl[:], in0=scl[:], scalar=1.0,
                 in1=gg[:], op0=ALU.add, op1=ALU.mult)
            return scl, sh

        def norm_mod_transpose_tm(ph, xt, G, Sh, pT, hT_ap, hT_b, rings, h1r):
            junk, ssr, _, hbr = rings
            jk = junk.next()
            ss = ssr.next()
            s.op("act", "activation", reads=[xt.b], writes=[jk.b, ss.b], out=jk[:], in_=xt[:], func=AF.Square,
                 accum_out=ss[:])
            r = rstd_from_ss(ph, ss, 1, D, "rs_x")
            yield
            h1 = h1r.next()
            s.op("dve", "scalar_tensor_tensor", reads=[xt.b, r.b, G.b], writes=[h1.b], out=h1[:], in0=xt[:],
                 scalar=r[:, 0:1], in1=G[:], op0=ALU.mult, op1=ALU.mult)
            hb = hbr.next()
            s.op("dve", "tensor_tensor", reads=[h1.b, Sh.b], writes=[hb.b], out=hb[:], in0=h1[:], in1=Sh[:], op=ALU.add)
            yield
            for k in range(8):
                s.op("pe", "transpose", reads=[hb.b, idb.b], writes=[pT.b], out=pT[:, k, :],
                     in_=hb[:, k * 128:(k + 1) * 128], identity=idb[:], inc=(k == 7))
            yield
            s.op("act", "copy", reads=[pT.b], writes=[hT_b], out=hT_ap, in_=pT[:, 0:8, :])

        def norm_rings(ph):
            return (ph.ring("junk", [128, D], BF16, 1), ph.ring("ssx", [128, 1], F32, 2),
                    None, ph.ring("hb", [128, D], BF16, 2))

        def head_rings(ph, w):
            ph.H_sq = ph.ring("hsq", [128, w], F32, 2)
            ph.H_ss = ph.ring("hss", [128, 32], F32, 2)
            ph.H_qn = ph.ring("hqn", [128, w], F32, 2)
            ph.H_qg = ph.ring("hqg", [128, w], F32, 2)
            ph.R_t1 = ph.ring("rt1", [128, w], F32, 1)
            ph.R_t2 = ph.ring("rt2", [128, w], F32, 1)

        ph = Phase(nc, s)
        GL, SL = mod_tiles(ph, 0, 0, 0, 1, n1g, "l")
        GC, SC = mod_tiles(ph, 0, 1, 0, 1, n1g, "c")
        G26 = ph.sb("g26", [128, 26 * 64], F32)
        g26v = G26[:, :].rearrange("p (h d) -> p h d", d=64)
        for (a, b_, off) in ((0, 8, V_QA), (8, 16, V_QB), (16, 18, V_KA), (18, 26, V_KB)):
            s.op("dve", "tensor_copy", reads=[vecb.b], writes=[G26.b], out=g26v[:, a:b_, :],
                 in_=vecb[:, off:off + 64].unsqueeze(1).to_broadcast([128, b_ - a, 64]))
        nr = norm_rings(ph)
        head_rings(ph, 1664)
        xr = ph.ring("x", [128, D], F32, 4)
        rpr = ph.ring("rp", [128, 128], F32, 7)
        hTr = ph.ring("hT", [128, 8, 128], BF16, 2)
        qfr = ph.ring("qf", [128, 1664], F32, 3)
        qbr = ph.ring("qb", [128, 1664], BF16, 2)
        var = ph.ring("va", [128, 2, 65], BF16, 2)
        vbr = ph.ring("vb", [128, 4, 129], BF16, 2)
        for t_ in var.items + vbr.items:
            s.op("pool", "memset", writes=[t_.b], ap=t_[:], constant=1.0)
        qkst = ph.ring("qkst", [128, 13, 256], BF16, 2)
        pT = ph.ps("pT", [128, 8, 128], BF16)
        pO = ph.ps("pO", [128, 2560], F32)
        pQ1 = ph.ps("pQ1", [128, 8, 128], BF16)
        pQ2 = ph.ps("pQ2", [128, 8, 128], BF16)
        def tile_l0p1(t):
            g0 = (t // 2) * 2
            ti = t - g0
            ntile_g = 2
            st_ = qkst.items[(t // 2) % 2]
            isctx = t >= NLAT
            xt = xr.next()
            s.dma("sp", xt[:], xs.ap[t * 128:(t + 1) * 128, :], writes=[xt.b])
            yield
            hT = hTr.next()
            yield from norm_mod_transpose(ph, xt, GC if isctx else GL, SC if isctx else SL, pT, hT[:], hT.b, nr)
            yield
            for k in range(8):
                for c in range(5):
                    n0, n1 = c * 512, min(2304, (c + 1) * 512)
                    s.op("pe", "matmul", reads=[hT.b] + W.kparts, writes=[pO.b], out=pO[:, n0:n1],
                         lhsT=hT[:, k, :], rhs=W[:, k, n0:n1], start=(k == 0), stop=(k == 7))
            if not isctx:
                rp = rpr.next()
                s.dma("sp", rp[:], rope.ap[t], writes=[rp.b])
            yield
            qf = qfr.next()
            s.op("act", "copy", reads=[pO.b], writes=[qf.b], out=qf[:], in_=pO[:, 0:1664])
            va = var.next()
            vb = vbr.next()
            s.op("act", "copy", reads=[pO.b], writes=[va.b], out=va[:, :, 0:64],
                 in_=pO[:, 1664:1792].rearrange("p (h d) -> p h d", d=64))
            s.op("act", "copy", reads=[pO.b], writes=[vb.b], out=vb[:, :, 0:128],
                 in_=pO[:, 1792:2304].rearrange("p (h d) -> p h d", d=128))
            s.dma("sp", VA.ap[t * 128:(t + 1) * 128, :], va[:].rearrange("p h d -> p (h d)"), reads=[va.b],
                  writes=[VA.b.part(t)])
            s.dma("sp", VB.ap[t * 128:(t + 1) * 128, :], vb[:].rearrange("p h d -> p (h d)"), reads=[vb.b],
                  writes=[VB.b.part(t)])
            qg = yield from head_norm(ph, qf, 26, G26, "0", wide=True)
            yield
            qb = qbr.next()
            if isctx:
                s.op("dve", "tensor_copy", reads=[qg.b], writes=[qb.b], out=qb[:], in_=qg[:, 0:1664])
            else:
                rope_apply(ph, qg[:, 0:1664].rearrange("p (h d) -> p h d", d=64),
                           qb[:, :].rearrange("p (h d) -> p h d", d=64), 26, rp, [qg.b], qb.b, t2eng="dve")
            yield
            for j in range(13):
                pq = pQ1 if j < 8 else pQ2
                s.op("pe", "transpose", reads=[qb.b, idb.b], writes=[pq.b], out=pq[:, j % 8, :],
                     in_=qb[:, j * 128:(j + 1) * 128], identity=idb[:], inc=(j in (7, 12)))
            yield
            s.op("act", "copy", reads=[pQ1.b], writes=[st_.b], out=st_[:, 0:8, ti * 128:(ti + 1) * 128], in_=pQ1[:])
            s.op("act", "copy", reads=[pQ2.b], writes=[st_.b], out=st_[:, 8:13, ti * 128:(ti + 1) * 128],
                 in_=pQ2[:, 0:5, :])
            if ti == ntile_g - 1:
                ntk = ntile_g * 128
                s.dma("sp", QKT0.ap[:, :, g0 * 128:g0 * 128 + ntk].rearrange("j p t -> p j t"), st_[:, :, 0:ntk],
                      reads=[st_.b], writes=[QKT0.b])

        run_pipelined((tile_l0p1(t) for t in range(NT)), first_stage=7)
        ph.close()
        pre0.close()
        if stop_after == "L0P1":
            s.finish()
            return nc

        def attn_pipeline(ph, units, nkmax):
            PT = [ph.sb(f"PT{i}", [128, nkmax, 512], BF16) for i in range(2)]
            Sp = [ph.ps(f"S{i}", [128, 1024], F32) for i in range(2)]
            acc = ph.ps("acc", [128, 4, 512], F32)
            si = 0
            n = len(units)
            for ui in range(n + 1):
                cur = units[ui] if ui < n else None
                prev = units[ui - 1] if ui > 0 else None
                if cur is not None and cur.get("pre") is not None:
                    cur["pre"]()
                nk = max(len(cur["keys"]) if cur else 0, len(prev["keys"]) if prev else 0)
                for k0 in range(0, nk, 2):
                    if cur is not None and k0 < len(cur["keys"]):
                        kts = [kt for kt in (k0, k0 + 1) if kt < len(cur["keys"])]
                        nq = cur["nq"]
                        sbk = Sp[si % 2]
                        si += 1
                        pt = PT[ui % 2]
                        for i_, kt in enumerate(kts):
                            key = cur["keys"][kt]
                            so = sbk[:, i_ * 512:i_ * 512 + nq]
                            if len(cur["qT"].shape) == 3:
                                so = so.rearrange("p (g q) -> p g q", q=128)
                            s.op("pe", "matmul", reads=cur["qb"] + key["kb"], writes=[sbk.b], out=so, lhsT=key["kT"],
                                 rhs=cur["qT"], start=True, stop=True)
                        if len(kts) == 2 and nq == 512:
                            s.op("act", "activation", reads=[sbk.b], writes=[pt.b.part(kts[0]), pt.b.part(kts[1])],
                                 out=pt[:, kts[0]:kts[0] + 2, :].rearrange("p k q -> p (k q)"), in_=sbk[:, 0:1024],
                                 func=AF.Exp, scale=cur["scale"])
                        else:
                            for i_, kt in enumerate(kts):
                                s.op("act", "activation", reads=[sbk.b], writes=[pt.b.part(kt)], out=pt[:, kt, 0:nq],
                                     in_=sbk[:, i_ * 512:i_ * 512 + nq], func=AF.Exp, scale=cur["scale"])
                        for kt in kts:
                            mk = cur["keys"][kt].get("mask")
                            if mk is not None:
                                ptb = pt.b.part(kt)
                                s.op("dve", "tensor_tensor", reads=[ptb, mk.b], writes=[ptb],
                                     out=pt[:, kt, 0:nq].rearrange("p (g q) -> p g q", q=128),
                                     in0=pt[:, kt, 0:nq].rearrange("p (g q) -> p g q", q=128),
                                     in1=mk[:, :].unsqueeze(1).to_broadcast([128, nq // 128, 128]), op=ALU.mult)
                    if prev is not None:
                        pt = PT[(ui - 1) % 2]
                        nkp = len(prev["keys"])
                        vd1 = prev["vd1"]
                        for kt in (k0, k0 + 1):
                            if kt >= nkp:
                                continue
                            key = prev["keys"][kt]
                            ptb = pt.b.part(kt)
                            for j in range(prev["nq"] // 128):
                                s.op("pe", "matmul", reads=[ptb] + key["vb"], writes=[acc.b], out=acc[:, j, 0:vd1],
                                     lhsT=pt[:, kt, j * 128:(j + 1) * 128], rhs=key["v"], start=(kt == 0), stop=(kt == nkp - 1))
                if prev is not None:
                    prev["fin"](prev, acc)

        mprev_f = gp.sb("mprevf", [128, 128], F32)
        mnext_f = gp.sb("mnextf", [128, 128], F32)
        mprev = gp.sb("mprev", [128, 128], BF16)
        mnext = gp.sb("mnext", [128, 128], BF16)
        s.dma("sp", mprev_f[:], masks.ap[0], writes=[mprev_f.b])
        s.dma("sp", mnext_f[:], masks.ap[1], writes=[mnext_f.b])
        s.op("dve", "tensor_copy", reads=[mprev_f.b], writes=[mprev.b], out=mprev[:], in_=mprev_f[:])
        s.op("dve", "tensor_copy", reads=[mnext_f.b], writes=[mnext.b], out=mnext[:], in_=mnext_f[:])

        preA = Phase(nc, s)
        Wm0 = load_w_bf16(preA, "wmix0", wmix.ap[0], 8, D, wmix.b)

        ph = Phase(nc, s)
        QA = ph.sb("QA", [128, 4, NTOK], BF16)
        KAz = [ph.sb(f"KAz{i}", [128, NTOK], BF16) for i in range(2)]
        s.op("pool", "memset", writes=[KAz[0].b], ap=KAz[0][64:128, :], constant=0.0)
        s.op("pool", "memset", writes=[KAz[1].b], ap=KAz[1][0:64, :], constant=0.0)
        VAs = ph.sb("VAs", [128, NT, 130], BF16)
        s.dma("sp", QA[:], QKT0.ap[0:4].rearrange("j p t -> p j t"), reads=[QKT0.b], writes=[QA.b])
        s.dma("sp", KAz[0][0:64, :], QKT0.ap[8, 0:64, :], reads=[QKT0.b], writes=[KAz[0].b])
        s.dma("sp", KAz[1][64:128, :], QKT0.ap[8, 64:128, :], reads=[QKT0.b], writes=[KAz[1].b])
        s.dma("sp", VAs[:], VA.ap.rearrange("(k p) e -> p k e", p=128), reads=[VA.b.part(t) for t in range(NT)],
              writes=[VAs.b])
        esink = ph.sb("esink", [128, 8], F32)
        s.op("act", "activation", reads=[vecb.b], writes=[esink.b], out=esink[:], in_=vecb[:, V_SINK:V_SINK + 8], func=AF.Exp)
        zr = ph.ring("za", [128, 4], F32, 2)
        rzr = ph.ring("rza", [128, 4], F32, 2)
        obr = ph.ring("oba", [128, 4, 64], BF16, 3)

        def fin_a(u, acc):
            kvh, n = u["kvh"], u["n"]
            z = zr.next()
            s.op("dve", "tensor_tensor", reads=[acc.b, esink.b], writes=[z.b], out=z[:], in0=acc[:, :, 64],
                 in1=esink[:, kvh * 4:(kvh + 1) * 4], op=ALU.add)
            rz = rzr.next()
            s.op("dve", "reciprocal", reads=[z.b], writes=[rz.b], out=rz[:], in_=z[:])
            ob = obr.next()
            s.op("dve", "tensor_tensor", reads=[acc.b, rz.b], writes=[ob.b], out=ob[:], in0=acc[:, :, 0:64],
                 in1=rz[:, :].unsqueeze(2).to_broadcast([128, 4, 64]), op=ALU.mult)
            s.dma("sp", MIX.ap[n * 128:(n + 1) * 128, kvh * 256:(kvh + 1) * 256], ob[:].rearrange("p g d -> p (g d)"),
                  reads=[ob.b], writes=[MIX.b.part(("a", n, kvh))])

        units = []
        for n in range(NT):
            if n < NLAT:
                kl = []
                if n > 0:
                    kl.append((n - 1, mprev))
                kl.append((n, None))
                if n < NLAT - 1:
                    kl.append((n + 1, mnext))
                kl += [(32, None), (33, None)]
            else:
                kl = [(32, None), (33, None)]
            for kvh in range(2):
                keys = [dict(kT=KAz[kvh][:, kt * 128:(kt + 1) * 128], kb=[KAz[kvh].b], v=VAs[:, kt, kvh * 65:(kvh + 1) * 65],
                             vb=[VAs.b], mask=mk) for (kt, mk) in kl]
                units.append(dict(qT=QA[:, :, n * 128:(n + 1) * 128], qb=[QA.b], keys=keys, nq=512, vd1=65,
                                  scale=0.125, fin=fin_a, kvh=kvh, n=n))
        attn_pipeline(ph, units, 5)
        ph.close()
        if stop_after == "L0P2A":
            s.finish()
            return nc

        lam_init0 = 0.8 - 0.6 * math.exp(-0.3 * 0)
        ph = Phase(nc, s)
        lt = ph.sb("lt", [128, 128], F32)
        lsum = ph.sb("lsum", [128, 2], F32)
        lexp = ph.sb("lexp", [128, 2], F32)
        nlam = ph.sb("nlam", [128, 1], F32)
        s.op("dve", "tensor_tensor", reads=[vecb.b], writes=[lt.b], out=lt[:, 0:64], in0=vecb[:, V_LQ1:V_LQ1 + 64],
             in1=vecb[:, V_LK1:V_LK1 + 64], op=ALU.mult)
        s.op("dve", "tensor_tensor", reads=[vecb.b], writes=[lt.b], out=lt[:, 64:128], in0=vecb[:, V_LQ2:V_LQ2 + 64],
             in1=vecb[:, V_LK2:V_LK2 + 64], op=ALU.mult)
        s.op("dve", "tensor_reduce", reads=[lt.b], writes=[lsum.b], out=lsum[:],
             in_=lt[:, :].rearrange("p (a d) -> p a d", d=64), axis=AX.X, op=ALU.add)
        s.op("act", "activation", reads=[lsum.b], writes=[lexp.b], out=lexp[:], in_=lsum[:], func=AF.Exp)
        s.op("dve", "tensor_tensor", reads=[lexp.b], writes=[nlam.b], out=nlam[:], in0=lexp[:, 1:2], in1=lexp[:, 0:1],
             op=ALU.subtract)
        s.op("dve", "tensor_scalar", reads=[nlam.b], writes=[nlam.b], out=nlam[:], in0=nlam[:], scalar1=-lam_init0,
             scalar2=None, op0=ALU.add)
        subl = ph.sb("subl", [128, 128], F32)
        s.op("dve", "tensor_scalar", reads=[vecb.b], writes=[subl.b], out=subl[:], in0=vecb[:, V_SUBLN:V_SUBLN + 128],
             scalar1=1.0 - lam_init0, scalar2=None, op0=ALU.mult)
        QBr = ph.ring("QB", [128, NTOK], BF16, 2)
        KBz = [[ph.sb(f"KBz{i}_{j}", [128, NTOK], BF16) for j in range(2)] for i in range(2)]
        for i in range(2):
            s.op("pool", "memset", writes=[KBz[i][0].b], ap=KBz[i][0][64:128, :], constant=0.0)
            s.op("pool", "memset", writes=[KBz[i][1].b], ap=KBz[i][1][0:64, :], constant=0.0)
        VBr = ph.ring("VBs", [128, NT, 129], BF16, 2)
        o1r = ph.ring("o1", [128, 4, 128], F32, 2)
        z1r = ph.ring("zb", [128, 4], F32, 4)
        tbr = ph.ring("tb", [128, 4, 128], F32, 2)
        obbr = ph.ring("obb", [128, 4, 128], F32, 2)
        sqbr = ph.ring("sqb", [128, 4, 128], F32, 1)
        ssbr = ph.ring("ssb", [128, 4], F32, 2)
        onr = ph.ring("onb", [128, 4, 128], F32, 1)
        outbr = ph.ring("outb", [128, 4, 128], BF16, 3)
        state = {}

        def fin_b(u, acc):
            h, q0, nj, sidx = u["h"], u["q0"], u["nq"] // 128, u["s"]
            rz = z1r.next()
            s.op("dve", "reciprocal", reads=[acc.b], writes=[rz.b], out=rz[:, 0:nj], in_=acc[:, 0:nj, 128])
            if sidx == 0:
                o1 = o1r.next()
                s.op("dve", "tensor_tensor", reads=[acc.b, rz.b], writes=[o1.b], out=o1[:, 0:nj, :], in0=acc[:, 0:nj, 0:128],
                     in1=rz[:, 0:nj].unsqueeze(2).to_broadcast([128, nj, 128]), op=ALU.mult)
                state["o1"] = o1
                return
            o1 = state["o1"]
            rzl = z1r.next()
            s.op("dve", "tensor_scalar", reads=[rz.b, nlam.b], writes=[rzl.b], out=rzl[:, 0:nj], in0=rz[:, 0:nj],
                 scalar1=nlam[:, 0:1], scalar2=None, op0=ALU.mult)
            tb = tbr.next()
            s.op("dve", "tensor_tensor", reads=[acc.b, rzl.b], writes=[tb.b], out=tb[:, 0:nj, :], in0=acc[:, 0:nj, 0:128],
                 in1=rzl[:, 0:nj].unsqueeze(2).to_broadcast([128, nj, 128]), op=ALU.mult)
            ob = obbr.next()
            s.op("pool", "tensor_tensor", reads=[tb.b, o1.b], writes=[ob.b], out=ob[:, 0:nj, :], in0=tb[:, 0:nj, :],
                 in1=o1[:, 0:nj, :], op=ALU.add)
            sq = sqbr.next()
            s.op("pool", "tensor_tensor", reads=[ob.b], writes=[sq.b], out=sq[:, 0:nj, :], in0=ob[:, 0:nj, :],
                 in1=ob[:, 0:nj, :], op=ALU.mult)
            ss = ssbr.next()
            s.op("dve", "tensor_reduce", reads=[sq.b], writes=[ss.b], out=ss[:, 0:nj], in_=sq[:, 0:nj, :], axis=AX.X,
                 op=ALU.add)
            r = rstd_from_ss(ph, ss, nj, 128, "rs_b%d" % nj)
            on = onr.next()
            s.op("dve", "tensor_tensor", reads=[ob.b, r.b], writes=[on.b], out=on[:, 0:nj, :], in0=ob[:, 0:nj, :],
                 in1=r[:, 0:nj].unsqueeze(2).to_broadcast([128, nj, 128]), op=ALU.mult)
            out = outbr.next()
            s.op("pool", "tensor_tensor", reads=[on.b, subl.b], writes=[out.b], out=out[:, 0:nj, :], in0=on[:, 0:nj, :],
                 in1=subl[:, :].unsqueeze(1).to_broadcast([128, nj, 128]), op=ALU.mult)
            s.dma("sp", MIX.ap[q0:q0 + nj * 128, 512 + h * 128:512 + (h + 1) * 128].rearrange("(j p) d -> p j d", p=128),
                  out[:, 0:nj, :], reads=[out.b], writes=[MIX.b.part(("b", h, q0))])

        def load_b(h):
            Qh, Kz, Vh = QBr.items[h % 2], KBz[h % 2], VBr.items[h % 2]
            s.dma("sp", Qh[:], QKT0.ap[4 + h], reads=[QKT0.b], writes=[Qh.b])
            s.dma("sp", Kz[0][0:64, :], QKT0.ap[9 + h, 0:64, :], reads=[QKT0.b], writes=[Kz[0].b])
            s.dma("sp", Kz[1][64:128, :], QKT0.ap[9 + h, 64:128, :], reads=[QKT0.b], writes=[Kz[1].b])
            s.dma("sp", Vh[:], VB.ap.rearrange("(k p) (h e) -> p k h e", p=128, e=129)[:, :, h, :],
                  reads=[VB.b.part(t) for t in range(NT)], writes=[Vh.b])

        units = []
        for h in range(4):
            Qh, Kz, Vh = QBr.items[h % 2], KBz[h % 2], VBr.items[h % 2]
            blocks = [(qb * 512, 512, list(range(NT))) for qb in range(8)] + [(S, 256, [32, 33])]
            first = len(units)
            for (q0, nq, kl) in blocks:
                for sidx in range(2):
                    Kh = Kz[sidx]
                    keys = [dict(kT=Kh[:, kt * 128:(kt + 1) * 128], kb=[Kh.b], v=Vh[:, kt, :], vb=[Vh.b], mask=None)
                            for kt in kl]
                    units.append(dict(qT=Qh[:, q0:q0 + nq], qb=[Qh.b], keys=keys, nq=nq, vd1=129, scale=0.125,
                                      fin=fin_b, h=h, q0=q0, s=sidx))
            if h == 0:
                units[first]["pre"] = (lambda: load_b(0))
            if h < 3:
                units[first + 1]["pre"] = (lambda hh=h + 1: load_b(hh))
        attn_pipeline(ph, units, NT)
        ph.close()
        if stop_after == "L0P2B":
            s.finish()
            return nc

        def phase3a(l, Xin, Xout, ntiles, Wm):
            ph = Phase(nc, s)
            G2L, S2L = mod_tiles_bc(ph, l, 0, 3, 4, n2g, "l")
            h1r_ = ph.ring("h1", [128, D], F32, 1)
            gateL = load_bcast(ph, "gateL", modV.ap[l, 0, 2 * D:3 * D], D, modV.b)
            if ntiles > NLAT:
                G2C, S2C = mod_tiles_bc(ph, l, 1, 3, 4, n2g, "c")
                gateC = load_bcast(ph, "gateC", modV.ap[l, 1, 2 * D:3 * D], D, modV.b)
            nr = norm_rings(ph)
            xr = ph.ring("x", [128, D], F32, 5)
            mr = ph.ring("mx", [128, D], BF16, 3)
            mTr = ph.ring("mT", [128, 8, 128], BF16, 2)
            tmr = ph.ring("tm", [128, D], F32, 1)
            x1r = ph.ring("x1", [128, D], F32, 4)
            hst = ph.ring("hst", [128, 8, 512], BF16, 2)
            pT = ph.ps("pT", [128, 8, 128], BF16)
            pT2 = ph.ps("pT2", [128, 8, 128], BF16)
            pP = ph.ps("pP", [128, D], F32)
            def tile_p3a(t):
                g0 = (t // 4) * 4
                ti = t - g0
                ntile_g = min(ntiles, g0 + 4) - g0
                st_ = hst.items[(t // 4) % 2]
                isctx = t >= NLAT
                xt = xr.next()
                s.dma("sp", xt[:], Xin.ap[t * 128:(t + 1) * 128, :], reads=[Xin.b.part(t)], writes=[xt.b])
                mx = mr.next()
                s.dma("sp", mx[:], MIX.ap[t * 128:(t + 1) * 128, :], reads=[MIX.b] + list(MIX.b.parts.values()),
                      writes=[mx.b])
                yield
                for k in range(8):
                    s.op("pe", "transpose", reads=[mx.b, idb.b], writes=[pT.b], out=pT[:, k, :],
                         in_=mx[:, k * 128:(k + 1) * 128], identity=idb[:], inc=(k == 7))
                yield
                mT = mTr.next()
                s.op("act", "copy", reads=[pT.b], writes=[mT.b], out=mT[:], in_=pT[:])
                yield
                for k in range(8):
                    for c in range(2):
                        s.op("pe", "matmul", reads=[mT.b] + Wm.kparts, writes=[pP.b], out=pP[:, c * 512:(c + 1) * 512],
                             lhsT=mT[:, k, :], rhs=Wm[:, k, c * 512:(c + 1) * 512], start=(k == 0), stop=(k == 7))
                yield
                tm = tmr.next()
                gate = gateC if isctx else gateL
                s.op("dve", "tensor_tensor", reads=[pP.b, gate.b], writes=[tm.b], out=tm[:], in0=pP[:], in1=gate[:],
                     op=ALU.mult)
                x1 = x1r.next()
                s.op("pool", "tensor_tensor", reads=[tm.b, xt.b], writes=[x1.b], out=x1[:], in0=tm[:], in1=xt[:],
                     op=ALU.add)
                s.dma("sp", Xout.ap[t * 128:(t + 1) * 128, :], x1[:], reads=[x1.b], writes=[Xout.b.part(t)])
                yield
                yield from norm_mod_transpose_tm(ph, x1, G2C if isctx else G2L, S2C if isctx else S2L, pT2,
                                                 st_[:, :, ti * 128:(ti + 1) * 128], st_.b, nr, h1r_)
                if ti == ntile_g - 1:
                    ntk = ntile_g * 128
                    s.dma("sp", H2T.ap[:, :, g0 * 128:g0 * 128 + ntk].rearrange("j p t -> p j t"), st_[:, :, 0:ntk],
                          reads=[st_.b], writes=[H2T.b.part(g0)])

            run_pipelined((tile_p3a(t) for t in range(ntiles)), first_stage=5)
            ph.close()

        def phase3b(l, Xin, Xout, ntiles):
            ph = Phase(nc, s)
            Wi = load_w_bf16_cols(ph, "wfi", wfi.ap[l], 8, 2 * FH, wfi.b, 512, [0, 5, 6, 1, 7, 2, 8, 3, 9, 4, 10])
            Wo = load_w_bf16(ph, "wfo", wfo.ap[l], 22, D, wfo.b)
            gate = load_bcast(ph, "gate", modV.ap[l, 0, 5 * D:6 * D], D, modV.b)
            h2r = ph.ring("h2", [128, 8, 512], BF16, 1)
            actT = ph.sb("actT", [128, 22, 512], BF16)
            sgr = ph.ring("sg", [128, 512], F32, 2)
            xr = ph.ring("x", [128, D], F32, 1)
            x2r = ph.ring("x2", [128, D], F32, 2)
            pG = [ph.ps(f"pG{i}", [128, 512], F32) for i in range(2)]
            pU = [ph.ps(f"pU{i}", [128, 512], F32) for i in range(2)]
            pY = [ph.ps(f"pY{i}", [128, D], F32) for i in range(2)]
            yi = 0
            for g0 in range(0, ntiles, 4):
                tiles = list(range(g0, min(ntiles, g0 + 4)))
                ntk = len(tiles) * 128
                if tiles[0] >= NLAT:
                    s.dma("sp", gate[:], modV.ap[l, 1, 5 * D:6 * D].partition_broadcast(128), reads=[modV.b],
                          writes=[gate.b])
                h2 = h2r.next()
                if g0 == 0:
                    s.dma("sp", h2[:, :, 0:ntk], H2T.ap[:, :, 0:ntk].rearrange("j p t -> p j t"),
                          reads=[H2T.b.part(0)], writes=[h2.b])
                for j in range(22):
                    pg, pu = pG[j % 2], pU[j % 2]
                    for k in range(8):
                        s.op("pe", "matmul", reads=[h2.b, Wi.cpart(j * 128)], writes=[pg.b], out=pg[:, 0:ntk],
                             lhsT=Wi[:, k, j * 128:(j + 1) * 128], rhs=h2[:, k, 0:ntk], start=(k == 0), stop=(k == 7))
                    for k in range(8):
                        s.op("pe", "matmul", reads=[h2.b, Wi.cpart(FH + j * 128)], writes=[pu.b], out=pu[:, 0:ntk],
                             lhsT=Wi[:, k, FH + j * 128:FH + (j + 1) * 128], rhs=h2[:, k, 0:ntk], start=(k == 0),
                             stop=(k == 7))
                    sg = sgr.next()
                    s.op("act", "activation", reads=[pg.b], writes=[sg.b], out=sg[:, 0:ntk], in_=pg[:, 0:ntk], func=AF.Silu)
                    s.op("dve", "tensor_tensor", reads=[pu.b, sg.b], writes=[actT.b.part(j)], out=actT[:, j, 0:ntk],
                         in0=pu[:, 0:ntk], in1=sg[:, 0:ntk], op=ALU.mult)
                if g0 + 4 < ntiles:
                    n0_ = g0 + 4
                    ntk2 = (min(ntiles, n0_ + 4) - n0_) * 128
                    s.dma("act", h2[:, :, 0:ntk2], H2T.ap[:, :, n0_ * 128:n0_ * 128 + ntk2].rearrange("j p t -> p j t"),
                          reads=[H2T.b.part(n0_)], writes=[h2.b])
                for ti, t in enumerate(tiles):
                    isctx = t >= NLAT
                    py = pY[yi % 2]
                    yi += 1
                    for c in range(2):
                        for j in range(22):
                            s.op("pe", "matmul", reads=[actT.b.part(j)] + Wo.kparts, writes=[py.b],
                                 out=py[:, c * 512:(c + 1) * 512], lhsT=actT[:, j, ti * 128:(ti + 1) * 128],
                                 rhs=Wo[:, j, c * 512:(c + 1) * 512], start=(j == 0), stop=(j == 21))
                    xt = xr.next()
                    s.dma("sp", xt[:], Xin.ap[t * 128:(t + 1) * 128, :], reads=[Xin.b.part(t)], writes=[xt.b])
                    x2 = x2r.next()
                    s.op("dve", "tensor_tensor", reads=[py.b, gate.b], writes=[x2.b], out=x2[:], in0=py[:], in1=gate[:],
                         op=ALU.mult)
                    s.op("pool", "tensor_tensor", reads=[x2.b, xt.b], writes=[x2.b], out=x2[:], in0=x2[:], in1=xt[:],
                         op=ALU.add)
                    s.dma("sp", Xout.ap[t * 128:(t + 1) * 128, :], x2[:], reads=[x2.b], writes=[Xout.b.part(t)])
            ph.close()

        phase3a(0, xs, X1A, NT, Wm0)
        preA.close()
        if stop_after == "L0P3A":
            s.finish()
            return nc
        phase3b(0, X1A, X1, NT)
        if stop_after == "L0":
            s.finish()
            return nc

        ph = Phase(nc, s)
        W1 = load_w_bf16(ph, "wod", wod.ap, 8, 1600, wod.b)
        Wq = load_w_bf16(ph, "wuq", wuq.ap, 2, 512, wuq.b)
        Wkv = load_w_bf16(ph, "wukv", wukv.ap, 2, 768, wukv.b)
        Ws = ph.sb("ws", [128, 4, 128], BF16)
        s.dma("pool", Ws[:], wsT.ap.rearrange("g q p -> q g p"), reads=[wsT.b], writes=[Ws.b])
        bs = ph.sb("bs", [128, 4], F32)
        s.dma("sp", bs[:], bsT.ap, writes=[bs.b])
        GL, SL = mod_tiles_bc(ph, 1, 0, 0, 1, n1g, "l")
        GC, SC = mod_tiles_bc(ph, 1, 1, 0, 1, n1g, "c")
        h1r_ = ph.ring("h1", [128, D], F32, 1)
        cvsr = ph.ring("cvs", [128, 512], F32, 2)
        G13 = ph.sb("g13", [128, 13 * 64], F32)
        g13v = G13[:, :].rearrange("p (h d) -> p h d", d=64)
        g13q = G13[:, 0:512].rearrange("p (h t d) -> p h t d", t=2, d=64)
        s.op("dve", "tensor_copy", reads=[vecb.b], writes=[G13.b], out=g13q[:, :, 0, :],
             in_=vecb[:, V_QNN:V_QNN + 64].unsqueeze(1).to_broadcast([128, 4, 64]))
        s.op("dve", "tensor_copy", reads=[vecb.b], writes=[G13.b], out=g13q[:, :, 1, :],
             in_=vecb[:, V_QNR:V_QNR + 64].unsqueeze(1).to_broadcast([128, 4, 64]))
        s.op("dve", "tensor_copy", reads=[vecb.b], writes=[G13.b], out=g13v[:, 8:12, :],
             in_=vecb[:, V_KNN:V_KNN + 64].unsqueeze(1).to_broadcast([128, 4, 64]))
        s.op("dve", "tensor_copy", reads=[vecb.b], writes=[G13.b], out=g13v[:, 12:13, :],
             in_=vecb[:, V_KNR:V_KNR + 64].unsqueeze(1).to_broadcast([128, 1, 64]))
        nr = norm_rings(ph)
        head_rings(ph, 832)
        xr = ph.ring("x", [128, D], F32, 4)
        rpr = ph.ring("rp", [128, 128], F32, 11)
        hTr = ph.ring("hT", [128, 8, 128], BF16, 2)
        glr = ph.ring("gl", [128, D], BF16, 5)
        vsqr = ph.ring("vsq", [128, 512], F32, 1)
        ssvr = ph.ring("ssv", [128, 2], F32, 2)
        vnr = ph.ring("vn", [128, 512], BF16, 2)
        mcr = ph.ring("mc", [128, 512], BF16, 2)
        cqfr = ph.ring("cqf", [128, 512], F32, 3)
        cnr = ph.ring("cn", [128, 512], F32, 1)
        cnbr = ph.ring("cnb", [128, 512], BF16, 2)
        cTr = ph.ring("cT", [128, 4, 128], BF16, 2)
        hqr = ph.ring("hq", [128, 832], F32, 3)
        qcbr = ph.ring("qcb", [128, 4, 128], BF16, 2)
        kcbr = ph.ring("kcb", [128, 4, 128], BF16, 2)
        kper = ph.ring("kpe", [128, 64], F32, 7)
        kprr = ph.ring("kpr", [128, 64], F32, 2)
        vdr = ph.ring("vd", [128, 4, 129], BF16, 2)
        for t_ in vdr.items:
            s.op("pool", "memset", writes=[t_.b], ap=t_[:], constant=1.0)
        qkst = ph.ring("qkst", [128, 8, 512], BF16, 2)
        pT = ph.ps("pT", [128, 8, 128], BF16)
        pO1 = ph.ps("pO1", [128, 2048], F32)
        pQ = ph.ps("pQ", [128, 512], F32)
        pKV = ph.ps("pKV", [128, 1024], F32)
        def cps(g):
            return pO1[:, 1600 + g * 128:1728 + g * 128] if g < 3 else pKV[:, 768:896]

        def cpb(g):
            return pO1.b if g < 3 else pKV.b

        ss3r = ph.ring("ss3", [128, 4], F32, 2)
        v3r = ph.ring("v3", [128, 4], F32, 2)
        r3r = ph.ring("r3", [128, 4], F32, 2)
        jk2r = ph.ring("jk2", [128, 512], BF16, 3)
        for t_ in ss3r.items:
            s.op("pool", "memset", writes=[t_.b], ap=t_[:], constant=1.0)

        def tile_l1p1(t):
            g0 = (t // 4) * 4
            ti = t - g0
            ntile_g = min(NT, g0 + 4) - g0
            st_ = qkst.items[(t // 4) % 2]
            isctx = t >= NLAT
            xt = xr.next()
            s.dma("sp", xt[:], X1.ap[t * 128:(t + 1) * 128, :], reads=[X1.b.part(t)], writes=[xt.b])
            yield
            hT = hTr.next()
            yield from norm_mod_transpose_tm(ph, xt, GC if isctx else GL, SC if isctx else SL, pT, hT[:], hT.b, nr, h1r_)
            yield
            chunks = [(1280, 1536), (1536, 1600)] if isctx else [(0, 512), (512, 1024), (1024, 1536), (1536, 1600)]
            for k in range(8):
                for (n0, n1) in chunks:
                    s.op("pe", "matmul", reads=[hT.b] + W1.kparts, writes=[pO1.b], out=pO1[:, n0:n1],
                         lhsT=hT[:, k, :], rhs=W1[:, k, n0:n1], start=(k == 0), stop=(k == 7))
            if not isctx:
                rp = rpr.next()
                s.dma("sp", rp[:], rope.ap[t], writes=[rp.b])
            yield
            cqf = cqfr.next()
            c0 = 256 if isctx else 0
            s.op("act", "copy", reads=[pO1.b], writes=[cqf.b], out=cqf[:, c0:512], in_=pO1[:, 1024 + c0:1536])
            kpe = kper.next()
            s.op("act", "copy", reads=[pO1.b], writes=[kpe.b], out=kpe[:], in_=pO1[:, 1536:1600])
            ss3 = ss3r.next()
            if not isctx:
                gl = glr.next()
                s.op("act", "activation", reads=[pO1.b], writes=[gl.b], out=gl[:], in_=pO1[:, 0:1024], func=AF.Gelu)
                jk = jk2r.next()
                s.op("act", "activation", reads=[gl.b], writes=[jk.b, ss3.b], out=jk[:], in_=gl[:, 512:1024], func=AF.Square,
                     accum_out=ss3[:, 0:1])
                jk = jk2r.next()
                s.op("act", "activation", reads=[cqf.b], writes=[jk.b, ss3.b], out=jk[:, 0:256], in_=cqf[:, 0:256],
                     func=AF.Square, accum_out=ss3[:, 1:2])
            jk = jk2r.next()
            s.op("act", "activation", reads=[cqf.b], writes=[jk.b, ss3.b], out=jk[:, 0:256], in_=cqf[:, 256:512],
                 func=AF.Square, accum_out=ss3[:, 2:3])
            yield
            v3 = v3r.next()
            s.op("dve", "tensor_scalar", reads=[ss3.b], writes=[v3.b], out=v3[:, 0:1], in0=ss3[:, 0:1], scalar1=1.0 / 512,
                 scalar2=EPS, op0=ALU.mult, op1=ALU.add)
            s.op("dve", "tensor_scalar", reads=[ss3.b], writes=[v3.b], out=v3[:, 1:3], in0=ss3[:, 1:3], scalar1=1.0 / 256,
                 scalar2=EPS, op0=ALU.mult, op1=ALU.add)
            r3 = r3r.next()
            s.op("pool", "tensor_tensor", reads=[v3.b, nh.b], writes=[r3.b], out=r3[:, 0:3], in0=v3[:, 0:3], in1=nh[:, 0:3],
                 op=ALU.pow)
            yield
            cnb = cnbr.next()
            if not isctx:
                vn = vnr.next()
                s.op("dve", "scalar_tensor_tensor", reads=[gl.b, r3.b, vecb.b], writes=[vn.b], out=vn[:],
                     in0=gl[:, 512:1024], scalar=r3[:, 0:1], in1=vecb[:, V_CVN:V_CVN + 512], op0=ALU.mult, op1=ALU.mult)
                s.op("dve", "scalar_tensor_tensor", reads=[cqf.b, r3.b, vecb.b], writes=[cnb.b], out=cnb[:, 0:256],
                     in0=cqf[:, 0:256], scalar=r3[:, 1:2], in1=vecb[:, V_QAN:V_QAN + 256], op0=ALU.mult, op1=ALU.mult)
            s.op("dve", "scalar_tensor_tensor", reads=[cqf.b, r3.b, vecb.b], writes=[cnb.b], out=cnb[:, 256:512],
                 in0=cqf[:, 256:512], scalar=r3[:, 2:3], in1=vecb[:, V_KVAN:V_KVAN + 256], op0=ALU.mult, op1=ALU.mult)
            yield
            if not isctx:
                for g in range(4):
                    s.op("pe", "matmul", reads=[vn.b, Ws.b], writes=[cpb(g)], out=cps(g),
                         lhsT=Ws[:, g, :], rhs=vn[:, g * 128:(g + 1) * 128], start=True, stop=True)
            kc0 = c0 // 128
            for k in range(kc0, 4):
                s.op("pe", "transpose", reads=[cnb.b, idb.b], writes=[pT.b], out=pT[:, k, :],
                     in_=cnb[:, k * 128:(k + 1) * 128], identity=idb[:], inc=(k == 3))
            cT = cTr.next()
            s.op("act", "copy", reads=[pT.b], writes=[cT.b], out=cT[:, kc0:4, :], in_=pT[:, kc0:4, :])
            if not isctx:
                cvs = cvsr.next()
                s.op("act", "copy", reads=[pO1.b], writes=[cvs.b], out=cvs[:, 0:384], in_=pO1[:, 1600:1984])
                s.op("act", "copy", reads=[pKV.b], writes=[cvs.b], out=cvs[:, 384:512], in_=pKV[:, 768:896])
            yield
            if not isctx:
                mc_ = mcr.next()
                for g in range(4):
                    s.op("dve", "scalar_tensor_tensor", reads=[cvs.b, bs.b, gl.b], writes=[mc_.b],
                         out=mc_[:, g * 128:(g + 1) * 128], in0=cvs[:, g * 128:(g + 1) * 128], scalar=bs[:, g:g + 1],
                         in1=gl[:, g * 128:(g + 1) * 128], op0=ALU.add, op1=ALU.mult)
                s.dma("sp", MIX.ap[t * 128:(t + 1) * 128, 0:512], mc_[:], reads=[mc_.b], writes=[MIX.b.part(("c", t))])
                for k in range(2):
                    s.op("pe", "matmul", reads=[cT.b] + Wq.kparts, writes=[pQ.b], out=pQ[:], lhsT=cT[:, k, :],
                         rhs=Wq[:, k, :], start=(k == 0), stop=(k == 1))
            for k in range(2):
                for (n0, n1) in ((0, 512), (512, 768)):
                    s.op("pe", "matmul", reads=[cT.b] + Wkv.kparts, writes=[pKV.b], out=pKV[:, n0:n1],
                         lhsT=cT[:, 2 + k, :], rhs=Wkv[:, k, n0:n1], start=(k == 0), stop=(k == 1))
            yield
            vd = vdr.next()
            s.op("act", "copy", reads=[pKV.b], writes=[vd.b], out=vd[:, :, 0:128],
                 in_=pKV[:, 256:768].rearrange("p (h d) -> p h d", d=128))
            s.dma("sp", VD.ap[t * 128:(t + 1) * 128, :], vd[:].rearrange("p h d -> p (h d)"), reads=[vd.b],
                  writes=[VD.b.part(t)])
            hq = hqr.next()
            if not isctx:
                s.op("act", "copy", reads=[pQ.b], writes=[hq.b], out=hq[:, 0:512], in_=pQ[:])
            else:
                s.op("pool", "memset", writes=[hq.b], ap=hq[:, 0:512], constant=1.0)
            s.op("act", "copy", reads=[pKV.b], writes=[hq.b], out=hq[:, 512:768], in_=pKV[:, 0:256])
            s.op("act", "copy", reads=[kpe.b], writes=[hq.b], out=hq[:, 768:832], in_=kpe[:])
            qg = yield from head_norm(ph, hq, 13, G13, "1")
            yield
            qgv = qg[:, 0:832].rearrange("p (h d) -> p h d", d=64)
            kcb = kcbr.next()
            s.op("dve", "tensor_copy", reads=[qg.b], writes=[kcb.b], out=kcb[:, :, 0:64], in_=qgv[:, 8:12, :])
            if isctx:
                s.op("dve", "tensor_copy", reads=[qg.b], writes=[kcb.b], out=kcb[:, :, 64:128],
                     in_=qgv[:, 12:13, :].to_broadcast([128, 4, 64]))
            else:
                kpr = kprr.next()
                rope_apply(ph, qgv[:, 12:13, :], kpr[:, :].unsqueeze(1), 1, rp, [qg.b], kpr.b, t2eng="dve")
                s.op("dve", "tensor_copy", reads=[kpr.b], writes=[kcb.b], out=kcb[:, :, 64:128],
                     in_=kpr[:, :].unsqueeze(1).to_broadcast([128, 4, 64]))
                qcb = qcbr.next()
                qg4 = qg[:, 0:512].rearrange("p (h t d) -> p h t d", t=2, d=64)
                s.op("act", "copy", reads=[qg.b], writes=[qcb.b], out=qcb[:, :, 0:64], in_=qg4[:, :, 0, :])
                rope_apply(ph, qg4[:, :, 1, :], qcb[:, :, 64:128], 4, rp, [qg.b], qcb.b, t2eng="dve")
            yield
            if not isctx:
                for h in range(4):
                    s.op("pe", "transpose", reads=[qcb.b, idb.b], writes=[pT.b], out=pT[:, h, :], in_=qcb[:, h, :],
                         identity=idb[:])
            for h in range(4):
                s.op("pe", "transpose", reads=[kcb.b, idb.b], writes=[pT.b], out=pT[:, 4 + h, :], in_=kcb[:, h, :],
                     identity=idb[:], inc=(h == 3))
            b0 = 4 if isctx else 0
            s.op("act", "copy", reads=[pT.b], writes=[st_.b], out=st_[:, b0:8, ti * 128:(ti + 1) * 128], in_=pT[:, b0:8, :])
            if ti == ntile_g - 1:
                ntk = ntile_g * 128
                s.dma("sp", QKT1.ap[b0:8, :, g0 * 128:g0 * 128 + ntk].rearrange("j p t -> p j t"), st_[:, b0:8, 0:ntk],
                      reads=[st_.b], writes=[QKT1.b])

        run_pipelined((tile_l1p1(t) for t in range(NT)), first_stage=[5, 12, 7])
        ph.close()
        if stop_after == "L1P1":
            s.finish()
            return nc

        preB = Phase(nc, s)
        Wm1 = load_w_bf16(preB, "wmix1", wmix.ap[1], 8, D, wmix.b)

        ph = Phase(nc, s)
        QDr = ph.ring("QD", [128, S], BF16, 2)
        KDr = ph.ring("KD", [128, NTOK], BF16, 2)
        VDr = ph.ring("VDs", [128, NT, 129], BF16, 2)
        zdr = ph.ring("zd", [128, 4], F32, 2)
        odr = ph.ring("od", [128, 4, 128], BF16, 3)

        def fin_d(u, acc):
            h, q0 = u["h"], u["q0"]
            rz = zdr.next()
            s.op("dve", "reciprocal", reads=[acc.b], writes=[rz.b], out=rz[:], in_=acc[:, :, 128])
            od = odr.next()
            s.op("dve", "tensor_tensor", reads=[acc.b, rz.b], writes=[od.b], out=od[:], in0=acc[:, :, 0:128],
                 in1=rz[:, :].unsqueeze(2).to_broadcast([128, 4, 128]), op=ALU.mult)
            s.dma("sp", MIX.ap[q0:q0 + 512, 512 + h * 128:512 + (h + 1) * 128].rearrange("(j p) d -> p j d", p=128),
                  od[:], reads=[od.b], writes=[MIX.b.part(("d", h, q0))])

        def load_d(h):
            Qh, Kh, Vh = QDr.items[h % 2], KDr.items[h % 2], VDr.items[h % 2]
            s.dma("sp", Qh[:], QKT1.ap[h, :, 0:S], reads=[QKT1.b], writes=[Qh.b])
            s.dma("sp", Kh[:], QKT1.ap[4 + h], reads=[QKT1.b], writes=[Kh.b])
            s.dma("sp", Vh[:], VD.ap.rearrange("(k p) (h e) -> p k h e", p=128, e=129)[:, :, h, :],
                  reads=[VD.b.part(t) for t in range(NT)], writes=[Vh.b])

        units = []
        for h in range(4):
            Qh, Kh, Vh = QDr.items[h % 2], KDr.items[h % 2], VDr.items[h % 2]
            first = len(units)
            for qb in range(8):
                keys = [dict(kT=Kh[:, kt * 128:(kt + 1) * 128], kb=[Kh.b], v=Vh[:, kt, :], vb=[Vh.b], mask=None)
                        for kt in range(NT)]
                units.append(dict(qT=Qh[:, qb * 512:(qb + 1) * 512], qb=[Qh.b], keys=keys, nq=512, vd1=129,
                                  scale=128.0 ** -0.5, fin=fin_d, h=h, q0=qb * 512))
            if h == 0:
                units[first]["pre"] = (lambda: load_d(0))
            if h < 3:
                units[first + 1]["pre"] = (lambda hh=h + 1: load_d(hh))
        attn_pipeline(ph, units, NT)
        ph.close()
        if stop_after == "L1P2":
            s.finish()
            return nc

        phase3a(1, X1, X1A, NLAT, Wm1)
        preB.close()
        phase3b(1, X1A, y, NLAT)
        gp.close()
        s.finish()
        print("program: instructions", s.n_ins, "waits", s.n_wait)
    return nc


def _rope_tables():
    rows = S // 64
    row = np.repeat(np.arange(rows, dtype=np.int32), 64).astype(np.float32)
    col = np.tile(np.arange(64, dtype=np.int32), rows).astype(np.float32)
    inv = (np.float32(10000.0) ** (-np.arange(16, dtype=np.float32) / np.float32(16))).astype(np.float32)
    ang = np.concatenate([row[:, None] * inv, col[:, None] * inv], axis=-1).astype(np.float32)
    c, sn = np.cos(ang).astype(np.float32), np.sin(ang).astype(np.float32)
    tab = np.concatenate([c, c, -sn, sn], axis=-1)
    return np.ascontiguousarray(tab.reshape(NLAT, 128, 128))


def _shared_inputs(inp):
    f = lambda a: np.ascontiguousarray(np.asarray(a, dtype=np.float32))
    ev = f(inp["ev_w_in"])[0]
    aq = ev[:, 0:512].reshape(D, 8, 64)
    aq_p = np.stack([aq[:, [j, 4 + j], :] for j in range(4)], axis=1).reshape(D, 512)
    wev = np.concatenate([aq_p, ev[:, 512:1024], ev[:, 1024:1152], ev[:, 1280:1792], ev[:, 1152:1280], ev[:, 1792:2304]], axis=1)
    ukv = f(inp["od_w_ukv"])[0].reshape(256, 4, 192)
    wukv = np.concatenate([ukv[:, :, 0:64].reshape(256, 256), ukv[:, :, 64:192].reshape(256, 512)], axis=1)
    vec = np.concatenate([
        f(inp["ev_qnorm_a"])[0], f(inp["ev_knorm_a"])[0], f(inp["ev_qnorm_b"])[0], f(inp["ev_knorm_b"])[0],
        f(inp["ev_sink"])[0], f(inp["ev_lam_q1"])[0], f(inp["ev_lam_k1"])[0], f(inp["ev_lam_q2"])[0], f(inp["ev_lam_k2"])[0],
        f(inp["ev_subln"])[0], f(inp["od_c_vnorm"])[0], f(inp["od_qa_norm"])[0], f(inp["od_kva_norm"])[0],
        f(inp["od_qnorm_nope"])[0], f(inp["od_knorm_nope"])[0], f(inp["od_qnorm_rope"])[0], f(inp["od_knorm_rope"])[0]])
    assert vec.shape[0] == NV
    j = np.arange(128)[:, None]
    i = np.arange(128)[None, :]
    masks = np.stack([(j >= i), (j <= i)]).astype(np.float32)
    return {
        "ada_w": f(inp["ada_w"]), "ada_b": f(inp["ada_b"]).reshape(2, 1, 6 * D),
        "n1g": f(inp["norm1_g"]).reshape(2, 1, D), "n2g": f(inp["norm2_g"]).reshape(2, 1, D),
        "wmix": f(inp["mix_w_out"]), "wfi": f(inp["ffn_w_in"]), "wfo": f(inp["ffn_w_out"]),
        "wev": f(wev), "wod": f(inp["od_w_in"])[0], "wuq": f(inp["od_w_uq"])[0], "wukv": f(wukv),
        "wsT": f(np.transpose(f(inp["od_c_ws"])[0], (0, 2, 1))), "bsT": f(f(inp["od_c_bs"])[0].T),
        "vecs": f(vec.reshape(1, NV)), "ident": np.eye(128, dtype=np.float32), "rope": _rope_tables(), "masks": masks,
    }


def make_in_maps(inp):
    shared = _shared_inputs(inp)
    x = np.asarray(inp["x"], dtype=np.float32)
    ctx = np.asarray(inp["ctx"], dtype=np.float32)
    c = np.asarray(inp["c"], dtype=np.float32)
    c_ctx = np.asarray(inp["c_ctx"], dtype=np.float32)
    maps = []
    for b in range(x.shape[0]):
        m = dict(shared)
        m["xs"] = np.ascontiguousarray(np.concatenate([x[b], ctx[b]], axis=0))
        cc = np.stack([c[b], c_ctx], axis=-1).reshape(8, 128, 2).transpose(1, 0, 2)
        m["cc"] = np.ascontiguousarray(cc)
        maps.append(m)
    return maps


_NC_CACHE = {}


def kernel(**inputs):
    if "nc" not in _NC_CACHE:
        _NC_CACHE["nc"] = build_program()
    nc = _NC_CACHE["nc"]
    in_maps = make_in_maps(inputs)
    res = run_bass_kernel_spmd(nc, in_maps, core_ids=list(range(len(in_maps))))
    return np.stack([np.asarray(r["y"], dtype=np.float32) for r in res.results], axis=0)
```
